# Optimizing a Trainium2 kernel written in Bass

```python
import math
import numpy as np
import jax
import jax.numpy as jnp
from jax import lax

D_MODEL = 2048
BATCH = 4
SEQ = 4096
DEPTH = 2

PLE_DIM = 256
N_EVEN = (DEPTH + 1) // 2
N_ODD = DEPTH // 2
DN_ALPHA = (2.0 * DEPTH) ** 0.25
DN_BETA = (8.0 * DEPTH) ** -0.25
LN_EPS = 1e-5
RMS_EPS = 1e-6
A_HEADS = 16
A_HEAD_DIM = 64
A_WIDTH = A_HEADS * A_HEAD_DIM
A_W_LORA = 96
A_A_LORA = 96
A_G_LORA = 256
A_GN_EPS = 64e-5
A_SPLITS = (A_WIDTH, A_WIDTH, A_WIDTH, A_W_LORA, A_A_LORA, A_G_LORA)
A_COLS = sum(A_SPLITS)
B_HEADS = 8
B_HEAD_DIM = 128
B_WIDTH = B_HEADS * B_HEAD_DIM
B_CONV = 4
B_CHUNK = 64
B_SPLITS = (B_WIDTH, B_WIDTH, B_WIDTH, B_WIDTH, B_HEADS, B_HEADS)
C_HEADS = 32
C_HEAD_DIM = 64
C_WIDTH = C_HEADS * C_HEAD_DIM
C_GROUPS = 4
C_STATE = 128
C_CONV = 4
C_CHUNK = 128
C_XBC = C_WIDTH + 2 * C_GROUPS * C_STATE
D_WIDTH = 2048
D_BLOCKS = 16
D_BLOCK_DIM = D_WIDTH // D_BLOCKS
D_CONV = 4
LRU_C = 8.0
D_FF = 5632
FFN_CONV = 3
EVEN_SPLITS = A_SPLITS + B_SPLITS
EVEN_IN = sum(EVEN_SPLITS)
EVEN_OUT = A_WIDTH + B_WIDTH
ODD_SPLITS = (C_WIDTH, C_XBC, C_HEADS, D_WIDTH, D_WIDTH)
ODD_IN = sum(ODD_SPLITS)
ODD_OUT = C_WIDTH + D_WIDTH

kernel_name = "hybrid_rwkv7_gdn_mamba2_rglru_trunk"


def _split(h, sizes):
    idx = [int(s) for s in np.cumsum(sizes)[:-1]]
    return jnp.split(h, idx, axis=-1)


def layer_norm(x, g, b, eps=LN_EPS):
    xf = x.astype(jnp.float32)
    mu = jnp.mean(xf, -1, keepdims=True)
    var = jnp.mean(jnp.square(xf - mu), -1, keepdims=True)
    return ((xf - mu) * lax.rsqrt(var + eps)).astype(x.dtype) * g + b


def rms_norm(x, g, eps=RMS_EPS):
    xf = x.astype(jnp.float32)
    y = xf * lax.rsqrt(jnp.mean(xf * xf, -1, keepdims=True) + eps)
    return y.astype(x.dtype) * g


def l2_normalize(x, eps=1e-6):
    xf = x.astype(jnp.float32)
    return (xf * lax.rsqrt(jnp.sum(xf * xf, -1, keepdims=True) + eps)).astype(x.dtype)


def causal_dwconv(x, w):
    K, C = w.shape
    return lax.conv_general_dilated(
        x, w[:, None, :].astype(x.dtype), window_strides=(1,), padding=[(K - 1, 0)],
        dimension_numbers=('NWC', 'WIO', 'NWC'), feature_group_count=C)


def token_shift_lerp(h, mu):
    prev = jnp.pad(h, ((0, 0), (1, 0), (0, 0)))[:, :-1]
    return h + (prev - h) * mu


def segsum(a):
    L = a.shape[-1]
    cs = jnp.cumsum(a, -1)
    return jnp.where(jnp.tril(jnp.ones((L, L), bool)), cs[..., :, None] - cs[..., None, :], -jnp.inf)


def linear_recurrence(a, u):
    def combine(left, right):
        a_l, u_l = left
        a_r, u_r = right
        return a_l * a_r, a_r * u_l + u_r
    _, h = lax.associative_scan(combine, (a, u), axis=1)
    return h


def rwkv7_scan(r, w, k, v, a, b):
    Bsz, T, H, N = r.shape
    tm = lambda t: jnp.moveaxis(t.astype(jnp.float32), 1, 0)

    def step(S, inp):
        r_t, w_t, k_t, v_t, a_t, b_t = inp
        sa = jnp.einsum('bhvk,bhk->bhv', S, a_t)
        S = S * w_t[:, :, None, :] + sa[..., None] * b_t[:, :, None, :] + v_t[..., None] * k_t[:, :, None, :]
        return S, jnp.einsum('bhvk,bhk->bhv', S, r_t)

    S0 = jnp.zeros((Bsz, H, N, N), jnp.float32)
    _, y = lax.scan(step, S0, (tm(r), tm(w), tm(k), tm(v), tm(a), tm(b)))
    return jnp.moveaxis(y, 0, 1).astype(r.dtype)


def rwkv7_mix(cols, mu, w0, w2, a0, a2, g2, k_k, k_a, r_k, gn_g, gn_b):
    Bsz, T, _ = cols.shape
    r, k, v, w_lo, a_lo, g_lo = _split(token_shift_lerp(cols, mu), A_SPLITS)
    w_raw = (w0 + jnp.tanh(w_lo) @ w2).astype(jnp.float32)
    decay = jnp.exp(-jnp.exp(-jax.nn.softplus(-w_raw) - 0.5))
    a = jax.nn.sigmoid(a0 + a_lo @ a2)
    g = jax.nn.sigmoid(g_lo) @ g2
    heads = lambda t: t.reshape(Bsz, T, A_HEADS, A_HEAD_DIM)
    kk = l2_normalize(heads(k * k_k))
    k = k * (1.0 + (a - 1.0) * k_a)
    rh, kh, vh = heads(r), heads(k), heads(v)
    out = rwkv7_scan(rh, heads(decay), kh, vh, -kk, kk * heads(a))
    out = layer_norm(out, gn_g, gn_b, eps=A_GN_EPS)
    bonus = jnp.sum(rh * kh * r_k, -1, keepdims=True) * vh
    return (out + bonus).reshape(Bsz, T, A_WIDTH) * g


def gated_delta_chunked(q, k, v, log_g, beta):
    Bsz, T, H, dk = q.shape
    dv = v.shape[-1]
    C = B_CHUNK
    n = T // C

    def chunks(t):
        t = t.astype(jnp.float32).reshape((Bsz, n, C, H) + t.shape[3:])
        return jnp.moveaxis(t, 3, 1)

    q = chunks(q) * dk ** -0.5
    k, v, beta = chunks(k), chunks(v), chunks(beta)
    gc = jnp.cumsum(chunks(log_g), -1)
    causal = jnp.tril(jnp.ones((C, C), bool))
    strict = jnp.tril(jnp.ones((C, C), bool), -1)
    decay = jnp.exp(jnp.where(causal, gc[..., :, None] - gc[..., None, :], -jnp.inf))
    k_beta = k * beta[..., None]
    M = jnp.where(strict, jnp.einsum('bhncd,bhnsd->bhncs', k_beta, k) * decay, 0.0)
    rhs = jnp.concatenate([v * beta[..., None], k_beta * jnp.exp(gc)[..., None]], -1)
    sol = lax.linalg.triangular_solve(M + jnp.eye(C, dtype=M.dtype), rhs,
                                      left_side=True, lower=True, unit_diagonal=True)
    u, w = sol[..., :dv], sol[..., dv:]
    attn = jnp.einsum('bhncd,bhnsd->bhncs', q, k) * decay
    q_dec = q * jnp.exp(gc)[..., None]
    k_dec = k * jnp.exp(gc[..., -1:] - gc)[..., None]
    g_last = jnp.exp(gc[..., -1])

    def step(S, inp):
        u_i, w_i, a_i, qd_i, kd_i, gl_i = inp
        v_new = u_i - jnp.einsum('bhcd,bhde->bhce', w_i, S)
        o = jnp.einsum('bhcd,bhde->bhce', qd_i, S) + jnp.einsum('bhcs,bhse->bhce', a_i, v_new)
        S = S * gl_i[..., None, None] + jnp.einsum('bhcd,bhce->bhde', kd_i, v_new)
        return S, o

    xs = tuple(jnp.moveaxis(t, 2, 0) for t in (u, w, attn, q_dec, k_dec, g_last))
    _, o = lax.scan(step, jnp.zeros((Bsz, H, dk, dv), jnp.float32), xs)
    o = jnp.moveaxis(jnp.moveaxis(o, 0, 2), 1, 3).reshape(Bsz, T, H, dv)
    return o


def gdn_mix(q, k, v, z, beta_raw, alpha_raw, conv_w, A_log, dt_bias, norm_g):
    Bsz, T, _ = q.shape
    qkv = jax.nn.silu(causal_dwconv(jnp.concatenate([q, k, v], -1), conv_w))
    q, k, v = jnp.split(qkv, 3, axis=-1)
    heads = lambda t: t.reshape(Bsz, T, B_HEADS, B_HEAD_DIM)
    q, k, v = l2_normalize(heads(q)), l2_normalize(heads(k)), heads(v)
    beta = jax.nn.sigmoid(beta_raw)
    log_g = -jnp.exp(A_log) * jax.nn.softplus(alpha_raw + dt_bias)
    o = gated_delta_chunked(q, k, v, log_g, beta).astype(q.dtype)
    o = rms_norm(o, norm_g) * jax.nn.silu(heads(z))
    return o.reshape(Bsz, T, B_WIDTH)


def ssd_chunked(X, A, Bm, Cm):
    Bsz, T, H, P = X.shape
    G, N = Bm.shape[2], Bm.shape[3]
    Hg = H // G
    L = C_CHUNK
    c = T // L
    X = X.astype(jnp.float32).reshape(Bsz, c, L, G, Hg, P)
    A = jnp.moveaxis(A.astype(jnp.float32).reshape(Bsz, c, L, G, Hg), 2, -1)
    Bm = Bm.astype(jnp.float32).reshape(Bsz, c, L, G, N)
    Cm = Cm.astype(jnp.float32).reshape(Bsz, c, L, G, N)
    A_cum = jnp.cumsum(A, -1)
    CB = jnp.einsum('bclgn,bcsgn->bcgls', Cm, Bm)
    Wd = CB[:, :, :, None] * jnp.exp(segsum(A))
    Y_diag = jnp.einsum('bcghls,bcsghp->bclghp', Wd, X)
    decay_states = jnp.moveaxis(jnp.exp(A_cum[..., -1:] - A_cum), -1, 2)
    states = jnp.einsum('bclgn,bclghp->bcghpn', Bm, X * decay_states[..., None])
    states = jnp.concatenate([jnp.zeros_like(states[:, :1]), states], 1)
    chunk_A = jnp.pad(jnp.moveaxis(A_cum[..., -1], 1, -1), ((0, 0), (0, 0), (0, 0), (1, 0)))
    states = jnp.einsum('bghzc,bcghpn->bzghpn', jnp.exp(segsum(chunk_A)), states)[:, :-1]
    Y_off = jnp.einsum('bclgn,bcghpn->bclghp', Cm, states) * jnp.moveaxis(jnp.exp(A_cum), -1, 2)[..., None]
    return (Y_diag + Y_off).reshape(Bsz, T, H, P)


def mamba2_mix(z, xbc, dt_raw, conv_w, conv_b, dt_bias, A_log, D_skip, norm_g):
    Bsz, T, _ = z.shape
    xbc = jax.nn.silu(causal_dwconv(xbc, conv_w) + conv_b)
    xs, Bm, Cm = _split(xbc, (C_WIDTH, C_GROUPS * C_STATE, C_GROUPS * C_STATE))
    xs = xs.reshape(Bsz, T, C_HEADS, C_HEAD_DIM)
    Bm = Bm.reshape(Bsz, T, C_GROUPS, C_STATE)
    Cm = Cm.reshape(Bsz, T, C_GROUPS, C_STATE)
    dt = jax.nn.softplus(dt_raw + dt_bias)
    y = ssd_chunked(xs * dt[..., None], dt * (-jnp.exp(A_log)), Bm, Cm).astype(xs.dtype)
    y = (y + xs * D_skip[:, None]).reshape(Bsz, T, C_WIDTH) * jax.nn.silu(z)
    y = rms_norm(y.reshape(Bsz, T, C_GROUPS, C_WIDTH // C_GROUPS), norm_g.reshape(C_GROUPS, -1))
    return y.reshape(Bsz, T, C_WIDTH)


def rglru_mix(y_br, x_br, conv_w, conv_b, wa, ba, wx, bx, lam):
    Bsz, T, _ = x_br.shape
    x = causal_dwconv(x_br, conv_w) + conv_b
    xb = x.reshape(Bsz, T, D_BLOCKS, D_BLOCK_DIM)
    r = jax.nn.sigmoid(jnp.einsum('btnd,nde->btne', xb, wa).reshape(Bsz, T, D_WIDTH) + ba)
    i = jax.nn.sigmoid(jnp.einsum('btnd,nde->btne', xb, wx).reshape(Bsz, T, D_WIDTH) + bx)
    log_a = LRU_C * r.astype(jnp.float32) * jax.nn.log_sigmoid(lam.astype(jnp.float32))
    u = jnp.sqrt(-jnp.expm1(2.0 * log_a)) * (i * x).astype(jnp.float32)
    h = linear_recurrence(jnp.exp(log_a), u)
    return h.astype(x.dtype) * jax.nn.gelu(y_br)


def conv_ffn(h, w_up, conv_w, conv_b, w_down):
    u = causal_dwconv(h @ w_up, conv_w) + conv_b
    gate, val = jnp.split(u, 2, axis=-1)
    return (jax.nn.silu(gate) * val) @ w_down


def even_mixer(x, w_in, w_out, a_mu, a_w0, a_w2, a_a0, a_a2, a_g2, a_k_k, a_k_a, a_r_k, a_gn_g, a_gn_b,
               b_conv_w, b_A_log, b_dt_bias, b_norm_g):
    hcols = x @ w_in
    a_cols, b_cols = hcols[..., :A_COLS], hcols[..., A_COLS:]
    ya = rwkv7_mix(a_cols, a_mu, a_w0, a_w2, a_a0, a_a2, a_g2, a_k_k, a_k_a, a_r_k, a_gn_g, a_gn_b)
    q, k, v, z, beta_raw, alpha_raw = _split(b_cols, B_SPLITS)
    yb = gdn_mix(q, k, v, z, beta_raw, alpha_raw, b_conv_w, b_A_log, b_dt_bias, b_norm_g)
    return jnp.concatenate([ya, yb], -1) @ w_out


def odd_mixer(x, w_in, w_out, m_conv_w, m_conv_b, m_dt_bias, m_A_log, m_D, m_norm_g,
              l_conv_w, l_conv_b, l_wa, l_ba, l_wx, l_bx, l_lam):
    z, xbc, dt_raw, y_br, x_br = _split(x @ w_in, ODD_SPLITS)
    yc = mamba2_mix(z, xbc, dt_raw, m_conv_w, m_conv_b, m_dt_bias, m_A_log, m_D, m_norm_g)
    yd = rglru_mix(y_br, x_br, l_conv_w, l_conv_b, l_wa, l_ba, l_wx, l_bx, l_lam)
    return jnp.concatenate([yc, yd], -1) @ w_out


def setup_inputs(seed: int = 0) -> dict:
    key = jax.random.key(seed)
    ks = iter(jax.random.split(key, 64))
    nrm = lambda shape, scale: scale * jax.random.normal(next(ks), shape, jnp.float32)
    unif = lambda shape, lo, hi: jax.random.uniform(next(ks), shape, jnp.float32, lo, hi)

    def dt_bias(shape):
        dt = jnp.exp(unif(shape, math.log(1e-3), math.log(1e-1)))
        return dt + jnp.log(-jnp.expm1(-dt))

    Ld, E, O, D = DEPTH, N_EVEN, N_ODD, D_MODEL
    chan = jnp.arange(A_WIDTH, dtype=jnp.float32) / (A_WIDTH - 1)
    lam_s = unif((O, D_WIDTH), 0.9, 0.999) ** (1.0 / LRU_C)
    return {
        "x": nrm((BATCH, SEQ, D), 1.0),
        "p": nrm((DEPTH, BATCH, SEQ, PLE_DIM), 1.0),
        "ln1_g": 1.0 + nrm((Ld, D), 0.02), "ln1_b": nrm((Ld, D), 0.02),
        "ln2_g": 1.0 + nrm((Ld, D), 0.02), "ln2_b": nrm((Ld, D), 0.02),
        "ffn_up": nrm((Ld, D, 2 * D_FF), D ** -0.5),
        "ffn_conv_w": nrm((Ld, FFN_CONV, 2 * D_FF), FFN_CONV ** -0.5),
        "ffn_conv_b": nrm((Ld, 2 * D_FF), 0.02),
        "ffn_down": nrm((Ld, D_FF, D), DN_BETA * D_FF ** -0.5),
        "ple_proj": nrm((Ld, PLE_DIM, D), PLE_DIM ** -0.5),
        "ple_norm_g": 1.0 + nrm((Ld, D), 0.02),
        "ple_gate_w": nrm((Ld, D, D), D ** -0.5),
        "ple_gate_b": nrm((Ld, D), 0.02),
        "even_w_in": nrm((E, D, EVEN_IN), D ** -0.5),
        "even_w_out": nrm((E, EVEN_OUT, D), DN_BETA * EVEN_OUT ** -0.5),
        "rwkv_mu": unif((E, A_COLS), 0.0, 1.0),
        "rwkv_w0": (-6.5 + 5.0 * chan ** 0.85)[None] + nrm((E, A_WIDTH), 0.1),
        "rwkv_w2": nrm((E, A_W_LORA, A_WIDTH), A_W_LORA ** -0.5),
        "rwkv_a0": nrm((E, A_WIDTH), 0.1),
        "rwkv_a2": nrm((E, A_A_LORA, A_WIDTH), A_A_LORA ** -0.5),
        "rwkv_g2": nrm((E, A_G_LORA, A_WIDTH), A_G_LORA ** -0.5),
        "rwkv_k_k": 0.85 + nrm((E, A_WIDTH), 0.02),
        "rwkv_k_a": 1.0 + nrm((E, A_WIDTH), 0.02),
        "rwkv_r_k": nrm((E, A_HEADS, A_HEAD_DIM), 0.1),
        "rwkv_gn_g": 1.0 + nrm((E, A_HEADS, A_HEAD_DIM), 0.02),
        "rwkv_gn_b": nrm((E, A_HEADS, A_HEAD_DIM), 0.02),
        "gdn_conv_w": nrm((E, B_CONV, 3 * B_WIDTH), B_CONV ** -0.5),
        "gdn_A_log": jnp.log(unif((E, B_HEADS), 1.0, 16.0)),
        "gdn_dt_bias": dt_bias((E, B_HEADS)),
        "gdn_norm_g": 1.0 + nrm((E, B_HEAD_DIM), 0.02),
        "odd_w_in": nrm((O, D, ODD_IN), D ** -0.5),
        "odd_w_out": nrm((O, ODD_OUT, D), DN_BETA * ODD_OUT ** -0.5),
        "mamba_conv_w": nrm((O, C_CONV, C_XBC), C_CONV ** -0.5),
        "mamba_conv_b": nrm((O, C_XBC), 0.02),
        "mamba_dt_bias": dt_bias((O, C_HEADS)),
        "mamba_A_log": jnp.log(unif((O, C_HEADS), 1.0, 16.0)),
        "mamba_D": 1.0 + nrm((O, C_HEADS), 0.02),
        "mamba_norm_g": 1.0 + nrm((O, C_WIDTH), 0.02),
        "lru_conv_w": nrm((O, D_CONV, D_WIDTH), D_CONV ** -0.5),
        "lru_conv_b": nrm((O, D_WIDTH), 0.02),
        "lru_wa": nrm((O, D_BLOCKS, D_BLOCK_DIM, D_BLOCK_DIM), D_BLOCK_DIM ** -0.5),
        "lru_ba": nrm((O, D_WIDTH), 0.02),
        "lru_wx": nrm((O, D_BLOCKS, D_BLOCK_DIM, D_BLOCK_DIM), D_BLOCK_DIM ** -0.5),
        "lru_bx": nrm((O, D_WIDTH), 0.02),
        "lru_lambda": jnp.log(lam_s) - jnp.log1p(-lam_s),
    }


def reference(x, p, ln1_g, ln1_b, ln2_g, ln2_b, ffn_up, ffn_conv_w, ffn_conv_b, ffn_down,
              ple_proj, ple_norm_g, ple_gate_w, ple_gate_b,
              even_w_in, even_w_out, rwkv_mu, rwkv_w0, rwkv_w2, rwkv_a0, rwkv_a2, rwkv_g2,
              rwkv_k_k, rwkv_k_a, rwkv_r_k, rwkv_gn_g, rwkv_gn_b,
              gdn_conv_w, gdn_A_log, gdn_dt_bias, gdn_norm_g,
              odd_w_in, odd_w_out, mamba_conv_w, mamba_conv_b, mamba_dt_bias, mamba_A_log, mamba_D,
              mamba_norm_g, lru_conv_w, lru_conv_b, lru_wa, lru_ba, lru_wx, lru_bx, lru_lambda):
    for i in range(DEPTH):
        j = i // 2
        if i % 2 == 0:
            y = even_mixer(x, even_w_in[j], even_w_out[j], rwkv_mu[j], rwkv_w0[j], rwkv_w2[j], rwkv_a0[j],
                           rwkv_a2[j], rwkv_g2[j], rwkv_k_k[j], rwkv_k_a[j], rwkv_r_k[j], rwkv_gn_g[j],
                           rwkv_gn_b[j], gdn_conv_w[j], gdn_A_log[j], gdn_dt_bias[j], gdn_norm_g[j])
        else:
            y = odd_mixer(x, odd_w_in[j], odd_w_out[j], mamba_conv_w[j], mamba_conv_b[j], mamba_dt_bias[j],
                          mamba_A_log[j], mamba_D[j], mamba_norm_g[j], lru_conv_w[j], lru_conv_b[j],
                          lru_wa[j], lru_ba[j], lru_wx[j], lru_bx[j], lru_lambda[j])
        h = layer_norm(DN_ALPHA * x + y, ln1_g[i], ln1_b[i])
        h = layer_norm(DN_ALPHA * h + conv_ffn(h, ffn_up[i], ffn_conv_w[i], ffn_conv_b[i], ffn_down[i]),
                       ln2_g[i], ln2_b[i])
        e = rms_norm(p[i] @ ple_proj[i], ple_norm_g[i])
        x = h + jax.nn.sigmoid(h @ ple_gate_w[i] + ple_gate_b[i]) * e
    return x
```

```python
from contextlib import ExitStack

import numpy as np
import concourse.bass as bass
import concourse.mybir as mybir
from concourse.bass_utils import run_bass_kernel_spmd

F32 = mybir.dt.float32
BF16 = mybir.dt.bfloat16
ALU = mybir.AluOpType
AF = mybir.ActivationFunctionType
AX = mybir.AxisListType

ENGS = ("pe", "dve", "act", "pool", "sp")


class Tl:
    def __init__(self, h, name, nsub=1):
        self.h, self.name, self.nsub = h, name, nsub

    def __getitem__(self, idx):
        return self.h[idx]

    def k(self, i=None):
        if i is None:
            return [(self.name, j) for j in range(self.nsub)]
        if isinstance(i, (list, tuple, range)):
            return [(self.name, j) for j in i]
        return [(self.name, i)]


class Prog:
    def __init__(self, nc):
        self.nc = nc
        self.ops = []
        self.stack = None
        self.ntiles = 0
        self.dsems = {}
        self.psum_names = set()
        Prog.ninst = getattr(Prog, "ninst", 0) + 1
        self.pfx = f"g{Prog.ninst}_"

    def sb(self, shape, dt, name=None, nsub=1):
        self.ntiles += 1
        name = self.pfx + (name or f"t{self.ntiles}")
        h = self.stack.enter_context(self.nc.sbuf_tensor(name, list(shape), dt))
        return Tl(h, name, nsub)

    def ps(self, shape, dt, name=None, nsub=1):
        self.ntiles += 1
        name = self.pfx + (name or f"p{self.ntiles}")
        h = self.stack.enter_context(self.nc.psum_tensor(name, list(shape), dt))
        self.psum_names.add(name)
        return Tl(h, name, 1)

    def dram(self, name, shape, dt, kind, nsub=1):
        h = self.nc.dram_tensor(name, list(shape), dt, kind=kind)
        return Tl(h.ap(), name, nsub)

    def op(self, eng, fn, r=(), w=(), acc=False):
        w = list(w) + [k for k in r if k[0] in self.psum_names and k not in w]
        self.ops.append(dict(eng=eng, fn=fn, r=list(r), w=list(w), dma=None, acc=acc))

    def dma(self, eng, out, in_, r=(), w=(), sem="d0", **kw):
        def fn(e):
            return e.dma_start(out=out, in_=in_, **kw)
        self.ops.append(dict(eng=eng, fn=fn, r=list(r), w=list(w), dma=sem, acc=False))

    def coll(self, kind, ins_ap, out_ap, groups, r, w, sem, inc=1):
        def fn(e):
            return e.collective_compute(kind, ALU.bypass, replica_groups=groups, ins=[ins_ap], outs=[out_ap])
        self.ops.append(dict(eng="pool", fn=fn, r=list(r), w=list(w), dma=sem, acc=False, inc=inc))

    def fence(self, tiles):
        sc = self.sb([128, 1], F32, f"fence{self.ntiles}")
        keys = [k for t in tiles for k in t.k()]
        self.op("pool", lambda e: e.memset(sc[:], 0.0), r=keys, w=keys + sc.k())

    def emit(self):
        nc = self.nc
        st = self.stack
        ss = getattr(self, "semstack", None) or st
        Prog.nprog = getattr(Prog, "nprog", 0) + 1
        pfx = f"s{Prog.nprog}_"
        sem = {e: ss.enter_context(nc.semaphore(pfx + e)) for e in ENGS}
        dnames = sorted({o["dma"] for o in self.ops if o["dma"]})
        for d in dnames:
            sem[d] = ss.enter_context(nc.semaphore(pfx + d))
        cnt = {s: 0 for s in sem}
        clock = {e: {} for e in ENGS}
        lastw = {}
        readers = {}
        per_eng = {e: [] for e in ENGS}

        def merge(a, b):
            for s, c in b.items():
                if a.get(s, 0) < c:
                    a[s] = c

        for o in self.ops:
            e = o["eng"]
            ck = clock[e]
            need = []
            for key in o["r"]:
                ev = lastw.get(key)
                if ev is not None:
                    need.append(ev)
            for key in o["w"]:
                ev = lastw.get(key)
                if ev is not None and not ((o["acc"] or e == "pe") and ev[3] == e):
                    need.append(ev)
                for ev in readers.get(key, ()):
                    if e == "pe" and ev[3] == "pe":
                        continue
                    need.append(ev)
            waits = {}
            for (s, c, evck, _) in need:
                if ck.get(s, 0) >= c:
                    continue
                if waits.get(s, 0) < c:
                    waits[s] = c
            for (s, c, evck, _) in need:
                if s in waits and waits[s] >= c and ck.get(s, 0) < c:
                    merge(ck, evck)
            for s, c in waits.items():
                if ck.get(s, 0) < c:
                    ck[s] = c
            if o["dma"]:
                s = o["dma"]
                cnt[s] += o.get("inc", 16)
                evs = s
            else:
                cnt[e] += 1
                evs = e
            evck = dict(ck)
            evck[evs] = cnt[evs]
            ev = (evs, cnt[evs], evck, e)
            for key in o["w"]:
                lastw[key] = ev
                readers[key] = []
            for key in o["r"]:
                readers.setdefault(key, []).append(ev)
            per_eng[e].append((o, sorted(waits.items()), evs))
        self.final = {s: c for s, c in cnt.items() if c > 0}
        self.sem = sem
        self.per_eng = per_eng
        nE = {e: len(v) for e, v in per_eng.items()}
        self.stats = nE

        block = st.enter_context(nc.Block())

        def run(engname, eng):
            for (o, waits, evs) in per_eng[engname]:
                for s, c in waits:
                    eng.wait_ge(sem[s], c)
                ins = o["fn"](eng)
                ins.then_inc(sem[evs], o.get("inc", 16) if o["dma"] else 1)
            if engname == "sp":
                for s, c in self.final.items():
                    eng.wait_ge(sem[s], c)

        @block.tensor
        def _(eng):
            run("pe", eng)

        @block.vector
        def _(eng):
            run("dve", eng)

        @block.scalar
        def _(eng):
            run("act", eng)

        @block.gpsimd
        def _(eng):
            run("pool", eng)

        @block.sync
        def _(eng):
            run("sp", eng)


def _tt(P, eng, out, in0, in1, op, r, w):
    P.op(eng, lambda e: e.tensor_tensor(out=out, in0=in0, in1=in1, op=op), r, w)


def _ts(P, eng, out, in0, s1, s2, op0, op1, r, w):
    if op1 is None:
        P.op(eng, lambda e: e.tensor_scalar(out=out, in0=in0, scalar1=s1, scalar2=None, op0=op0), r, w)
    else:
        P.op(eng, lambda e: e.tensor_scalar(out=out, in0=in0, scalar1=s1, scalar2=s2, op0=op0, op1=op1), r, w)


def _stt(P, eng, out, in0, sc, in1, op0, op1, r, w):
    P.op(eng, lambda e: e.scalar_tensor_tensor(out=out, in0=in0, scalar=sc, in1=in1, op0=op0, op1=op1), r, w)


def _act(P, out, in_, func, r, w, bias=None, scale=None):
    kw = {}
    if bias is not None:
        kw["bias"] = bias
    if scale is not None:
        kw["scale"] = scale
    P.op("act", lambda e: e.activation(out=out, in_=in_, func=func, **kw), r, w)


def _cp(P, eng, out, in_, r, w):
    if eng == "act":
        P.op("act", lambda e: e.copy(out=out, in_=in_), r, w)
    else:
        P.op(eng, lambda e: e.tensor_copy(out=out, in_=in_), r, w)


def _mm(P, out, lhsT, rhs, start, stop, r, w):
    P.op("pe", lambda e: e.matmul(out, lhsT=lhsT, rhs=rhs, start=start, stop=stop), r, w, acc=not start)


def _tr(P, out, in_, ident, r, w):
    P.op("pe", lambda e: e.transpose(out, in_, ident), r, w)


Prog.tt, Prog.ts, Prog.stt, Prog.act, Prog.cp, Prog.mm, Prog.tr = _tt, _ts, _stt, _act, _cp, _mm, _tr


class TlV(Tl):
    def __init__(self, ap, parent):
        self.h, self.name, self.nsub = ap, parent.name, parent.nsub

    def k(self, i=None):
        return [(self.name, j) for j in range(self.nsub)]


class ChunkT:
    def __init__(self, chunks, W):
        self.chunks, self.W = chunks, W

    def ap(self, rows, c0, n):
        q = c0 // self.W
        assert (c0 + n - 1) // self.W == q, (c0, n, self.W)
        return self.chunks[q][rows, (c0 % self.W):(c0 % self.W) + n]

    def k(self, i=None):
        return [k for c in self.chunks for k in c.k()]


LN_EPS = 1e-5
RMS_EPS = 1e-6


KS = 8


def relayout_w(w):
    K, N = w.shape
    return np.ascontiguousarray(w.reshape(K // 128, 128, N // 128, 128).transpose(2, 1, 0, 3))


def relayout_v(v):
    return np.ascontiguousarray(v.reshape(-1, 128).T)


class WStream:
    def __init__(self, P, kcmax=KS, nst=6, nbf=4):
        self.P = P
        self.st = [P.sb([128, KS, 128], F32, f"wst{i}") for i in range(nst)]
        self.bf = [P.sb([128, KS, 128], BF16, f"wbf{i}") for i in range(nbf)]
        self.i = 0

    def get(self, wd, ct, k0, k1, cw=128):
        P = self.P
        i = self.i
        self.i += 1
        st, bf = self.st[i % len(self.st)], self.bf[i % len(self.bf)]
        n = k1 - k0
        P.dma("sp", st[:, :n, :cw], wd[ct][:, k0:k1, :], w=st.k(), sem=f"dw{i % len(self.st)}")
        P.cp("pool", bf[:, :n, :cw], st[:, :n, :cw], r=st.k(), w=bf.k())
        return bf


def proj(P, ws, wd, order, KC, act, N, pss, consume, cw=128):
    for j, ct in enumerate(order):
        ps = pss[j % len(pss)]
        for k0 in range(0, KC, KS):
            k1 = min(KC, k0 + KS)
            bf = ws.get(wd, ct, k0, k1, cw)
            for kk in range(k0, k1):
                P.mm(ps[:cw, :N], bf[:, kk - k0, :cw], act[:, kk, :N], kk == 0, kk == KC - 1, r=bf.k() + act.k(), w=ps.k())
        consume(j, ct, ps)


def phase_C(nc, outer, io, NB, CO, TB=512, D=2048, FF=5632, PLE=256, alpha=1.0, THALF=2048):
    P = Prog(nc)
    P.semstack = outer
    KY, KD, KF, KP = CO // 128, D // 128, FF // 128, PLE // 128
    NFT = 2 * KF
    NTOK = NB * TB
    with ExitStack() as st:
        P.stack = st
        yrcv, pT, msk = io["yrcv"], io["pT"], io["msk"]
        w_out, w_up, w_down, w_gate, w_ple, vecs, cvw = (io[k] for k in ("w_out", "w_up", "w_down", "w_gate", "w_ple", "vecs", "cvw"))
        ybB = P.sb([128, 8, TB], BF16, "ybB")

        vs = P.sb([128, 6, KD], F32, "vs")
        cw = P.sb([128, 4, NFT], F32, "cw")
        hms = P.sb([128, 3], F32, "hms")
        ones = P.sb([128, 128], F32, "ones")
        eT = P.sb([128, KD, TB], F32, "eT")
        assert KY <= 2 * KD
        yb = Tl(eT.h[:].rearrange("p k t -> p (k t)").bitcast(BF16)[:, :KY * TB].rearrange("p (k t) -> p k t", k=KY), eT.name)
        sx = P.sb([128, KD, TB], F32, "sx")
        hb = P.sb([128, KD, TB], BF16, "hb")
        actT = P.sb([128, KF, TB], BF16, "actT")
        pst = P.sb([128, KP, TB], F32, "pst")
        pb = P.sb([128, KP, TB], BF16, "pb")
        halo = P.sb([128, NFT, 2], F32, "halo", nsub=NFT)
        ub = [P.sb([128, TB + 2], F32, f"ub{i}") for i in range(3)]
        cv = [P.sb([128, TB], F32, f"cv{i}") for i in range(3)]
        sg = [P.sb([128, TB], F32, f"sg{i}") for i in range(2)]
        sq = [P.sb([128, TB], F32, f"sq{i}") for i in range(2)]
        mean = P.sb([128, TB], F32, "mean")
        rstd = P.sb([128, TB], F32, "rstd")
        tmp = [P.sb([128, TB], F32, f"tmp{i}") for i in range(2)]
        pss = [P.ps([128, 512], F32, f"ps{i}") for i in range(4)]
        pm = P.ps([128, 512], F32, "pm")
        pv = P.ps([128, 512], F32, "pv")
        ws = WStream(P)

        P.dma("act", vs[:], vecs[:], w=vs.k(), sem="dc")
        P.dma("act", cw[:], cvw[:], w=cw.k(), sem="dc")
        P.dma("act", hms[:], msk[:], w=hms.k(), sem="dc")
        P.fence([vs, cw, hms])
        P.op("pool", lambda e: e.memset(ones[:], 1.0 / D), w=ones.k())

        def layer_norm(N, gi, bi):
            for k in range(KD):
                s = sq[k % 2]
                P.tt("pool", s[:, :N], sx[:, k, :N], sx[:, k, :N], ALU.mult, r=sx.k(), w=s.k())
                P.mm(pm[:, :N], ones[:], sx[:, k, :N], k == 0, k == KD - 1, r=ones.k() + sx.k(), w=pm.k())
                P.mm(pv[:, :N], ones[:], s[:, :N], k == 0, k == KD - 1, r=ones.k() + s.k(), w=pv.k())
            P.cp("act", mean[:, :N], pm[:, :N], r=pm.k(), w=mean.k())
            t = tmp[0]
            P.tt("dve", t[:, :N], mean[:, :N], mean[:, :N], ALU.mult, r=mean.k(), w=t.k())
            P.tt("dve", t[:, :N], pv[:, :N], t[:, :N], ALU.subtract, r=pv.k() + t.k(), w=t.k())
            P.act(rstd[:, :N], t[:, :N], AF.Sqrt, r=t.k(), w=rstd.k(), bias=LN_EPS)
            P.op("dve", lambda e: e.reciprocal(out=rstd[:, :N], in_=rstd[:, :N]), r=rstd.k(), w=rstd.k())
            for k in range(KD):
                t = tmp[k % 2]
                P.tt("dve", t[:, :N], sx[:, k, :N], mean[:, :N], ALU.subtract, r=sx.k() + mean.k(), w=t.k())
                P.tt("dve", t[:, :N], t[:, :N], rstd[:, :N], ALU.mult, r=t.k() + rstd.k(), w=t.k())
                P.ts("dve", sx[:, k, :N], t[:, :N], vs[:, gi, k:k + 1], vs[:, bi, k:k + 1], ALU.mult, ALU.add,
                     r=t.k() + vs.k(), w=sx.k())
                P.cp("act", hb[:, k, :N], sx[:, k, :N], r=sx.k(), w=hb.k())

        for blk in range(-1, NB):
            N = 2 if blk < 0 else TB
            t0 = 0 if blk < 0 else 2 + blk * TB
            cA = 0 if blk < 0 else blk * TB
            cB = THALF - 2 if blk < 0 else THALF + blk * TB
            P.dma("act", yb[:, :, :N], yrcv.ap(slice(None), cA, N).rearrange("(k p) t -> p k t", p=128), r=yrcv.k(), w=yb.k(), sem="dy")
            for k0 in range(0, KY, 8):
                P.dma("act", ybB[:, :, :N], yrcv.ap(slice(k0 * 128, (k0 + 8) * 128), cB, N).rearrange("(k p) t -> p k t", p=128),
                      r=yrcv.k(), w=ybB.k(), sem="dyb")
                P.ts("dve", yb[:, k0:k0 + 8, :N], yb[:, k0:k0 + 8, :N], hms[:, 1:2], None, ALU.mult, None, r=yb.k() + hms.k(), w=yb.k())
                P.stt("dve", yb[:, k0:k0 + 8, :N], ybB[:, :, :N], hms[:, 2:3], yb[:, k0:k0 + 8, :N], ALU.mult, ALU.add,
                      r=ybB.k() + hms.k() + yb.k(), w=yb.k())
            for (k0_, k1_, ap_) in io["xsrc"](blk):
                P.dma("act", sx[:, k0_:k1_, :N], ap_.rearrange("(k p) t -> p k t", p=128), r=io["xsrc_k"], w=sx.k(), sem="dx")

            def c_out(j, ct, ps, N=N):
                P.stt("dve", sx[:, ct, :N], sx[:, ct, :N], float(alpha), ps[:, :N], ALU.mult, ALU.add,
                      r=sx.k() + ps.k(), w=sx.k())
            proj(P, ws, w_out, range(KD), KY, yb, N, pss, c_out)
            layer_norm(N, 0, 1)

            order = []
            for c in range(KF):
                order += [c, KF + c]

            def c_up(j, ct, ps, N=N, blk=blk):
                if blk < 0:
                    P.ts("dve", halo[:, ct, :], ps[:, :2], hms[:, 0:1], None, ALU.mult, None,
                         r=ps.k() + hms.k(), w=halo.k(ct))
                    return
                u = ub[j % 3]
                c_ = cv[j % 3]
                P.cp("act", u[:, 2:2 + N], ps[:, :N], r=ps.k(), w=u.k())
                P.cp("pool", u[:, 0:2], halo[:, ct, :], r=halo.k(ct), w=u.k())
                P.cp("pool", halo[:, ct, :], u[:, N:N + 2], r=u.k(), w=halo.k(ct))
                P.ts("dve", c_[:, :N], u[:, 2:2 + N], cw[:, 2, ct:ct + 1], cw[:, 3, ct:ct + 1], ALU.mult, ALU.add,
                     r=u.k() + cw.k(), w=c_.k())
                P.stt("dve", c_[:, :N], u[:, 1:1 + N], cw[:, 1, ct:ct + 1], c_[:, :N], ALU.mult, ALU.add,
                      r=u.k() + cw.k() + c_.k(), w=c_.k())
                P.stt("dve", c_[:, :N], u[:, 0:N], cw[:, 0, ct:ct + 1], c_[:, :N], ALU.mult, ALU.add,
                      r=u.k() + cw.k() + c_.k(), w=c_.k())
                if ct < KF:
                    s = sg[ct % 2]
                    P.act(s[:, :N], c_[:, :N], AF.Silu, r=c_.k(), w=s.k())
                else:
                    c = ct - KF
                    s = sg[c % 2]
                    P.tt("dve", actT[:, c, :N], s[:, :N], c_[:, :N], ALU.mult, r=s.k() + c_.k(), w=actT.k())
            proj(P, ws, w_up, order, KD, hb, N, pss, c_up)
            if blk < 0:
                continue

            def c_down(j, ct, ps, N=N):
                P.stt("dve", sx[:, ct, :N], sx[:, ct, :N], float(alpha), ps[:, :N], ALU.mult, ALU.add,
                      r=sx.k() + ps.k(), w=sx.k())
            proj(P, ws, w_down, range(KD), KF, actT, N, pss, c_down)
            layer_norm(N, 2, 3)

            P.dma("act", pst[:], pT[:, blk * TB:(blk + 1) * TB].rearrange("(k p) t -> p k t", p=128), r=[], w=pst.k(), sem="dp")
            P.cp("pool", pb[:], pst[:], r=pst.k(), w=pb.k())

            def c_ple(j, ct, ps, N=N):
                P.cp("act", eT[:, ct, :N], ps[:, :N], r=ps.k(), w=eT.k())
                s = sq[ct % 2]
                P.tt("pool", s[:, :N], eT[:, ct, :N], eT[:, ct, :N], ALU.mult, r=eT.k(), w=s.k())
                P.mm(pv[:, :N], ones[:], s[:, :N], ct == 0, ct == KD - 1, r=ones.k() + s.k(), w=pv.k())
            proj(P, ws, w_ple, range(KD), KP, pb, N, pss, c_ple)
            P.act(rstd[:, :N], pv[:, :N], AF.Sqrt, r=pv.k(), w=rstd.k(), bias=RMS_EPS)
            P.op("dve", lambda e, N=N: e.reciprocal(out=rstd[:, :N], in_=rstd[:, :N]), r=rstd.k(), w=rstd.k())

            def c_gate(j, ct, ps, N=N, blk=blk):
                g = sg[ct % 2]
                P.act(g[:, :N], ps[:, :N], AF.Sigmoid, r=ps.k() + vs.k(), w=g.k(), bias=vs[:, 5, ct:ct + 1])
                t = tmp[ct % 2]
                P.stt("dve", t[:, :N], eT[:, ct, :N], vs[:, 4, ct:ct + 1], rstd[:, :N], ALU.mult, ALU.mult,
                      r=eT.k() + vs.k() + rstd.k(), w=t.k())
                P.tt("dve", t[:, :N], t[:, :N], g[:, :N], ALU.mult, r=t.k() + g.k(), w=t.k())
                P.tt("dve", eT[:, ct, :N], t[:, :N], sx[:, ct, :N], ALU.add, r=t.k() + sx.k(), w=eT.k())
            proj(P, ws, w_gate, range(KD), KD, hb, N, pss, c_gate)
            for (k0_, k1_, ap_) in io["xdst"](blk):
                P.dma("act", ap_.rearrange("(k p) t -> p k t", p=128), eT[:, k0_:k1_, :], r=eT.k(), w=io["xdst_k"], sem="do")
        if io.get("post"):
            io["post"](P)
        P.emit()
    return P


NEG = -30000.0


def consts_np():
    i = np.arange(128)
    ident = np.eye(128, dtype=np.float32)
    tri = (i[:, None] <= i[None, :]).astype(np.float32)
    maskneg = np.where(i[None, :] >= i[:, None], 0.0, NEG).astype(np.float32)
    return np.ascontiguousarray(np.stack([ident, tri, maskneg], 1))


def phase_O(nc, outer, io, T, TB=512, D=2048, L=128):
    P = Prog(nc)
    P.semstack = outer
    KD = D // 128
    NB = T // TB
    NCH = TB // L
    NH = 16
    with ExitStack() as st:
        P.stack = st
        w_in, cst, mcw, mv, mdn, lcw, lv, lwa, lwx, yo = (io[k] for k in ("w_in", "cst", "mcw", "mv", "mdn", "lcw", "lv", "lwa", "lwx", "yo"))

        cs = P.sb([128, 3, 128], F32, "cs")
        ident, tri, maskneg = cs[:, 0, :], cs[:, 1, :], cs[:, 2, :]
        ones = P.sb([128, 128], F32, "ones")
        onesg = P.sb([128, 128], F32, "onesg")
        mcs = P.sb([128, 5, 12], F32, "mcs")
        mvs = P.sb([128, 4], F32, "mvs")
        negA = P.sb([128, 1], F32, "negA")
        mds = P.sb([128, 2, 8], F32, "mds")
        lcs = P.sb([128, 5, 8], F32, "lcs")
        lvs = P.sb([128, 3, 8], F32, "lvs")
        c8 = P.sb([128, 8], F32, "c8")
        was = P.sb([128, 8, 128], F32, "was")
        wxs = P.sb([128, 8, 128], F32, "wxs")
        wab = P.sb([128, 8, 128], BF16, "wab")
        wxb = P.sb([128, 8, 128], BF16, "wxb")
        for (t_, d_) in ((cs, cst), (mcs, mcw), (mvs, mv), (mds, mdn), (lcs, lcw), (lvs, lv)):
            P.dma("act", t_[:], d_[:], w=t_.k(), sem="dc")
        P.dma("act", was[:], lwa[:].rearrange("n d e -> d n e"), w=was.k(), sem="dc")
        P.dma("act", wxs[:], lwx[:].rearrange("n d e -> d n e"), w=wxs.k(), sem="dc")
        P.fence([cs, mcs, mvs, mds, lcs, lvs, was, wxs])
        P.cp("pool", wab[:], was[:], r=was.k(), w=wab.k())
        P.cp("pool", wxb[:], wxs[:], r=wxs.k(), w=wxb.k())
        P.op("pool", lambda e: e.memset(ones[:], 1.0), w=ones.k())
        P.op("pool", lambda e: e.memset(onesg[:], 1.0 / 512.0), w=onesg.k())
        P.act(negA[:], mvs[:, 1:2], AF.Exp, r=mvs.k(), w=negA.k())
        P.ts("dve", negA[:], negA[:], -1.0, None, ALU.mult, None, r=negA.k(), w=negA.k())
        P.act(c8[:], lvs[:, 2, :], AF.Exp, r=lvs.k(), w=c8.k(), scale=-1.0)
        P.act(c8[:], c8[:], AF.Ln, r=c8.k(), w=c8.k(), bias=1.0)
        P.ts("dve", c8[:], c8[:], -8.0, None, ALU.mult, None, r=c8.k(), w=c8.k())

        xst = P.sb([128, KD // 2, TB], F32, "xst")
        xb = P.sb([128, KD, TB], BF16, "xb")
        ws = WStream(P)
        pss = [P.ps([128, 512], F32, f"ps{i}") for i in range(2)]
        pY = P.ps([128, 1024], F32, "pY")
        pO = P.ps([128, 1024], F32, "pO")
        pT = P.ps([128, 512], F32, "pT")
        pB = P.ps([128, 512], F32, "pB")
        pbs = [pB, pss[0], pss[1]]

        craw = P.sb([128, 12, TB + 3], F32, "craw", nsub=12)
        cfm = P.sb([128, 12, TB], F32, "cfm", nsub=12)
        bcb = P.sb([128, 4, TB], BF16, "bcb", nsub=4)
        P.op("pool", lambda e: e.memset(craw[:], 0.0), w=craw.k())
        dtf = P.sb([128, TB], F32, "dtf")
        Af = P.sb([128, TB], F32, "Af")
        S = P.sb([128, NH, 64], F32, "S")
        Sb = P.sb([128, NH, 64], BF16, "Sb")
        P.op("pool", lambda e: e.memset(S[:], 0.0), w=S.k())
        P.op("pool", lambda e: e.memset(Sb[:], 0.0), w=Sb.k())
        yfm = P.sb([128, 8, TB], F32, "yfm", nsub=8)
        xt = P.sb([128, NH, 64], F32, "xt")
        Xd = P.sb([128, NH, 64], BF16, "Xd")
        Xdec = P.sb([128, NH, 64], BF16, "Xdec")
        Bt = P.sb([128, 2, 128], BF16, "Bt")
        tm = P.sb([128, 3, NH], F32, "tm")
        cumT = P.sb([128, NH], F32, "cumT")
        ncum = P.sb([128, NH], F32, "ncum")
        ecum = P.sb([128, NH], F32, "ecum")
        decs = P.sb([128, NH], F32, "decs")
        etot = P.sb([128, NH], F32, "etot")
        Abc = P.sb([128, NH, 128], F32, "Abc")
        Gt = P.sb([128, 2, 128], F32, "Gt")
        wt = [P.sb([128, 128], F32, f"wt{i}") for i in range(2)]
        we = [P.sb([128, 128], F32, f"we{i}") for i in range(2)]
        Stt = [P.sb([128, 128], BF16, f"Stt{i}") for i in range(3)]
        Ytm = P.sb([128, NH, 64], F32, "Ytm")
        Yof = P.sb([128, NH, 64], F32, "Yof")
        ft = [P.sb([128, TB + 3], F32, f"ft{i}") for i in range(6)]
        fb = [P.sb([128, TB], BF16, f"fb{i}") for i in range(2)]
        ob = [P.sb([128, TB], BF16, f"ob{i}") for i in range(2)]
        lhalo = P.sb([128, 8, 3], F32, "lhalo", nsub=8)
        hst = P.sb([128, 8], F32, "hst", nsub=8)
        msq = P.sb([128, TB], F32, "msq")
        P.op("pool", lambda e: e.memset(lhalo[:], 0.0), w=lhalo.k())
        P.op("pool", lambda e: e.memset(hst[:], 0.0), w=hst.k())
        oi = [0]

        def conv4(out, src, cwt, ci, N, eng="dve"):
            P.ts(eng, out, src[:, 3:3 + N], cwt[:, 3, ci:ci + 1], cwt[:, 4, ci:ci + 1], ALU.mult, ALU.add,
                 r=src_k[0] + cwt_k[0], w=out_k[0])
            for j in range(3):
                P.stt(eng, out, src[:, j:j + N], cwt[:, j, ci:ci + 1], out, ALU.mult, ALU.add,
                      r=src_k[0] + cwt_k[0] + out_k[0], w=out_k[0])
        src_k, cwt_k, out_k = [None], [None], [None]

        for blk in range(NB):
            N = TB
            t0 = blk * TB
            for hf in range(2):
                P.dma("act", xst[:], io["xsrc"](t0, hf).rearrange("(k p) t -> p k t", p=128), r=io["xsrc_k"], w=xst.k(), sem="dx")
                P.cp("dve" if hf else "pool", xb[:, hf * (KD // 2):(hf + 1) * (KD // 2), :], xst[:], r=xst.k(), w=xb.k())

            def c_ssd(j, ct, ps):
                if ct < 12:
                    P.cp("act", craw[:, ct, 3:3 + N], ps[:, :N], r=ps.k(), w=craw.k(ct))
                    src_k[0], cwt_k[0], out_k[0] = craw.k(ct), mcs.k(), cfm.k(ct)
                    conv4(cfm[:, ct, :], craw[:, ct, :], mcs, ct, N)
                    P.cp("pool", craw[:, ct, 0:3], craw[:, ct, N:N + 3], r=craw.k(ct), w=craw.k(ct))
                    P.act(cfm[:, ct, :], cfm[:, ct, :], AF.Silu, r=cfm.k(ct), w=cfm.k(ct))
                    if ct >= 8:
                        P.cp("pool", bcb[:, ct - 8, :], cfm[:, ct, :], r=cfm.k(ct), w=bcb.k(ct - 8))
                else:
                    P.act(dtf[:], ps[:, :N], AF.Exp, r=ps.k() + mvs.k(), w=dtf.k(), bias=mvs[:, 0:1])
                    P.act(dtf[:], dtf[:], AF.Ln, r=dtf.k(), w=dtf.k(), bias=1.0)
                    P.ts("dve", Af[:], dtf[:], negA[:, 0:1], None, ALU.mult, None, r=dtf.k() + negA.k(), w=Af.k())
            proj(P, ws, w_in, range(13), KD, xb, N, pss, c_ssd)

            for ch in range(NCH):
                c0 = ch * L
                for half in range(2):
                    for q in range(4):
                        i = half * 4 + q
                        P.tr(pT[:, q * 128:(q + 1) * 128], cfm[:, i, c0:c0 + L], ident, r=cfm.k(i) + cs.k(), w=pT.k())
                    P.cp("act", xt[:, half * 8:(half + 1) * 8, :].rearrange("p h d -> p (h d)"), pT[:, :], r=pT.k(), w=xt.k())
                for g in range(2):
                    P.tr(pT[:, g * 128:(g + 1) * 128], cfm[:, 8 + g, c0:c0 + L], ident, r=cfm.k(8 + g) + cs.k(), w=pT.k())
                P.tr(pT[:, 256:384], dtf[:, c0:c0 + L], ident, r=dtf.k() + cs.k(), w=pT.k())
                P.tr(pT[:, 384:512], Af[:, c0:c0 + L], ident, r=Af.k() + cs.k(), w=pT.k())
                P.cp("act", Bt[:].rearrange("p g n -> p (g n)"), pT[:, 0:256], r=pT.k(), w=Bt.k())
                P.cp("dve", tm[:, 0, :], pT[:, 256:256 + NH], r=pT.k(), w=tm.k())
                P.cp("dve", tm[:, 1, :], pT[:, 384:384 + NH], r=pT.k(), w=tm.k())
                P.mm(pT[:, 0:NH], tri, tm[:, 1, :], True, True, r=cs.k() + tm.k(), w=pT.k())
                P.mm(pT[:, 32:32 + NH], ones[:], tm[:, 1, :], True, True, r=ones.k() + tm.k(), w=pT.k())
                P.cp("dve", cumT[:], pT[:, 0:NH], r=pT.k(), w=cumT.k())
                P.ts("dve", ncum[:], pT[:, 0:NH], -1.0, None, ALU.mult, None, r=pT.k(), w=ncum.k())
                P.act(ecum[:], pT[:, 0:NH], AF.Exp, r=pT.k(), w=ecum.k())
                P.act(etot[:], pT[:, 32:32 + NH], AF.Exp, r=pT.k(), w=etot.k())
                P.tt("dve", decs[:], pT[:, 32:32 + NH], cumT[:], ALU.subtract, r=pT.k() + cumT.k(), w=decs.k())
                P.act(decs[:], decs[:], AF.Exp, r=decs.k(), w=decs.k())
                P.tt("dve", Xd[:], xt[:], tm[:, 0, :].unsqueeze(2).to_broadcast([128, NH, 64]), ALU.mult,
                     r=xt.k() + tm.k(), w=Xd.k())
                P.tt("pool", Xdec[:], Xd[:], decs[:].unsqueeze(2).to_broadcast([128, NH, 64]), ALU.mult,
                     r=Xd.k() + decs.k(), w=Xdec.k())
                P.cp("pool", Abc[:], tm[:, 1, :].unsqueeze(2).to_broadcast([128, NH, 128]), r=tm.k(), w=Abc.k())
                for g in range(2):
                    P.mm(pT[:, 128 + g * 128:256 + g * 128], bcb[:, g, c0:c0 + L], bcb[:, 2 + g, c0:c0 + L], True, True,
                         r=bcb.k(g) + bcb.k(2 + g), w=pT.k())
                P.cp("act", Gt[:].rearrange("p g n -> p (g n)"), pT[:, 128:384], r=pT.k(), w=Gt.k())
                for h in range(NH):
                    g = h // 8
                    P.mm(pO[:, h * 64:(h + 1) * 64], bcb[:, 2 + g, c0:c0 + L], Sb[:, h, :], True, True,
                         r=bcb.k(2 + g) + Sb.k(), w=pO.k())
                P.tt("dve", Yof[:], pO[:].rearrange("p (h d) -> p h d", h=NH), ecum[:].unsqueeze(2).to_broadcast([128, NH, 64]),
                     ALU.mult, r=pO.k() + ecum.k(), w=Yof.k())
                for h in range(NH):
                    g = h // 8
                    pb_ = pbs[h % 3]
                    P.mm(pb_[:, 0:128], Abc[:, h, :], tri, True, True, r=Abc.k() + cs.k(), w=pb_.k())
                    w_ = wt[h % 2]
                    e_ = we[h % 2]
                    s_ = Stt[h % 3]
                    P.tt("dve", w_[:], pb_[:, 0:128], maskneg, ALU.add, r=pb_.k() + cs.k(), w=w_.k())
                    P.act(e_[:], w_[:], AF.Exp, r=w_.k() + ncum.k(), w=e_.k(), bias=ncum[:, h:h + 1])
                    P.tt("pool", s_[:], e_[:], Gt[:, g, :], ALU.mult, r=e_.k() + Gt.k(), w=s_.k())
                    P.mm(pY[:, h * 64:(h + 1) * 64], s_[:], Xd[:, h, :], True, True, r=s_.k() + Xd.k(), w=pY.k())
                P.tt("dve", Ytm[:].rearrange("p h d -> p (h d)"), pY[:], Yof[:].rearrange("p h d -> p (h d)"), ALU.add,
                     r=pY.k() + Yof.k(), w=Ytm.k())
                for h in range(NH):
                    g = h // 8
                    P.mm(pO[:, h * 64:(h + 1) * 64], Bt[:, g, :], Xdec[:, h, :], True, True, r=Bt.k() + Xdec.k(), w=pO.k())
                P.tt("dve", S[:], S[:], etot[:].unsqueeze(2).to_broadcast([128, NH, 64]), ALU.mult, r=S.k() + etot.k(), w=S.k())
                P.tt("dve", S[:].rearrange("p h d -> p (h d)"), S[:].rearrange("p h d -> p (h d)"), pO[:], ALU.add,
                     r=S.k() + pO.k(), w=S.k())
                P.cp("act", Sb[:], S[:], r=S.k(), w=Sb.k())
                for half in range(2):
                    for q in range(4):
                        i = half * 4 + q
                        P.tr(pT[:, q * 128:(q + 1) * 128], Ytm[:, 2 * i:2 * i + 2, :].rearrange("p h d -> p (h d)"), ident,
                             r=Ytm.k() + cs.k(), w=pT.k())
                    for q in range(4):
                        i = half * 4 + q
                        P.cp("act" if q % 2 else "dve", yfm[:, i, c0:c0 + L], pT[:, q * 128:(q + 1) * 128], r=pT.k(), w=yfm.k(i))

            def c_z(j, ct, ps):
                i = ct - 13
                zs = ft[0]
                P.act(zs[:, :N], ps[:, :N], AF.Silu, r=ps.k(), w=zs.k())
                P.stt("dve", yfm[:, i, :], cfm[:, i, :], mds[:, 0, i:i + 1], yfm[:, i, :], ALU.mult, ALU.add,
                      r=cfm.k(i) + mds.k() + yfm.k(i), w=yfm.k(i))
                P.tt("dve", yfm[:, i, :], yfm[:, i, :], zs[:, :N], ALU.mult, r=yfm.k(i) + zs.k(), w=yfm.k(i))
                sq_ = ft[1 + (i % 2)]
                P.tt("pool", sq_[:, :N], yfm[:, i, :], yfm[:, i, :], ALU.mult, r=yfm.k(i), w=sq_.k())
                P.mm(pT[:, :N], onesg[:], sq_[:, :N], i % 4 == 0, i % 4 == 3, r=onesg.k() + sq_.k(), w=pT.k())
                if i % 4 == 3:
                    P.act(msq[:], pT[:, :N], AF.Sqrt, r=pT.k(), w=msq.k(), bias=RMS_EPS)
                    P.op("dve", lambda e: e.reciprocal(out=msq[:], in_=msq[:]), r=msq.k(), w=msq.k())
                    for i2 in range(i - 3, i + 1):
                        o_ = ob[oi[0] % 2]
                        oi[0] += 1
                        P.stt("dve", o_[:], yfm[:, i2, :], mds[:, 1, i2:i2 + 1], msq[:], ALU.mult, ALU.mult,
                              r=yfm.k(i2) + mds.k() + msq.k(), w=o_.k())
                        P.dma("act", yo.ap(slice(i2 * 128, (i2 + 1) * 128), t0, N), o_[:], r=o_.k(), w=yo.k(), sem=f"do{oi[0] % 2}")
            proj(P, ws, w_in, range(13, 21), KD, xb, N, pss, c_z)

            order = []
            for n in range(8):
                order += [29 + n, 21 + n]
            xc, xcb, gl = ft[3], fb[0], ft[5]

            def c_lru(j, ct, ps):
                if ct >= 29:
                    n = ct - 29
                    u = ft[2]
                    P.cp("act", u[:, 3:3 + N], ps[:, :N], r=ps.k(), w=u.k())
                    P.cp("pool", u[:, 0:3], lhalo[:, n, :], r=lhalo.k(n), w=u.k())
                    P.cp("pool", lhalo[:, n, :], u[:, N:N + 3], r=u.k(), w=lhalo.k(n))
                    src_k[0], cwt_k[0], out_k[0] = u.k(), lcs.k(), xc.k()
                    conv4(xc[:, :N], u, lcs, n, N)
                    P.cp("act", xcb[:], xc[:, :N], r=xc.k(), w=xcb.k())
                    P.mm(pY[:, :N], wab[:, n, :], xcb[:], True, True, r=wab.k() + xcb.k(), w=pY.k())
                    P.mm(pO[:, :N], wxb[:, n, :], xcb[:], True, True, r=wxb.k() + xcb.k(), w=pO.k())
                    r_, i_ = ft[0], ft[1]
                    P.act(r_[:, :N], pY[:, :N], AF.Sigmoid, r=pY.k() + lvs.k(), w=r_.k(), bias=lvs[:, 0, n:n + 1])
                    P.act(i_[:, :N], pO[:, :N], AF.Sigmoid, r=pO.k() + lvs.k(), w=i_.k(), bias=lvs[:, 1, n:n + 1])
                    a_ = ft[4]
                    P.act(a_[:, :N], r_[:, :N], AF.Exp, r=r_.k() + c8.k(), w=a_.k(), scale=c8[:, n:n + 1])
                    P.tt("pool", r_[:, :N], a_[:, :N], a_[:, :N], ALU.mult, r=a_.k(), w=r_.k())
                    P.ts("dve", r_[:, :N], r_[:, :N], -1.0, 1.0, ALU.mult, ALU.add, r=r_.k(), w=r_.k())
                    P.act(r_[:, :N], r_[:, :N], AF.Sqrt, r=r_.k(), w=r_.k())
                    P.tt("dve", i_[:, :N], i_[:, :N], xc[:, :N], ALU.mult, r=i_.k() + xc.k(), w=i_.k())
                    P.tt("dve", i_[:, :N], i_[:, :N], r_[:, :N], ALU.mult, r=i_.k() + r_.k(), w=i_.k())
                    P.op("dve", lambda e, n=n: e.tensor_tensor_scan(out=xc[:, :N], data0=a_[:, :N], data1=i_[:, :N],
                                                                   initial=hst[:, n:n + 1], op0=ALU.mult, op1=ALU.add),
                         r=a_.k() + i_.k() + hst.k(n), w=xc.k())
                    P.cp("pool", hst[:, n:n + 1], xc[:, N - 1:N], r=xc.k(), w=hst.k(n))
                else:
                    n = ct - 21
                    y_ = ft[0]
                    P.cp("act", y_[:, :N], ps[:, :N], r=ps.k(), w=y_.k())
                    y2 = ft[1]
                    P.tt("pool", y2[:, :N], y_[:, :N], y_[:, :N], ALU.mult, r=y_.k(), w=y2.k())
                    P.ts("dve", y2[:, :N], y2[:, :N], 0.044715, 1.0, ALU.mult, ALU.add, r=y2.k(), w=y2.k())
                    P.tt("dve", y2[:, :N], y2[:, :N], y_[:, :N], ALU.mult, r=y2.k() + y_.k(), w=y2.k())
                    P.act(y2[:, :N], y2[:, :N], AF.Sigmoid, r=y2.k(), w=y2.k(), scale=1.5957691216057308)
                    P.tt("dve", y2[:, :N], y2[:, :N], y_[:, :N], ALU.mult, r=y2.k() + y_.k(), w=y2.k())
                    o_ = ob[oi[0] % 2]
                    oi[0] += 1
                    P.tt("dve", o_[:], y2[:, :N], xc[:, :N], ALU.mult, r=y2.k() + xc.k(), w=o_.k())
                    P.dma("act", yo.ap(slice(1024 + n * 128, 1024 + (n + 1) * 128), t0, N), o_[:], r=o_.k(), w=yo.k(), sem=f"do{oi[0] % 2}")
            proj(P, ws, w_in, order, KD, xb, N, pss, c_lru)
        if io.get("post"):
            io["post"](P)
        P.emit()
    return P


NEG = -30000.0
C = 64
GN_EPS = 64e-5


def consts_e():
    i = np.arange(64)
    su = (i[None, :] > i[:, None]).astype(np.float32)
    iu = (i[None, :] >= i[:, None]).astype(np.float32)
    out = np.zeros((128, 5, 128), np.float32)
    out[:, 0, :] = np.eye(128)
    out[:64, 1, :] = np.concatenate([su, iu], 1)
    out[:64, 2, :64] = np.where(iu > 0, 0.0, NEG)
    out[:64, 2, 64:] = np.where(su > 0, 0.0, NEG)
    cm = np.ones(128, np.float32); cm[0] = 0; cm[64] = 0
    out[:, 3, :] = cm[None, :]
    return out


def sel_np():
    s = np.zeros((128, 8, 128), np.float32)
    for i in range(8):
        s[i, i, :] = 1.0
    return s


def relayout_w64(w):
    K, N = w.shape
    return np.ascontiguousarray(w.reshape(K // 128, 128, N // 64, 64).transpose(2, 1, 0, 3))


def neumann(P, Nall, Lall, X, pN, pL, pX, NHD, ident64b):
    P.tt("dve", X[:], Nall[:], ident64b, ALU.add, r=Nall.k(), w=X.k())
    for step in range(5):
        last = step == 4
        for h in range(NHD):
            P.mm(pL[0:64, h * 64:(h + 1) * 64], Nall[:, h, :], Lall[:, h, :], True, True, r=Nall.k() + Lall.k(), w=pL.k())
        if not last:
            for h in range(NHD):
                P.mm(pN[0:64, h * 64:(h + 1) * 64], Lall[:, h, :], Nall[:, h, :], True, True, r=Nall.k() + Lall.k(), w=pN.k())
        P.cp("act", Lall[:].rearrange("p h d -> p (h d)"), pL[0:64, :NHD * 64], r=pL.k(), w=Lall.k())
        if not last:
            P.cp("dve", Nall[:].rearrange("p h d -> p (h d)"), pN[0:64, :NHD * 64], r=pN.k(), w=Nall.k())
        for h in range(NHD):
            P.mm(pX[0:64, h * 64:(h + 1) * 64], Lall[:, h, :], X[:, h, :], True, True, r=Lall.k() + X.k(), w=pX.k())
        P.tt("dve", X[:].rearrange("p h d -> p (h d)"), X[:].rearrange("p h d -> p (h d)"), pX[0:64, :NHD * 64], ALU.add,
             r=X.k() + pX.k(), w=X.k())


def phase_E(nc, outer, io, T, TB=256, D=2048, do_gdn=True):
    P = Prog(nc)
    P.semstack = outer
    KD = D // 128
    NB = T // TB
    NCH = TB // C
    N = TB
    with ExitStack() as st:
        P.stack = st
        w_a, w_b, cst, rv, rmu, rw2, gcw, gv, yo = (io[k] for k in ("w_a", "w_b", "cst", "rv", "rmu", "rw2", "gcw", "gv", "yo"))

        cs = P.sb([128, 5, 128], F32, "cs")
        ident = cs[:, 0, :]
        mask2 = cs[0:64, 1, :]
        cmask = cs[:, 3, :]
        rvs = P.sb([64, 10, 8], F32, "rvs")
        omka = P.sb([64, 8], F32, "omka")
        rmus = P.sb([128, 4], F32, "rmus")
        xst = P.sb([128, KD // 2, TB], F32, "xst")
        assert (KD // 2) * TB == 2048
        w2s = TlV(xst.h[:].rearrange("p k t -> p (k t)").rearrange("p (a b) -> p a b", a=4), xst)
        w2b = P.sb([128, 4, 512], BF16, "w2b")
        gcs = P.sb([128, 4, 12], F32, "gcs")
        gvs = P.sb([128, 4], F32, "gvs")
        ones64 = P.sb([64, 64], F32, "ones64")
        ones64m = P.sb([64, 64], F32, "ones64m")
        for (t_, d_) in ((cs, cst), (rvs, rv), (rmus, rmu), (w2s, rw2), (gcs, gcw), (gvs, gv)):
            P.dma("act", t_[:], d_[:], w=t_.k(), sem="dc")
        selc = P.sb([128, 8, 128], F32, "selc")
        P.dma("act", selc[:], io["sel"][:], w=selc.k(), sem="dc")
        P.fence([cs, rvs, rmus, w2s, gcs, gvs, selc])
        P.cp("pool", w2b[:], w2s[:], r=w2s.k(), w=w2b.k())
        P.op("pool", lambda e: e.memset(ones64[:], 1.0), w=ones64.k())
        P.op("pool", lambda e: e.memset(ones64m[:], 1.0 / 64.0), w=ones64m.k())
        P.ts("dve", omka[:], rvs[:, 6, :], -1.0, 1.0, ALU.mult, ALU.add, r=rvs.k(), w=omka.k())

        xb = P.sb([128, KD, TB], BF16, "xb")
        ws = WStream(P)
        pq = [P.ps([128, 512], F32, f"pq{i}") for i in range(8)]
        pss = pq[0:2]

        halo1 = P.sb([128, 28, 1], F32, "halo1", nsub=28)
        P.op("pool", lambda e: e.memset(halo1[:], 0.0), w=halo1.k())
        raw = [P.sb([128, N + 1], F32, f"raw{i}") for i in range(3)]
        lo_b = P.sb([128, 4, N], BF16, "lo_b", nsub=4)
        ar128 = P.sb([128, 8, NCH, 128], F32, "ar", nsub=8)
        ar = Tl(ar128.h[0:64], ar128.name, 8)
        bt128 = P.sb([128, 8, N], F32, "bt", nsub=8)
        bt = Tl(bt128.h[0:64], bt128.name, 8)
        kt128 = P.sb([128, 8, N], F32, "kt", nsub=8)
        kt = Tl(kt128.h[0:64], kt128.name, 8)
        vv128 = P.sb([128, 8, N], F32, "vv", nsub=8)
        vv = Tl(vv128.h[0:64], vv128.name, 8)
        bon128 = P.sb([128, 8, N], F32, "bon", nsub=8)
        bon = Tl(bon128.h[0:64], bon128.name, 8)
        gg128 = P.sb([128, 8, N], F32, "gg", nsub=8)
        gg = Tl(gg128.h[0:64], gg128.name, 8)
        PC = P.sb([64, 8, NCH], F32, "PC", nsub=8)
        yall = P.sb([64, 8, N], F32, "yall")
        H = P.sb([64, 8, 64], F32, "H")
        P.op("pool", lambda e: e.memset(H[:], 0.0), w=H.k())
        rr = [P.sb([64, N], F32, f"rr{i}") for i in range(2)]
        kk_ = [P.sb([64, N], F32, f"kk{i}") for i in range(2)]
        tA = [P.sb([64, N], F32, f"tA{i}") for i in range(8)]
        vtm = P.sb([64, 8, 64], F32, "vtm")
        btm = P.sb([64, 8, 64], F32, "btm")
        ktm = P.sb([64, 8, 64], F32, "ktm")
        sc1 = P.sb([64, 8, 128], F32, "sc1")
        sc2 = P.sb([64, 8, 128], F32, "sc2")
        Nall = P.sb([64, 8, 64], F32, "Nall")
        Lall = P.sb([64, 8, 64], F32, "Lall")
        X = P.sb([64, 8, 64], F32, "X")
        rhs_sb = P.sb([64, 8, 64], F32, "rhs_sb")
        U = P.sb([64, 8, 64], F32, "U")
        ob = [P.sb([128, N], BF16, f"ob{i}") for i in range(2)]
        oi = [0]
        ident64b = cs[0:64, 0, 0:64].unsqueeze(1).to_broadcast([64, 8, 64])


        ones128m = P.sb([128, 128], F32, "ones128m")
        ones128 = P.sb([128, 128], F32, "ones128")
        P.op("pool", lambda e: e.memset(ones128m[:], 1.0 / 128.0), w=ones128m.k())
        P.op("pool", lambda e: e.memset(ones128[:], 1.0), w=ones128.k())
        gnegA = P.sb([128, 1], F32, "gnegA")
        P.act(gnegA[:], gvs[:, 0:1], AF.Exp, r=gvs.k(), w=gnegA.k())
        P.ts("dve", gnegA[:], gnegA[:], -1.0, None, ALU.mult, None, r=gnegA.k(), w=gnegA.k())
        ghalo = P.sb([128, 12, 3], F32, "ghalo", nsub=12)
        P.op("pool", lambda e: e.memset(ghalo[:], 0.0), w=ghalo.k())
        graw = [P.sb([128, N + 3], F32, f"graw{i}") for i in range(2)]
        gq = TlV(bt128.h[:, 0:4, :], bt128)
        gk = TlV(bt128.h[:, 4:8, :], bt128)
        gvv = TlV(kt128.h[:, 0:4, :], kt128)
        kq = TlV(ar128.h[:, 0:4], ar128)
        vb = TlV(kt128.h[:, 4:8, :], kt128)
        nkbg = TlV(vv128.h[:, 0:4, :], vv128)
        qd = TlV(vv128.h[:, 4:8, :], vv128)
        kdec = TlV(bon128.h[:, 0:4, :], bon128)
        gcbc = TlV(bon128.h[:, 4:8, :], bon128)
        egc = TlV(gg128.h[:, 0:4, :], gg128)
        oall = TlV(gg128.h[:, 4:8, :], gg128)
        sgm = P.sb([128, N], F32, "sgm")
        gcf = P.sb([128, N], F32, "gcf")
        gS = P.sb([128, 4, 128], F32, "gS")
        P.op("pool", lambda e: e.memset(gS[:], 0.0), w=gS.k())
        gT = [P.sb([128, N], F32, f"gT{i}") for i in range(3)]
        ngc = P.sb([64, 128], F32, "ngc")
        wd_ = P.sb([64, 4, 64], F32, "wd_")
        Di = P.sb([64, 4, 64], F32, "Di")
        Ds = P.sb([64, 4, 64], F32, "Ds")
        attT = P.sb([64, 4, 64], F32, "attT")
        kdtm = P.sb([64, 4, 128], F32, "kdtm")
        ident64b4 = cs[0:64, 0, 0:64].unsqueeze(1).to_broadcast([64, 4, 64])
        mneg_i = cs[0:64, 2, 0:64]
        su_b4 = cs[0:64, 1, 0:64].unsqueeze(1).to_broadcast([64, 4, 64])

        cmk128 = P.sb([128, N], F32, "cmk128")
        for i_ in range(N // 128):
            P.cp("pool", cmk128[:, i_ * 128:(i_ + 1) * 128], cs[:, 3, :], r=cs.k(), w=cmk128.k())
        cmk = P.sb([64, N], F32, "cmk")
        for i_ in range(N // 128):
            P.cp("pool", cmk[:, i_ * 128:(i_ + 1) * 128], cs[0:64, 3, :], r=cs.k(), w=cmk.k())

        for blk in range(NB):
            t0 = blk * TB
            for hf in range(2):
                P.dma("act", xst[:], io["xsrc"](t0, hf).rearrange("(k p) t -> p k t", p=128), r=io["xsrc_k"], w=xst.k(), sem="dx")
                P.cp("dve" if hf else "pool", xb[:, hf * (KD // 2):(hf + 1) * (KD // 2), :], xst[:], r=xst.k(), w=xb.k())

            def c_lo(j, ct, ps):
                rawt = raw[j % 3]
                P.cp("act", rawt[:, 1:N + 1], ps[:, :N], r=ps.k(), w=rawt.k())
                P.cp("pool", rawt[:, 0:1], halo1[:, 24 + ct, :], r=halo1.k(24 + ct), w=rawt.k())
                P.cp("pool", halo1[:, 24 + ct, :], rawt[:, N:N + 1], r=rawt.k(), w=halo1.k(24 + ct))
                d32 = raw[(j + 1) % 3]
                P.tt("dve", d32[:, 0:N], rawt[:, 0:N], rawt[:, 1:N + 1], ALU.subtract, r=rawt.k(), w=d32.k())
                P.stt("dve", d32[:, 0:N], d32[:, 0:N], rmus[:, ct:ct + 1], rawt[:, 1:N + 1], ALU.mult, ALU.add,
                      r=d32.k() + rmus.k() + rawt.k(), w=d32.k())
                if ct == 0:
                    P.act(lo_b[:, 0, :], d32[:, 0:N], AF.Tanh, r=d32.k(), w=lo_b.k(0))
                elif ct == 1:
                    P.cp("act", lo_b[:, 1, :], d32[:, 0:N], r=d32.k(), w=lo_b.k(1))
                else:
                    P.act(lo_b[:, ct, :], d32[:, 0:N], AF.Sigmoid, r=d32.k(), w=lo_b.k(ct))
            proj(P, ws, w_b, range(4), KD, xb, N, pss, c_lo)

            def c_rkv(j, ct, ps):
                h, which = ct // 3, ct % 3
                rawt = raw[j % 3]
                P.cp("act", rawt[0:64, 1:N + 1], ps[0:64, :N], r=ps.k(), w=rawt.k())
                P.cp("pool", rawt[0:64, 0:1], halo1[0:64, ct, :], r=halo1.k(ct), w=rawt.k())
                P.cp("pool", halo1[0:64, ct, :], rawt[0:64, N:N + 1], r=rawt.k(), w=halo1.k(ct))
                dst = (rr[h % 2], kk_[h % 2], None)[which]
                dst_ap = vv[:, h, :] if which == 2 else dst[:, :]
                dst_k = vv.k(h) if which == 2 else dst.k()
                d = tA[7]
                P.tt("dve", d[:, :], rawt[0:64, 0:N], rawt[0:64, 1:N + 1], ALU.subtract, r=rawt.k(), w=d.k())
                P.stt("dve", dst_ap, d[:, :], rvs[:, which, h:h + 1], rawt[0:64, 1:N + 1], ALU.mult, ALU.add,
                      r=d.k() + rvs.k() + rawt.k(), w=dst_k)
                if which != 2:
                    return
                r_, k_ = rr[h % 2], kk_[h % 2]
                hs = slice(h * 64, (h + 1) * 64)
                pw, pa, pg, pn = pq[2], pq[3], pq[4], pq[5]
                P.mm(pw[0:64, :N], w2b[:, 0, hs], lo_b[:, 0, :], True, True, r=w2b.k() + lo_b.k(0), w=pw.k())
                P.mm(pa[0:64, :N], w2b[:, 1, hs], lo_b[:, 1, :], True, True, r=w2b.k() + lo_b.k(1), w=pa.k())
                P.mm(pg[0:64, :N], w2b[:, 2, hs], lo_b[:, 2, :], True, False, r=w2b.k() + lo_b.k(2), w=pg.k())
                P.mm(pg[0:64, :N], w2b[:, 3, hs], lo_b[:, 3, :], False, True, r=w2b.k() + lo_b.k(3), w=pg.k())
                lw, cl, asig, e1, e2, kkn, tmp = tA[0], tA[1], tA[2], tA[3], tA[4], tA[5], tA[6]
                P.act(lw[:], pw[0:64, :N], AF.Sigmoid, r=pw.k() + rvs.k(), w=lw.k(), bias=rvs[:, 3, h:h + 1])
                P.ts("dve", lw[:], lw[:], -0.6065306597126334, None, ALU.mult, None, r=lw.k(), w=lw.k())
                P.act(asig[:], pa[0:64, :N], AF.Sigmoid, r=pa.k() + rvs.k(), w=asig.k(), bias=rvs[:, 4, h:h + 1])
                P.cp("act", gg[:, h, :], pg[0:64, :N], r=pg.k(), w=gg.k(h))
                P.op("dve", lambda e: e.tensor_tensor_scan(out=cl[:], data0=cmk[:], data1=lw[:], initial=0.0,
                                                           op0=ALU.mult, op1=ALU.add),
                     r=lw.k() + cmk.k(), w=cl.k())
                P.act(e1[:], cl[:], AF.Exp, r=cl.k(), w=e1.k())
                P.tt("dve", ar[:, h, :, 64:128], r_[:].rearrange("p (c t) -> p c t", t=64), e1[:].rearrange("p (c t) -> p c t", t=64),
                     ALU.mult, r=r_.k() + e1.k(), w=ar.k(h))
                P.cp("pool", PC[:, h, :], e1[:].rearrange("p (c t) -> p c t", t=64)[:, :, 63], r=e1.k(), w=PC.k(h))
                P.act(e2[:], cl[:], AF.Exp, r=cl.k(), w=e2.k(), scale=-1.0)
                P.ts("dve", kkn[:], k_[:], rvs[:, 5, h:h + 1], None, ALU.mult, None, r=k_.k() + rvs.k(), w=kkn.k())
                P.tt("pool", tmp[:], kkn[:], kkn[:], ALU.mult, r=kkn.k(), w=tmp.k())
                P.mm(pn[0:64, :N], ones64[:], tmp[:], True, True, r=ones64.k() + tmp.k(), w=pn.k())
                P.act(tmp[:], pn[0:64, :N], AF.Sqrt, r=pn.k(), w=tmp.k(), bias=1e-6)
                P.op("dve", lambda e: e.reciprocal(out=tmp[:], in_=tmp[:]), r=tmp.k(), w=tmp.k())
                P.tt("dve", kkn[:], kkn[:], tmp[:], ALU.mult, r=kkn.k() + tmp.k(), w=kkn.k())
                P.tt("dve", tmp[:], kkn[:], asig[:], ALU.mult, r=kkn.k() + asig.k(), w=tmp.k())
                P.tt("dve", bt[:, h, :], tmp[:], e2[:], ALU.mult, r=tmp.k() + e2.k(), w=bt.k(h))
                P.tt("dve", tmp[:], cl[:], lw[:], ALU.subtract, r=cl.k() + lw.k(), w=tmp.k())
                P.act(tmp[:], tmp[:], AF.Exp, r=tmp.k(), w=tmp.k())
                P.stt("dve", ar[:, h, :, 0:64], kkn[:].rearrange("p (c t) -> p c t", t=64), -1.0,
                      tmp[:].rearrange("p (c t) -> p c t", t=64), ALU.mult, ALU.mult, r=kkn.k() + tmp.k(), w=ar.k(h))
                P.ts("dve", tmp[:], asig[:], rvs[:, 6, h:h + 1], omka[:, h:h + 1], ALU.mult, ALU.add, r=asig.k() + rvs.k() + omka.k(), w=tmp.k())
                P.tt("dve", tmp[:], tmp[:], k_[:], ALU.mult, r=tmp.k() + k_.k(), w=tmp.k())
                P.tt("dve", kt[:, h, :], tmp[:], e2[:], ALU.mult, r=tmp.k() + e2.k(), w=kt.k(h))
                P.stt("dve", tmp[:], tmp[:], rvs[:, 7, h:h + 1], r_[:], ALU.mult, ALU.mult, r=tmp.k() + rvs.k() + r_.k(), w=tmp.k())
                P.mm(pn[0:64, :N], ones64[:], tmp[:], True, True, r=ones64.k() + tmp.k(), w=pn.k())
                P.tt("dve", bon[:, h, :], pn[0:64, :N], vv[:, h, :], ALU.mult, r=pn.k() + vv.k(h), w=bon.k(h))
            proj(P, ws, w_a, range(24), KD, xb, N, pss, c_rkv, cw=64)

            for c in range(NCH):
                cc = slice(c * C, (c + 1) * C)
                pT1, pT2, pT3 = pq[7], pq[6], pq[5]
                for h in range(8):
                    hs = slice(h * 64, (h + 1) * 64)
                    P.tr(pT1[0:64, hs], vv[:, h, cc], ident[0:64, 0:64], r=vv.k(h) + cs.k(), w=pT1.k())
                    P.tr(pT2[0:64, hs], bt[:, h, cc], ident[0:64, 0:64], r=bt.k(h) + cs.k(), w=pT2.k())
                    P.tr(pT3[0:64, hs], kt[:, h, cc], ident[0:64, 0:64], r=kt.k(h) + cs.k(), w=pT3.k())
                P.cp("act", vtm[:].rearrange("p h d -> p (h d)"), pT1[0:64, :], r=pT1.k(), w=vtm.k())
                P.cp("dve", btm[:].rearrange("p h d -> p (h d)"), pT2[0:64, :], r=pT2.k(), w=btm.k())
                P.cp("act", ktm[:].rearrange("p h d -> p (h d)"), pT3[0:64, :], r=pT3.k(), w=ktm.k())
                for h in range(8):
                    pa_ = pq[0] if h < 4 else pq[1]
                    pb_ = pq[2] if h < 4 else pq[3]
                    o = (h % 4) * 128
                    P.mm(pa_[0:64, o:o + 128], bt[:, h, cc], ar[:, h, c, :], True, True, r=bt.k(h) + ar.k(h), w=pa_.k())
                    P.mm(pb_[0:64, o:o + 128], kt[:, h, cc], ar[:, h, c, :], True, True, r=kt.k(h) + ar.k(h), w=pb_.k())
                m2b = mask2.unsqueeze(1).to_broadcast([64, 4, 128])
                for half in range(2):
                    P.tt("dve", sc1[:, half * 4:(half + 1) * 4, :], pq[half][0:64, :].rearrange("p (h d) -> p h d", h=4), m2b, ALU.mult,
                         r=pq[half].k() + cs.k(), w=sc1.k())
                    P.tt("dve", sc2[:, half * 4:(half + 1) * 4, :], pq[2 + half][0:64, :].rearrange("p (h d) -> p h d", h=4), m2b, ALU.mult,
                         r=pq[2 + half].k() + cs.k(), w=sc2.k())
                P.cp("pool", Nall[:], sc1[:, :, 0:64], r=sc1.k(), w=Nall.k())
                pL, pN, pX = pq[4], pq[5], pq[6]
                for h in range(8):
                    P.tr(pL[0:64, h * 64:(h + 1) * 64], sc1[:, h, 0:64], ident[0:64, 0:64], r=sc1.k() + cs.k(), w=pL.k())
                P.cp("act", Lall[:].rearrange("p h d -> p (h d)"), pL[0:64, :], r=pL.k(), w=Lall.k())
                neumann(P, Nall, Lall, X, pN, pL, pX, 8, ident64b)
                pR, pU, pY, pH = pq[0], pq[1], pq[2], pq[3]
                for h in range(8):
                    hs = slice(h * 64, (h + 1) * 64)
                    P.mm(pR[0:64, hs], ar[:, h, c, 0:64], H[:, h, :], True, False, r=ar.k(h) + H.k(), w=pR.k())
                    P.mm(pR[0:64, hs], sc2[:, h, 0:64], vtm[:, h, :], False, True, r=sc2.k() + vtm.k(), w=pR.k())
                P.cp("act", rhs_sb[:].rearrange("p h d -> p (h d)"), pR[0:64, :], r=pR.k(), w=rhs_sb.k())
                for h in range(8):
                    hs = slice(h * 64, (h + 1) * 64)
                    P.mm(pU[0:64, hs], X[:, h, :], rhs_sb[:, h, :], True, True, r=X.k() + rhs_sb.k(), w=pU.k())
                P.cp("act", U[:].rearrange("p h d -> p (h d)"), pU[0:64, :], r=pU.k(), w=U.k())
                for h in range(8):
                    hs = slice(h * 64, (h + 1) * 64)
                    P.mm(pY[0:64, hs], H[:, h, :], ar[:, h, c, 64:128], True, False, r=ar.k(h) + H.k(), w=pY.k())
                    P.mm(pY[0:64, hs], U[:, h, :], sc1[:, h, 64:128], False, False, r=U.k() + sc1.k(), w=pY.k())
                    P.mm(pY[0:64, hs], vtm[:, h, :], sc2[:, h, 64:128], False, True, r=vtm.k() + sc2.k(), w=pY.k())
                    P.mm(pH[0:64, hs], btm[:, h, :], U[:, h, :], True, False, r=btm.k() + U.k(), w=pH.k())
                    P.mm(pH[0:64, hs], ktm[:, h, :], vtm[:, h, :], False, True, r=ktm.k() + vtm.k(), w=pH.k())
                P.cp("act", yall[:, :, cc], pY[0:64, :].rearrange("p (h d) -> p h d", h=8), r=pY.k(), w=yall.k())
                P.tt("dve", H[:].rearrange("p h d -> p (h d)"), H[:].rearrange("p h d -> p (h d)"), pH[0:64, :], ALU.add, r=H.k() + pH.k(), w=H.k())
                P.tt("dve", H[:], H[:], PC[:, :, c].unsqueeze(2).to_broadcast([64, 8, 64]), ALU.mult, r=H.k() + PC.k(), w=H.k())

            for h in range(8):
                pm, pv = pq[4], pq[5]
                y_ = yall[:, h, :]
                sq_, mean, t_ = tA[0], tA[1], tA[2]
                P.tt("pool", sq_[:], y_, y_, ALU.mult, r=yall.k(), w=sq_.k())
                P.mm(pm[0:64, :N], ones64m[:], y_, True, True, r=ones64m.k() + yall.k(), w=pm.k())
                P.mm(pv[0:64, :N], ones64m[:], sq_[:], True, True, r=ones64m.k() + sq_.k(), w=pv.k())
                P.cp("act", mean[:], pm[0:64, :N], r=pm.k(), w=mean.k())
                P.tt("dve", t_[:], mean[:], mean[:], ALU.mult, r=mean.k(), w=t_.k())
                P.tt("dve", t_[:], pv[0:64, :N], t_[:], ALU.subtract, r=pv.k() + t_.k(), w=t_.k())
                P.act(t_[:], t_[:], AF.Sqrt, r=t_.k(), w=t_.k(), bias=GN_EPS)
                P.op("dve", lambda e, t_=t_: e.reciprocal(out=t_[:], in_=t_[:]), r=t_.k(), w=t_.k())
                P.tt("dve", mean[:], y_, mean[:], ALU.subtract, r=yall.k() + mean.k(), w=mean.k())
                P.tt("dve", mean[:], mean[:], t_[:], ALU.mult, r=mean.k() + t_.k(), w=mean.k())
                P.ts("dve", mean[:], mean[:], rvs[:, 8, h:h + 1], rvs[:, 9, h:h + 1], ALU.mult, ALU.add, r=mean.k() + rvs.k(), w=mean.k())
                P.tt("dve", mean[:], mean[:], bon[:, h, :], ALU.add, r=mean.k() + bon.k(h), w=mean.k())
                o_ = ob[oi[0] % 2]
                oi[0] += 1
                P.tt("dve", o_[0:64, :], mean[:], gg[:, h, :], ALU.mult, r=mean.k() + gg.k(h), w=o_.k())
                P.dma("act", yo.ap(slice(h * 64, (h + 1) * 64), t0, N), o_[0:64, :], r=o_.k(), w=yo.k(), sem=f"do{oi[0] % 2}")

            if not do_gdn:
                continue
            def c_ba(j, ct, ps):
                P.act(sgm[:], ps[:, :N], AF.Sigmoid, r=ps.k(), w=sgm.k())
                t_ = gT[0]
                P.act(t_[:], ps[:, :N], AF.Exp, r=ps.k() + gvs.k(), w=t_.k(), bias=gvs[:, 1:2])
                P.act(t_[:], t_[:], AF.Ln, r=t_.k(), w=t_.k(), bias=1.0)
                P.ts("dve", t_[:], t_[:], gnegA[:, 0:1], None, ALU.mult, None, r=t_.k() + gnegA.k(), w=t_.k())
                P.op("dve", lambda e: e.tensor_tensor_scan(out=gcf[:], data0=cmk128[:], data1=t_[:], initial=0.0,
                                                           op0=ALU.mult, op1=ALU.add), r=t_.k() + cmk128.k(), w=gcf.k())
                for h in range(4):
                    pb_ = pq[2 + (h % 2)]
                    P.mm(pb_[:, :N], selc[:, 4 + h, :], gcf[:], True, True, r=selc.k() + gcf.k(), w=pb_.k())
                    P.cp("act", gcbc[:, h, :], pb_[:, :N], r=pb_.k(), w=gcbc.k(h))
                    P.act(egc[:, h, :], pb_[:, :N], AF.Exp, r=pb_.k(), w=egc.k(h))
            proj(P, ws, w_b, [20], KD, xb, N, pss, c_ba)

            def c_qkv(j, ct, ps):
                ti = ct - 4
                which, h = ti // 4, ti % 4
                u = graw[j % 2]
                P.cp("act", u[:, 3:3 + N], ps[:, :N], r=ps.k(), w=u.k())
                P.cp("pool", u[:, 0:3], ghalo[:, ti, :], r=ghalo.k(ti), w=u.k())
                P.cp("pool", ghalo[:, ti, :], u[:, N:N + 3], r=u.k(), w=ghalo.k(ti))
                dst = (gq, gk, gvv)[which]
                o_ = dst[:, h, :]
                P.ts("dve", o_, u[:, 3:3 + N], gcs[:, 3, ti:ti + 1], None, ALU.mult, None, r=u.k() + gcs.k(), w=dst.k(h))
                for jj in range(3):
                    P.stt("dve", o_, u[:, jj:jj + N], gcs[:, jj, ti:ti + 1], o_, ALU.mult, ALU.add, r=u.k() + gcs.k() + dst.k(h), w=dst.k(h))
                P.act(o_, o_, AF.Silu, r=dst.k(h), w=dst.k(h))
                if which < 2:
                    sq_ = gT[1]
                    pn = pq[4]
                    P.tt("pool", sq_[:], o_, o_, ALU.mult, r=dst.k(h), w=sq_.k())
                    P.mm(pn[:, :N], ones128[:], sq_[:], True, True, r=ones128.k() + sq_.k(), w=pn.k())
                    P.act(sq_[:], pn[:, :N], AF.Sqrt, r=pn.k(), w=sq_.k(), bias=1e-6)
                    P.op("dve", lambda e: e.reciprocal(out=sq_[:], in_=sq_[:]), r=sq_.k(), w=sq_.k())
                    if which == 0:
                        P.stt("dve", o_, o_, float(128 ** -0.5), sq_[:], ALU.mult, ALU.mult, r=dst.k(h) + sq_.k(), w=dst.k(h))
                    else:
                        P.tt("dve", o_, o_, sq_[:], ALU.mult, r=dst.k(h) + sq_.k(), w=dst.k(h))
                if which != 2:
                    return
                pbb = pq[5]
                P.mm(pbb[:, :N], selc[:, h, :], sgm[:], True, True, r=selc.k() + sgm.k(), w=pbb.k())
                c3 = lambda ap: ap.rearrange("p (c t) -> p c t", t=64)
                P.tt("dve", kq[:, h, :, 0:64], c3(gk[:, h, :]), c3(pbb[:, :N]), ALU.mult, r=gk.k(h) + pbb.k(), w=kq.k(h))
                P.cp("pool", kq[:, h, :, 64:128], c3(gq[:, h, :]), r=gq.k(h), w=kq.k(h))
                P.tt("dve", vb[:, h, :], gvv[:, h, :], pbb[:, :N], ALU.mult, r=gvv.k(h) + pbb.k(), w=vb.k(h))
                t1 = gT[2]
                P.tt("dve", t1[:], gk[:, h, :], pbb[:, :N], ALU.mult, r=gk.k(h) + pbb.k(), w=t1.k())
                P.stt("dve", nkbg[:, h, :], t1[:], -1.0, egc[:, h, :], ALU.mult, ALU.mult, r=t1.k() + egc.k(h), w=nkbg.k(h))
                P.tt("dve", qd[:, h, :], gq[:, h, :], egc[:, h, :], ALU.mult, r=gq.k(h) + egc.k(h), w=qd.k(h))
                P.tt("dve", c3(t1[:]), c3(gcbc[:, h, :])[:, :, 63:64].to_broadcast([128, NCH, 64]), c3(gcbc[:, h, :]), ALU.subtract,
                     r=gcbc.k(h), w=t1.k())
                P.act(t1[:], t1[:], AF.Exp, r=t1.k(), w=t1.k())
                P.tt("dve", kdec[:, h, :], gk[:, h, :], t1[:], ALU.mult, r=gk.k(h) + t1.k(), w=kdec.k(h))
            proj(P, ws, w_b, range(4, 16), KD, xb, N, pss, c_qkv)

            for c in range(NCH):
                cc = slice(c * C, (c + 1) * C)
                pSc, pTr, pL, pN, pX, pR, pO, pSt = pq[0], pq[1], pq[2], pq[3], pq[4], pq[5], pq[6], pq[7]
                P.tr(pTr[0:64, 0:128], gcf[:, cc], ident, r=gcf.k() + cs.k(), w=pTr.k())
                P.ts("dve", ngc[:], pTr[0:64, 0:128], -1.0, None, ALU.mult, None, r=pTr.k(), w=ngc.k())
                for h in range(4):
                    P.mm(pSc[0:64, h * 128:(h + 1) * 128], gk[:, h, cc], kq[:, h, c, :], True, True, r=gk.k(h) + kq.k(h), w=pSc.k())
                P.tt("dve", wd_[:], gcbc[0:64, :, cc], mneg_i.unsqueeze(1).to_broadcast([64, 4, 64]), ALU.add, r=gcbc.k() + cs.k(), w=wd_.k())
                for h in range(4):
                    P.act(Di[:, h, :], wd_[:, h, :], AF.Exp, r=wd_.k() + ngc.k(), w=Di.k(), bias=ngc[:, 4 + h:5 + h])
                P.tt("dve", Ds[:], Di[:], su_b4, ALU.mult, r=Di.k() + cs.k(), w=Ds.k())
                ps3 = pSc[0:64, :].rearrange("p (h d) -> p h d", h=4)
                P.stt("dve", Nall[:, 0:4, :], ps3[:, :, 0:64], -1.0, Ds[:], ALU.mult, ALU.mult, r=pSc.k() + Ds.k(), w=Nall.k())
                P.tt("dve", attT[:], ps3[:, :, 64:128], Di[:], ALU.mult, r=pSc.k() + Di.k(), w=attT.k())
                for h in range(4):
                    P.tr(pL[0:64, h * 64:(h + 1) * 64], Nall[:, h, :], ident[0:64, 0:64], r=Nall.k() + cs.k(), w=pL.k())
                P.cp("act", Lall[:, 0:4, :].rearrange("p h d -> p (h d)"), pL[0:64, 0:256], r=pL.k(), w=Lall.k())
                neumann(P, Tl(Nall.h[:, 0:4, :], Nall.name), Tl(Lall.h[:, 0:4, :], Lall.name), Tl(X.h[:, 0:4, :], X.name), pN, pL, pX, 4, ident64b4)
                for h in range(4):
                    P.tr(pTr[0:64, h * 128:(h + 1) * 128], kdec[:, h, cc], ident, r=kdec.k(h) + cs.k(), w=pTr.k())
                P.cp("act", kdtm[:].rearrange("p h d -> p (h d)"), pTr[0:64, :], r=pTr.k(), w=kdtm.k())
                rhs4 = rhs_sb[:].rearrange("p h d -> p (h d)")
                U4 = U[:].rearrange("p h d -> p (h d)")
                for h in range(4):
                    hs = slice(h * 128, (h + 1) * 128)
                    P.mm(pR[0:64, hs], vb[:, h, cc], ident, True, False, r=vb.k(h) + cs.k(), w=pR.k())
                    P.mm(pR[0:64, hs], nkbg[:, h, cc], gS[:, h, :], False, True, r=nkbg.k(h) + gS.k(), w=pR.k())
                P.cp("act", rhs4, pR[0:64, :], r=pR.k(), w=rhs_sb.k())
                for h in range(4):
                    hs = slice(h * 128, (h + 1) * 128)
                    P.mm(pO[0:64, hs], X[:, h, :], rhs4[:, hs], True, True, r=X.k() + rhs_sb.k(), w=pO.k())
                P.cp("act", U4, pO[0:64, :], r=pO.k(), w=U.k())
                for h in range(4):
                    hs = slice(h * 128, (h + 1) * 128)
                    P.mm(pR[:, h * 64:(h + 1) * 64], gS[:, h, :], qd[:, h, cc], True, True, r=gS.k() + qd.k(h), w=pR.k())
                    P.mm(pX[:, h * 64:(h + 1) * 64], U4[:, hs], attT[:, h, :], True, True, r=U.k() + attT.k(), w=pX.k())
                    P.mm(pSt[:, hs], kdtm[:, h, :], U4[:, hs], True, True, r=kdtm.k() + U.k(), w=pSt.k())
                t_ = gT[0]
                P.cp("act", t_[:, 0:256], pR[:, 0:256], r=pR.k(), w=t_.k())
                P.tt("dve", oall[:, :, cc], pX[:, 0:256].rearrange("p (h d) -> p h d", h=4), t_[:, 0:256].rearrange("p (h d) -> p h d", h=4), ALU.add,
                     r=pX.k() + t_.k(), w=oall.k())
                for h in range(4):
                    hs = slice(h * 128, (h + 1) * 128)
                    P.stt("dve", gS[:, h, :], gS[:, h, :], egc[:, h, c * C + C - 1:c * C + C], pSt[:, hs], ALU.mult, ALU.add,
                          r=gS.k() + egc.k(h) + pSt.k(), w=gS.k())

            def c_z(j, ct, ps):
                h = ct - 16
                zs, sq_ = gT[0], gT[1]
                pn = pq[4]
                P.act(zs[:], ps[:, :N], AF.Silu, r=ps.k(), w=zs.k())
                P.tt("pool", sq_[:], oall[:, h, :], oall[:, h, :], ALU.mult, r=oall.k(), w=sq_.k())
                P.mm(pn[:, :N], ones128m[:], sq_[:], True, True, r=ones128m.k() + sq_.k(), w=pn.k())
                P.act(sq_[:], pn[:, :N], AF.Sqrt, r=pn.k(), w=sq_.k(), bias=RMS_EPS)
                P.op("dve", lambda e: e.reciprocal(out=sq_[:], in_=sq_[:]), r=sq_.k(), w=sq_.k())
                P.stt("dve", sq_[:], oall[:, h, :], gvs[:, 2:3], sq_[:], ALU.mult, ALU.mult, r=oall.k() + gvs.k() + sq_.k(), w=sq_.k())
                o_ = ob[oi[0] % 2]
                oi[0] += 1
                P.tt("dve", o_[:], sq_[:], zs[:], ALU.mult, r=sq_.k() + zs.k(), w=o_.k())
                P.dma("act", yo.ap(slice(512 + h * 128, 512 + (h + 1) * 128), t0, N), o_[:], r=o_.k(), w=yo.k(), sem=f"do{oi[0] % 2}")
            proj(P, ws, w_b, range(16, 20), KD, xb, N, pss, c_z)
        if io.get("post"):
            io["post"](P)
        P.emit()
    return P

DN_ALPHA = (2.0 * 2) ** 0.25
A_COLS = 3520


def pad128(a):
    o = np.zeros((128,) + a.shape[1:], np.float32)
    o[:a.shape[0]] = a
    return o


def prep_even(xb_, inp, hh):
    w = inp["even_w_in"][0]
    o = 512 * hh
    hv = lambda v: np.ascontiguousarray(v.reshape(-1, 64).T)
    cols = []
    for h in range(8):
        for which in range(3):
            cols.append(w[:, which * 1024 + o + h * 64: which * 1024 + o + (h + 1) * 64])
    w_a = relayout_w64(np.concatenate(cols, 1))
    wlo = np.zeros((2048, 128), np.float32); wlo[:, :96] = w[:, 3072:3168]
    alo = np.zeros((2048, 128), np.float32); alo[:, :96] = w[:, 3168:3264]
    glo = w[:, 3264:3520]
    gb = A_COLS
    q = w[:, gb + o: gb + o + 512]; k = w[:, gb + 1024 + o: gb + 1024 + o + 512]; v = w[:, gb + 2048 + o: gb + 2048 + o + 512]
    z = w[:, gb + 3072 + o: gb + 3072 + o + 512]
    ba = np.zeros((2048, 128), np.float32)
    ba[:, 0:4] = w[:, gb + 4096 + 4 * hh: gb + 4096 + 4 * hh + 4]; ba[:, 4:8] = w[:, gb + 4104 + 4 * hh: gb + 4104 + 4 * hh + 4]
    w_b = relayout_w(np.concatenate([wlo, alo, glo, q, k, v, z, ba], 1))
    mu = inp["rwkv_mu"][0]
    rv = np.stack([hv(mu[o:o + 512]), hv(mu[1024 + o:1024 + o + 512]), hv(mu[2048 + o:2048 + o + 512]),
                   hv(inp["rwkv_w0"][0][o:o + 512]), hv(inp["rwkv_a0"][0][o:o + 512]), hv(inp["rwkv_k_k"][0][o:o + 512]),
                   hv(inp["rwkv_k_a"][0][o:o + 512]), hv(inp["rwkv_r_k"][0].reshape(-1)[o:o + 512]),
                   hv(inp["rwkv_gn_g"][0].reshape(-1)[o:o + 512]), hv(inp["rwkv_gn_b"][0].reshape(-1)[o:o + 512])], 1)
    rmu = np.zeros((128, 4), np.float32)
    rmu[:96, 0] = mu[3072:3168]; rmu[:96, 1] = mu[3168:3264]; rmu[:, 2] = mu[3264:3392]; rmu[:, 3] = mu[3392:3520]
    rw2 = np.stack([pad128(inp["rwkv_w2"][0][:, o:o + 512]), pad128(inp["rwkv_a2"][0][:, o:o + 512]),
                    inp["rwkv_g2"][0][0:128, o:o + 512], inp["rwkv_g2"][0][128:256, o:o + 512]], 1)
    gc = inp["gdn_conv_w"][0]
    idx = np.concatenate([o + np.arange(512), 1024 + o + np.arange(512), 2048 + o + np.arange(512)])
    gcw = np.stack([relayout_v(gc[j, idx]) for j in range(4)], 1)
    gv = np.zeros((128, 4), np.float32)
    gv[4:8, 0] = inp["gdn_A_log"][0][4 * hh:4 * hh + 4]; gv[4:8, 1] = inp["gdn_dt_bias"][0][4 * hh:4 * hh + 4]
    gv[:, 2] = inp["gdn_norm_g"][0]
    return dict(xT=np.ascontiguousarray(xb_.T), w_a=w_a, w_b=w_b, cst=consts_e(), rv=np.ascontiguousarray(rv), rmu=rmu,
                rw2=np.ascontiguousarray(rw2), gcw=np.ascontiguousarray(gcw), gv=gv, sel=sel_np())


def prep_odd(xb_, inp, hh):
    w = inp["odd_w_in"][0]
    o = 1024 * hh
    zc = w[:, o:o + 1024]
    xs = w[:, 2048 + o:2048 + o + 1024]
    Bc = w[:, 4096 + 256 * hh:4096 + 256 * hh + 256]
    Cc = w[:, 4608 + 256 * hh:4608 + 256 * hh + 256]
    dtc = np.zeros((2048, 128), np.float32); dtc[:, :16] = w[:, 5120 + 16 * hh:5120 + 16 * hh + 16]
    yb = w[:, 5152 + o:5152 + o + 1024]
    xbr = w[:, 7200 + o:7200 + o + 1024]
    wc = np.concatenate([xs, Bc, Cc, dtc, zc, yb, xbr], 1)
    mc = inp["mamba_conv_w"][0]; mb = inp["mamba_conv_b"][0]
    idx = np.concatenate([np.arange(o, o + 1024), 2048 + 256 * hh + np.arange(256), 2560 + 256 * hh + np.arange(256)])
    mcw = np.stack([relayout_v(mc[j, idx]) for j in range(4)] + [relayout_v(mb[idx])], 1)
    mv = np.zeros((128, 4), np.float32)
    mv[:16, 0] = inp["mamba_dt_bias"][0][16 * hh:16 * hh + 16]; mv[:16, 1] = inp["mamba_A_log"][0][16 * hh:16 * hh + 16]
    Dexp = np.repeat(inp["mamba_D"][0][16 * hh:16 * hh + 16], 64)
    mdn = np.stack([relayout_v(Dexp), relayout_v(inp["mamba_norm_g"][0][o:o + 1024])], 1)
    lc = inp["lru_conv_w"][0][:, o:o + 1024]
    lcw = np.stack([relayout_v(lc[j]) for j in range(4)] + [relayout_v(inp["lru_conv_b"][0][o:o + 1024])], 1)
    lv = np.stack([relayout_v(inp[k][0][o:o + 1024]) for k in ("lru_ba", "lru_bx", "lru_lambda")], 1)
    return dict(xT=np.ascontiguousarray(xb_.T), w_in=relayout_w(wc), cst=consts_np(),
                mcw=np.ascontiguousarray(mcw), mv=mv, mdn=np.ascontiguousarray(mdn), lcw=np.ascontiguousarray(lcw),
                lv=np.ascontiguousarray(lv), lwa=np.ascontiguousarray(inp["lru_wa"][0][8 * hh:8 * hh + 8]),
                lwx=np.ascontiguousarray(inp["lru_wx"][0][8 * hh:8 * hh + 8]))


def prep_C_weights(inp, i, w_out):
    cw = inp["ffn_conv_w"][i]
    return dict(w_out=relayout_w(w_out), w_up=relayout_w(inp["ffn_up"][i]), w_down=relayout_w(inp["ffn_down"][i]),
                w_gate=relayout_w(inp["ple_gate_w"][i]), w_ple=relayout_w(inp["ple_proj"][i]),
                vecs=np.ascontiguousarray(np.stack([relayout_v(inp[k][i]) for k in ("ln1_g", "ln1_b", "ln2_g", "ln2_b", "ple_norm_g", "ple_gate_b")], 1)),
                cvw=np.ascontiguousarray(np.stack([relayout_v(v) for v in (cw[0], cw[1], cw[2], inp["ffn_conv_b"][i])], 1)))


GROUPS = [[0, 1], [2, 3], [4, 5], [6, 7]]


def build_all(shapes, T=4096, TH=2048):
    import ml_dtypes
    nc = bass.Bass("TRN2", target_bir_lowering=False)
    D = 2048
    with ExitStack() as outer:
        din = {}
        for name, (shp, dt) in shapes.items():
            bdt = BF16 if dt == ml_dtypes.bfloat16 else F32
            din[name] = Tl(nc.dram_tensor(name, list(shp), bdt, kind="ExternalInput").ap(), name)
        def internal(name, shp, dt):
            return Tl(nc.dram_tensor(name, list(shp), dt, kind="Internal").ap(), name)
        def chunked(name, rows, cols, dt, W):
            return ChunkT([internal(f"{name}_{q}", [rows, W], dt) for q in range(cols // W)], W)
        ysnd0 = chunked("ysnd0", 1024, T, BF16, 1024)
        yrcv0 = chunked("yrcv0", 2048, T, BF16, 1024)
        ysnd1 = chunked("ysnd1", 2048, T, BF16, 512)
        yrcv1 = chunked("yrcv1", 4096, T, BF16, 512)
        NQ = TH // 512
        x1snd = [[internal(f"x1snd_{h}_{q}", [1024, 512], F32) for q in range(NQ)] for h in range(2)]
        x1rcv = [[internal(f"x1rcv_{h}_{q}", [2048, 512], F32) for q in range(NQ)] for h in range(2)]
        x1snd_k = [k for h in range(2) for c in x1snd[h] for k in c.k()]
        x1rcv_k = [k for h in range(2) for c in x1rcv[h] for k in c.k()]
        xo = Tl(nc.dram_tensor("xoT", [D, TH], F32, kind="ExternalOutput").ap(), "xoT")
        TBC = 512
        import os
        PH = os.environ.get("PHASES", "E,C0,O,C1").split(",")

        def gather_chunks(snd, rcv):
            def post(P):
                for s_, r_ in zip(snd, rcv):
                    P.coll("AllGather", s_[:], r_[:], GROUPS, s_.k(), r_.k() + [("collchain", 0)], "dcc")
            return post

        io = {k[2:]: v for k, v in din.items() if k.startswith("e_")}
        io.update(yo=ysnd0, xsrc_k=[], xsrc=lambda t0, hf: din["xT"][hf * 1024:(hf + 1) * 1024, t0:t0 + 256],
                  post=gather_chunks(ysnd0.chunks, yrcv0.chunks))
        if "E" in PH:
            phase_E(nc, outer, io, T)
        nc.all_engine_barrier()
        io = {k[3:]: v for k, v in din.items() if k.startswith("c0_")}
        io.update(yrcv=yrcv0, pT=din["pT0"], msk=din["msk"], xsrc_k=[], xdst_k=x1snd_k,
                  xsrc=lambda blk: [(0, 16, din["xTc"][:, 0:2] if blk < 0 else din["xTc"][:, 2 + blk * TBC:2 + (blk + 1) * TBC])],
                  xdst=lambda blk: [(8 * h, 8 * h + 8, x1snd[h][blk][:, :]) for h in range(2)],
                  post=gather_chunks([c for h in range(2) for c in x1snd[h]], [c for h in range(2) for c in x1rcv[h]]))
        if "C0" in PH:
            phase_C(nc, outer, io, TH // TBC, 2048, TB=TBC, alpha=DN_ALPHA, THALF=TH)
        nc.all_engine_barrier()
        io = {k[2:]: v for k, v in din.items() if k.startswith("o_")}
        io.update(yo=ysnd1, xsrc_k=[],
                  xsrc=lambda t0, hf: x1rcv[hf][(t0 % TH) // 512][(t0 // TH) * 1024:(t0 // TH + 1) * 1024, :],
                  post=gather_chunks(ysnd1.chunks, yrcv1.chunks))
        if "O" in PH:
            phase_O(nc, outer, io, T)
        nc.all_engine_barrier()
        io = {k[3:]: v for k, v in din.items() if k.startswith("c1_")}
        io.update(yrcv=yrcv1, pT=din["pT1"], msk=din["msk"], xsrc_k=[], xdst_k=xo.k(),
                  xsrc=lambda blk: [(8 * h, 8 * h + 8, x1rcv[h][NQ - 1][0:1024, 510:512] if blk < 0 else x1snd[h][blk][:, :]) for h in range(2)],
                  xdst=lambda blk: [(0, 16, xo[:, blk * TBC:(blk + 1) * TBC])])
        if "C1" in PH:
            phase_C(nc, outer, io, TH // TBC, 4096, TB=TBC, alpha=DN_ALPHA, THALF=TH)
    return nc


def kernel(**inp):
    inp = {k: np.asarray(v, dtype=np.float32) for k, v in inp.items()}
    x = inp["x"]
    p = inp["p"]
    B, T, D = x.shape
    TH = T // 2
    cores = list(range(8))
    perm_e = np.concatenate([np.arange(0, 512), np.arange(1024, 1536), np.arange(512, 1024), np.arange(1536, 2048)])
    perm_o = np.concatenate([np.arange(0, 1024), np.arange(2048, 3072), np.arange(1024, 2048), np.arange(3072, 4096)])
    c0 = prep_C_weights(inp, 0, inp["even_w_out"][0][perm_e])
    c1 = prep_C_weights(inp, 1, inp["odd_w_out"][0][perm_o])
    ins = []
    for c in cores:
        b, h = c // 2, c % 2
        m = {}
        e = prep_even(x[b], inp, h)
        m["xT"] = e.pop("xT")
        m.update({"e_" + k: v for k, v in e.items()})
        o = prep_odd(x[b][:8], inp, h)
        o.pop("xT")
        m.update({"o_" + k: v for k, v in o.items()})
        m.update({"c0_" + k: v for k, v in c0.items()})
        m.update({"c1_" + k: v for k, v in c1.items()})
        xTc = np.zeros((D, 2 + TH), np.float32)
        xTc[:, 2:] = x[b, h * TH:(h + 1) * TH].T
        if h > 0:
            xTc[:, :2] = x[b, TH - 2:TH].T
        m["xTc"] = xTc
        m["pT0"] = np.ascontiguousarray(p[0, b, h * TH:(h + 1) * TH].T)
        m["pT1"] = np.ascontiguousarray(p[1, b, h * TH:(h + 1) * TH].T)
        msk = np.zeros((128, 3), np.float32)
        msk[:, 0] = float(h > 0); msk[:, 1] = float(h == 0); msk[:, 2] = float(h == 1)
        m["msk"] = msk
        ins.append(m)
    shapes = {k: (v.shape, v.dtype) for k, v in ins[0].items()}
    nc = build_all(shapes, T=T, TH=TH)
    res = run_bass_kernel_spmd(nc, ins, core_ids=cores)
    out = np.empty_like(x)
    for c in cores:
        out[c // 2, (c % 2) * TH:(c % 2 + 1) * TH] = res.results[c]["xoT"].T
    return out
```

```python
from contextlib import ExitStack

import numpy as np
import concourse.bass as bass
import concourse.mybir as mybir
from concourse.bass_utils import run_bass_kernel_spmd

F32 = mybir.dt.float32
BF16 = mybir.dt.bfloat16
ALU = mybir.AluOpType
AF = mybir.ActivationFunctionType
AX = mybir.AxisListType

ENGS = ("pe", "dve", "act", "pool", "sp")


class Tl:
    def __init__(self, h, name, nsub=1):
        self.h, self.name, self.nsub = h, name, nsub

    def __getitem__(self, idx):
        return self.h[idx]

    def k(self, i=None):
        if i is None:
            return [(self.name, j) for j in range(self.nsub)]
        if isinstance(i, (list, tuple, range)):
            return [(self.name, j) for j in i]
        return [(self.name, i)]


class Prog:
    def __init__(self, nc):
        self.nc = nc
        self.ops = []
        self.stack = None
        self.ntiles = 0
        self.dsems = {}
        self.psum_names = set()
        Prog.ninst = getattr(Prog, "ninst", 0) + 1
        self.pfx = f"g{Prog.ninst}_"

    def sb(self, shape, dt, name=None, nsub=1):
        self.ntiles += 1
        name = self.pfx + (name or f"t{self.ntiles}")
        h = self.stack.enter_context(self.nc.sbuf_tensor(name, list(shape), dt))
        return Tl(h, name, nsub)

    def ps(self, shape, dt, name=None, nsub=1):
        self.ntiles += 1
        name = self.pfx + (name or f"p{self.ntiles}")
        h = self.stack.enter_context(self.nc.psum_tensor(name, list(shape), dt))
        self.psum_names.add(name)
        return Tl(h, name, 1)

    def dram(self, name, shape, dt, kind, nsub=1):
        h = self.nc.dram_tensor(name, list(shape), dt, kind=kind)
        return Tl(h.ap(), name, nsub)

    def op(self, eng, fn, r=(), w=(), acc=False):
        w = list(w) + [k for k in r if k[0] in self.psum_names and k not in w]
        self.ops.append(dict(eng=eng, fn=fn, r=list(r), w=list(w), dma=None, acc=acc))

    def dma(self, eng, out, in_, r=(), w=(), sem="d0", **kw):
        def fn(e):
            return e.dma_start(out=out, in_=in_, **kw)
        self.ops.append(dict(eng=eng, fn=fn, r=list(r), w=list(w), dma=sem, acc=False))

    def coll(self, kind, ins_ap, out_ap, groups, r, w, sem, inc=1):
        def fn(e):
            return e.collective_compute(kind, ALU.bypass, replica_groups=groups, ins=[ins_ap], outs=[out_ap])
        self.ops.append(dict(eng="pool", fn=fn, r=list(r), w=list(w), dma=sem, acc=False, inc=inc))

    def fence(self, tiles):
        sc = self.sb([128, 1], F32, f"fence{self.ntiles}")
        keys = [k for t in tiles for k in t.k()]
        self.op("pool", lambda e: e.memset(sc[:], 0.0), r=keys, w=keys + sc.k())

    def emit(self):
        nc = self.nc
        st = self.stack
        ss = getattr(self, "semstack", None) or st
        Prog.nprog = getattr(Prog, "nprog", 0) + 1
        pfx = f"s{Prog.nprog}_"
        sem = {e: ss.enter_context(nc.semaphore(pfx + e)) for e in ENGS}
        dnames = sorted({o["dma"] for o in self.ops if o["dma"]})
        for d in dnames:
            sem[d] = ss.enter_context(nc.semaphore(pfx + d))
        cnt = {s: 0 for s in sem}
        clock = {e: {} for e in ENGS}
        lastw = {}
        readers = {}
        per_eng = {e: [] for e in ENGS}

        def merge(a, b):
            for s, c in b.items():
                if a.get(s, 0) < c:
                    a[s] = c

        for o in self.ops:
            e = o["eng"]
            ck = clock[e]
            need = []
            for key in o["r"]:
                ev = lastw.get(key)
                if ev is not None:
                    need.append(ev)
            for key in o["w"]:
                ev = lastw.get(key)
                if ev is not None and not ((o["acc"] or e == "pe") and ev[3] == e):
                    need.append(ev)
                for ev in readers.get(key, ()):
                    if e == "pe" and ev[3] == "pe":
                        continue
                    need.append(ev)
            waits = {}
            for (s, c, evck, _) in need:
                if ck.get(s, 0) >= c:
                    continue
                if waits.get(s, 0) < c:
                    waits[s] = c
            for (s, c, evck, _) in need:
                if s in waits and waits[s] >= c and ck.get(s, 0) < c:
                    merge(ck, evck)
            for s, c in waits.items():
                if ck.get(s, 0) < c:
                    ck[s] = c
            if o["dma"]:
                s = o["dma"]
                cnt[s] += o.get("inc", 16)
                evs = s
            else:
                cnt[e] += 1
                evs = e
            evck = dict(ck)
            evck[evs] = cnt[evs]
            ev = (evs, cnt[evs], evck, e)
            for key in o["w"]:
                lastw[key] = ev
                readers[key] = []
            for key in o["r"]:
                readers.setdefault(key, []).append(ev)
            per_eng[e].append((o, sorted(waits.items()), evs))
        self.final = {s: c for s, c in cnt.items() if c > 0}
        self.sem = sem
        self.per_eng = per_eng
        nE = {e: len(v) for e, v in per_eng.items()}
        self.stats = nE

        block = st.enter_context(nc.Block())

        def run(engname, eng):
            for (o, waits, evs) in per_eng[engname]:
                for s, c in waits:
                    eng.wait_ge(sem[s], c)
                ins = o["fn"](eng)
                ins.then_inc(sem[evs], o.get("inc", 16) if o["dma"] else 1)
            if engname == "sp":
                for s, c in self.final.items():
                    eng.wait_ge(sem[s], c)

        @block.tensor
        def _(eng):
            run("pe", eng)

        @block.vector
        def _(eng):
            run("dve", eng)

        @block.scalar
        def _(eng):
            run("act", eng)

        @block.gpsimd
        def _(eng):
            run("pool", eng)

        @block.sync
        def _(eng):
            run("sp", eng)


def _tt(P, eng, out, in0, in1, op, r, w):
    P.op(eng, lambda e: e.tensor_tensor(out=out, in0=in0, in1=in1, op=op), r, w)


def _ts(P, eng, out, in0, s1, s2, op0, op1, r, w):
    if op1 is None:
        P.op(eng, lambda e: e.tensor_scalar(out=out, in0=in0, scalar1=s1, scalar2=None, op0=op0), r, w)
    else:
        P.op(eng, lambda e: e.tensor_scalar(out=out, in0=in0, scalar1=s1, scalar2=s2, op0=op0, op1=op1), r, w)


def _stt(P, eng, out, in0, sc, in1, op0, op1, r, w):
    P.op(eng, lambda e: e.scalar_tensor_tensor(out=out, in0=in0, scalar=sc, in1=in1, op0=op0, op1=op1), r, w)


def _act(P, out, in_, func, r, w, bias=None, scale=None):
    kw = {}
    if bias is not None:
        kw["bias"] = bias
    if scale is not None:
        kw["scale"] = scale
    P.op("act", lambda e: e.activation(out=out, in_=in_, func=func, **kw), r, w)


def _cp(P, eng, out, in_, r, w):
    if eng == "act":
        P.op("act", lambda e: e.copy(out=out, in_=in_), r, w)
    else:
        P.op(eng, lambda e: e.tensor_copy(out=out, in_=in_), r, w)


def _mm(P, out, lhsT, rhs, start, stop, r, w):
    P.op("pe", lambda e: e.matmul(out, lhsT=lhsT, rhs=rhs, start=start, stop=stop), r, w, acc=not start)


def _tr(P, out, in_, ident, r, w):
    P.op("pe", lambda e: e.transpose(out, in_, ident), r, w)


Prog.tt, Prog.ts, Prog.stt, Prog.act, Prog.cp, Prog.mm, Prog.tr = _tt, _ts, _stt, _act, _cp, _mm, _tr


class TlV(Tl):
    def __init__(self, ap, parent):
        self.h, self.name, self.nsub = ap, parent.name, parent.nsub

    def k(self, i=None):
        return [(self.name, j) for j in range(self.nsub)]


class ChunkT:
    def __init__(self, chunks, W):
        self.chunks, self.W = chunks, W

    def ap(self, rows, c0, n):
        q = c0 // self.W
        assert (c0 + n - 1) // self.W == q, (c0, n, self.W)
        return self.chunks[q][rows, (c0 % self.W):(c0 % self.W) + n]

    def k(self, i=None):
        return [k for c in self.chunks for k in c.k()]


LN_EPS = 1e-5
RMS_EPS = 1e-6


KS = 8


def relayout_w(w):
    K, N = w.shape
    return np.ascontiguousarray(w.reshape(K // 128, 128, N // 128, 128).transpose(2, 1, 0, 3))


def relayout_v(v):
    return np.ascontiguousarray(v.reshape(-1, 128).T)


class WStream:
    def __init__(self, P, kcmax=KS, nst=6, nbf=4, cast=("pool",)):
        self.P = P
        self.cast = cast
        self.st = [P.sb([128, KS, 128], F32, f"wst{i}") for i in range(nst)]
        self.bf = [P.sb([128, KS, 128], BF16, f"wbf{i}") for i in range(nbf)]
        self.i = 0

    def get(self, wd, ct, k0, k1, cw=128):
        P = self.P
        i = self.i
        self.i += 1
        st, bf = self.st[i % len(self.st)], self.bf[i % len(self.bf)]
        n = k1 - k0
        P.dma("sp", st[:, :n, :cw], wd[ct][:, k0:k1, :], w=st.k(), sem=f"dw{i % len(self.st)}")
        P.cp(self.cast[i % len(self.cast)], bf[:, :n, :cw], st[:, :n, :cw], r=st.k(), w=bf.k())
        return bf


def proj(P, ws, wd, order, KC, act, N, pss, consume, cw=128):
    for j, ct in enumerate(order):
        ps = pss[j % len(pss)]
        for k0 in range(0, KC, KS):
            k1 = min(KC, k0 + KS)
            bf = ws.get(wd, ct, k0, k1, cw)
            for kk in range(k0, k1):
                P.mm(ps[:cw, :N], bf[:, kk - k0, :cw], act[:, kk, :N], kk == 0, kk == KC - 1, r=bf.k() + act.k(), w=ps.k())
        consume(j, ct, ps)


def phase_C(nc, outer, io, NB, CO, TB=512, D=2048, FF=5632, PLE=256, alpha=1.0, THALF=2048):
    P = Prog(nc)
    P.semstack = outer
    KY, KD, KF, KP = CO // 128, D // 128, FF // 128, PLE // 128
    NFT = 2 * KF
    NTOK = NB * TB
    with ExitStack() as st:
        P.stack = st
        yrcv, pT, msk = io["yrcv"], io["pT"], io["msk"]
        w_out, w_up, w_down, w_gate, w_ple, vecs, cvw = (io[k] for k in ("w_out", "w_up", "w_down", "w_gate", "w_ple", "vecs", "cvw"))
        ybB = P.sb([128, 8, TB], BF16, "ybB")

        vs = P.sb([128, 6, KD], F32, "vs")
        cw = P.sb([128, 4, NFT], F32, "cw")
        hms = P.sb([128, 3], F32, "hms")
        ones = P.sb([128, 128], F32, "ones")
        eT = P.sb([128, KD, TB], F32, "eT")
        assert KY <= 2 * KD
        yb = Tl(eT.h[:].rearrange("p k t -> p (k t)").bitcast(BF16)[:, :KY * TB].rearrange("p (k t) -> p k t", k=KY), eT.name)
        sx = P.sb([128, KD, TB], F32, "sx")
        hb = P.sb([128, KD, TB], BF16, "hb")
        actT = P.sb([128, KF, TB], BF16, "actT")
        pst = P.sb([128, KP, TB], F32, "pst")
        pb = P.sb([128, KP, TB], BF16, "pb")
        halo = P.sb([128, NFT, 2], F32, "halo", nsub=NFT)
        ub = [P.sb([128, TB + 2], F32, f"ub{i}") for i in range(3)]
        cv = [P.sb([128, TB], F32, f"cv{i}") for i in range(3)]
        sg = [P.sb([128, TB], F32, f"sg{i}") for i in range(2)]
        sq = [P.sb([128, TB], F32, f"sq{i}") for i in range(2)]
        mean = P.sb([128, TB], F32, "mean")
        rstd = P.sb([128, TB], F32, "rstd")
        tmp = [P.sb([128, TB], F32, f"tmp{i}") for i in range(2)]
        pss = [P.ps([128, 512], F32, f"ps{i}") for i in range(4)]
        pm = P.ps([128, 512], F32, "pm")
        pv = P.ps([128, 512], F32, "pv")
        ws = WStream(P, cast=("act", "act", "dve"))

        P.dma("act", vs[:], vecs[:], w=vs.k(), sem="dc")
        P.dma("act", cw[:], cvw[:], w=cw.k(), sem="dc")
        P.dma("act", hms[:], msk[:], w=hms.k(), sem="dc")
        P.fence([vs, cw, hms])
        P.op("pool", lambda e: e.memset(ones[:], 1.0 / D), w=ones.k())

        def layer_norm(N, gi, bi):
            for k in range(KD):
                s = sq[k % 2]
                P.tt("pool", s[:, :N], sx[:, k, :N], sx[:, k, :N], ALU.mult, r=sx.k(), w=s.k())
                P.mm(pm[:, :N], ones[:], sx[:, k, :N], k == 0, k == KD - 1, r=ones.k() + sx.k(), w=pm.k())
                P.mm(pv[:, :N], ones[:], s[:, :N], k == 0, k == KD - 1, r=ones.k() + s.k(), w=pv.k())
            P.cp("act", mean[:, :N], pm[:, :N], r=pm.k(), w=mean.k())
            t = tmp[0]
            P.tt("dve", t[:, :N], mean[:, :N], mean[:, :N], ALU.mult, r=mean.k(), w=t.k())
            P.tt("dve", t[:, :N], pv[:, :N], t[:, :N], ALU.subtract, r=pv.k() + t.k(), w=t.k())
            P.act(rstd[:, :N], t[:, :N], AF.Sqrt, r=t.k(), w=rstd.k(), bias=LN_EPS)
            P.op("dve", lambda e: e.reciprocal(out=rstd[:, :N], in_=rstd[:, :N]), r=rstd.k(), w=rstd.k())
            for k in range(KD):
                t = tmp[k % 2]
                P.tt("dve", t[:, :N], sx[:, k, :N], mean[:, :N], ALU.subtract, r=sx.k() + mean.k(), w=t.k())
                P.tt("dve", t[:, :N], t[:, :N], rstd[:, :N], ALU.mult, r=t.k() + rstd.k(), w=t.k())
                P.ts("dve", sx[:, k, :N], t[:, :N], vs[:, gi, k:k + 1], vs[:, bi, k:k + 1], ALU.mult, ALU.add,
                     r=t.k() + vs.k(), w=sx.k())
                P.cp("act", hb[:, k, :N], sx[:, k, :N], r=sx.k(), w=hb.k())

        for blk in range(-1, NB):
            N = 2 if blk < 0 else TB
            t0 = 0 if blk < 0 else 2 + blk * TB
            cA = 0 if blk < 0 else blk * TB
            cB = THALF - 2 if blk < 0 else THALF + blk * TB
            P.dma("act", yb[:, :, :N], yrcv.ap(slice(None), cA, N).rearrange("(k p) t -> p k t", p=128), r=yrcv.k(), w=yb.k(), sem="dy")
            for k0 in range(0, KY, 8):
                P.dma("act", ybB[:, :, :N], yrcv.ap(slice(k0 * 128, (k0 + 8) * 128), cB, N).rearrange("(k p) t -> p k t", p=128),
                      r=yrcv.k(), w=ybB.k(), sem="dyb")
                P.ts("dve", yb[:, k0:k0 + 8, :N], yb[:, k0:k0 + 8, :N], hms[:, 1:2], None, ALU.mult, None, r=yb.k() + hms.k(), w=yb.k())
                P.stt("dve", yb[:, k0:k0 + 8, :N], ybB[:, :, :N], hms[:, 2:3], yb[:, k0:k0 + 8, :N], ALU.mult, ALU.add,
                      r=ybB.k() + hms.k() + yb.k(), w=yb.k())
            for (k0_, k1_, ap_) in io["xsrc"](blk):
                P.dma("act", sx[:, k0_:k1_, :N], ap_.rearrange("(k p) t -> p k t", p=128), r=io["xsrc_k"], w=sx.k(), sem="dx")

            def c_out(j, ct, ps, N=N):
                P.stt("dve", sx[:, ct, :N], sx[:, ct, :N], float(alpha), ps[:, :N], ALU.mult, ALU.add,
                      r=sx.k() + ps.k(), w=sx.k())
            proj(P, ws, w_out, range(KD), KY, yb, N, pss, c_out)
            layer_norm(N, 0, 1)

            order = []
            for c in range(KF):
                order += [c, KF + c]

            def c_up(j, ct, ps, N=N, blk=blk):
                if blk < 0:
                    P.ts("dve", halo[:, ct, :], ps[:, :2], hms[:, 0:1], None, ALU.mult, None,
                         r=ps.k() + hms.k(), w=halo.k(ct))
                    return
                u = ub[j % 3]
                c_ = cv[j % 3]
                P.cp("act", u[:, 2:2 + N], ps[:, :N], r=ps.k(), w=u.k())
                P.cp("pool", u[:, 0:2], halo[:, ct, :], r=halo.k(ct), w=u.k())
                P.cp("act", halo[:, ct, :], u[:, N:N + 2], r=u.k(), w=halo.k(ct))
                P.ts("dve", c_[:, :N], u[:, 2:2 + N], cw[:, 2, ct:ct + 1], cw[:, 3, ct:ct + 1], ALU.mult, ALU.add,
                     r=u.k() + cw.k(), w=c_.k())
                P.stt("dve", c_[:, :N], u[:, 1:1 + N], cw[:, 1, ct:ct + 1], c_[:, :N], ALU.mult, ALU.add,
                      r=u.k() + cw.k() + c_.k(), w=c_.k())
                P.stt("dve", c_[:, :N], u[:, 0:N], cw[:, 0, ct:ct + 1], c_[:, :N], ALU.mult, ALU.add,
                      r=u.k() + cw.k() + c_.k(), w=c_.k())
                if ct < KF:
                    s = sg[ct % 2]
                    P.act(s[:, :N], c_[:, :N], AF.Silu, r=c_.k(), w=s.k())
                else:
                    c = ct - KF
                    s = sg[c % 2]
                    P.tt("dve", actT[:, c, :N], s[:, :N], c_[:, :N], ALU.mult, r=s.k() + c_.k(), w=actT.k())
            proj(P, ws, w_up, order, KD, hb, N, pss, c_up)
            if blk < 0:
                continue

            def c_down(j, ct, ps, N=N):
                P.stt("dve", sx[:, ct, :N], sx[:, ct, :N], float(alpha), ps[:, :N], ALU.mult, ALU.add,
                      r=sx.k() + ps.k(), w=sx.k())
            proj(P, ws, w_down, range(KD), KF, actT, N, pss, c_down)
            layer_norm(N, 2, 3)

            P.dma("act", pst[:], pT[:, blk * TB:(blk + 1) * TB].rearrange("(k p) t -> p k t", p=128), r=[], w=pst.k(), sem="dp")
            P.cp("pool", pb[:], pst[:], r=pst.k(), w=pb.k())

            def c_ple(j, ct, ps, N=N):
                P.cp("act", eT[:, ct, :N], ps[:, :N], r=ps.k(), w=eT.k())
                s = sq[ct % 2]
                P.tt("pool", s[:, :N], eT[:, ct, :N], eT[:, ct, :N], ALU.mult, r=eT.k(), w=s.k())
                P.mm(pv[:, :N], ones[:], s[:, :N], ct == 0, ct == KD - 1, r=ones.k() + s.k(), w=pv.k())
            proj(P, ws, w_ple, range(KD), KP, pb, N, pss, c_ple)
            P.act(rstd[:, :N], pv[:, :N], AF.Sqrt, r=pv.k(), w=rstd.k(), bias=RMS_EPS)
            P.op("dve", lambda e, N=N: e.reciprocal(out=rstd[:, :N], in_=rstd[:, :N]), r=rstd.k(), w=rstd.k())

            def c_gate(j, ct, ps, N=N, blk=blk):
                g = sg[ct % 2]
                P.act(g[:, :N], ps[:, :N], AF.Sigmoid, r=ps.k() + vs.k(), w=g.k(), bias=vs[:, 5, ct:ct + 1])
                t = tmp[ct % 2]
                P.stt("dve", t[:, :N], eT[:, ct, :N], vs[:, 4, ct:ct + 1], rstd[:, :N], ALU.mult, ALU.mult,
                      r=eT.k() + vs.k() + rstd.k(), w=t.k())
                P.tt("dve", t[:, :N], t[:, :N], g[:, :N], ALU.mult, r=t.k() + g.k(), w=t.k())
                P.tt("dve", eT[:, ct, :N], t[:, :N], sx[:, ct, :N], ALU.add, r=t.k() + sx.k(), w=eT.k())
            proj(P, ws, w_gate, range(KD), KD, hb, N, pss, c_gate)
            for (k0_, k1_, ap_) in io["xdst"](blk):
                P.dma("act", ap_.rearrange("(k p) t -> p k t", p=128), eT[:, k0_:k1_, :], r=eT.k(), w=io["xdst_k"], sem="do")
        if io.get("post"):
            io["post"](P)
        P.emit()
    return P


NEG = -30000.0


def consts_np():
    i = np.arange(128)
    ident = np.eye(128, dtype=np.float32)
    tri = (i[:, None] <= i[None, :]).astype(np.float32)
    maskneg = np.where(i[None, :] >= i[:, None], 0.0, NEG).astype(np.float32)
    return np.ascontiguousarray(np.stack([ident, tri, maskneg], 1))


def phase_O(nc, outer, io, T, TB=512, D=2048, L=128):
    P = Prog(nc)
    P.semstack = outer
    KD = D // 128
    NB = T // TB
    NCH = TB // L
    NH = 16
    with ExitStack() as st:
        P.stack = st
        w_in, cst, mcw, mv, mdn, lcw, lv, lwa, lwx, yo = (io[k] for k in ("w_in", "cst", "mcw", "mv", "mdn", "lcw", "lv", "lwa", "lwx", "yo"))

        cs = P.sb([128, 3, 128], F32, "cs")
        ident, tri, maskneg = cs[:, 0, :], cs[:, 1, :], cs[:, 2, :]
        ones = P.sb([128, 128], F32, "ones")
        onesg = P.sb([128, 128], F32, "onesg")
        mcs = P.sb([128, 5, 12], F32, "mcs")
        mvs = P.sb([128, 4], F32, "mvs")
        negA = P.sb([128, 1], F32, "negA")
        mds = P.sb([128, 2, 8], F32, "mds")
        lcs = P.sb([128, 5, 8], F32, "lcs")
        lvs = P.sb([128, 3, 8], F32, "lvs")
        c8 = P.sb([128, 8], F32, "c8")
        was = P.sb([128, 8, 128], F32, "was")
        wxs = P.sb([128, 8, 128], F32, "wxs")
        wab = P.sb([128, 8, 128], BF16, "wab")
        wxb = P.sb([128, 8, 128], BF16, "wxb")
        for (t_, d_) in ((cs, cst), (mcs, mcw), (mvs, mv), (mds, mdn), (lcs, lcw), (lvs, lv)):
            P.dma("act", t_[:], d_[:], w=t_.k(), sem="dc")
        P.dma("act", was[:], lwa[:].rearrange("n d e -> d n e"), w=was.k(), sem="dc")
        P.dma("act", wxs[:], lwx[:].rearrange("n d e -> d n e"), w=wxs.k(), sem="dc")
        P.fence([cs, mcs, mvs, mds, lcs, lvs, was, wxs])
        P.cp("pool", wab[:], was[:], r=was.k(), w=wab.k())
        P.cp("pool", wxb[:], wxs[:], r=wxs.k(), w=wxb.k())
        P.op("pool", lambda e: e.memset(ones[:], 1.0), w=ones.k())
        P.op("pool", lambda e: e.memset(onesg[:], 1.0 / 512.0), w=onesg.k())
        P.act(negA[:], mvs[:, 1:2], AF.Exp, r=mvs.k(), w=negA.k())
        P.ts("dve", negA[:], negA[:], -1.0, None, ALU.mult, None, r=negA.k(), w=negA.k())
        P.act(c8[:], lvs[:, 2, :], AF.Exp, r=lvs.k(), w=c8.k(), scale=-1.0)
        P.act(c8[:], c8[:], AF.Ln, r=c8.k(), w=c8.k(), bias=1.0)
        P.ts("dve", c8[:], c8[:], -8.0, None, ALU.mult, None, r=c8.k(), w=c8.k())

        xst = P.sb([128, KD // 2, TB], F32, "xst")
        xb = P.sb([128, KD, TB], BF16, "xb")
        ws = WStream(P)
        pss = [P.ps([128, 512], F32, f"ps{i}") for i in range(2)]
        pY = P.ps([128, 1024], F32, "pY")
        pO = P.ps([128, 1024], F32, "pO")
        pT = P.ps([128, 512], F32, "pT")
        pB = P.ps([128, 512], F32, "pB")
        pbs = [pB, pss[0], pss[1]]

        craw = P.sb([128, 12, TB + 3], F32, "craw", nsub=12)
        cfm = P.sb([128, 12, TB], F32, "cfm", nsub=12)
        bcb = P.sb([128, 4, TB], BF16, "bcb", nsub=4)
        P.op("pool", lambda e: e.memset(craw[:], 0.0), w=craw.k())
        dtf = P.sb([128, TB], F32, "dtf")
        Af = P.sb([128, TB], F32, "Af")
        S = P.sb([128, NH, 64], F32, "S")
        Sb = P.sb([128, NH, 64], BF16, "Sb")
        P.op("pool", lambda e: e.memset(S[:], 0.0), w=S.k())
        P.op("pool", lambda e: e.memset(Sb[:], 0.0), w=Sb.k())
        yfm = P.sb([128, 8, TB], F32, "yfm", nsub=8)
        xt = P.sb([128, NH, 64], F32, "xt")
        Xd = P.sb([128, NH, 64], BF16, "Xd")
        Xdec = P.sb([128, NH, 64], BF16, "Xdec")
        Bt = P.sb([128, 2, 128], BF16, "Bt")
        tm = P.sb([128, 3, NH], F32, "tm")
        cumT = P.sb([128, NH], F32, "cumT")
        ncum = P.sb([128, NH], F32, "ncum")
        ecum = P.sb([128, NH], F32, "ecum")
        decs = P.sb([128, NH], F32, "decs")
        etot = P.sb([128, NH], F32, "etot")
        Abc = P.sb([128, NH, 128], F32, "Abc")
        Gt = P.sb([128, 2, 128], F32, "Gt")
        wt = [P.sb([128, 128], F32, f"wt{i}") for i in range(2)]
        we = [P.sb([128, 128], F32, f"we{i}") for i in range(2)]
        Stt = [P.sb([128, 128], BF16, f"Stt{i}") for i in range(3)]
        Ytm = P.sb([128, NH, 64], F32, "Ytm")
        Yof = P.sb([128, NH, 64], F32, "Yof")
        ft = [P.sb([128, TB + 3], F32, f"ft{i}") for i in range(6)]
        fb = [P.sb([128, TB], BF16, f"fb{i}") for i in range(2)]
        ob = [P.sb([128, TB], BF16, f"ob{i}") for i in range(2)]
        lhalo = P.sb([128, 8, 3], F32, "lhalo", nsub=8)
        hst = P.sb([128, 8], F32, "hst", nsub=8)
        msq = P.sb([128, TB], F32, "msq")
        P.op("pool", lambda e: e.memset(lhalo[:], 0.0), w=lhalo.k())
        P.op("pool", lambda e: e.memset(hst[:], 0.0), w=hst.k())
        oi = [0]

        def conv4(out, src, cwt, ci, N, eng="dve"):
            P.ts(eng, out, src[:, 3:3 + N], cwt[:, 3, ci:ci + 1], cwt[:, 4, ci:ci + 1], ALU.mult, ALU.add,
                 r=src_k[0] + cwt_k[0], w=out_k[0])
            for j in range(3):
                P.stt(eng, out, src[:, j:j + N], cwt[:, j, ci:ci + 1], out, ALU.mult, ALU.add,
                      r=src_k[0] + cwt_k[0] + out_k[0], w=out_k[0])
        src_k, cwt_k, out_k = [None], [None], [None]

        for blk in range(NB):
            N = TB
            t0 = blk * TB
            for hf in range(2):
                P.dma("act", xst[:], io["xsrc"](t0, hf).rearrange("(k p) t -> p k t", p=128), r=io["xsrc_k"], w=xst.k(), sem="dx")
                P.cp("dve" if hf else "pool", xb[:, hf * (KD // 2):(hf + 1) * (KD // 2), :], xst[:], r=xst.k(), w=xb.k())

            def c_ssd(j, ct, ps):
                if ct < 12:
                    P.cp("act", craw[:, ct, 3:3 + N], ps[:, :N], r=ps.k(), w=craw.k(ct))
                    src_k[0], cwt_k[0], out_k[0] = craw.k(ct), mcs.k(), cfm.k(ct)
                    conv4(cfm[:, ct, :], craw[:, ct, :], mcs, ct, N)
                    P.cp("pool", craw[:, ct, 0:3], craw[:, ct, N:N + 3], r=craw.k(ct), w=craw.k(ct))
                    P.act(cfm[:, ct, :], cfm[:, ct, :], AF.Silu, r=cfm.k(ct), w=cfm.k(ct))
                    if ct >= 8:
                        P.cp("pool", bcb[:, ct - 8, :], cfm[:, ct, :], r=cfm.k(ct), w=bcb.k(ct - 8))
                else:
                    P.act(dtf[:], ps[:, :N], AF.Exp, r=ps.k() + mvs.k(), w=dtf.k(), bias=mvs[:, 0:1])
                    P.act(dtf[:], dtf[:], AF.Ln, r=dtf.k(), w=dtf.k(), bias=1.0)
                    P.ts("dve", Af[:], dtf[:], negA[:, 0:1], None, ALU.mult, None, r=dtf.k() + negA.k(), w=Af.k())
            proj(P, ws, w_in, range(13), KD, xb, N, pss, c_ssd)

            for ch in range(NCH):
                c0 = ch * L
                for half in range(2):
                    for q in range(4):
                        i = half * 4 + q
                        P.tr(pT[:, q * 128:(q + 1) * 128], cfm[:, i, c0:c0 + L], ident, r=cfm.k(i) + cs.k(), w=pT.k())
                    P.cp("act", xt[:, half * 8:(half + 1) * 8, :].rearrange("p h d -> p (h d)"), pT[:, :], r=pT.k(), w=xt.k())
                for g in range(2):
                    P.tr(pT[:, g * 128:(g + 1) * 128], cfm[:, 8 + g, c0:c0 + L], ident, r=cfm.k(8 + g) + cs.k(), w=pT.k())
                P.tr(pT[:, 256:384], dtf[:, c0:c0 + L], ident, r=dtf.k() + cs.k(), w=pT.k())
                P.tr(pT[:, 384:512], Af[:, c0:c0 + L], ident, r=Af.k() + cs.k(), w=pT.k())
                P.cp("act", Bt[:].rearrange("p g n -> p (g n)"), pT[:, 0:256], r=pT.k(), w=Bt.k())
                P.cp("dve", tm[:, 0, :], pT[:, 256:256 + NH], r=pT.k(), w=tm.k())
                P.cp("dve", tm[:, 1, :], pT[:, 384:384 + NH], r=pT.k(), w=tm.k())
                P.mm(pT[:, 0:NH], tri, tm[:, 1, :], True, True, r=cs.k() + tm.k(), w=pT.k())
                P.mm(pT[:, 32:32 + NH], ones[:], tm[:, 1, :], True, True, r=ones.k() + tm.k(), w=pT.k())
                P.cp("dve", cumT[:], pT[:, 0:NH], r=pT.k(), w=cumT.k())
                P.ts("dve", ncum[:], pT[:, 0:NH], -1.0, None, ALU.mult, None, r=pT.k(), w=ncum.k())
                P.act(ecum[:], pT[:, 0:NH], AF.Exp, r=pT.k(), w=ecum.k())
                P.act(etot[:], pT[:, 32:32 + NH], AF.Exp, r=pT.k(), w=etot.k())
                P.tt("dve", decs[:], pT[:, 32:32 + NH], cumT[:], ALU.subtract, r=pT.k() + cumT.k(), w=decs.k())
                P.act(decs[:], decs[:], AF.Exp, r=decs.k(), w=decs.k())
                P.tt("dve", Xd[:], xt[:], tm[:, 0, :].unsqueeze(2).to_broadcast([128, NH, 64]), ALU.mult,
                     r=xt.k() + tm.k(), w=Xd.k())
                P.tt("pool", Xdec[:], Xd[:], decs[:].unsqueeze(2).to_broadcast([128, NH, 64]), ALU.mult,
                     r=Xd.k() + decs.k(), w=Xdec.k())
                P.cp("pool", Abc[:], tm[:, 1, :].unsqueeze(2).to_broadcast([128, NH, 128]), r=tm.k(), w=Abc.k())
                for g in range(2):
                    P.mm(pT[:, 128 + g * 128:256 + g * 128], bcb[:, g, c0:c0 + L], bcb[:, 2 + g, c0:c0 + L], True, True,
                         r=bcb.k(g) + bcb.k(2 + g), w=pT.k())
                P.cp("act", Gt[:].rearrange("p g n -> p (g n)"), pT[:, 128:384], r=pT.k(), w=Gt.k())
                for h in range(NH):
                    g = h // 8
                    P.mm(pO[:, h * 64:(h + 1) * 64], bcb[:, 2 + g, c0:c0 + L], Sb[:, h, :], True, True,
                         r=bcb.k(2 + g) + Sb.k(), w=pO.k())
                P.tt("dve", Yof[:], pO[:].rearrange("p (h d) -> p h d", h=NH), ecum[:].unsqueeze(2).to_broadcast([128, NH, 64]),
                     ALU.mult, r=pO.k() + ecum.k(), w=Yof.k())
                for h in range(NH):
                    g = h // 8
                    pb_ = pbs[h % 3]
                    P.mm(pb_[:, 0:128], Abc[:, h, :], tri, True, True, r=Abc.k() + cs.k(), w=pb_.k())
                    w_ = wt[h % 2]
                    e_ = we[h % 2]
                    s_ = Stt[h % 3]
                    P.tt("dve", w_[:], pb_[:, 0:128], maskneg, ALU.add, r=pb_.k() + cs.k(), w=w_.k())
                    P.act(e_[:], w_[:], AF.Exp, r=w_.k() + ncum.k(), w=e_.k(), bias=ncum[:, h:h + 1])
                    P.tt("pool", s_[:], e_[:], Gt[:, g, :], ALU.mult, r=e_.k() + Gt.k(), w=s_.k())
                    P.mm(pY[:, h * 64:(h + 1) * 64], s_[:], Xd[:, h, :], True, True, r=s_.k() + Xd.k(), w=pY.k())
                P.tt("dve", Ytm[:].rearrange("p h d -> p (h d)"), pY[:], Yof[:].rearrange("p h d -> p (h d)"), ALU.add,
                     r=pY.k() + Yof.k(), w=Ytm.k())
                for h in range(NH):
                    g = h // 8
                    P.mm(pO[:, h * 64:(h + 1) * 64], Bt[:, g, :], Xdec[:, h, :], True, True, r=Bt.k() + Xdec.k(), w=pO.k())
                P.tt("dve", S[:], S[:], etot[:].unsqueeze(2).to_broadcast([128, NH, 64]), ALU.mult, r=S.k() + etot.k(), w=S.k())
                P.tt("dve", S[:].rearrange("p h d -> p (h d)"), S[:].rearrange("p h d -> p (h d)"), pO[:], ALU.add,
                     r=S.k() + pO.k(), w=S.k())
                P.cp("act", Sb[:], S[:], r=S.k(), w=Sb.k())
                for half in range(2):
                    for q in range(4):
                        i = half * 4 + q
                        P.tr(pT[:, q * 128:(q + 1) * 128], Ytm[:, 2 * i:2 * i + 2, :].rearrange("p h d -> p (h d)"), ident,
                             r=Ytm.k() + cs.k(), w=pT.k())
                    for q in range(4):
                        i = half * 4 + q
                        P.cp("act" if q % 2 else "dve", yfm[:, i, c0:c0 + L], pT[:, q * 128:(q + 1) * 128], r=pT.k(), w=yfm.k(i))

            def c_z(j, ct, ps):
                i = ct - 13
                zs = ft[0]
                P.act(zs[:, :N], ps[:, :N], AF.Silu, r=ps.k(), w=zs.k())
                P.stt("dve", yfm[:, i, :], cfm[:, i, :], mds[:, 0, i:i + 1], yfm[:, i, :], ALU.mult, ALU.add,
                      r=cfm.k(i) + mds.k() + yfm.k(i), w=yfm.k(i))
                P.tt("dve", yfm[:, i, :], yfm[:, i, :], zs[:, :N], ALU.mult, r=yfm.k(i) + zs.k(), w=yfm.k(i))
                sq_ = ft[1 + (i % 2)]
                P.tt("pool", sq_[:, :N], yfm[:, i, :], yfm[:, i, :], ALU.mult, r=yfm.k(i), w=sq_.k())
                P.mm(pT[:, :N], onesg[:], sq_[:, :N], i % 4 == 0, i % 4 == 3, r=onesg.k() + sq_.k(), w=pT.k())
                if i % 4 == 3:
                    P.act(msq[:], pT[:, :N], AF.Sqrt, r=pT.k(), w=msq.k(), bias=RMS_EPS)
                    P.op("dve", lambda e: e.reciprocal(out=msq[:], in_=msq[:]), r=msq.k(), w=msq.k())
                    for i2 in range(i - 3, i + 1):
                        o_ = ob[oi[0] % 2]
                        oi[0] += 1
                        P.stt("dve", o_[:], yfm[:, i2, :], mds[:, 1, i2:i2 + 1], msq[:], ALU.mult, ALU.mult,
                              r=yfm.k(i2) + mds.k() + msq.k(), w=o_.k())
                        P.dma("act", yo.ap(slice(i2 * 128, (i2 + 1) * 128), t0, N), o_[:], r=o_.k(), w=yo.k(), sem=f"do{oi[0] % 2}")
            proj(P, ws, w_in, range(13, 21), KD, xb, N, pss, c_z)

            order = []
            for n in range(8):
                order += [29 + n, 21 + n]
            xc, xcb, gl = ft[3], fb[0], ft[5]

            def c_lru(j, ct, ps):
                if ct >= 29:
                    n = ct - 29
                    u = ft[2]
                    P.cp("act", u[:, 3:3 + N], ps[:, :N], r=ps.k(), w=u.k())
                    P.cp("pool", u[:, 0:3], lhalo[:, n, :], r=lhalo.k(n), w=u.k())
                    P.cp("pool", lhalo[:, n, :], u[:, N:N + 3], r=u.k(), w=lhalo.k(n))
                    src_k[0], cwt_k[0], out_k[0] = u.k(), lcs.k(), xc.k()
                    conv4(xc[:, :N], u, lcs, n, N)
                    P.cp("act", xcb[:], xc[:, :N], r=xc.k(), w=xcb.k())
                    P.mm(pY[:, :N], wab[:, n, :], xcb[:], True, True, r=wab.k() + xcb.k(), w=pY.k())
                    P.mm(pO[:, :N], wxb[:, n, :], xcb[:], True, True, r=wxb.k() + xcb.k(), w=pO.k())
                    r_, i_ = ft[0], ft[1]
                    P.act(r_[:, :N], pY[:, :N], AF.Sigmoid, r=pY.k() + lvs.k(), w=r_.k(), bias=lvs[:, 0, n:n + 1])
                    P.act(i_[:, :N], pO[:, :N], AF.Sigmoid, r=pO.k() + lvs.k(), w=i_.k(), bias=lvs[:, 1, n:n + 1])
                    a_ = ft[4]
                    P.act(a_[:, :N], r_[:, :N], AF.Exp, r=r_.k() + c8.k(), w=a_.k(), scale=c8[:, n:n + 1])
                    P.tt("pool", r_[:, :N], a_[:, :N], a_[:, :N], ALU.mult, r=a_.k(), w=r_.k())
                    P.ts("dve", r_[:, :N], r_[:, :N], -1.0, 1.0, ALU.mult, ALU.add, r=r_.k(), w=r_.k())
                    P.act(r_[:, :N], r_[:, :N], AF.Sqrt, r=r_.k(), w=r_.k())
                    P.tt("dve", i_[:, :N], i_[:, :N], xc[:, :N], ALU.mult, r=i_.k() + xc.k(), w=i_.k())
                    P.tt("dve", i_[:, :N], i_[:, :N], r_[:, :N], ALU.mult, r=i_.k() + r_.k(), w=i_.k())
                    P.op("dve", lambda e, n=n: e.tensor_tensor_scan(out=xc[:, :N], data0=a_[:, :N], data1=i_[:, :N],
                                                                   initial=hst[:, n:n + 1], op0=ALU.mult, op1=ALU.add),
                         r=a_.k() + i_.k() + hst.k(n), w=xc.k())
                    P.cp("pool", hst[:, n:n + 1], xc[:, N - 1:N], r=xc.k(), w=hst.k(n))
                else:
                    n = ct - 21
                    y_ = ft[0]
                    P.cp("act", y_[:, :N], ps[:, :N], r=ps.k(), w=y_.k())
                    y2 = ft[1]
                    P.tt("pool", y2[:, :N], y_[:, :N], y_[:, :N], ALU.mult, r=y_.k(), w=y2.k())
                    P.ts("dve", y2[:, :N], y2[:, :N], 0.044715, 1.0, ALU.mult, ALU.add, r=y2.k(), w=y2.k())
                    P.tt("dve", y2[:, :N], y2[:, :N], y_[:, :N], ALU.mult, r=y2.k() + y_.k(), w=y2.k())
                    P.act(y2[:, :N], y2[:, :N], AF.Sigmoid, r=y2.k(), w=y2.k(), scale=1.5957691216057308)
                    P.tt("dve", y2[:, :N], y2[:, :N], y_[:, :N], ALU.mult, r=y2.k() + y_.k(), w=y2.k())
                    o_ = ob[oi[0] % 2]
                    oi[0] += 1
                    P.tt("dve", o_[:], y2[:, :N], xc[:, :N], ALU.mult, r=y2.k() + xc.k(), w=o_.k())
                    P.dma("act", yo.ap(slice(1024 + n * 128, 1024 + (n + 1) * 128), t0, N), o_[:], r=o_.k(), w=yo.k(), sem=f"do{oi[0] % 2}")
            proj(P, ws, w_in, order, KD, xb, N, pss, c_lru)
        if io.get("post"):
            io["post"](P)
        P.emit()
    return P


NEG = -30000.0
C = 64
GN_EPS = 64e-5


def consts_e():
    i = np.arange(64)
    su = (i[None, :] > i[:, None]).astype(np.float32)
    iu = (i[None, :] >= i[:, None]).astype(np.float32)
    out = np.zeros((128, 5, 128), np.float32)
    out[:, 0, :] = np.eye(128)
    out[:64, 1, :] = np.concatenate([su, iu], 1)
    out[:64, 2, :64] = np.where(iu > 0, 0.0, NEG)
    out[:64, 2, 64:] = np.where(su > 0, 0.0, NEG)
    cm = np.ones(128, np.float32); cm[0] = 0; cm[64] = 0
    out[:, 3, :] = cm[None, :]
    return out


def sel_np():
    s = np.zeros((128, 8, 128), np.float32)
    for i in range(8):
        s[i, i, :] = 1.0
    return s


def relayout_w64(w):
    K, N = w.shape
    return np.ascontiguousarray(w.reshape(K // 128, 128, N // 64, 64).transpose(2, 1, 0, 3))


def neumann(P, Nall, Lall, X, pN, pL, pX, NHD, ident64b):
    P.tt("dve", X[:], Nall[:], ident64b, ALU.add, r=Nall.k(), w=X.k())
    for step in range(5):
        last = step == 4
        for h in range(NHD):
            P.mm(pL[0:64, h * 64:(h + 1) * 64], Nall[:, h, :], Lall[:, h, :], True, True, r=Nall.k() + Lall.k(), w=pL.k())
        if not last:
            for h in range(NHD):
                P.mm(pN[0:64, h * 64:(h + 1) * 64], Lall[:, h, :], Nall[:, h, :], True, True, r=Nall.k() + Lall.k(), w=pN.k())
        P.cp("act", Lall[:].rearrange("p h d -> p (h d)"), pL[0:64, :NHD * 64], r=pL.k(), w=Lall.k())
        if not last:
            P.cp("dve", Nall[:].rearrange("p h d -> p (h d)"), pN[0:64, :NHD * 64], r=pN.k(), w=Nall.k())
        for h in range(NHD):
            P.mm(pX[0:64, h * 64:(h + 1) * 64], Lall[:, h, :], X[:, h, :], True, True, r=Lall.k() + X.k(), w=pX.k())
        P.tt("dve", X[:].rearrange("p h d -> p (h d)"), X[:].rearrange("p h d -> p (h d)"), pX[0:64, :NHD * 64], ALU.add,
             r=X.k() + pX.k(), w=X.k())


def phase_E(nc, outer, io, T, TB=256, D=2048, do_gdn=True):
    P = Prog(nc)
    P.semstack = outer
    KD = D // 128
    NB = T // TB
    NCH = TB // C
    N = TB
    with ExitStack() as st:
        P.stack = st
        w_a, w_b, cst, rv, rmu, rw2, gcw, gv, yo = (io[k] for k in ("w_a", "w_b", "cst", "rv", "rmu", "rw2", "gcw", "gv", "yo"))

        cs = P.sb([128, 5, 128], F32, "cs")
        ident = cs[:, 0, :]
        mask2 = cs[0:64, 1, :]
        cmask = cs[:, 3, :]
        rvs = P.sb([64, 10, 8], F32, "rvs")
        omka = P.sb([64, 8], F32, "omka")
        rmus = P.sb([128, 4], F32, "rmus")
        xst = P.sb([128, KD // 2, TB], F32, "xst")
        assert (KD // 2) * TB == 2048
        w2s = TlV(xst.h[:].rearrange("p k t -> p (k t)").rearrange("p (a b) -> p a b", a=4), xst)
        w2b = P.sb([128, 4, 512], BF16, "w2b")
        gcs = P.sb([128, 4, 12], F32, "gcs")
        gvs = P.sb([128, 4], F32, "gvs")
        ones64 = P.sb([64, 64], F32, "ones64")
        ones64m = P.sb([64, 64], F32, "ones64m")
        for (t_, d_) in ((cs, cst), (rvs, rv), (rmus, rmu), (w2s, rw2), (gcs, gcw), (gvs, gv)):
            P.dma("act", t_[:], d_[:], w=t_.k(), sem="dc")
        selc = P.sb([128, 8, 128], F32, "selc")
        P.dma("act", selc[:], io["sel"][:], w=selc.k(), sem="dc")
        P.fence([cs, rvs, rmus, w2s, gcs, gvs, selc])
        P.cp("pool", w2b[:], w2s[:], r=w2s.k(), w=w2b.k())
        P.op("pool", lambda e: e.memset(ones64[:], 1.0), w=ones64.k())
        P.op("pool", lambda e: e.memset(ones64m[:], 1.0 / 64.0), w=ones64m.k())
        P.ts("dve", omka[:], rvs[:, 6, :], -1.0, 1.0, ALU.mult, ALU.add, r=rvs.k(), w=omka.k())

        xb = P.sb([128, KD, TB], BF16, "xb")
        ws = WStream(P)
        pq = [P.ps([128, 512], F32, f"pq{i}") for i in range(8)]
        pss = pq[0:2]

        halo1 = P.sb([128, 28, 1], F32, "halo1", nsub=28)
        P.op("pool", lambda e: e.memset(halo1[:], 0.0), w=halo1.k())
        raw = [P.sb([128, N + 1], F32, f"raw{i}") for i in range(3)]
        lo_b = P.sb([128, 4, N], BF16, "lo_b", nsub=4)
        ar128 = P.sb([128, 8, NCH, 128], F32, "ar", nsub=8)
        ar = Tl(ar128.h[0:64], ar128.name, 8)
        bt128 = P.sb([128, 8, N], F32, "bt", nsub=8)
        bt = Tl(bt128.h[0:64], bt128.name, 8)
        kt128 = P.sb([128, 8, N], F32, "kt", nsub=8)
        kt = Tl(kt128.h[0:64], kt128.name, 8)
        vv128 = P.sb([128, 8, N], F32, "vv", nsub=8)
        vv = Tl(vv128.h[0:64], vv128.name, 8)
        bon128 = P.sb([128, 8, N], F32, "bon", nsub=8)
        bon = Tl(bon128.h[0:64], bon128.name, 8)
        gg128 = P.sb([128, 8, N], F32, "gg", nsub=8)
        gg = Tl(gg128.h[0:64], gg128.name, 8)
        PC = P.sb([64, 8, NCH], F32, "PC", nsub=8)
        yall = P.sb([64, 8, N], F32, "yall")
        H = P.sb([64, 8, 64], F32, "H")
        P.op("pool", lambda e: e.memset(H[:], 0.0), w=H.k())
        rr = [P.sb([64, N], F32, f"rr{i}") for i in range(2)]
        kk_ = [P.sb([64, N], F32, f"kk{i}") for i in range(2)]
        tA = [P.sb([64, N], F32, f"tA{i}") for i in range(8)]
        vtm = P.sb([64, 8, 64], F32, "vtm")
        btm = P.sb([64, 8, 64], F32, "btm")
        ktm = P.sb([64, 8, 64], F32, "ktm")
        sc1 = P.sb([64, 8, 128], F32, "sc1")
        sc2 = P.sb([64, 8, 128], F32, "sc2")
        Nall = P.sb([64, 8, 64], F32, "Nall")
        Lall = P.sb([64, 8, 64], F32, "Lall")
        X = P.sb([64, 8, 64], F32, "X")
        rhs_sb = P.sb([64, 8, 64], F32, "rhs_sb")
        U = P.sb([64, 8, 64], F32, "U")
        ob = [P.sb([128, N], BF16, f"ob{i}") for i in range(2)]
        oi = [0]
        ident64b = cs[0:64, 0, 0:64].unsqueeze(1).to_broadcast([64, 8, 64])


        ones128m = P.sb([128, 128], F32, "ones128m")
        ones128 = P.sb([128, 128], F32, "ones128")
        P.op("pool", lambda e: e.memset(ones128m[:], 1.0 / 128.0), w=ones128m.k())
        P.op("pool", lambda e: e.memset(ones128[:], 1.0), w=ones128.k())
        gnegA = P.sb([128, 1], F32, "gnegA")
        P.act(gnegA[:], gvs[:, 0:1], AF.Exp, r=gvs.k(), w=gnegA.k())
        P.ts("dve", gnegA[:], gnegA[:], -1.0, None, ALU.mult, None, r=gnegA.k(), w=gnegA.k())
        ghalo = P.sb([128, 12, 3], F32, "ghalo", nsub=12)
        P.op("pool", lambda e: e.memset(ghalo[:], 0.0), w=ghalo.k())
        graw = [P.sb([128, N + 3], F32, f"graw{i}") for i in range(2)]
        gq = TlV(bt128.h[:, 0:4, :], bt128)
        gk = TlV(bt128.h[:, 4:8, :], bt128)
        gvv = TlV(kt128.h[:, 0:4, :], kt128)
        kq = TlV(ar128.h[:, 0:4], ar128)
        vb = TlV(kt128.h[:, 4:8, :], kt128)
        nkbg = TlV(vv128.h[:, 0:4, :], vv128)
        qd = TlV(vv128.h[:, 4:8, :], vv128)
        kdec = TlV(bon128.h[:, 0:4, :], bon128)
        gcbc = TlV(bon128.h[:, 4:8, :], bon128)
        egc = TlV(gg128.h[:, 0:4, :], gg128)
        oall = TlV(gg128.h[:, 4:8, :], gg128)
        sgm = P.sb([128, N], F32, "sgm")
        gcf = P.sb([128, N], F32, "gcf")
        gS = P.sb([128, 4, 128], F32, "gS")
        P.op("pool", lambda e: e.memset(gS[:], 0.0), w=gS.k())
        gT = [P.sb([128, N], F32, f"gT{i}") for i in range(3)]
        ngc = P.sb([64, 128], F32, "ngc")
        wd_ = P.sb([64, 4, 64], F32, "wd_")
        Di = P.sb([64, 4, 64], F32, "Di")
        Ds = P.sb([64, 4, 64], F32, "Ds")
        attT = P.sb([64, 4, 64], F32, "attT")
        kdtm = P.sb([64, 4, 128], F32, "kdtm")
        ident64b4 = cs[0:64, 0, 0:64].unsqueeze(1).to_broadcast([64, 4, 64])
        mneg_i = cs[0:64, 2, 0:64]
        su_b4 = cs[0:64, 1, 0:64].unsqueeze(1).to_broadcast([64, 4, 64])

        cmk128 = P.sb([128, N], F32, "cmk128")
        for i_ in range(N // 128):
            P.cp("pool", cmk128[:, i_ * 128:(i_ + 1) * 128], cs[:, 3, :], r=cs.k(), w=cmk128.k())
        cmk = P.sb([64, N], F32, "cmk")
        for i_ in range(N // 128):
            P.cp("pool", cmk[:, i_ * 128:(i_ + 1) * 128], cs[0:64, 3, :], r=cs.k(), w=cmk.k())

        for blk in range(NB):
            t0 = blk * TB
            for hf in range(2):
                P.dma("act", xst[:], io["xsrc"](t0, hf).rearrange("(k p) t -> p k t", p=128), r=io["xsrc_k"], w=xst.k(), sem="dx")
                P.cp("dve" if hf else "pool", xb[:, hf * (KD // 2):(hf + 1) * (KD // 2), :], xst[:], r=xst.k(), w=xb.k())

            def c_lo(j, ct, ps):
                rawt = raw[j % 3]
                P.cp("act", rawt[:, 1:N + 1], ps[:, :N], r=ps.k(), w=rawt.k())
                P.cp("pool", rawt[:, 0:1], halo1[:, 24 + ct, :], r=halo1.k(24 + ct), w=rawt.k())
                P.cp("pool", halo1[:, 24 + ct, :], rawt[:, N:N + 1], r=rawt.k(), w=halo1.k(24 + ct))
                d32 = raw[(j + 1) % 3]
                P.tt("dve", d32[:, 0:N], rawt[:, 0:N], rawt[:, 1:N + 1], ALU.subtract, r=rawt.k(), w=d32.k())
                P.stt("dve", d32[:, 0:N], d32[:, 0:N], rmus[:, ct:ct + 1], rawt[:, 1:N + 1], ALU.mult, ALU.add,
                      r=d32.k() + rmus.k() + rawt.k(), w=d32.k())
                if ct == 0:
                    P.act(lo_b[:, 0, :], d32[:, 0:N], AF.Tanh, r=d32.k(), w=lo_b.k(0))
                elif ct == 1:
                    P.cp("act", lo_b[:, 1, :], d32[:, 0:N], r=d32.k(), w=lo_b.k(1))
                else:
                    P.act(lo_b[:, ct, :], d32[:, 0:N], AF.Sigmoid, r=d32.k(), w=lo_b.k(ct))
            proj(P, ws, w_b, range(4), KD, xb, N, pss, c_lo)

            def c_rkv(j, ct, ps):
                h, which = ct // 3, ct % 3
                rawt = raw[j % 3]
                P.cp("act", rawt[0:64, 1:N + 1], ps[0:64, :N], r=ps.k(), w=rawt.k())
                P.cp("pool", rawt[0:64, 0:1], halo1[0:64, ct, :], r=halo1.k(ct), w=rawt.k())
                P.cp("pool", halo1[0:64, ct, :], rawt[0:64, N:N + 1], r=rawt.k(), w=halo1.k(ct))
                dst = (rr[h % 2], kk_[h % 2], None)[which]
                dst_ap = vv[:, h, :] if which == 2 else dst[:, :]
                dst_k = vv.k(h) if which == 2 else dst.k()
                d = tA[7]
                P.tt("dve", d[:, :], rawt[0:64, 0:N], rawt[0:64, 1:N + 1], ALU.subtract, r=rawt.k(), w=d.k())
                P.stt("dve", dst_ap, d[:, :], rvs[:, which, h:h + 1], rawt[0:64, 1:N + 1], ALU.mult, ALU.add,
                      r=d.k() + rvs.k() + rawt.k(), w=dst_k)
                if which != 2:
                    return
                r_, k_ = rr[h % 2], kk_[h % 2]
                hs = slice(h * 64, (h + 1) * 64)
                pw, pa, pg, pn = pq[2], pq[3], pq[4], pq[5]
                P.mm(pw[0:64, :N], w2b[:, 0, hs], lo_b[:, 0, :], True, True, r=w2b.k() + lo_b.k(0), w=pw.k())
                P.mm(pa[0:64, :N], w2b[:, 1, hs], lo_b[:, 1, :], True, True, r=w2b.k() + lo_b.k(1), w=pa.k())
                P.mm(pg[0:64, :N], w2b[:, 2, hs], lo_b[:, 2, :], True, False, r=w2b.k() + lo_b.k(2), w=pg.k())
                P.mm(pg[0:64, :N], w2b[:, 3, hs], lo_b[:, 3, :], False, True, r=w2b.k() + lo_b.k(3), w=pg.k())
                lw, cl, asig, e1, e2, kkn, tmp = tA[0], tA[1], tA[2], tA[3], tA[4], tA[5], tA[6]
                P.act(lw[:], pw[0:64, :N], AF.Sigmoid, r=pw.k() + rvs.k(), w=lw.k(), bias=rvs[:, 3, h:h + 1])
                P.ts("dve", lw[:], lw[:], -0.6065306597126334, None, ALU.mult, None, r=lw.k(), w=lw.k())
                P.act(asig[:], pa[0:64, :N], AF.Sigmoid, r=pa.k() + rvs.k(), w=asig.k(), bias=rvs[:, 4, h:h + 1])
                P.cp("act", gg[:, h, :], pg[0:64, :N], r=pg.k(), w=gg.k(h))
                P.op("dve", lambda e: e.tensor_tensor_scan(out=cl[:], data0=cmk[:], data1=lw[:], initial=0.0,
                                                           op0=ALU.mult, op1=ALU.add),
                     r=lw.k() + cmk.k(), w=cl.k())
                P.act(e1[:], cl[:], AF.Exp, r=cl.k(), w=e1.k())
                P.tt("dve", ar[:, h, :, 64:128], r_[:].rearrange("p (c t) -> p c t", t=64), e1[:].rearrange("p (c t) -> p c t", t=64),
                     ALU.mult, r=r_.k() + e1.k(), w=ar.k(h))
                P.cp("pool", PC[:, h, :], e1[:].rearrange("p (c t) -> p c t", t=64)[:, :, 63], r=e1.k(), w=PC.k(h))
                P.act(e2[:], cl[:], AF.Exp, r=cl.k(), w=e2.k(), scale=-1.0)
                P.ts("dve", kkn[:], k_[:], rvs[:, 5, h:h + 1], None, ALU.mult, None, r=k_.k() + rvs.k(), w=kkn.k())
                P.tt("pool", tmp[:], kkn[:], kkn[:], ALU.mult, r=kkn.k(), w=tmp.k())
                P.mm(pn[0:64, :N], ones64[:], tmp[:], True, True, r=ones64.k() + tmp.k(), w=pn.k())
                P.act(tmp[:], pn[0:64, :N], AF.Sqrt, r=pn.k(), w=tmp.k(), bias=1e-6)
                P.op("dve", lambda e: e.reciprocal(out=tmp[:], in_=tmp[:]), r=tmp.k(), w=tmp.k())
                P.tt("dve", kkn[:], kkn[:], tmp[:], ALU.mult, r=kkn.k() + tmp.k(), w=kkn.k())
                P.tt("dve", tmp[:], kkn[:], asig[:], ALU.mult, r=kkn.k() + asig.k(), w=tmp.k())
                P.tt("dve", bt[:, h, :], tmp[:], e2[:], ALU.mult, r=tmp.k() + e2.k(), w=bt.k(h))
                P.tt("dve", tmp[:], cl[:], lw[:], ALU.subtract, r=cl.k() + lw.k(), w=tmp.k())
                P.act(tmp[:], tmp[:], AF.Exp, r=tmp.k(), w=tmp.k())
                P.stt("dve", ar[:, h, :, 0:64], kkn[:].rearrange("p (c t) -> p c t", t=64), -1.0,
                      tmp[:].rearrange("p (c t) -> p c t", t=64), ALU.mult, ALU.mult, r=kkn.k() + tmp.k(), w=ar.k(h))
                P.ts("dve", tmp[:], asig[:], rvs[:, 6, h:h + 1], omka[:, h:h + 1], ALU.mult, ALU.add, r=asig.k() + rvs.k() + omka.k(), w=tmp.k())
                P.tt("dve", tmp[:], tmp[:], k_[:], ALU.mult, r=tmp.k() + k_.k(), w=tmp.k())
                P.tt("dve", kt[:, h, :], tmp[:], e2[:], ALU.mult, r=tmp.k() + e2.k(), w=kt.k(h))
                P.stt("dve", tmp[:], tmp[:], rvs[:, 7, h:h + 1], r_[:], ALU.mult, ALU.mult, r=tmp.k() + rvs.k() + r_.k(), w=tmp.k())
                P.mm(pn[0:64, :N], ones64[:], tmp[:], True, True, r=ones64.k() + tmp.k(), w=pn.k())
                P.tt("dve", bon[:, h, :], pn[0:64, :N], vv[:, h, :], ALU.mult, r=pn.k() + vv.k(h), w=bon.k(h))
            proj(P, ws, w_a, range(24), KD, xb, N, pss, c_rkv, cw=64)

            for c in range(NCH):
                cc = slice(c * C, (c + 1) * C)
                pT1, pT2, pT3 = pq[7], pq[6], pq[5]
                for h in range(8):
                    hs = slice(h * 64, (h + 1) * 64)
                    P.tr(pT1[0:64, hs], vv[:, h, cc], ident[0:64, 0:64], r=vv.k(h) + cs.k(), w=pT1.k())
                    P.tr(pT2[0:64, hs], bt[:, h, cc], ident[0:64, 0:64], r=bt.k(h) + cs.k(), w=pT2.k())
                    P.tr(pT3[0:64, hs], kt[:, h, cc], ident[0:64, 0:64], r=kt.k(h) + cs.k(), w=pT3.k())
                P.cp("act", vtm[:].rearrange("p h d -> p (h d)"), pT1[0:64, :], r=pT1.k(), w=vtm.k())
                P.cp("dve", btm[:].rearrange("p h d -> p (h d)"), pT2[0:64, :], r=pT2.k(), w=btm.k())
                P.cp("act", ktm[:].rearrange("p h d -> p (h d)"), pT3[0:64, :], r=pT3.k(), w=ktm.k())
                for h in range(8):
                    pa_ = pq[0] if h < 4 else pq[1]
                    pb_ = pq[2] if h < 4 else pq[3]
                    o = (h % 4) * 128
                    P.mm(pa_[0:64, o:o + 128], bt[:, h, cc], ar[:, h, c, :], True, True, r=bt.k(h) + ar.k(h), w=pa_.k())
                    P.mm(pb_[0:64, o:o + 128], kt[:, h, cc], ar[:, h, c, :], True, True, r=kt.k(h) + ar.k(h), w=pb_.k())
                m2b = mask2.unsqueeze(1).to_broadcast([64, 4, 128])
                for half in range(2):
                    P.tt("dve", sc1[:, half * 4:(half + 1) * 4, :], pq[half][0:64, :].rearrange("p (h d) -> p h d", h=4), m2b, ALU.mult,
                         r=pq[half].k() + cs.k(), w=sc1.k())
                    P.tt("dve", sc2[:, half * 4:(half + 1) * 4, :], pq[2 + half][0:64, :].rearrange("p (h d) -> p h d", h=4), m2b, ALU.mult,
                         r=pq[2 + half].k() + cs.k(), w=sc2.k())
                P.cp("pool", Nall[:], sc1[:, :, 0:64], r=sc1.k(), w=Nall.k())
                pL, pN, pX = pq[4], pq[5], pq[6]
                for h in range(8):
                    P.tr(pL[0:64, h * 64:(h + 1) * 64], sc1[:, h, 0:64], ident[0:64, 0:64], r=sc1.k() + cs.k(), w=pL.k())
                P.cp("act", Lall[:].rearrange("p h d -> p (h d)"), pL[0:64, :], r=pL.k(), w=Lall.k())
                neumann(P, Nall, Lall, X, pN, pL, pX, 8, ident64b)
                pR, pU, pY, pH = pq[0], pq[1], pq[2], pq[3]
                for h in range(8):
                    hs = slice(h * 64, (h + 1) * 64)
                    P.mm(pR[0:64, hs], ar[:, h, c, 0:64], H[:, h, :], True, False, r=ar.k(h) + H.k(), w=pR.k())
                    P.mm(pR[0:64, hs], sc2[:, h, 0:64], vtm[:, h, :], False, True, r=sc2.k() + vtm.k(), w=pR.k())
                P.cp("act", rhs_sb[:].rearrange("p h d -> p (h d)"), pR[0:64, :], r=pR.k(), w=rhs_sb.k())
                for h in range(8):
                    hs = slice(h * 64, (h + 1) * 64)
                    P.mm(pU[0:64, hs], X[:, h, :], rhs_sb[:, h, :], True, True, r=X.k() + rhs_sb.k(), w=pU.k())
                P.cp("act", U[:].rearrange("p h d -> p (h d)"), pU[0:64, :], r=pU.k(), w=U.k())
                for h in range(8):
                    hs = slice(h * 64, (h + 1) * 64)
                    P.mm(pY[0:64, hs], H[:, h, :], ar[:, h, c, 64:128], True, False, r=ar.k(h) + H.k(), w=pY.k())
                    P.mm(pY[0:64, hs], U[:, h, :], sc1[:, h, 64:128], False, False, r=U.k() + sc1.k(), w=pY.k())
                    P.mm(pY[0:64, hs], vtm[:, h, :], sc2[:, h, 64:128], False, True, r=vtm.k() + sc2.k(), w=pY.k())
                    P.mm(pH[0:64, hs], btm[:, h, :], U[:, h, :], True, False, r=btm.k() + U.k(), w=pH.k())
                    P.mm(pH[0:64, hs], ktm[:, h, :], vtm[:, h, :], False, True, r=ktm.k() + vtm.k(), w=pH.k())
                P.cp("act", yall[:, :, cc], pY[0:64, :].rearrange("p (h d) -> p h d", h=8), r=pY.k(), w=yall.k())
                P.tt("dve", H[:].rearrange("p h d -> p (h d)"), H[:].rearrange("p h d -> p (h d)"), pH[0:64, :], ALU.add, r=H.k() + pH.k(), w=H.k())
                P.tt("dve", H[:], H[:], PC[:, :, c].unsqueeze(2).to_broadcast([64, 8, 64]), ALU.mult, r=H.k() + PC.k(), w=H.k())

            for h in range(8):
                pm, pv = pq[4], pq[5]
                y_ = yall[:, h, :]
                sq_, mean, t_ = tA[0], tA[1], tA[2]
                P.tt("pool", sq_[:], y_, y_, ALU.mult, r=yall.k(), w=sq_.k())
                P.mm(pm[0:64, :N], ones64m[:], y_, True, True, r=ones64m.k() + yall.k(), w=pm.k())
                P.mm(pv[0:64, :N], ones64m[:], sq_[:], True, True, r=ones64m.k() + sq_.k(), w=pv.k())
                P.cp("act", mean[:], pm[0:64, :N], r=pm.k(), w=mean.k())
                P.tt("dve", t_[:], mean[:], mean[:], ALU.mult, r=mean.k(), w=t_.k())
                P.tt("dve", t_[:], pv[0:64, :N], t_[:], ALU.subtract, r=pv.k() + t_.k(), w=t_.k())
                P.act(t_[:], t_[:], AF.Sqrt, r=t_.k(), w=t_.k(), bias=GN_EPS)
                P.op("dve", lambda e, t_=t_: e.reciprocal(out=t_[:], in_=t_[:]), r=t_.k(), w=t_.k())
                P.tt("dve", mean[:], y_, mean[:], ALU.subtract, r=yall.k() + mean.k(), w=mean.k())
                P.tt("dve", mean[:], mean[:], t_[:], ALU.mult, r=mean.k() + t_.k(), w=mean.k())
                P.ts("dve", mean[:], mean[:], rvs[:, 8, h:h + 1], rvs[:, 9, h:h + 1], ALU.mult, ALU.add, r=mean.k() + rvs.k(), w=mean.k())
                P.tt("dve", mean[:], mean[:], bon[:, h, :], ALU.add, r=mean.k() + bon.k(h), w=mean.k())
                o_ = ob[oi[0] % 2]
                oi[0] += 1
                P.tt("dve", o_[0:64, :], mean[:], gg[:, h, :], ALU.mult, r=mean.k() + gg.k(h), w=o_.k())
                P.dma("act", yo.ap(slice(h * 64, (h + 1) * 64), t0, N), o_[0:64, :], r=o_.k(), w=yo.k(), sem=f"do{oi[0] % 2}")

            if not do_gdn:
                continue
            def c_ba(j, ct, ps):
                P.act(sgm[:], ps[:, :N], AF.Sigmoid, r=ps.k(), w=sgm.k())
                t_ = gT[0]
                P.act(t_[:], ps[:, :N], AF.Exp, r=ps.k() + gvs.k(), w=t_.k(), bias=gvs[:, 1:2])
                P.act(t_[:], t_[:], AF.Ln, r=t_.k(), w=t_.k(), bias=1.0)
                P.ts("dve", t_[:], t_[:], gnegA[:, 0:1], None, ALU.mult, None, r=t_.k() + gnegA.k(), w=t_.k())
                P.op("dve", lambda e: e.tensor_tensor_scan(out=gcf[:], data0=cmk128[:], data1=t_[:], initial=0.0,
                                                           op0=ALU.mult, op1=ALU.add), r=t_.k() + cmk128.k(), w=gcf.k())
                for h in range(4):
                    pb_ = pq[2 + (h % 2)]
                    P.mm(pb_[:, :N], selc[:, 4 + h, :], gcf[:], True, True, r=selc.k() + gcf.k(), w=pb_.k())
                    P.cp("act", gcbc[:, h, :], pb_[:, :N], r=pb_.k(), w=gcbc.k(h))
                    P.act(egc[:, h, :], pb_[:, :N], AF.Exp, r=pb_.k(), w=egc.k(h))
            proj(P, ws, w_b, [20], KD, xb, N, pss, c_ba)

            def c_qkv(j, ct, ps):
                ti = ct - 4
                which, h = ti // 4, ti % 4
                u = graw[j % 2]
                P.cp("act", u[:, 3:3 + N], ps[:, :N], r=ps.k(), w=u.k())
                P.cp("pool", u[:, 0:3], ghalo[:, ti, :], r=ghalo.k(ti), w=u.k())
                P.cp("pool", ghalo[:, ti, :], u[:, N:N + 3], r=u.k(), w=ghalo.k(ti))
                dst = (gq, gk, gvv)[which]
                o_ = dst[:, h, :]
                P.ts("dve", o_, u[:, 3:3 + N], gcs[:, 3, ti:ti + 1], None, ALU.mult, None, r=u.k() + gcs.k(), w=dst.k(h))
                for jj in range(3):
                    P.stt("dve", o_, u[:, jj:jj + N], gcs[:, jj, ti:ti + 1], o_, ALU.mult, ALU.add, r=u.k() + gcs.k() + dst.k(h), w=dst.k(h))
                P.act(o_, o_, AF.Silu, r=dst.k(h), w=dst.k(h))
                if which < 2:
                    sq_ = gT[1]
                    pn = pq[4]
                    P.tt("pool", sq_[:], o_, o_, ALU.mult, r=dst.k(h), w=sq_.k())
                    P.mm(pn[:, :N], ones128[:], sq_[:], True, True, r=ones128.k() + sq_.k(), w=pn.k())
                    P.act(sq_[:], pn[:, :N], AF.Sqrt, r=pn.k(), w=sq_.k(), bias=1e-6)
                    P.op("dve", lambda e: e.reciprocal(out=sq_[:], in_=sq_[:]), r=sq_.k(), w=sq_.k())
                    if which == 0:
                        P.stt("dve", o_, o_, float(128 ** -0.5), sq_[:], ALU.mult, ALU.mult, r=dst.k(h) + sq_.k(), w=dst.k(h))
                    else:
                        P.tt("dve", o_, o_, sq_[:], ALU.mult, r=dst.k(h) + sq_.k(), w=dst.k(h))
                if which != 2:
                    return
                pbb = pq[5]
                P.mm(pbb[:, :N], selc[:, h, :], sgm[:], True, True, r=selc.k() + sgm.k(), w=pbb.k())
                c3 = lambda ap: ap.rearrange("p (c t) -> p c t", t=64)
                P.tt("dve", kq[:, h, :, 0:64], c3(gk[:, h, :]), c3(pbb[:, :N]), ALU.mult, r=gk.k(h) + pbb.k(), w=kq.k(h))
                P.cp("pool", kq[:, h, :, 64:128], c3(gq[:, h, :]), r=gq.k(h), w=kq.k(h))
                P.tt("dve", vb[:, h, :], gvv[:, h, :], pbb[:, :N], ALU.mult, r=gvv.k(h) + pbb.k(), w=vb.k(h))
                t1 = gT[2]
                P.tt("dve", t1[:], gk[:, h, :], pbb[:, :N], ALU.mult, r=gk.k(h) + pbb.k(), w=t1.k())
                P.stt("dve", nkbg[:, h, :], t1[:], -1.0, egc[:, h, :], ALU.mult, ALU.mult, r=t1.k() + egc.k(h), w=nkbg.k(h))
                P.tt("dve", qd[:, h, :], gq[:, h, :], egc[:, h, :], ALU.mult, r=gq.k(h) + egc.k(h), w=qd.k(h))
                P.tt("dve", c3(t1[:]), c3(gcbc[:, h, :])[:, :, 63:64].to_broadcast([128, NCH, 64]), c3(gcbc[:, h, :]), ALU.subtract,
                     r=gcbc.k(h), w=t1.k())
                P.act(t1[:], t1[:], AF.Exp, r=t1.k(), w=t1.k())
                P.tt("dve", kdec[:, h, :], gk[:, h, :], t1[:], ALU.mult, r=gk.k(h) + t1.k(), w=kdec.k(h))
            proj(P, ws, w_b, range(4, 16), KD, xb, N, pss, c_qkv)

            for c in range(NCH):
                cc = slice(c * C, (c + 1) * C)
                pSc, pTr, pL, pN, pX, pR, pO, pSt = pq[0], pq[1], pq[2], pq[3], pq[4], pq[5], pq[6], pq[7]
                P.tr(pTr[0:64, 0:128], gcf[:, cc], ident, r=gcf.k() + cs.k(), w=pTr.k())
                P.ts("dve", ngc[:], pTr[0:64, 0:128], -1.0, None, ALU.mult, None, r=pTr.k(), w=ngc.k())
                for h in range(4):
                    P.mm(pSc[0:64, h * 128:(h + 1) * 128], gk[:, h, cc], kq[:, h, c, :], True, True, r=gk.k(h) + kq.k(h), w=pSc.k())
                P.tt("dve", wd_[:], gcbc[0:64, :, cc], mneg_i.unsqueeze(1).to_broadcast([64, 4, 64]), ALU.add, r=gcbc.k() + cs.k(), w=wd_.k())
                for h in range(4):
                    P.act(Di[:, h, :], wd_[:, h, :], AF.Exp, r=wd_.k() + ngc.k(), w=Di.k(), bias=ngc[:, 4 + h:5 + h])
                P.tt("dve", Ds[:], Di[:], su_b4, ALU.mult, r=Di.k() + cs.k(), w=Ds.k())
                ps3 = pSc[0:64, :].rearrange("p (h d) -> p h d", h=4)
                P.stt("dve", Nall[:, 0:4, :], ps3[:, :, 0:64], -1.0, Ds[:], ALU.mult, ALU.mult, r=pSc.k() + Ds.k(), w=Nall.k())
                P.tt("dve", attT[:], ps3[:, :, 64:128], Di[:], ALU.mult, r=pSc.k() + Di.k(), w=attT.k())
                for h in range(4):
                    P.tr(pL[0:64, h * 64:(h + 1) * 64], Nall[:, h, :], ident[0:64, 0:64], r=Nall.k() + cs.k(), w=pL.k())
                P.cp("act", Lall[:, 0:4, :].rearrange("p h d -> p (h d)"), pL[0:64, 0:256], r=pL.k(), w=Lall.k())
                neumann(P, Tl(Nall.h[:, 0:4, :], Nall.name), Tl(Lall.h[:, 0:4, :], Lall.name), Tl(X.h[:, 0:4, :], X.name), pN, pL, pX, 4, ident64b4)
                for h in range(4):
                    P.tr(pTr[0:64, h * 128:(h + 1) * 128], kdec[:, h, cc], ident, r=kdec.k(h) + cs.k(), w=pTr.k())
                P.cp("act", kdtm[:].rearrange("p h d -> p (h d)"), pTr[0:64, :], r=pTr.k(), w=kdtm.k())
                rhs4 = rhs_sb[:].rearrange("p h d -> p (h d)")
                U4 = U[:].rearrange("p h d -> p (h d)")
                for h in range(4):
                    hs = slice(h * 128, (h + 1) * 128)
                    P.mm(pR[0:64, hs], vb[:, h, cc], ident, True, False, r=vb.k(h) + cs.k(), w=pR.k())
                    P.mm(pR[0:64, hs], nkbg[:, h, cc], gS[:, h, :], False, True, r=nkbg.k(h) + gS.k(), w=pR.k())
                P.cp("act", rhs4, pR[0:64, :], r=pR.k(), w=rhs_sb.k())
                for h in range(4):
                    hs = slice(h * 128, (h + 1) * 128)
                    P.mm(pO[0:64, hs], X[:, h, :], rhs4[:, hs], True, True, r=X.k() + rhs_sb.k(), w=pO.k())
                P.cp("act", U4, pO[0:64, :], r=pO.k(), w=U.k())
                for h in range(4):
                    hs = slice(h * 128, (h + 1) * 128)
                    P.mm(pR[:, h * 64:(h + 1) * 64], gS[:, h, :], qd[:, h, cc], True, True, r=gS.k() + qd.k(h), w=pR.k())
                    P.mm(pX[:, h * 64:(h + 1) * 64], U4[:, hs], attT[:, h, :], True, True, r=U.k() + attT.k(), w=pX.k())
                    P.mm(pSt[:, hs], kdtm[:, h, :], U4[:, hs], True, True, r=kdtm.k() + U.k(), w=pSt.k())
                t_ = gT[0]
                P.cp("act", t_[:, 0:256], pR[:, 0:256], r=pR.k(), w=t_.k())
                P.tt("dve", oall[:, :, cc], pX[:, 0:256].rearrange("p (h d) -> p h d", h=4), t_[:, 0:256].rearrange("p (h d) -> p h d", h=4), ALU.add,
                     r=pX.k() + t_.k(), w=oall.k())
                for h in range(4):
                    hs = slice(h * 128, (h + 1) * 128)
                    P.stt("dve", gS[:, h, :], gS[:, h, :], egc[:, h, c * C + C - 1:c * C + C], pSt[:, hs], ALU.mult, ALU.add,
                          r=gS.k() + egc.k(h) + pSt.k(), w=gS.k())

            def c_z(j, ct, ps):
                h = ct - 16
                zs, sq_ = gT[0], gT[1]
                pn = pq[4]
                P.act(zs[:], ps[:, :N], AF.Silu, r=ps.k(), w=zs.k())
                P.tt("pool", sq_[:], oall[:, h, :], oall[:, h, :], ALU.mult, r=oall.k(), w=sq_.k())
                P.mm(pn[:, :N], ones128m[:], sq_[:], True, True, r=ones128m.k() + sq_.k(), w=pn.k())
                P.act(sq_[:], pn[:, :N], AF.Sqrt, r=pn.k(), w=sq_.k(), bias=RMS_EPS)
                P.op("dve", lambda e: e.reciprocal(out=sq_[:], in_=sq_[:]), r=sq_.k(), w=sq_.k())
                P.stt("dve", sq_[:], oall[:, h, :], gvs[:, 2:3], sq_[:], ALU.mult, ALU.mult, r=oall.k() + gvs.k() + sq_.k(), w=sq_.k())
                o_ = ob[oi[0] % 2]
                oi[0] += 1
                P.tt("dve", o_[:], sq_[:], zs[:], ALU.mult, r=sq_.k() + zs.k(), w=o_.k())
                P.dma("act", yo.ap(slice(512 + h * 128, 512 + (h + 1) * 128), t0, N), o_[:], r=o_.k(), w=yo.k(), sem=f"do{oi[0] % 2}")
            proj(P, ws, w_b, range(16, 20), KD, xb, N, pss, c_z)
        if io.get("post"):
            io["post"](P)
        P.emit()
    return P

DN_ALPHA = (2.0 * 2) ** 0.25
A_COLS = 3520


def pad128(a):
    o = np.zeros((128,) + a.shape[1:], np.float32)
    o[:a.shape[0]] = a
    return o


def prep_even(xb_, inp, hh):
    w = inp["even_w_in"][0]
    o = 512 * hh
    hv = lambda v: np.ascontiguousarray(v.reshape(-1, 64).T)
    cols = []
    for h in range(8):
        for which in range(3):
            cols.append(w[:, which * 1024 + o + h * 64: which * 1024 + o + (h + 1) * 64])
    w_a = relayout_w64(np.concatenate(cols, 1))
    wlo = np.zeros((2048, 128), np.float32); wlo[:, :96] = w[:, 3072:3168]
    alo = np.zeros((2048, 128), np.float32); alo[:, :96] = w[:, 3168:3264]
    glo = w[:, 3264:3520]
    gb = A_COLS
    q = w[:, gb + o: gb + o + 512]; k = w[:, gb + 1024 + o: gb + 1024 + o + 512]; v = w[:, gb + 2048 + o: gb + 2048 + o + 512]
    z = w[:, gb + 3072 + o: gb + 3072 + o + 512]
    ba = np.zeros((2048, 128), np.float32)
    ba[:, 0:4] = w[:, gb + 4096 + 4 * hh: gb + 4096 + 4 * hh + 4]; ba[:, 4:8] = w[:, gb + 4104 + 4 * hh: gb + 4104 + 4 * hh + 4]
    w_b = relayout_w(np.concatenate([wlo, alo, glo, q, k, v, z, ba], 1))
    mu = inp["rwkv_mu"][0]
    rv = np.stack([hv(mu[o:o + 512]), hv(mu[1024 + o:1024 + o + 512]), hv(mu[2048 + o:2048 + o + 512]),
                   hv(inp["rwkv_w0"][0][o:o + 512]), hv(inp["rwkv_a0"][0][o:o + 512]), hv(inp["rwkv_k_k"][0][o:o + 512]),
                   hv(inp["rwkv_k_a"][0][o:o + 512]), hv(inp["rwkv_r_k"][0].reshape(-1)[o:o + 512]),
                   hv(inp["rwkv_gn_g"][0].reshape(-1)[o:o + 512]), hv(inp["rwkv_gn_b"][0].reshape(-1)[o:o + 512])], 1)
    rmu = np.zeros((128, 4), np.float32)
    rmu[:96, 0] = mu[3072:3168]; rmu[:96, 1] = mu[3168:3264]; rmu[:, 2] = mu[3264:3392]; rmu[:, 3] = mu[3392:3520]
    rw2 = np.stack([pad128(inp["rwkv_w2"][0][:, o:o + 512]), pad128(inp["rwkv_a2"][0][:, o:o + 512]),
                    inp["rwkv_g2"][0][0:128, o:o + 512], inp["rwkv_g2"][0][128:256, o:o + 512]], 1)
    gc = inp["gdn_conv_w"][0]
    idx = np.concatenate([o + np.arange(512), 1024 + o + np.arange(512), 2048 + o + np.arange(512)])
    gcw = np.stack([relayout_v(gc[j, idx]) for j in range(4)], 1)
    gv = np.zeros((128, 4), np.float32)
    gv[4:8, 0] = inp["gdn_A_log"][0][4 * hh:4 * hh + 4]; gv[4:8, 1] = inp["gdn_dt_bias"][0][4 * hh:4 * hh + 4]
    gv[:, 2] = inp["gdn_norm_g"][0]
    return dict(xT=np.ascontiguousarray(xb_.T), w_a=w_a, w_b=w_b, cst=consts_e(), rv=np.ascontiguousarray(rv), rmu=rmu,
                rw2=np.ascontiguousarray(rw2), gcw=np.ascontiguousarray(gcw), gv=gv, sel=sel_np())


def prep_odd(xb_, inp, hh):
    w = inp["odd_w_in"][0]
    o = 1024 * hh
    zc = w[:, o:o + 1024]
    xs = w[:, 2048 + o:2048 + o + 1024]
    Bc = w[:, 4096 + 256 * hh:4096 + 256 * hh + 256]
    Cc = w[:, 4608 + 256 * hh:4608 + 256 * hh + 256]
    dtc = np.zeros((2048, 128), np.float32); dtc[:, :16] = w[:, 5120 + 16 * hh:5120 + 16 * hh + 16]
    yb = w[:, 5152 + o:5152 + o + 1024]
    xbr = w[:, 7200 + o:7200 + o + 1024]
    wc = np.concatenate([xs, Bc, Cc, dtc, zc, yb, xbr], 1)
    mc = inp["mamba_conv_w"][0]; mb = inp["mamba_conv_b"][0]
    idx = np.concatenate([np.arange(o, o + 1024), 2048 + 256 * hh + np.arange(256), 2560 + 256 * hh + np.arange(256)])
    mcw = np.stack([relayout_v(mc[j, idx]) for j in range(4)] + [relayout_v(mb[idx])], 1)
    mv = np.zeros((128, 4), np.float32)
    mv[:16, 0] = inp["mamba_dt_bias"][0][16 * hh:16 * hh + 16]; mv[:16, 1] = inp["mamba_A_log"][0][16 * hh:16 * hh + 16]
    Dexp = np.repeat(inp["mamba_D"][0][16 * hh:16 * hh + 16], 64)
    mdn = np.stack([relayout_v(Dexp), relayout_v(inp["mamba_norm_g"][0][o:o + 1024])], 1)
    lc = inp["lru_conv_w"][0][:, o:o + 1024]
    lcw = np.stack([relayout_v(lc[j]) for j in range(4)] + [relayout_v(inp["lru_conv_b"][0][o:o + 1024])], 1)
    lv = np.stack([relayout_v(inp[k][0][o:o + 1024]) for k in ("lru_ba", "lru_bx", "lru_lambda")], 1)
    return dict(xT=np.ascontiguousarray(xb_.T), w_in=relayout_w(wc), cst=consts_np(),
                mcw=np.ascontiguousarray(mcw), mv=mv, mdn=np.ascontiguousarray(mdn), lcw=np.ascontiguousarray(lcw),
                lv=np.ascontiguousarray(lv), lwa=np.ascontiguousarray(inp["lru_wa"][0][8 * hh:8 * hh + 8]),
                lwx=np.ascontiguousarray(inp["lru_wx"][0][8 * hh:8 * hh + 8]))


def prep_C_weights(inp, i, w_out):
    cw = inp["ffn_conv_w"][i]
    return dict(w_out=relayout_w(w_out), w_up=relayout_w(inp["ffn_up"][i]), w_down=relayout_w(inp["ffn_down"][i]),
                w_gate=relayout_w(inp["ple_gate_w"][i]), w_ple=relayout_w(inp["ple_proj"][i]),
                vecs=np.ascontiguousarray(np.stack([relayout_v(inp[k][i]) for k in ("ln1_g", "ln1_b", "ln2_g", "ln2_b", "ple_norm_g", "ple_gate_b")], 1)),
                cvw=np.ascontiguousarray(np.stack([relayout_v(v) for v in (cw[0], cw[1], cw[2], inp["ffn_conv_b"][i])], 1)))


GROUPS = [[0, 1], [2, 3], [4, 5], [6, 7]]


def build_all(shapes, T=4096, TH=2048):
    import ml_dtypes
    nc = bass.Bass("TRN2", target_bir_lowering=False)
    D = 2048
    with ExitStack() as outer:
        din = {}
        for name, (shp, dt) in shapes.items():
            bdt = BF16 if dt == ml_dtypes.bfloat16 else F32
            din[name] = Tl(nc.dram_tensor(name, list(shp), bdt, kind="ExternalInput").ap(), name)
        def internal(name, shp, dt):
            return Tl(nc.dram_tensor(name, list(shp), dt, kind="Internal").ap(), name)
        def chunked(name, rows, cols, dt, W):
            return ChunkT([internal(f"{name}_{q}", [rows, W], dt) for q in range(cols // W)], W)
        ysnd0 = chunked("ysnd0", 1024, T, BF16, 1024)
        yrcv0 = chunked("yrcv0", 2048, T, BF16, 1024)
        ysnd1 = chunked("ysnd1", 2048, T, BF16, 512)
        yrcv1 = chunked("yrcv1", 4096, T, BF16, 512)
        NQ = TH // 512
        x1snd = [[internal(f"x1snd_{h}_{q}", [1024, 512], F32) for q in range(NQ)] for h in range(2)]
        x1rcv = [[internal(f"x1rcv_{h}_{q}", [2048, 512], F32) for q in range(NQ)] for h in range(2)]
        x1snd_k = [k for h in range(2) for c in x1snd[h] for k in c.k()]
        x1rcv_k = [k for h in range(2) for c in x1rcv[h] for k in c.k()]
        xo = Tl(nc.dram_tensor("xoT", [D, TH], F32, kind="ExternalOutput").ap(), "xoT")
        TBC = 512
        import os
        PH = os.environ.get("PHASES", "E,C0,O,C1").split(",")

        def gather_chunks(snd, rcv):
            def post(P):
                for s_, r_ in zip(snd, rcv):
                    P.coll("AllGather", s_[:], r_[:], GROUPS, s_.k(), r_.k() + [("collchain", 0)], "dcc")
            return post

        io = {k[2:]: v for k, v in din.items() if k.startswith("e_")}
        io.update(yo=ysnd0, xsrc_k=[], xsrc=lambda t0, hf: din["xT"][hf * 1024:(hf + 1) * 1024, t0:t0 + 256],
                  post=gather_chunks(ysnd0.chunks, yrcv0.chunks))
        if "E" in PH:
            phase_E(nc, outer, io, T)
        nc.all_engine_barrier()
        io = {k[3:]: v for k, v in din.items() if k.startswith("c0_")}
        io.update(yrcv=yrcv0, pT=din["pT0"], msk=din["msk"], xsrc_k=[], xdst_k=x1snd_k,
                  xsrc=lambda blk: [(0, 16, din["xTc"][:, 0:2] if blk < 0 else din["xTc"][:, 2 + blk * TBC:2 + (blk + 1) * TBC])],
                  xdst=lambda blk: [(8 * h, 8 * h + 8, x1snd[h][blk][:, :]) for h in range(2)],
                  post=gather_chunks([c for h in range(2) for c in x1snd[h]], [c for h in range(2) for c in x1rcv[h]]))
        if "C0" in PH:
            phase_C(nc, outer, io, TH // TBC, 2048, TB=TBC, alpha=DN_ALPHA, THALF=TH)
        nc.all_engine_barrier()
        io = {k[2:]: v for k, v in din.items() if k.startswith("o_")}
        io.update(yo=ysnd1, xsrc_k=[],
                  xsrc=lambda t0, hf: x1rcv[hf][(t0 % TH) // 512][(t0 // TH) * 1024:(t0 // TH + 1) * 1024, :],
                  post=gather_chunks(ysnd1.chunks, yrcv1.chunks))
        if "O" in PH:
            phase_O(nc, outer, io, T)
        nc.all_engine_barrier()
        io = {k[3:]: v for k, v in din.items() if k.startswith("c1_")}
        io.update(yrcv=yrcv1, pT=din["pT1"], msk=din["msk"], xsrc_k=[], xdst_k=xo.k(),
                  xsrc=lambda blk: [(8 * h, 8 * h + 8, x1rcv[h][NQ - 1][0:1024, 510:512] if blk < 0 else x1snd[h][blk][:, :]) for h in range(2)],
                  xdst=lambda blk: [(0, 16, xo[:, blk * TBC:(blk + 1) * TBC])])
        if "C1" in PH:
            phase_C(nc, outer, io, TH // TBC, 4096, TB=TBC, alpha=DN_ALPHA, THALF=TH)
    return nc


def kernel(**inp):
    inp = {k: np.asarray(v, dtype=np.float32) for k, v in inp.items()}
    x = inp["x"]
    p = inp["p"]
    B, T, D = x.shape
    TH = T // 2
    cores = list(range(8))
    perm_e = np.concatenate([np.arange(0, 512), np.arange(1024, 1536), np.arange(512, 1024), np.arange(1536, 2048)])
    perm_o = np.concatenate([np.arange(0, 1024), np.arange(2048, 3072), np.arange(1024, 2048), np.arange(3072, 4096)])
    c0 = prep_C_weights(inp, 0, inp["even_w_out"][0][perm_e])
    c1 = prep_C_weights(inp, 1, inp["odd_w_out"][0][perm_o])
    ins = []
    for c in cores:
        b, h = c // 2, c % 2
        m = {}
        e = prep_even(x[b], inp, h)
        m["xT"] = e.pop("xT")
        m.update({"e_" + k: v for k, v in e.items()})
        o = prep_odd(x[b][:8], inp, h)
        o.pop("xT")
        m.update({"o_" + k: v for k, v in o.items()})
        m.update({"c0_" + k: v for k, v in c0.items()})
        m.update({"c1_" + k: v for k, v in c1.items()})
        xTc = np.zeros((D, 2 + TH), np.float32)
        xTc[:, 2:] = x[b, h * TH:(h + 1) * TH].T
        if h > 0:
            xTc[:, :2] = x[b, TH - 2:TH].T
        m["xTc"] = xTc
        m["pT0"] = np.ascontiguousarray(p[0, b, h * TH:(h + 1) * TH].T)
        m["pT1"] = np.ascontiguousarray(p[1, b, h * TH:(h + 1) * TH].T)
        msk = np.zeros((128, 3), np.float32)
        msk[:, 0] = float(h > 0); msk[:, 1] = float(h == 0); msk[:, 2] = float(h == 1)
        m["msk"] = msk
        ins.append(m)
    shapes = {k: (v.shape, v.dtype) for k, v in ins[0].items()}
    nc = build_all(shapes, T=T, TH=TH)
    res = run_bass_kernel_spmd(nc, ins, core_ids=cores)
    out = np.empty_like(x)
    for c in cores:
        out[c // 2, (c % 2) * TH:(c % 2 + 1) * TH] = res.results[c]["xoT"].T
    return out
```

```python
from contextlib import ExitStack

import numpy as np
import concourse.bass as bass
import concourse.mybir as mybir
from concourse.bass_utils import run_bass_kernel_spmd

F32 = mybir.dt.float32
BF16 = mybir.dt.bfloat16
ALU = mybir.AluOpType
AF = mybir.ActivationFunctionType
AX = mybir.AxisListType

ENGS = ("pe", "dve", "act", "pool", "sp")


class Tl:
    def __init__(self, h, name, nsub=1):
        self.h, self.name, self.nsub = h, name, nsub

    def __getitem__(self, idx):
        return self.h[idx]

    def k(self, i=None):
        if i is None:
            return [(self.name, j) for j in range(self.nsub)]
        if isinstance(i, (list, tuple, range)):
            return [(self.name, j) for j in i]
        return [(self.name, i)]


class Prog:
    def __init__(self, nc):
        self.nc = nc
        self.ops = []
        self.stack = None
        self.ntiles = 0
        self.dsems = {}
        self.psum_names = set()
        Prog.ninst = getattr(Prog, "ninst", 0) + 1
        self.pfx = f"g{Prog.ninst}_"

    def sb(self, shape, dt, name=None, nsub=1):
        self.ntiles += 1
        name = self.pfx + (name or f"t{self.ntiles}")
        h = self.stack.enter_context(self.nc.sbuf_tensor(name, list(shape), dt))
        return Tl(h, name, nsub)

    def ps(self, shape, dt, name=None, nsub=1):
        self.ntiles += 1
        name = self.pfx + (name or f"p{self.ntiles}")
        h = self.stack.enter_context(self.nc.psum_tensor(name, list(shape), dt))
        self.psum_names.add(name)
        return Tl(h, name, 1)

    def dram(self, name, shape, dt, kind, nsub=1):
        h = self.nc.dram_tensor(name, list(shape), dt, kind=kind)
        return Tl(h.ap(), name, nsub)

    def op(self, eng, fn, r=(), w=(), acc=False):
        w = list(w) + [k for k in r if k[0] in self.psum_names and k not in w]
        self.ops.append(dict(eng=eng, fn=fn, r=list(r), w=list(w), dma=None, acc=acc))

    def dma(self, eng, out, in_, r=(), w=(), sem="d0", **kw):
        def fn(e):
            return e.dma_start(out=out, in_=in_, **kw)
        self.ops.append(dict(eng=eng, fn=fn, r=list(r), w=list(w), dma=sem, acc=False))

    def coll(self, kind, ins_ap, out_ap, groups, r, w, sem, inc=1):
        def fn(e):
            return e.collective_compute(kind, ALU.bypass, replica_groups=groups, ins=[ins_ap], outs=[out_ap])
        self.ops.append(dict(eng="pool", fn=fn, r=list(r), w=list(w), dma=sem, acc=False, inc=inc))

    def fence(self, tiles):
        sc = self.sb([128, 1], F32, f"fence{self.ntiles}")
        keys = [k for t in tiles for k in t.k()]
        self.op("pool", lambda e: e.memset(sc[:], 0.0), r=keys, w=keys + sc.k())

    def emit(self):
        nc = self.nc
        st = self.stack
        ss = getattr(self, "semstack", None) or st
        Prog.nprog = getattr(Prog, "nprog", 0) + 1
        pfx = f"s{Prog.nprog}_"
        sem = {e: ss.enter_context(nc.semaphore(pfx + e)) for e in ENGS}
        dnames = sorted({o["dma"] for o in self.ops if o["dma"]})
        for d in dnames:
            sem[d] = ss.enter_context(nc.semaphore(pfx + d))
        cnt = {s: 0 for s in sem}
        clock = {e: {} for e in ENGS}
        lastw = {}
        readers = {}
        per_eng = {e: [] for e in ENGS}

        def merge(a, b):
            for s, c in b.items():
                if a.get(s, 0) < c:
                    a[s] = c

        for o in self.ops:
            e = o["eng"]
            ck = clock[e]
            need = []
            for key in o["r"]:
                ev = lastw.get(key)
                if ev is not None:
                    need.append(ev)
            for key in o["w"]:
                ev = lastw.get(key)
                if ev is not None and not ((o["acc"] or e == "pe") and ev[3] == e):
                    need.append(ev)
                for ev in readers.get(key, ()):
                    if e == "pe" and ev[3] == "pe":
                        continue
                    need.append(ev)
            waits = {}
            for (s, c, evck, _) in need:
                if ck.get(s, 0) >= c:
                    continue
                if waits.get(s, 0) < c:
                    waits[s] = c
            for (s, c, evck, _) in need:
                if s in waits and waits[s] >= c and ck.get(s, 0) < c:
                    merge(ck, evck)
            for s, c in waits.items():
                if ck.get(s, 0) < c:
                    ck[s] = c
            if o["dma"]:
                s = o["dma"]
                cnt[s] += o.get("inc", 16)
                evs = s
            else:
                cnt[e] += 1
                evs = e
            evck = dict(ck)
            evck[evs] = cnt[evs]
            ev = (evs, cnt[evs], evck, e)
            for key in o["w"]:
                lastw[key] = ev
                readers[key] = []
            for key in o["r"]:
                readers.setdefault(key, []).append(ev)
            per_eng[e].append((o, sorted(waits.items()), evs))
        self.final = {s: c for s, c in cnt.items() if c > 0}
        self.sem = sem
        self.per_eng = per_eng
        nE = {e: len(v) for e, v in per_eng.items()}
        self.stats = nE

        block = st.enter_context(nc.Block())

        def run(engname, eng):
            for (o, waits, evs) in per_eng[engname]:
                for s, c in waits:
                    eng.wait_ge(sem[s], c)
                ins = o["fn"](eng)
                ins.then_inc(sem[evs], o.get("inc", 16) if o["dma"] else 1)
            if engname == "sp":
                for s, c in self.final.items():
                    eng.wait_ge(sem[s], c)

        @block.tensor
        def _(eng):
            run("pe", eng)

        @block.vector
        def _(eng):
            run("dve", eng)

        @block.scalar
        def _(eng):
            run("act", eng)

        @block.gpsimd
        def _(eng):
            run("pool", eng)

        @block.sync
        def _(eng):
            run("sp", eng)


def _tt(P, eng, out, in0, in1, op, r, w):
    P.op(eng, lambda e: e.tensor_tensor(out=out, in0=in0, in1=in1, op=op), r, w)


def _ts(P, eng, out, in0, s1, s2, op0, op1, r, w):
    if op1 is None:
        P.op(eng, lambda e: e.tensor_scalar(out=out, in0=in0, scalar1=s1, scalar2=None, op0=op0), r, w)
    else:
        P.op(eng, lambda e: e.tensor_scalar(out=out, in0=in0, scalar1=s1, scalar2=s2, op0=op0, op1=op1), r, w)


def _stt(P, eng, out, in0, sc, in1, op0, op1, r, w):
    P.op(eng, lambda e: e.scalar_tensor_tensor(out=out, in0=in0, scalar=sc, in1=in1, op0=op0, op1=op1), r, w)


def _act(P, out, in_, func, r, w, bias=None, scale=None):
    kw = {}
    if bias is not None:
        kw["bias"] = bias
    if scale is not None:
        kw["scale"] = scale
    P.op("act", lambda e: e.activation(out=out, in_=in_, func=func, **kw), r, w)


def _cp(P, eng, out, in_, r, w):
    if eng == "act":
        P.op("act", lambda e: e.copy(out=out, in_=in_), r, w)
    else:
        P.op(eng, lambda e: e.tensor_copy(out=out, in_=in_), r, w)


def _mm(P, out, lhsT, rhs, start, stop, r, w):
    P.op("pe", lambda e: e.matmul(out, lhsT=lhsT, rhs=rhs, start=start, stop=stop), r, w, acc=not start)


def _tr(P, out, in_, ident, r, w):
    P.op("pe", lambda e: e.transpose(out, in_, ident), r, w)


Prog.tt, Prog.ts, Prog.stt, Prog.act, Prog.cp, Prog.mm, Prog.tr = _tt, _ts, _stt, _act, _cp, _mm, _tr


class TlV(Tl):
    def __init__(self, ap, parent):
        self.h, self.name, self.nsub = ap, parent.name, parent.nsub

    def k(self, i=None):
        return [(self.name, j) for j in range(self.nsub)]


class ChunkT:
    def __init__(self, chunks, W):
        self.chunks, self.W = chunks, W

    def ap(self, rows, c0, n):
        q = c0 // self.W
        assert (c0 + n - 1) // self.W == q, (c0, n, self.W)
        return self.chunks[q][rows, (c0 % self.W):(c0 % self.W) + n]

    def k(self, i=None):
        return [k for c in self.chunks for k in c.k()]


LN_EPS = 1e-5
RMS_EPS = 1e-6


KS = 8


def relayout_w(w):
    K, N = w.shape
    return np.ascontiguousarray(w.reshape(K // 128, 128, N // 128, 128).transpose(2, 1, 0, 3))


def relayout_v(v):
    return np.ascontiguousarray(v.reshape(-1, 128).T)


class WStream:
    def __init__(self, P, kcmax=KS, nst=6, nbf=4, cast=("pool",), dmaq=("sp",)):
        self.P = P
        self.cast = cast
        self.dmaq = dmaq
        self.st = [P.sb([128, KS, 128], F32, f"wst{i}") for i in range(nst)]
        self.bf = [P.sb([128, KS, 128], BF16, f"wbf{i}") for i in range(nbf)]
        self.i = 0

    def get(self, wd, ct, k0, k1, cw=128):
        P = self.P
        i = self.i
        self.i += 1
        st, bf = self.st[i % len(self.st)], self.bf[i % len(self.bf)]
        n = k1 - k0
        P.dma(self.dmaq[i % len(self.dmaq)], st[:, :n, :cw], wd[ct][:, k0:k1, :], w=st.k(), sem=f"dw{i % len(self.st)}")
        P.cp(self.cast[i % len(self.cast)], bf[:, :n, :cw], st[:, :n, :cw], r=st.k(), w=bf.k())
        return bf


def proj(P, ws, wd, order, KC, act, N, pss, consume, cw=128):
    for j, ct in enumerate(order):
        ps = pss[j % len(pss)]
        for k0 in range(0, KC, KS):
            k1 = min(KC, k0 + KS)
            bf = ws.get(wd, ct, k0, k1, cw)
            for kk in range(k0, k1):
                P.mm(ps[:cw, :N], bf[:, kk - k0, :cw], act[:, kk, :N], kk == 0, kk == KC - 1, r=bf.k() + act.k(), w=ps.k())
        consume(j, ct, ps)


def phase_C(nc, outer, io, NB, CO, TB=512, D=2048, FF=5632, PLE=256, alpha=1.0, THALF=2048):
    P = Prog(nc)
    P.semstack = outer
    KY, KD, KF, KP = CO // 128, D // 128, FF // 128, PLE // 128
    NFT = 2 * KF
    NTOK = NB * TB
    with ExitStack() as st:
        P.stack = st
        yrcv, pT, msk = io["yrcv"], io["pT"], io["msk"]
        w_out, w_up, w_down, w_gate, w_ple, vecs, cvw = (io[k] for k in ("w_out", "w_up", "w_down", "w_gate", "w_ple", "vecs", "cvw"))
        ybB = P.sb([128, 4, TB], BF16, "ybB")

        vs = P.sb([128, 6, KD], F32, "vs")
        cw = P.sb([128, 4, NFT], F32, "cw")
        hms = P.sb([128, 3], F32, "hms")
        ones = P.sb([128, 128], F32, "ones")
        eT = P.sb([128, KD, TB], F32, "eT")
        assert KY <= 2 * KD
        yb = Tl(eT.h[:].rearrange("p k t -> p (k t)").bitcast(BF16)[:, :KY * TB].rearrange("p (k t) -> p k t", k=KY), eT.name)
        sx = P.sb([128, KD, TB], F32, "sx")
        hb = P.sb([128, KD, TB], BF16, "hb")
        actT = P.sb([128, KF, TB], BF16, "actT")
        pst = P.sb([128, KP, TB], F32, "pst")
        pb = P.sb([128, KP, TB], BF16, "pb")
        halo = P.sb([128, NFT, 2], F32, "halo", nsub=NFT)
        ub = [P.sb([128, TB + 2], F32, f"ub{i}") for i in range(3)]
        cv = [P.sb([128, TB], F32, f"cv{i}") for i in range(3)]
        sg = [P.sb([128, TB], F32, f"sg{i}") for i in range(2)]
        sq = [P.sb([128, TB], F32, f"sq{i}") for i in range(2)]
        mean = P.sb([128, TB], F32, "mean")
        rstd = P.sb([128, TB], F32, "rstd")
        tmp = [P.sb([128, TB], F32, f"tmp{i}") for i in range(2)]
        pss = [P.ps([128, 512], F32, f"ps{i}") for i in range(4)]
        pm = P.ps([128, 512], F32, "pm")
        pv = P.ps([128, 512], F32, "pv")
        ws = WStream(P, nst=8, cast=("act", "act", "dve"))

        P.dma("act", vs[:], vecs[:], w=vs.k(), sem="dc")
        P.dma("act", cw[:], cvw[:], w=cw.k(), sem="dc")
        P.dma("act", hms[:], msk[:], w=hms.k(), sem="dc")
        P.fence([vs, cw, hms])
        P.op("pool", lambda e: e.memset(ones[:], 1.0 / D), w=ones.k())

        def layer_norm(N, gi, bi):
            for k in range(KD):
                s = sq[k % 2]
                P.tt("pool", s[:, :N], sx[:, k, :N], sx[:, k, :N], ALU.mult, r=sx.k(), w=s.k())
                P.mm(pm[:, :N], ones[:], sx[:, k, :N], k == 0, k == KD - 1, r=ones.k() + sx.k(), w=pm.k())
                P.mm(pv[:, :N], ones[:], s[:, :N], k == 0, k == KD - 1, r=ones.k() + s.k(), w=pv.k())
            P.cp("act", mean[:, :N], pm[:, :N], r=pm.k(), w=mean.k())
            t = tmp[0]
            P.tt("dve", t[:, :N], mean[:, :N], mean[:, :N], ALU.mult, r=mean.k(), w=t.k())
            P.tt("dve", t[:, :N], pv[:, :N], t[:, :N], ALU.subtract, r=pv.k() + t.k(), w=t.k())
            P.act(rstd[:, :N], t[:, :N], AF.Sqrt, r=t.k(), w=rstd.k(), bias=LN_EPS)
            P.op("dve", lambda e: e.reciprocal(out=rstd[:, :N], in_=rstd[:, :N]), r=rstd.k(), w=rstd.k())
            for k in range(KD):
                t = tmp[k % 2]
                P.tt("dve", t[:, :N], sx[:, k, :N], mean[:, :N], ALU.subtract, r=sx.k() + mean.k(), w=t.k())
                P.tt("dve", t[:, :N], t[:, :N], rstd[:, :N], ALU.mult, r=t.k() + rstd.k(), w=t.k())
                P.ts("dve", sx[:, k, :N], t[:, :N], vs[:, gi, k:k + 1], vs[:, bi, k:k + 1], ALU.mult, ALU.add,
                     r=t.k() + vs.k(), w=sx.k())
                P.cp("act", hb[:, k, :N], sx[:, k, :N], r=sx.k(), w=hb.k())

        for blk in range(-1, NB):
            N = 2 if blk < 0 else TB
            t0 = 0 if blk < 0 else 2 + blk * TB
            cA = 0 if blk < 0 else blk * TB
            cB = THALF - 2 if blk < 0 else THALF + blk * TB
            P.dma("act", yb[:, :, :N], yrcv.ap(slice(None), cA, N).rearrange("(k p) t -> p k t", p=128), r=yrcv.k(), w=yb.k(), sem="dy")
            for k0 in range(0, KY, 4):
                P.dma("act", ybB[:, :, :N], yrcv.ap(slice(k0 * 128, (k0 + 4) * 128), cB, N).rearrange("(k p) t -> p k t", p=128),
                      r=yrcv.k(), w=ybB.k(), sem="dyb")
                P.ts("dve", yb[:, k0:k0 + 4, :N], yb[:, k0:k0 + 4, :N], hms[:, 1:2], None, ALU.mult, None, r=yb.k() + hms.k(), w=yb.k())
                P.stt("dve", yb[:, k0:k0 + 4, :N], ybB[:, :, :N], hms[:, 2:3], yb[:, k0:k0 + 4, :N], ALU.mult, ALU.add,
                      r=ybB.k() + hms.k() + yb.k(), w=yb.k())
            for (k0_, k1_, ap_) in io["xsrc"](blk):
                P.dma("act", sx[:, k0_:k1_, :N], ap_.rearrange("(k p) t -> p k t", p=128), r=io["xsrc_k"], w=sx.k(), sem="dx")

            def c_out(j, ct, ps, N=N):
                P.stt("dve", sx[:, ct, :N], sx[:, ct, :N], float(alpha), ps[:, :N], ALU.mult, ALU.add,
                      r=sx.k() + ps.k(), w=sx.k())
            proj(P, ws, w_out, range(KD), KY, yb, N, pss, c_out)
            layer_norm(N, 0, 1)

            order = []
            for c in range(KF):
                order += [c, KF + c]

            def c_up(j, ct, ps, N=N, blk=blk):
                if blk < 0:
                    P.ts("dve", halo[:, ct, :], ps[:, :2], hms[:, 0:1], None, ALU.mult, None,
                         r=ps.k() + hms.k(), w=halo.k(ct))
                    return
                u = ub[j % 3]
                c_ = cv[j % 3]
                P.cp("act", u[:, 2:2 + N], ps[:, :N], r=ps.k(), w=u.k())
                P.cp("pool", u[:, 0:2], halo[:, ct, :], r=halo.k(ct), w=u.k())
                P.cp("act", halo[:, ct, :], u[:, N:N + 2], r=u.k(), w=halo.k(ct))
                P.ts("dve", c_[:, :N], u[:, 2:2 + N], cw[:, 2, ct:ct + 1], cw[:, 3, ct:ct + 1], ALU.mult, ALU.add,
                     r=u.k() + cw.k(), w=c_.k())
                P.stt("dve", c_[:, :N], u[:, 1:1 + N], cw[:, 1, ct:ct + 1], c_[:, :N], ALU.mult, ALU.add,
                      r=u.k() + cw.k() + c_.k(), w=c_.k())
                P.stt("dve", c_[:, :N], u[:, 0:N], cw[:, 0, ct:ct + 1], c_[:, :N], ALU.mult, ALU.add,
                      r=u.k() + cw.k() + c_.k(), w=c_.k())
                if ct < KF:
                    s = sg[ct % 2]
                    P.act(s[:, :N], c_[:, :N], AF.Silu, r=c_.k(), w=s.k())
                else:
                    c = ct - KF
                    s = sg[c % 2]
                    P.tt("dve", actT[:, c, :N], s[:, :N], c_[:, :N], ALU.mult, r=s.k() + c_.k(), w=actT.k())
            proj(P, ws, w_up, order, KD, hb, N, pss, c_up)
            if blk < 0:
                continue

            def c_down(j, ct, ps, N=N):
                P.stt("dve", sx[:, ct, :N], sx[:, ct, :N], float(alpha), ps[:, :N], ALU.mult, ALU.add,
                      r=sx.k() + ps.k(), w=sx.k())
            proj(P, ws, w_down, range(KD), KF, actT, N, pss, c_down)
            layer_norm(N, 2, 3)

            P.dma("act", pst[:], pT[:, blk * TB:(blk + 1) * TB].rearrange("(k p) t -> p k t", p=128), r=[], w=pst.k(), sem="dp")
            P.cp("pool", pb[:], pst[:], r=pst.k(), w=pb.k())

            def c_ple(j, ct, ps, N=N):
                P.cp("act", eT[:, ct, :N], ps[:, :N], r=ps.k(), w=eT.k())
                s = sq[ct % 2]
                P.tt("pool", s[:, :N], eT[:, ct, :N], eT[:, ct, :N], ALU.mult, r=eT.k(), w=s.k())
                P.mm(pv[:, :N], ones[:], s[:, :N], ct == 0, ct == KD - 1, r=ones.k() + s.k(), w=pv.k())
            proj(P, ws, w_ple, range(KD), KP, pb, N, pss, c_ple)
            P.act(rstd[:, :N], pv[:, :N], AF.Sqrt, r=pv.k(), w=rstd.k(), bias=RMS_EPS)
            P.op("dve", lambda e, N=N: e.reciprocal(out=rstd[:, :N], in_=rstd[:, :N]), r=rstd.k(), w=rstd.k())

            def c_gate(j, ct, ps, N=N, blk=blk):
                g = sg[ct % 2]
                P.act(g[:, :N], ps[:, :N], AF.Sigmoid, r=ps.k() + vs.k(), w=g.k(), bias=vs[:, 5, ct:ct + 1])
                t = tmp[ct % 2]
                P.stt("dve", t[:, :N], eT[:, ct, :N], vs[:, 4, ct:ct + 1], rstd[:, :N], ALU.mult, ALU.mult,
                      r=eT.k() + vs.k() + rstd.k(), w=t.k())
                P.tt("dve", t[:, :N], t[:, :N], g[:, :N], ALU.mult, r=t.k() + g.k(), w=t.k())
                P.tt("dve", eT[:, ct, :N], t[:, :N], sx[:, ct, :N], ALU.add, r=t.k() + sx.k(), w=eT.k())
            proj(P, ws, w_gate, range(KD), KD, hb, N, pss, c_gate)
            for (k0_, k1_, ap_) in io["xdst"](blk):
                P.dma("act", ap_.rearrange("(k p) t -> p k t", p=128), eT[:, k0_:k1_, :], r=eT.k(), w=io["xdst_k"], sem="do")
        if io.get("post"):
            io["post"](P)
        P.emit()
    return P


NEG = -30000.0


def consts_np():
    i = np.arange(128)
    ident = np.eye(128, dtype=np.float32)
    tri = (i[:, None] <= i[None, :]).astype(np.float32)
    maskneg = np.where(i[None, :] >= i[:, None], 0.0, NEG).astype(np.float32)
    return np.ascontiguousarray(np.stack([ident, tri, maskneg], 1))


def phase_O(nc, outer, io, T, TB=512, D=2048, L=128):
    P = Prog(nc)
    P.semstack = outer
    KD = D // 128
    NB = T // TB
    NCH = TB // L
    NH = 16
    with ExitStack() as st:
        P.stack = st
        w_in, cst, mcw, mv, mdn, lcw, lv, lwa, lwx, yo = (io[k] for k in ("w_in", "cst", "mcw", "mv", "mdn", "lcw", "lv", "lwa", "lwx", "yo"))

        cs = P.sb([128, 3, 128], F32, "cs")
        ident, tri, maskneg = cs[:, 0, :], cs[:, 1, :], cs[:, 2, :]
        ones = P.sb([128, 128], F32, "ones")
        onesg = P.sb([128, 128], F32, "onesg")
        mcs = P.sb([128, 5, 12], F32, "mcs")
        mvs = P.sb([128, 4], F32, "mvs")
        negA = P.sb([128, 1], F32, "negA")
        mds = P.sb([128, 2, 8], F32, "mds")
        lcs = P.sb([128, 5, 8], F32, "lcs")
        lvs = P.sb([128, 3, 8], F32, "lvs")
        c8 = P.sb([128, 8], F32, "c8")
        was = P.sb([128, 8, 128], F32, "was")
        wxs = P.sb([128, 8, 128], F32, "wxs")
        wab = P.sb([128, 8, 128], BF16, "wab")
        wxb = P.sb([128, 8, 128], BF16, "wxb")
        for (t_, d_) in ((cs, cst), (mcs, mcw), (mvs, mv), (mds, mdn), (lcs, lcw), (lvs, lv)):
            P.dma("act", t_[:], d_[:], w=t_.k(), sem="dc")
        P.dma("act", was[:], lwa[:].rearrange("n d e -> d n e"), w=was.k(), sem="dc")
        P.dma("act", wxs[:], lwx[:].rearrange("n d e -> d n e"), w=wxs.k(), sem="dc")
        P.fence([cs, mcs, mvs, mds, lcs, lvs, was, wxs])
        P.cp("pool", wab[:], was[:], r=was.k(), w=wab.k())
        P.cp("pool", wxb[:], wxs[:], r=wxs.k(), w=wxb.k())
        P.op("pool", lambda e: e.memset(ones[:], 1.0), w=ones.k())
        P.op("pool", lambda e: e.memset(onesg[:], 1.0 / 512.0), w=onesg.k())
        P.act(negA[:], mvs[:, 1:2], AF.Exp, r=mvs.k(), w=negA.k())
        P.ts("dve", negA[:], negA[:], -1.0, None, ALU.mult, None, r=negA.k(), w=negA.k())
        P.act(c8[:], lvs[:, 2, :], AF.Exp, r=lvs.k(), w=c8.k(), scale=-1.0)
        P.act(c8[:], c8[:], AF.Ln, r=c8.k(), w=c8.k(), bias=1.0)
        P.ts("dve", c8[:], c8[:], -8.0, None, ALU.mult, None, r=c8.k(), w=c8.k())

        xst = P.sb([128, KD // 2, TB], F32, "xst")
        xb = P.sb([128, KD, TB], BF16, "xb")
        ws = WStream(P)
        pss = [P.ps([128, 512], F32, f"ps{i}") for i in range(2)]
        pY = P.ps([128, 1024], F32, "pY")
        pO = P.ps([128, 1024], F32, "pO")
        pT = P.ps([128, 512], F32, "pT")
        pB = P.ps([128, 512], F32, "pB")
        pbs = [pB, pss[0], pss[1]]

        craw = P.sb([128, 12, TB + 3], F32, "craw", nsub=12)
        cfm = P.sb([128, 12, TB], F32, "cfm", nsub=12)
        bcb = P.sb([128, 4, TB], BF16, "bcb", nsub=4)
        P.op("pool", lambda e: e.memset(craw[:], 0.0), w=craw.k())
        dtf = P.sb([128, TB], F32, "dtf")
        Af = P.sb([128, TB], F32, "Af")
        S = P.sb([128, NH, 64], F32, "S")
        Sb = P.sb([128, NH, 64], BF16, "Sb")
        P.op("pool", lambda e: e.memset(S[:], 0.0), w=S.k())
        P.op("pool", lambda e: e.memset(Sb[:], 0.0), w=Sb.k())
        yfm = P.sb([128, 8, TB], F32, "yfm", nsub=8)
        xt = P.sb([128, NH, 64], F32, "xt")
        Xd = P.sb([128, NH, 64], BF16, "Xd")
        Xdec = P.sb([128, NH, 64], BF16, "Xdec")
        Bt = P.sb([128, 2, 128], BF16, "Bt")
        tm = P.sb([128, 3, NH], F32, "tm")
        cumT = P.sb([128, NH], F32, "cumT")
        ncum = P.sb([128, NH], F32, "ncum")
        ecum = P.sb([128, NH], F32, "ecum")
        decs = P.sb([128, NH], F32, "decs")
        etot = P.sb([128, NH], F32, "etot")
        Abc = P.sb([128, NH, 128], F32, "Abc")
        Gt = P.sb([128, 2, 128], F32, "Gt")
        wt = [P.sb([128, 128], F32, f"wt{i}") for i in range(2)]
        we = [P.sb([128, 128], F32, f"we{i}") for i in range(2)]
        Stt = [P.sb([128, 128], BF16, f"Stt{i}") for i in range(3)]
        Ytm = P.sb([128, NH, 64], F32, "Ytm")
        Yof = P.sb([128, NH, 64], F32, "Yof")
        ft = [P.sb([128, TB + 3], F32, f"ft{i}") for i in range(6)]
        fb = [P.sb([128, TB], BF16, f"fb{i}") for i in range(2)]
        ob = [P.sb([128, TB], BF16, f"ob{i}") for i in range(2)]
        lhalo = P.sb([128, 8, 3], F32, "lhalo", nsub=8)
        hst = P.sb([128, 8], F32, "hst", nsub=8)
        msq = P.sb([128, TB], F32, "msq")
        P.op("pool", lambda e: e.memset(lhalo[:], 0.0), w=lhalo.k())
        P.op("pool", lambda e: e.memset(hst[:], 0.0), w=hst.k())
        oi = [0]

        def conv4(out, src, cwt, ci, N, eng="dve"):
            P.ts(eng, out, src[:, 3:3 + N], cwt[:, 3, ci:ci + 1], cwt[:, 4, ci:ci + 1], ALU.mult, ALU.add,
                 r=src_k[0] + cwt_k[0], w=out_k[0])
            for j in range(3):
                P.stt(eng, out, src[:, j:j + N], cwt[:, j, ci:ci + 1], out, ALU.mult, ALU.add,
                      r=src_k[0] + cwt_k[0] + out_k[0], w=out_k[0])
        src_k, cwt_k, out_k = [None], [None], [None]

        for blk in range(NB):
            N = TB
            t0 = blk * TB
            for hf in range(2):
                P.dma("act", xst[:], io["xsrc"](t0, hf).rearrange("(k p) t -> p k t", p=128), r=io["xsrc_k"], w=xst.k(), sem="dx")
                P.cp("dve" if hf else "pool", xb[:, hf * (KD // 2):(hf + 1) * (KD // 2), :], xst[:], r=xst.k(), w=xb.k())

            def c_ssd(j, ct, ps):
                if ct < 12:
                    P.cp("act", craw[:, ct, 3:3 + N], ps[:, :N], r=ps.k(), w=craw.k(ct))
                    src_k[0], cwt_k[0], out_k[0] = craw.k(ct), mcs.k(), cfm.k(ct)
                    conv4(cfm[:, ct, :], craw[:, ct, :], mcs, ct, N)
                    P.cp("pool", craw[:, ct, 0:3], craw[:, ct, N:N + 3], r=craw.k(ct), w=craw.k(ct))
                    P.act(cfm[:, ct, :], cfm[:, ct, :], AF.Silu, r=cfm.k(ct), w=cfm.k(ct))
                    if ct >= 8:
                        P.cp("pool", bcb[:, ct - 8, :], cfm[:, ct, :], r=cfm.k(ct), w=bcb.k(ct - 8))
                else:
                    P.act(dtf[:], ps[:, :N], AF.Exp, r=ps.k() + mvs.k(), w=dtf.k(), bias=mvs[:, 0:1])
                    P.act(dtf[:], dtf[:], AF.Ln, r=dtf.k(), w=dtf.k(), bias=1.0)
                    P.ts("dve", Af[:], dtf[:], negA[:, 0:1], None, ALU.mult, None, r=dtf.k() + negA.k(), w=Af.k())
            proj(P, ws, w_in, range(13), KD, xb, N, pss, c_ssd)

            for ch in range(NCH):
                c0 = ch * L
                for half in range(2):
                    for q in range(4):
                        i = half * 4 + q
                        P.tr(pT[:, q * 128:(q + 1) * 128], cfm[:, i, c0:c0 + L], ident, r=cfm.k(i) + cs.k(), w=pT.k())
                    P.cp("act", xt[:, half * 8:(half + 1) * 8, :].rearrange("p h d -> p (h d)"), pT[:, :], r=pT.k(), w=xt.k())
                for g in range(2):
                    P.tr(pT[:, g * 128:(g + 1) * 128], cfm[:, 8 + g, c0:c0 + L], ident, r=cfm.k(8 + g) + cs.k(), w=pT.k())
                P.tr(pT[:, 256:384], dtf[:, c0:c0 + L], ident, r=dtf.k() + cs.k(), w=pT.k())
                P.tr(pT[:, 384:512], Af[:, c0:c0 + L], ident, r=Af.k() + cs.k(), w=pT.k())
                P.cp("act", Bt[:].rearrange("p g n -> p (g n)"), pT[:, 0:256], r=pT.k(), w=Bt.k())
                P.cp("dve", tm[:, 0, :], pT[:, 256:256 + NH], r=pT.k(), w=tm.k())
                P.cp("dve", tm[:, 1, :], pT[:, 384:384 + NH], r=pT.k(), w=tm.k())
                P.mm(pT[:, 0:NH], tri, tm[:, 1, :], True, True, r=cs.k() + tm.k(), w=pT.k())
                P.mm(pT[:, 32:32 + NH], ones[:], tm[:, 1, :], True, True, r=ones.k() + tm.k(), w=pT.k())
                P.cp("dve", cumT[:], pT[:, 0:NH], r=pT.k(), w=cumT.k())
                P.ts("dve", ncum[:], pT[:, 0:NH], -1.0, None, ALU.mult, None, r=pT.k(), w=ncum.k())
                P.act(ecum[:], pT[:, 0:NH], AF.Exp, r=pT.k(), w=ecum.k())
                P.act(etot[:], pT[:, 32:32 + NH], AF.Exp, r=pT.k(), w=etot.k())
                P.tt("dve", decs[:], pT[:, 32:32 + NH], cumT[:], ALU.subtract, r=pT.k() + cumT.k(), w=decs.k())
                P.act(decs[:], decs[:], AF.Exp, r=decs.k(), w=decs.k())
                P.tt("dve", Xd[:], xt[:], tm[:, 0, :].unsqueeze(2).to_broadcast([128, NH, 64]), ALU.mult,
                     r=xt.k() + tm.k(), w=Xd.k())
                P.tt("pool", Xdec[:], Xd[:], decs[:].unsqueeze(2).to_broadcast([128, NH, 64]), ALU.mult,
                     r=Xd.k() + decs.k(), w=Xdec.k())
                P.cp("pool", Abc[:], tm[:, 1, :].unsqueeze(2).to_broadcast([128, NH, 128]), r=tm.k(), w=Abc.k())
                for g in range(2):
                    P.mm(pT[:, 128 + g * 128:256 + g * 128], bcb[:, g, c0:c0 + L], bcb[:, 2 + g, c0:c0 + L], True, True,
                         r=bcb.k(g) + bcb.k(2 + g), w=pT.k())
                P.cp("act", Gt[:].rearrange("p g n -> p (g n)"), pT[:, 128:384], r=pT.k(), w=Gt.k())
                for h in range(NH):
                    g = h // 8
                    P.mm(pO[:, h * 64:(h + 1) * 64], bcb[:, 2 + g, c0:c0 + L], Sb[:, h, :], True, True,
                         r=bcb.k(2 + g) + Sb.k(), w=pO.k())
                P.tt("dve", Yof[:], pO[:].rearrange("p (h d) -> p h d", h=NH), ecum[:].unsqueeze(2).to_broadcast([128, NH, 64]),
                     ALU.mult, r=pO.k() + ecum.k(), w=Yof.k())
                for h in range(NH):
                    g = h // 8
                    pb_ = pbs[h % 3]
                    P.mm(pb_[:, 0:128], Abc[:, h, :], tri, True, True, r=Abc.k() + cs.k(), w=pb_.k())
                    w_ = wt[h % 2]
                    e_ = we[h % 2]
                    s_ = Stt[h % 3]
                    P.tt("dve", w_[:], pb_[:, 0:128], maskneg, ALU.add, r=pb_.k() + cs.k(), w=w_.k())
                    P.act(e_[:], w_[:], AF.Exp, r=w_.k() + ncum.k(), w=e_.k(), bias=ncum[:, h:h + 1])
                    P.tt("pool", s_[:], e_[:], Gt[:, g, :], ALU.mult, r=e_.k() + Gt.k(), w=s_.k())
                    P.mm(pY[:, h * 64:(h + 1) * 64], s_[:], Xd[:, h, :], True, True, r=s_.k() + Xd.k(), w=pY.k())
                P.tt("dve", Ytm[:].rearrange("p h d -> p (h d)"), pY[:], Yof[:].rearrange("p h d -> p (h d)"), ALU.add,
                     r=pY.k() + Yof.k(), w=Ytm.k())
                for h in range(NH):
                    g = h // 8
                    P.mm(pO[:, h * 64:(h + 1) * 64], Bt[:, g, :], Xdec[:, h, :], True, True, r=Bt.k() + Xdec.k(), w=pO.k())
                P.tt("dve", S[:], S[:], etot[:].unsqueeze(2).to_broadcast([128, NH, 64]), ALU.mult, r=S.k() + etot.k(), w=S.k())
                P.tt("dve", S[:].rearrange("p h d -> p (h d)"), S[:].rearrange("p h d -> p (h d)"), pO[:], ALU.add,
                     r=S.k() + pO.k(), w=S.k())
                P.cp("act", Sb[:], S[:], r=S.k(), w=Sb.k())
                for half in range(2):
                    for q in range(4):
                        i = half * 4 + q
                        P.tr(pT[:, q * 128:(q + 1) * 128], Ytm[:, 2 * i:2 * i + 2, :].rearrange("p h d -> p (h d)"), ident,
                             r=Ytm.k() + cs.k(), w=pT.k())
                    for q in range(4):
                        i = half * 4 + q
                        P.cp("act" if q % 2 else "dve", yfm[:, i, c0:c0 + L], pT[:, q * 128:(q + 1) * 128], r=pT.k(), w=yfm.k(i))

            def c_z(j, ct, ps):
                i = ct - 13
                zs = ft[0]
                P.act(zs[:, :N], ps[:, :N], AF.Silu, r=ps.k(), w=zs.k())
                P.stt("dve", yfm[:, i, :], cfm[:, i, :], mds[:, 0, i:i + 1], yfm[:, i, :], ALU.mult, ALU.add,
                      r=cfm.k(i) + mds.k() + yfm.k(i), w=yfm.k(i))
                P.tt("dve", yfm[:, i, :], yfm[:, i, :], zs[:, :N], ALU.mult, r=yfm.k(i) + zs.k(), w=yfm.k(i))
                sq_ = ft[1 + (i % 2)]
                P.tt("pool", sq_[:, :N], yfm[:, i, :], yfm[:, i, :], ALU.mult, r=yfm.k(i), w=sq_.k())
                P.mm(pT[:, :N], onesg[:], sq_[:, :N], i % 4 == 0, i % 4 == 3, r=onesg.k() + sq_.k(), w=pT.k())
                if i % 4 == 3:
                    P.act(msq[:], pT[:, :N], AF.Sqrt, r=pT.k(), w=msq.k(), bias=RMS_EPS)
                    P.op("dve", lambda e: e.reciprocal(out=msq[:], in_=msq[:]), r=msq.k(), w=msq.k())
                    for i2 in range(i - 3, i + 1):
                        o_ = ob[oi[0] % 2]
                        oi[0] += 1
                        P.stt("dve", o_[:], yfm[:, i2, :], mds[:, 1, i2:i2 + 1], msq[:], ALU.mult, ALU.mult,
                              r=yfm.k(i2) + mds.k() + msq.k(), w=o_.k())
                        P.dma("act", yo.ap(slice(i2 * 128, (i2 + 1) * 128), t0, N), o_[:], r=o_.k(), w=yo.k(), sem=f"do{oi[0] % 2}")
            proj(P, ws, w_in, range(13, 21), KD, xb, N, pss, c_z)

            order = []
            for n in range(8):
                order += [29 + n, 21 + n]
            xc, xcb, gl = ft[3], fb[0], ft[5]

            def c_lru(j, ct, ps):
                if ct >= 29:
                    n = ct - 29
                    u = ft[2]
                    P.cp("act", u[:, 3:3 + N], ps[:, :N], r=ps.k(), w=u.k())
                    P.cp("pool", u[:, 0:3], lhalo[:, n, :], r=lhalo.k(n), w=u.k())
                    P.cp("pool", lhalo[:, n, :], u[:, N:N + 3], r=u.k(), w=lhalo.k(n))
                    src_k[0], cwt_k[0], out_k[0] = u.k(), lcs.k(), xc.k()
                    conv4(xc[:, :N], u, lcs, n, N)
                    P.cp("act", xcb[:], xc[:, :N], r=xc.k(), w=xcb.k())
                    P.mm(pY[:, :N], wab[:, n, :], xcb[:], True, True, r=wab.k() + xcb.k(), w=pY.k())
                    P.mm(pO[:, :N], wxb[:, n, :], xcb[:], True, True, r=wxb.k() + xcb.k(), w=pO.k())
                    r_, i_ = ft[0], ft[1]
                    P.act(r_[:, :N], pY[:, :N], AF.Sigmoid, r=pY.k() + lvs.k(), w=r_.k(), bias=lvs[:, 0, n:n + 1])
                    P.act(i_[:, :N], pO[:, :N], AF.Sigmoid, r=pO.k() + lvs.k(), w=i_.k(), bias=lvs[:, 1, n:n + 1])
                    a_ = ft[4]
                    P.act(a_[:, :N], r_[:, :N], AF.Exp, r=r_.k() + c8.k(), w=a_.k(), scale=c8[:, n:n + 1])
                    P.tt("pool", r_[:, :N], a_[:, :N], a_[:, :N], ALU.mult, r=a_.k(), w=r_.k())
                    P.ts("dve", r_[:, :N], r_[:, :N], -1.0, 1.0, ALU.mult, ALU.add, r=r_.k(), w=r_.k())
                    P.act(r_[:, :N], r_[:, :N], AF.Sqrt, r=r_.k(), w=r_.k())
                    P.tt("dve", i_[:, :N], i_[:, :N], xc[:, :N], ALU.mult, r=i_.k() + xc.k(), w=i_.k())
                    P.tt("dve", i_[:, :N], i_[:, :N], r_[:, :N], ALU.mult, r=i_.k() + r_.k(), w=i_.k())
                    P.op("dve", lambda e, n=n: e.tensor_tensor_scan(out=xc[:, :N], data0=a_[:, :N], data1=i_[:, :N],
                                                                   initial=hst[:, n:n + 1], op0=ALU.mult, op1=ALU.add),
                         r=a_.k() + i_.k() + hst.k(n), w=xc.k())
                    P.cp("pool", hst[:, n:n + 1], xc[:, N - 1:N], r=xc.k(), w=hst.k(n))
                else:
                    n = ct - 21
                    y_ = ft[0]
                    P.cp("act", y_[:, :N], ps[:, :N], r=ps.k(), w=y_.k())
                    y2 = ft[1]
                    P.tt("pool", y2[:, :N], y_[:, :N], y_[:, :N], ALU.mult, r=y_.k(), w=y2.k())
                    P.ts("dve", y2[:, :N], y2[:, :N], 0.044715, 1.0, ALU.mult, ALU.add, r=y2.k(), w=y2.k())
                    P.tt("dve", y2[:, :N], y2[:, :N], y_[:, :N], ALU.mult, r=y2.k() + y_.k(), w=y2.k())
                    P.act(y2[:, :N], y2[:, :N], AF.Sigmoid, r=y2.k(), w=y2.k(), scale=1.5957691216057308)
                    P.tt("dve", y2[:, :N], y2[:, :N], y_[:, :N], ALU.mult, r=y2.k() + y_.k(), w=y2.k())
                    o_ = ob[oi[0] % 2]
                    oi[0] += 1
                    P.tt("dve", o_[:], y2[:, :N], xc[:, :N], ALU.mult, r=y2.k() + xc.k(), w=o_.k())
                    P.dma("act", yo.ap(slice(1024 + n * 128, 1024 + (n + 1) * 128), t0, N), o_[:], r=o_.k(), w=yo.k(), sem=f"do{oi[0] % 2}")
            proj(P, ws, w_in, order, KD, xb, N, pss, c_lru)
        if io.get("post"):
            io["post"](P)
        P.emit()
    return P


NEG = -30000.0
C = 64
GN_EPS = 64e-5


def consts_e():
    i = np.arange(64)
    su = (i[None, :] > i[:, None]).astype(np.float32)
    iu = (i[None, :] >= i[:, None]).astype(np.float32)
    out = np.zeros((128, 5, 128), np.float32)
    out[:, 0, :] = np.eye(128)
    out[:64, 1, :] = np.concatenate([su, iu], 1)
    out[:64, 2, :64] = np.where(iu > 0, 0.0, NEG)
    out[:64, 2, 64:] = np.where(su > 0, 0.0, NEG)
    cm = np.ones(128, np.float32); cm[0] = 0; cm[64] = 0
    out[:, 3, :] = cm[None, :]
    return out


def sel_np():
    s = np.zeros((128, 8, 128), np.float32)
    for i in range(8):
        s[i, i, :] = 1.0
    return s


def relayout_w64(w):
    K, N = w.shape
    return np.ascontiguousarray(w.reshape(K // 128, 128, N // 64, 64).transpose(2, 1, 0, 3))


def neumann(P, Nall, Lall, X, pN, pL, pX, NHD, ident64b):
    P.tt("dve", X[:], Nall[:], ident64b, ALU.add, r=Nall.k(), w=X.k())
    for step in range(5):
        last = step == 4
        for h in range(NHD):
            P.mm(pL[0:64, h * 64:(h + 1) * 64], Nall[:, h, :], Lall[:, h, :], True, True, r=Nall.k() + Lall.k(), w=pL.k())
        if not last:
            for h in range(NHD):
                P.mm(pN[0:64, h * 64:(h + 1) * 64], Lall[:, h, :], Nall[:, h, :], True, True, r=Nall.k() + Lall.k(), w=pN.k())
        P.cp("act", Lall[:].rearrange("p h d -> p (h d)"), pL[0:64, :NHD * 64], r=pL.k(), w=Lall.k())
        if not last:
            P.cp("dve", Nall[:].rearrange("p h d -> p (h d)"), pN[0:64, :NHD * 64], r=pN.k(), w=Nall.k())
        for h in range(NHD):
            P.mm(pX[0:64, h * 64:(h + 1) * 64], Lall[:, h, :], X[:, h, :], True, True, r=Lall.k() + X.k(), w=pX.k())
        P.tt("dve", X[:].rearrange("p h d -> p (h d)"), X[:].rearrange("p h d -> p (h d)"), pX[0:64, :NHD * 64], ALU.add,
             r=X.k() + pX.k(), w=X.k())


def phase_E(nc, outer, io, T, TB=256, D=2048, do_gdn=True):
    P = Prog(nc)
    P.semstack = outer
    KD = D // 128
    NB = T // TB
    NCH = TB // C
    N = TB
    with ExitStack() as st:
        P.stack = st
        w_a, w_b, cst, rv, rmu, rw2, gcw, gv, yo = (io[k] for k in ("w_a", "w_b", "cst", "rv", "rmu", "rw2", "gcw", "gv", "yo"))

        cs = P.sb([128, 5, 128], F32, "cs")
        ident = cs[:, 0, :]
        mask2 = cs[0:64, 1, :]
        cmask = cs[:, 3, :]
        rvs = P.sb([64, 10, 8], F32, "rvs")
        omka = P.sb([64, 8], F32, "omka")
        rmus = P.sb([128, 4], F32, "rmus")
        xst = P.sb([128, KD // 2, TB], F32, "xst")
        assert (KD // 2) * TB == 2048
        w2s = TlV(xst.h[:].rearrange("p k t -> p (k t)").rearrange("p (a b) -> p a b", a=4), xst)
        w2b = P.sb([128, 4, 512], BF16, "w2b")
        gcs = P.sb([128, 4, 12], F32, "gcs")
        gvs = P.sb([128, 4], F32, "gvs")
        ones64 = P.sb([64, 64], F32, "ones64")
        ones64m = P.sb([64, 64], F32, "ones64m")
        for (t_, d_) in ((cs, cst), (rvs, rv), (rmus, rmu), (w2s, rw2), (gcs, gcw), (gvs, gv)):
            P.dma("act", t_[:], d_[:], w=t_.k(), sem="dc")
        selc = P.sb([128, 8, 128], F32, "selc")
        P.dma("act", selc[:], io["sel"][:], w=selc.k(), sem="dc")
        P.fence([cs, rvs, rmus, w2s, gcs, gvs, selc])
        P.cp("pool", w2b[:], w2s[:], r=w2s.k(), w=w2b.k())
        P.op("pool", lambda e: e.memset(ones64[:], 1.0), w=ones64.k())
        P.op("pool", lambda e: e.memset(ones64m[:], 1.0 / 64.0), w=ones64m.k())
        P.ts("dve", omka[:], rvs[:, 6, :], -1.0, 1.0, ALU.mult, ALU.add, r=rvs.k(), w=omka.k())

        xb = P.sb([128, KD, TB], BF16, "xb")
        ws = WStream(P)
        pq = [P.ps([128, 512], F32, f"pq{i}") for i in range(8)]
        pss = pq[0:2]

        halo1 = P.sb([128, 28, 1], F32, "halo1", nsub=28)
        P.op("pool", lambda e: e.memset(halo1[:], 0.0), w=halo1.k())
        raw = [P.sb([128, N + 1], F32, f"raw{i}") for i in range(3)]
        lo_b = P.sb([128, 4, N], BF16, "lo_b", nsub=4)
        ar128 = P.sb([128, 8, NCH, 128], F32, "ar", nsub=8)
        ar = Tl(ar128.h[0:64], ar128.name, 8)
        bt128 = P.sb([128, 8, N], F32, "bt", nsub=8)
        bt = Tl(bt128.h[0:64], bt128.name, 8)
        kt128 = P.sb([128, 8, N], F32, "kt", nsub=8)
        kt = Tl(kt128.h[0:64], kt128.name, 8)
        vv128 = P.sb([128, 8, N], F32, "vv", nsub=8)
        vv = Tl(vv128.h[0:64], vv128.name, 8)
        bon128 = P.sb([128, 8, N], F32, "bon", nsub=8)
        bon = Tl(bon128.h[0:64], bon128.name, 8)
        gg128 = P.sb([128, 8, N], F32, "gg", nsub=8)
        gg = Tl(gg128.h[0:64], gg128.name, 8)
        PC = P.sb([64, 8, NCH], F32, "PC", nsub=8)
        yall = P.sb([64, 8, N], F32, "yall")
        H = P.sb([64, 8, 64], F32, "H")
        P.op("pool", lambda e: e.memset(H[:], 0.0), w=H.k())
        rr = [P.sb([64, N], F32, f"rr{i}") for i in range(2)]
        kk_ = [P.sb([64, N], F32, f"kk{i}") for i in range(2)]
        tA = [P.sb([64, N], F32, f"tA{i}") for i in range(8)]
        vtm = P.sb([64, 8, 64], F32, "vtm")
        btm = P.sb([64, 8, 64], F32, "btm")
        ktm = P.sb([64, 8, 64], F32, "ktm")
        sc1 = P.sb([64, 8, 128], F32, "sc1")
        sc2 = P.sb([64, 8, 128], F32, "sc2")
        Nall = P.sb([64, 8, 64], F32, "Nall")
        Lall = P.sb([64, 8, 64], F32, "Lall")
        X = P.sb([64, 8, 64], F32, "X")
        rhs_sb = P.sb([64, 8, 64], F32, "rhs_sb")
        U = P.sb([64, 8, 64], F32, "U")
        ob = [P.sb([128, N], BF16, f"ob{i}") for i in range(2)]
        oi = [0]
        ident64b = cs[0:64, 0, 0:64].unsqueeze(1).to_broadcast([64, 8, 64])


        ones128m = P.sb([128, 128], F32, "ones128m")
        ones128 = P.sb([128, 128], F32, "ones128")
        P.op("pool", lambda e: e.memset(ones128m[:], 1.0 / 128.0), w=ones128m.k())
        P.op("pool", lambda e: e.memset(ones128[:], 1.0), w=ones128.k())
        gnegA = P.sb([128, 1], F32, "gnegA")
        P.act(gnegA[:], gvs[:, 0:1], AF.Exp, r=gvs.k(), w=gnegA.k())
        P.ts("dve", gnegA[:], gnegA[:], -1.0, None, ALU.mult, None, r=gnegA.k(), w=gnegA.k())
        ghalo = P.sb([128, 12, 3], F32, "ghalo", nsub=12)
        P.op("pool", lambda e: e.memset(ghalo[:], 0.0), w=ghalo.k())
        graw = [P.sb([128, N + 3], F32, f"graw{i}") for i in range(2)]
        gq = TlV(bt128.h[:, 0:4, :], bt128)
        gk = TlV(bt128.h[:, 4:8, :], bt128)
        gvv = TlV(kt128.h[:, 0:4, :], kt128)
        kq = TlV(ar128.h[:, 0:4], ar128)
        vb = TlV(kt128.h[:, 4:8, :], kt128)
        nkbg = TlV(vv128.h[:, 0:4, :], vv128)
        qd = TlV(vv128.h[:, 4:8, :], vv128)
        kdec = TlV(bon128.h[:, 0:4, :], bon128)
        gcbc = TlV(bon128.h[:, 4:8, :], bon128)
        egc = TlV(gg128.h[:, 0:4, :], gg128)
        oall = TlV(gg128.h[:, 4:8, :], gg128)
        sgm = P.sb([128, N], F32, "sgm")
        gcf = P.sb([128, N], F32, "gcf")
        gS = P.sb([128, 4, 128], F32, "gS")
        P.op("pool", lambda e: e.memset(gS[:], 0.0), w=gS.k())
        gT = [P.sb([128, N], F32, f"gT{i}") for i in range(3)]
        ngc = P.sb([64, 128], F32, "ngc")
        wd_ = P.sb([64, 4, 64], F32, "wd_")
        Di = P.sb([64, 4, 64], F32, "Di")
        Ds = P.sb([64, 4, 64], F32, "Ds")
        attT = P.sb([64, 4, 64], F32, "attT")
        kdtm = P.sb([64, 4, 128], F32, "kdtm")
        ident64b4 = cs[0:64, 0, 0:64].unsqueeze(1).to_broadcast([64, 4, 64])
        mneg_i = cs[0:64, 2, 0:64]
        su_b4 = cs[0:64, 1, 0:64].unsqueeze(1).to_broadcast([64, 4, 64])

        cmk128 = P.sb([128, N], F32, "cmk128")
        for i_ in range(N // 128):
            P.cp("pool", cmk128[:, i_ * 128:(i_ + 1) * 128], cs[:, 3, :], r=cs.k(), w=cmk128.k())
        cmk = P.sb([64, N], F32, "cmk")
        for i_ in range(N // 128):
            P.cp("pool", cmk[:, i_ * 128:(i_ + 1) * 128], cs[0:64, 3, :], r=cs.k(), w=cmk.k())

        for blk in range(NB):
            t0 = blk * TB
            for hf in range(2):
                P.dma("act", xst[:], io["xsrc"](t0, hf).rearrange("(k p) t -> p k t", p=128), r=io["xsrc_k"], w=xst.k(), sem="dx")
                P.cp("dve" if hf else "pool", xb[:, hf * (KD // 2):(hf + 1) * (KD // 2), :], xst[:], r=xst.k(), w=xb.k())

            def c_lo(j, ct, ps):
                rawt = raw[j % 3]
                P.cp("act", rawt[:, 1:N + 1], ps[:, :N], r=ps.k(), w=rawt.k())
                P.cp("pool", rawt[:, 0:1], halo1[:, 24 + ct, :], r=halo1.k(24 + ct), w=rawt.k())
                P.cp("pool", halo1[:, 24 + ct, :], rawt[:, N:N + 1], r=rawt.k(), w=halo1.k(24 + ct))
                d32 = raw[(j + 1) % 3]
                P.tt("dve", d32[:, 0:N], rawt[:, 0:N], rawt[:, 1:N + 1], ALU.subtract, r=rawt.k(), w=d32.k())
                P.stt("dve", d32[:, 0:N], d32[:, 0:N], rmus[:, ct:ct + 1], rawt[:, 1:N + 1], ALU.mult, ALU.add,
                      r=d32.k() + rmus.k() + rawt.k(), w=d32.k())
                if ct == 0:
                    P.act(lo_b[:, 0, :], d32[:, 0:N], AF.Tanh, r=d32.k(), w=lo_b.k(0))
                elif ct == 1:
                    P.cp("act", lo_b[:, 1, :], d32[:, 0:N], r=d32.k(), w=lo_b.k(1))
                else:
                    P.act(lo_b[:, ct, :], d32[:, 0:N], AF.Sigmoid, r=d32.k(), w=lo_b.k(ct))
            proj(P, ws, w_b, range(4), KD, xb, N, pss, c_lo)

            def c_rkv(j, ct, ps):
                h, which = ct // 3, ct % 3
                rawt = raw[j % 3]
                P.cp("act", rawt[0:64, 1:N + 1], ps[0:64, :N], r=ps.k(), w=rawt.k())
                P.cp("pool", rawt[0:64, 0:1], halo1[0:64, ct, :], r=halo1.k(ct), w=rawt.k())
                P.cp("pool", halo1[0:64, ct, :], rawt[0:64, N:N + 1], r=rawt.k(), w=halo1.k(ct))
                dst = (rr[h % 2], kk_[h % 2], None)[which]
                dst_ap = vv[:, h, :] if which == 2 else dst[:, :]
                dst_k = vv.k(h) if which == 2 else dst.k()
                d = tA[7]
                P.tt("dve", d[:, :], rawt[0:64, 0:N], rawt[0:64, 1:N + 1], ALU.subtract, r=rawt.k(), w=d.k())
                P.stt("dve", dst_ap, d[:, :], rvs[:, which, h:h + 1], rawt[0:64, 1:N + 1], ALU.mult, ALU.add,
                      r=d.k() + rvs.k() + rawt.k(), w=dst_k)
                if which != 2:
                    return
                r_, k_ = rr[h % 2], kk_[h % 2]
                hs = slice(h * 64, (h + 1) * 64)
                pw, pa, pg, pn = pq[2], pq[3], pq[4], pq[5]
                P.mm(pw[0:64, :N], w2b[:, 0, hs], lo_b[:, 0, :], True, True, r=w2b.k() + lo_b.k(0), w=pw.k())
                P.mm(pa[0:64, :N], w2b[:, 1, hs], lo_b[:, 1, :], True, True, r=w2b.k() + lo_b.k(1), w=pa.k())
                P.mm(pg[0:64, :N], w2b[:, 2, hs], lo_b[:, 2, :], True, False, r=w2b.k() + lo_b.k(2), w=pg.k())
                P.mm(pg[0:64, :N], w2b[:, 3, hs], lo_b[:, 3, :], False, True, r=w2b.k() + lo_b.k(3), w=pg.k())
                lw, cl, asig, e1, e2, kkn, tmp = tA[0], tA[1], tA[2], tA[3], tA[4], tA[5], tA[6]
                P.act(lw[:], pw[0:64, :N], AF.Sigmoid, r=pw.k() + rvs.k(), w=lw.k(), bias=rvs[:, 3, h:h + 1])
                P.ts("dve", lw[:], lw[:], -0.6065306597126334, None, ALU.mult, None, r=lw.k(), w=lw.k())
                P.act(asig[:], pa[0:64, :N], AF.Sigmoid, r=pa.k() + rvs.k(), w=asig.k(), bias=rvs[:, 4, h:h + 1])
                P.cp("act", gg[:, h, :], pg[0:64, :N], r=pg.k(), w=gg.k(h))
                P.op("dve", lambda e: e.tensor_tensor_scan(out=cl[:], data0=cmk[:], data1=lw[:], initial=0.0,
                                                           op0=ALU.mult, op1=ALU.add),
                     r=lw.k() + cmk.k(), w=cl.k())
                P.act(e1[:], cl[:], AF.Exp, r=cl.k(), w=e1.k())
                P.tt("dve", ar[:, h, :, 64:128], r_[:].rearrange("p (c t) -> p c t", t=64), e1[:].rearrange("p (c t) -> p c t", t=64),
                     ALU.mult, r=r_.k() + e1.k(), w=ar.k(h))
                P.cp("pool", PC[:, h, :], e1[:].rearrange("p (c t) -> p c t", t=64)[:, :, 63], r=e1.k(), w=PC.k(h))
                P.act(e2[:], cl[:], AF.Exp, r=cl.k(), w=e2.k(), scale=-1.0)
                P.ts("dve", kkn[:], k_[:], rvs[:, 5, h:h + 1], None, ALU.mult, None, r=k_.k() + rvs.k(), w=kkn.k())
                P.tt("pool", tmp[:], kkn[:], kkn[:], ALU.mult, r=kkn.k(), w=tmp.k())
                P.mm(pn[0:64, :N], ones64[:], tmp[:], True, True, r=ones64.k() + tmp.k(), w=pn.k())
                P.act(tmp[:], pn[0:64, :N], AF.Sqrt, r=pn.k(), w=tmp.k(), bias=1e-6)
                P.op("dve", lambda e: e.reciprocal(out=tmp[:], in_=tmp[:]), r=tmp.k(), w=tmp.k())
                P.tt("dve", kkn[:], kkn[:], tmp[:], ALU.mult, r=kkn.k() + tmp.k(), w=kkn.k())
                P.tt("dve", tmp[:], kkn[:], asig[:], ALU.mult, r=kkn.k() + asig.k(), w=tmp.k())
                P.tt("dve", bt[:, h, :], tmp[:], e2[:], ALU.mult, r=tmp.k() + e2.k(), w=bt.k(h))
                P.tt("dve", tmp[:], cl[:], lw[:], ALU.subtract, r=cl.k() + lw.k(), w=tmp.k())
                P.act(tmp[:], tmp[:], AF.Exp, r=tmp.k(), w=tmp.k())
                P.stt("dve", ar[:, h, :, 0:64], kkn[:].rearrange("p (c t) -> p c t", t=64), -1.0,
                      tmp[:].rearrange("p (c t) -> p c t", t=64), ALU.mult, ALU.mult, r=kkn.k() + tmp.k(), w=ar.k(h))
                P.ts("dve", tmp[:], asig[:], rvs[:, 6, h:h + 1], omka[:, h:h + 1], ALU.mult, ALU.add, r=asig.k() + rvs.k() + omka.k(), w=tmp.k())
                P.tt("dve", tmp[:], tmp[:], k_[:], ALU.mult, r=tmp.k() + k_.k(), w=tmp.k())
                P.tt("dve", kt[:, h, :], tmp[:], e2[:], ALU.mult, r=tmp.k() + e2.k(), w=kt.k(h))
                P.stt("dve", tmp[:], tmp[:], rvs[:, 7, h:h + 1], r_[:], ALU.mult, ALU.mult, r=tmp.k() + rvs.k() + r_.k(), w=tmp.k())
                P.mm(pn[0:64, :N], ones64[:], tmp[:], True, True, r=ones64.k() + tmp.k(), w=pn.k())
                P.tt("dve", bon[:, h, :], pn[0:64, :N], vv[:, h, :], ALU.mult, r=pn.k() + vv.k(h), w=bon.k(h))
            proj(P, ws, w_a, range(24), KD, xb, N, pss, c_rkv, cw=64)

            for c in range(NCH):
                cc = slice(c * C, (c + 1) * C)
                pT1, pT2, pT3 = pq[7], pq[6], pq[5]
                for h in range(8):
                    hs = slice(h * 64, (h + 1) * 64)
                    P.tr(pT1[0:64, hs], vv[:, h, cc], ident[0:64, 0:64], r=vv.k(h) + cs.k(), w=pT1.k())
                    P.tr(pT2[0:64, hs], bt[:, h, cc], ident[0:64, 0:64], r=bt.k(h) + cs.k(), w=pT2.k())
                    P.tr(pT3[0:64, hs], kt[:, h, cc], ident[0:64, 0:64], r=kt.k(h) + cs.k(), w=pT3.k())
                P.cp("act", vtm[:].rearrange("p h d -> p (h d)"), pT1[0:64, :], r=pT1.k(), w=vtm.k())
                P.cp("dve", btm[:].rearrange("p h d -> p (h d)"), pT2[0:64, :], r=pT2.k(), w=btm.k())
                P.cp("act", ktm[:].rearrange("p h d -> p (h d)"), pT3[0:64, :], r=pT3.k(), w=ktm.k())
                for h in range(8):
                    pa_ = pq[0] if h < 4 else pq[1]
                    pb_ = pq[2] if h < 4 else pq[3]
                    o = (h % 4) * 128
                    P.mm(pa_[0:64, o:o + 128], bt[:, h, cc], ar[:, h, c, :], True, True, r=bt.k(h) + ar.k(h), w=pa_.k())
                    P.mm(pb_[0:64, o:o + 128], kt[:, h, cc], ar[:, h, c, :], True, True, r=kt.k(h) + ar.k(h), w=pb_.k())
                m2b = mask2.unsqueeze(1).to_broadcast([64, 4, 128])
                for half in range(2):
                    P.tt("dve", sc1[:, half * 4:(half + 1) * 4, :], pq[half][0:64, :].rearrange("p (h d) -> p h d", h=4), m2b, ALU.mult,
                         r=pq[half].k() + cs.k(), w=sc1.k())
                    P.tt("dve", sc2[:, half * 4:(half + 1) * 4, :], pq[2 + half][0:64, :].rearrange("p (h d) -> p h d", h=4), m2b, ALU.mult,
                         r=pq[2 + half].k() + cs.k(), w=sc2.k())
                P.cp("pool", Nall[:], sc1[:, :, 0:64], r=sc1.k(), w=Nall.k())
                pL, pN, pX = pq[4], pq[5], pq[6]
                for h in range(8):
                    P.tr(pL[0:64, h * 64:(h + 1) * 64], sc1[:, h, 0:64], ident[0:64, 0:64], r=sc1.k() + cs.k(), w=pL.k())
                P.cp("act", Lall[:].rearrange("p h d -> p (h d)"), pL[0:64, :], r=pL.k(), w=Lall.k())
                neumann(P, Nall, Lall, X, pN, pL, pX, 8, ident64b)
                pR, pU, pY, pH = pq[0], pq[1], pq[2], pq[3]
                for h in range(8):
                    hs = slice(h * 64, (h + 1) * 64)
                    P.mm(pR[0:64, hs], ar[:, h, c, 0:64], H[:, h, :], True, False, r=ar.k(h) + H.k(), w=pR.k())
                    P.mm(pR[0:64, hs], sc2[:, h, 0:64], vtm[:, h, :], False, True, r=sc2.k() + vtm.k(), w=pR.k())
                P.cp("act", rhs_sb[:].rearrange("p h d -> p (h d)"), pR[0:64, :], r=pR.k(), w=rhs_sb.k())
                for h in range(8):
                    hs = slice(h * 64, (h + 1) * 64)
                    P.mm(pU[0:64, hs], X[:, h, :], rhs_sb[:, h, :], True, True, r=X.k() + rhs_sb.k(), w=pU.k())
                P.cp("act", U[:].rearrange("p h d -> p (h d)"), pU[0:64, :], r=pU.k(), w=U.k())
                for h in range(8):
                    hs = slice(h * 64, (h + 1) * 64)
                    P.mm(pY[0:64, hs], H[:, h, :], ar[:, h, c, 64:128], True, False, r=ar.k(h) + H.k(), w=pY.k())
                    P.mm(pY[0:64, hs], U[:, h, :], sc1[:, h, 64:128], False, False, r=U.k() + sc1.k(), w=pY.k())
                    P.mm(pY[0:64, hs], vtm[:, h, :], sc2[:, h, 64:128], False, True, r=vtm.k() + sc2.k(), w=pY.k())
                    P.mm(pH[0:64, hs], btm[:, h, :], U[:, h, :], True, False, r=btm.k() + U.k(), w=pH.k())
                    P.mm(pH[0:64, hs], ktm[:, h, :], vtm[:, h, :], False, True, r=ktm.k() + vtm.k(), w=pH.k())
                P.cp("act", yall[:, :, cc], pY[0:64, :].rearrange("p (h d) -> p h d", h=8), r=pY.k(), w=yall.k())
                P.tt("dve", H[:].rearrange("p h d -> p (h d)"), H[:].rearrange("p h d -> p (h d)"), pH[0:64, :], ALU.add, r=H.k() + pH.k(), w=H.k())
                P.tt("dve", H[:], H[:], PC[:, :, c].unsqueeze(2).to_broadcast([64, 8, 64]), ALU.mult, r=H.k() + PC.k(), w=H.k())

            for h in range(8):
                pm, pv = pq[4], pq[5]
                y_ = yall[:, h, :]
                sq_, mean, t_ = tA[0], tA[1], tA[2]
                P.tt("pool", sq_[:], y_, y_, ALU.mult, r=yall.k(), w=sq_.k())
                P.mm(pm[0:64, :N], ones64m[:], y_, True, True, r=ones64m.k() + yall.k(), w=pm.k())
                P.mm(pv[0:64, :N], ones64m[:], sq_[:], True, True, r=ones64m.k() + sq_.k(), w=pv.k())
                P.cp("act", mean[:], pm[0:64, :N], r=pm.k(), w=mean.k())
                P.tt("dve", t_[:], mean[:], mean[:], ALU.mult, r=mean.k(), w=t_.k())
                P.tt("dve", t_[:], pv[0:64, :N], t_[:], ALU.subtract, r=pv.k() + t_.k(), w=t_.k())
                P.act(t_[:], t_[:], AF.Sqrt, r=t_.k(), w=t_.k(), bias=GN_EPS)
                P.op("dve", lambda e, t_=t_: e.reciprocal(out=t_[:], in_=t_[:]), r=t_.k(), w=t_.k())
                P.tt("dve", mean[:], y_, mean[:], ALU.subtract, r=yall.k() + mean.k(), w=mean.k())
                P.tt("dve", mean[:], mean[:], t_[:], ALU.mult, r=mean.k() + t_.k(), w=mean.k())
                P.ts("dve", mean[:], mean[:], rvs[:, 8, h:h + 1], rvs[:, 9, h:h + 1], ALU.mult, ALU.add, r=mean.k() + rvs.k(), w=mean.k())
                P.tt("dve", mean[:], mean[:], bon[:, h, :], ALU.add, r=mean.k() + bon.k(h), w=mean.k())
                o_ = ob[oi[0] % 2]
                oi[0] += 1
                P.tt("dve", o_[0:64, :], mean[:], gg[:, h, :], ALU.mult, r=mean.k() + gg.k(h), w=o_.k())
                P.dma("act", yo.ap(slice(h * 64, (h + 1) * 64), t0, N), o_[0:64, :], r=o_.k(), w=yo.k(), sem=f"do{oi[0] % 2}")

            if not do_gdn:
                continue
            def c_ba(j, ct, ps):
                P.act(sgm[:], ps[:, :N], AF.Sigmoid, r=ps.k(), w=sgm.k())
                t_ = gT[0]
                P.act(t_[:], ps[:, :N], AF.Exp, r=ps.k() + gvs.k(), w=t_.k(), bias=gvs[:, 1:2])
                P.act(t_[:], t_[:], AF.Ln, r=t_.k(), w=t_.k(), bias=1.0)
                P.ts("dve", t_[:], t_[:], gnegA[:, 0:1], None, ALU.mult, None, r=t_.k() + gnegA.k(), w=t_.k())
                P.op("dve", lambda e: e.tensor_tensor_scan(out=gcf[:], data0=cmk128[:], data1=t_[:], initial=0.0,
                                                           op0=ALU.mult, op1=ALU.add), r=t_.k() + cmk128.k(), w=gcf.k())
                for h in range(4):
                    pb_ = pq[2 + (h % 2)]
                    P.mm(pb_[:, :N], selc[:, 4 + h, :], gcf[:], True, True, r=selc.k() + gcf.k(), w=pb_.k())
                    P.cp("act", gcbc[:, h, :], pb_[:, :N], r=pb_.k(), w=gcbc.k(h))
                    P.act(egc[:, h, :], pb_[:, :N], AF.Exp, r=pb_.k(), w=egc.k(h))
            proj(P, ws, w_b, [20], KD, xb, N, pss, c_ba)

            def c_qkv(j, ct, ps):
                ti = ct - 4
                which, h = ti // 4, ti % 4
                u = graw[j % 2]
                P.cp("act", u[:, 3:3 + N], ps[:, :N], r=ps.k(), w=u.k())
                P.cp("pool", u[:, 0:3], ghalo[:, ti, :], r=ghalo.k(ti), w=u.k())
                P.cp("pool", ghalo[:, ti, :], u[:, N:N + 3], r=u.k(), w=ghalo.k(ti))
                dst = (gq, gk, gvv)[which]
                o_ = dst[:, h, :]
                P.ts("dve", o_, u[:, 3:3 + N], gcs[:, 3, ti:ti + 1], None, ALU.mult, None, r=u.k() + gcs.k(), w=dst.k(h))
                for jj in range(3):
                    P.stt("dve", o_, u[:, jj:jj + N], gcs[:, jj, ti:ti + 1], o_, ALU.mult, ALU.add, r=u.k() + gcs.k() + dst.k(h), w=dst.k(h))
                P.act(o_, o_, AF.Silu, r=dst.k(h), w=dst.k(h))
                if which < 2:
                    sq_ = gT[1]
                    pn = pq[4]
                    P.tt("pool", sq_[:], o_, o_, ALU.mult, r=dst.k(h), w=sq_.k())
                    P.mm(pn[:, :N], ones128[:], sq_[:], True, True, r=ones128.k() + sq_.k(), w=pn.k())
                    P.act(sq_[:], pn[:, :N], AF.Sqrt, r=pn.k(), w=sq_.k(), bias=1e-6)
                    P.op("dve", lambda e: e.reciprocal(out=sq_[:], in_=sq_[:]), r=sq_.k(), w=sq_.k())
                    if which == 0:
                        P.stt("dve", o_, o_, float(128 ** -0.5), sq_[:], ALU.mult, ALU.mult, r=dst.k(h) + sq_.k(), w=dst.k(h))
                    else:
                        P.tt("dve", o_, o_, sq_[:], ALU.mult, r=dst.k(h) + sq_.k(), w=dst.k(h))
                if which != 2:
                    return
                pbb = pq[5]
                P.mm(pbb[:, :N], selc[:, h, :], sgm[:], True, True, r=selc.k() + sgm.k(), w=pbb.k())
                c3 = lambda ap: ap.rearrange("p (c t) -> p c t", t=64)
                P.tt("dve", kq[:, h, :, 0:64], c3(gk[:, h, :]), c3(pbb[:, :N]), ALU.mult, r=gk.k(h) + pbb.k(), w=kq.k(h))
                P.cp("pool", kq[:, h, :, 64:128], c3(gq[:, h, :]), r=gq.k(h), w=kq.k(h))
                P.tt("dve", vb[:, h, :], gvv[:, h, :], pbb[:, :N], ALU.mult, r=gvv.k(h) + pbb.k(), w=vb.k(h))
                t1 = gT[2]
                P.tt("dve", t1[:], gk[:, h, :], pbb[:, :N], ALU.mult, r=gk.k(h) + pbb.k(), w=t1.k())
                P.stt("dve", nkbg[:, h, :], t1[:], -1.0, egc[:, h, :], ALU.mult, ALU.mult, r=t1.k() + egc.k(h), w=nkbg.k(h))
                P.tt("dve", qd[:, h, :], gq[:, h, :], egc[:, h, :], ALU.mult, r=gq.k(h) + egc.k(h), w=qd.k(h))
                P.tt("dve", c3(t1[:]), c3(gcbc[:, h, :])[:, :, 63:64].to_broadcast([128, NCH, 64]), c3(gcbc[:, h, :]), ALU.subtract,
                     r=gcbc.k(h), w=t1.k())
                P.act(t1[:], t1[:], AF.Exp, r=t1.k(), w=t1.k())
                P.tt("dve", kdec[:, h, :], gk[:, h, :], t1[:], ALU.mult, r=gk.k(h) + t1.k(), w=kdec.k(h))
            proj(P, ws, w_b, range(4, 16), KD, xb, N, pss, c_qkv)

            for c in range(NCH):
                cc = slice(c * C, (c + 1) * C)
                pSc, pTr, pL, pN, pX, pR, pO, pSt = pq[0], pq[1], pq[2], pq[3], pq[4], pq[5], pq[6], pq[7]
                P.tr(pTr[0:64, 0:128], gcf[:, cc], ident, r=gcf.k() + cs.k(), w=pTr.k())
                P.ts("dve", ngc[:], pTr[0:64, 0:128], -1.0, None, ALU.mult, None, r=pTr.k(), w=ngc.k())
                for h in range(4):
                    P.mm(pSc[0:64, h * 128:(h + 1) * 128], gk[:, h, cc], kq[:, h, c, :], True, True, r=gk.k(h) + kq.k(h), w=pSc.k())
                P.tt("dve", wd_[:], gcbc[0:64, :, cc], mneg_i.unsqueeze(1).to_broadcast([64, 4, 64]), ALU.add, r=gcbc.k() + cs.k(), w=wd_.k())
                for h in range(4):
                    P.act(Di[:, h, :], wd_[:, h, :], AF.Exp, r=wd_.k() + ngc.k(), w=Di.k(), bias=ngc[:, 4 + h:5 + h])
                P.tt("dve", Ds[:], Di[:], su_b4, ALU.mult, r=Di.k() + cs.k(), w=Ds.k())
                ps3 = pSc[0:64, :].rearrange("p (h d) -> p h d", h=4)
                P.stt("dve", Nall[:, 0:4, :], ps3[:, :, 0:64], -1.0, Ds[:], ALU.mult, ALU.mult, r=pSc.k() + Ds.k(), w=Nall.k())
                P.tt("dve", attT[:], ps3[:, :, 64:128], Di[:], ALU.mult, r=pSc.k() + Di.k(), w=attT.k())
                for h in range(4):
                    P.tr(pL[0:64, h * 64:(h + 1) * 64], Nall[:, h, :], ident[0:64, 0:64], r=Nall.k() + cs.k(), w=pL.k())
                P.cp("act", Lall[:, 0:4, :].rearrange("p h d -> p (h d)"), pL[0:64, 0:256], r=pL.k(), w=Lall.k())
                neumann(P, Tl(Nall.h[:, 0:4, :], Nall.name), Tl(Lall.h[:, 0:4, :], Lall.name), Tl(X.h[:, 0:4, :], X.name), pN, pL, pX, 4, ident64b4)
                for h in range(4):
                    P.tr(pTr[0:64, h * 128:(h + 1) * 128], kdec[:, h, cc], ident, r=kdec.k(h) + cs.k(), w=pTr.k())
                P.cp("act", kdtm[:].rearrange("p h d -> p (h d)"), pTr[0:64, :], r=pTr.k(), w=kdtm.k())
                rhs4 = rhs_sb[:].rearrange("p h d -> p (h d)")
                U4 = U[:].rearrange("p h d -> p (h d)")
                for h in range(4):
                    hs = slice(h * 128, (h + 1) * 128)
                    P.mm(pR[0:64, hs], vb[:, h, cc], ident, True, False, r=vb.k(h) + cs.k(), w=pR.k())
                    P.mm(pR[0:64, hs], nkbg[:, h, cc], gS[:, h, :], False, True, r=nkbg.k(h) + gS.k(), w=pR.k())
                P.cp("act", rhs4, pR[0:64, :], r=pR.k(), w=rhs_sb.k())
                for h in range(4):
                    hs = slice(h * 128, (h + 1) * 128)
                    P.mm(pO[0:64, hs], X[:, h, :], rhs4[:, hs], True, True, r=X.k() + rhs_sb.k(), w=pO.k())
                P.cp("act", U4, pO[0:64, :], r=pO.k(), w=U.k())
                for h in range(4):
                    hs = slice(h * 128, (h + 1) * 128)
                    P.mm(pR[:, h * 64:(h + 1) * 64], gS[:, h, :], qd[:, h, cc], True, True, r=gS.k() + qd.k(h), w=pR.k())
                    P.mm(pX[:, h * 64:(h + 1) * 64], U4[:, hs], attT[:, h, :], True, True, r=U.k() + attT.k(), w=pX.k())
                    P.mm(pSt[:, hs], kdtm[:, h, :], U4[:, hs], True, True, r=kdtm.k() + U.k(), w=pSt.k())
                t_ = gT[0]
                P.cp("act", t_[:, 0:256], pR[:, 0:256], r=pR.k(), w=t_.k())
                P.tt("dve", oall[:, :, cc], pX[:, 0:256].rearrange("p (h d) -> p h d", h=4), t_[:, 0:256].rearrange("p (h d) -> p h d", h=4), ALU.add,
                     r=pX.k() + t_.k(), w=oall.k())
                for h in range(4):
                    hs = slice(h * 128, (h + 1) * 128)
                    P.stt("dve", gS[:, h, :], gS[:, h, :], egc[:, h, c * C + C - 1:c * C + C], pSt[:, hs], ALU.mult, ALU.add,
                          r=gS.k() + egc.k(h) + pSt.k(), w=gS.k())

            def c_z(j, ct, ps):
                h = ct - 16
                zs, sq_ = gT[0], gT[1]
                pn = pq[4]
                P.act(zs[:], ps[:, :N], AF.Silu, r=ps.k(), w=zs.k())
                P.tt("pool", sq_[:], oall[:, h, :], oall[:, h, :], ALU.mult, r=oall.k(), w=sq_.k())
                P.mm(pn[:, :N], ones128m[:], sq_[:], True, True, r=ones128m.k() + sq_.k(), w=pn.k())
                P.act(sq_[:], pn[:, :N], AF.Sqrt, r=pn.k(), w=sq_.k(), bias=RMS_EPS)
                P.op("dve", lambda e: e.reciprocal(out=sq_[:], in_=sq_[:]), r=sq_.k(), w=sq_.k())
                P.stt("dve", sq_[:], oall[:, h, :], gvs[:, 2:3], sq_[:], ALU.mult, ALU.mult, r=oall.k() + gvs.k() + sq_.k(), w=sq_.k())
                o_ = ob[oi[0] % 2]
                oi[0] += 1
                P.tt("dve", o_[:], sq_[:], zs[:], ALU.mult, r=sq_.k() + zs.k(), w=o_.k())
                P.dma("act", yo.ap(slice(512 + h * 128, 512 + (h + 1) * 128), t0, N), o_[:], r=o_.k(), w=yo.k(), sem=f"do{oi[0] % 2}")
            proj(P, ws, w_b, range(16, 20), KD, xb, N, pss, c_z)
        if io.get("post"):
            io["post"](P)
        P.emit()
    return P

DN_ALPHA = (2.0 * 2) ** 0.25
A_COLS = 3520


def pad128(a):
    o = np.zeros((128,) + a.shape[1:], np.float32)
    o[:a.shape[0]] = a
    return o


def prep_even(xb_, inp, hh):
    w = inp["even_w_in"][0]
    o = 512 * hh
    hv = lambda v: np.ascontiguousarray(v.reshape(-1, 64).T)
    cols = []
    for h in range(8):
        for which in range(3):
            cols.append(w[:, which * 1024 + o + h * 64: which * 1024 + o + (h + 1) * 64])
    w_a = relayout_w64(np.concatenate(cols, 1))
    wlo = np.zeros((2048, 128), np.float32); wlo[:, :96] = w[:, 3072:3168]
    alo = np.zeros((2048, 128), np.float32); alo[:, :96] = w[:, 3168:3264]
    glo = w[:, 3264:3520]
    gb = A_COLS
    q = w[:, gb + o: gb + o + 512]; k = w[:, gb + 1024 + o: gb + 1024 + o + 512]; v = w[:, gb + 2048 + o: gb + 2048 + o + 512]
    z = w[:, gb + 3072 + o: gb + 3072 + o + 512]
    ba = np.zeros((2048, 128), np.float32)
    ba[:, 0:4] = w[:, gb + 4096 + 4 * hh: gb + 4096 + 4 * hh + 4]; ba[:, 4:8] = w[:, gb + 4104 + 4 * hh: gb + 4104 + 4 * hh + 4]
    w_b = relayout_w(np.concatenate([wlo, alo, glo, q, k, v, z, ba], 1))
    mu = inp["rwkv_mu"][0]
    rv = np.stack([hv(mu[o:o + 512]), hv(mu[1024 + o:1024 + o + 512]), hv(mu[2048 + o:2048 + o + 512]),
                   hv(inp["rwkv_w0"][0][o:o + 512]), hv(inp["rwkv_a0"][0][o:o + 512]), hv(inp["rwkv_k_k"][0][o:o + 512]),
                   hv(inp["rwkv_k_a"][0][o:o + 512]), hv(inp["rwkv_r_k"][0].reshape(-1)[o:o + 512]),
                   hv(inp["rwkv_gn_g"][0].reshape(-1)[o:o + 512]), hv(inp["rwkv_gn_b"][0].reshape(-1)[o:o + 512])], 1)
    rmu = np.zeros((128, 4), np.float32)
    rmu[:96, 0] = mu[3072:3168]; rmu[:96, 1] = mu[3168:3264]; rmu[:, 2] = mu[3264:3392]; rmu[:, 3] = mu[3392:3520]
    rw2 = np.stack([pad128(inp["rwkv_w2"][0][:, o:o + 512]), pad128(inp["rwkv_a2"][0][:, o:o + 512]),
                    inp["rwkv_g2"][0][0:128, o:o + 512], inp["rwkv_g2"][0][128:256, o:o + 512]], 1)
    gc = inp["gdn_conv_w"][0]
    idx = np.concatenate([o + np.arange(512), 1024 + o + np.arange(512), 2048 + o + np.arange(512)])
    gcw = np.stack([relayout_v(gc[j, idx]) for j in range(4)], 1)
    gv = np.zeros((128, 4), np.float32)
    gv[4:8, 0] = inp["gdn_A_log"][0][4 * hh:4 * hh + 4]; gv[4:8, 1] = inp["gdn_dt_bias"][0][4 * hh:4 * hh + 4]
    gv[:, 2] = inp["gdn_norm_g"][0]
    return dict(xT=np.ascontiguousarray(xb_.T), w_a=w_a, w_b=w_b, cst=consts_e(), rv=np.ascontiguousarray(rv), rmu=rmu,
                rw2=np.ascontiguousarray(rw2), gcw=np.ascontiguousarray(gcw), gv=gv, sel=sel_np())


def prep_odd(xb_, inp, hh):
    w = inp["odd_w_in"][0]
    o = 1024 * hh
    zc = w[:, o:o + 1024]
    xs = w[:, 2048 + o:2048 + o + 1024]
    Bc = w[:, 4096 + 256 * hh:4096 + 256 * hh + 256]
    Cc = w[:, 4608 + 256 * hh:4608 + 256 * hh + 256]
    dtc = np.zeros((2048, 128), np.float32); dtc[:, :16] = w[:, 5120 + 16 * hh:5120 + 16 * hh + 16]
    yb = w[:, 5152 + o:5152 + o + 1024]
    xbr = w[:, 7200 + o:7200 + o + 1024]
    wc = np.concatenate([xs, Bc, Cc, dtc, zc, yb, xbr], 1)
    mc = inp["mamba_conv_w"][0]; mb = inp["mamba_conv_b"][0]
    idx = np.concatenate([np.arange(o, o + 1024), 2048 + 256 * hh + np.arange(256), 2560 + 256 * hh + np.arange(256)])
    mcw = np.stack([relayout_v(mc[j, idx]) for j in range(4)] + [relayout_v(mb[idx])], 1)
    mv = np.zeros((128, 4), np.float32)
    mv[:16, 0] = inp["mamba_dt_bias"][0][16 * hh:16 * hh + 16]; mv[:16, 1] = inp["mamba_A_log"][0][16 * hh:16 * hh + 16]
    Dexp = np.repeat(inp["mamba_D"][0][16 * hh:16 * hh + 16], 64)
    mdn = np.stack([relayout_v(Dexp), relayout_v(inp["mamba_norm_g"][0][o:o + 1024])], 1)
    lc = inp["lru_conv_w"][0][:, o:o + 1024]
    lcw = np.stack([relayout_v(lc[j]) for j in range(4)] + [relayout_v(inp["lru_conv_b"][0][o:o + 1024])], 1)
    lv = np.stack([relayout_v(inp[k][0][o:o + 1024]) for k in ("lru_ba", "lru_bx", "lru_lambda")], 1)
    return dict(xT=np.ascontiguousarray(xb_.T), w_in=relayout_w(wc), cst=consts_np(),
                mcw=np.ascontiguousarray(mcw), mv=mv, mdn=np.ascontiguousarray(mdn), lcw=np.ascontiguousarray(lcw),
                lv=np.ascontiguousarray(lv), lwa=np.ascontiguousarray(inp["lru_wa"][0][8 * hh:8 * hh + 8]),
                lwx=np.ascontiguousarray(inp["lru_wx"][0][8 * hh:8 * hh + 8]))


def prep_C_weights(inp, i, w_out):
    cw = inp["ffn_conv_w"][i]
    return dict(w_out=relayout_w(w_out), w_up=relayout_w(inp["ffn_up"][i]), w_down=relayout_w(inp["ffn_down"][i]),
                w_gate=relayout_w(inp["ple_gate_w"][i]), w_ple=relayout_w(inp["ple_proj"][i]),
                vecs=np.ascontiguousarray(np.stack([relayout_v(inp[k][i]) for k in ("ln1_g", "ln1_b", "ln2_g", "ln2_b", "ple_norm_g", "ple_gate_b")], 1)),
                cvw=np.ascontiguousarray(np.stack([relayout_v(v) for v in (cw[0], cw[1], cw[2], inp["ffn_conv_b"][i])], 1)))


GROUPS = [[0, 1], [2, 3], [4, 5], [6, 7]]


def build_all(shapes, T=4096, TH=2048):
    import ml_dtypes
    nc = bass.Bass("TRN2", target_bir_lowering=False)
    D = 2048
    with ExitStack() as outer:
        din = {}
        for name, (shp, dt) in shapes.items():
            bdt = BF16 if dt == ml_dtypes.bfloat16 else F32
            din[name] = Tl(nc.dram_tensor(name, list(shp), bdt, kind="ExternalInput").ap(), name)
        def internal(name, shp, dt):
            return Tl(nc.dram_tensor(name, list(shp), dt, kind="Internal").ap(), name)
        def chunked(name, rows, cols, dt, W):
            return ChunkT([internal(f"{name}_{q}", [rows, W], dt) for q in range(cols // W)], W)
        ysnd0 = chunked("ysnd0", 1024, T, BF16, 1024)
        yrcv0 = chunked("yrcv0", 2048, T, BF16, 1024)
        ysnd1 = chunked("ysnd1", 2048, T, BF16, 512)
        yrcv1 = chunked("yrcv1", 4096, T, BF16, 512)
        NQ = TH // 512
        x1snd = [[internal(f"x1snd_{h}_{q}", [1024, 512], F32) for q in range(NQ)] for h in range(2)]
        x1rcv = [[internal(f"x1rcv_{h}_{q}", [2048, 512], F32) for q in range(NQ)] for h in range(2)]
        x1snd_k = [k for h in range(2) for c in x1snd[h] for k in c.k()]
        x1rcv_k = [k for h in range(2) for c in x1rcv[h] for k in c.k()]
        xo = Tl(nc.dram_tensor("xoT", [D, TH], F32, kind="ExternalOutput").ap(), "xoT")
        TBC = 512
        import os
        PH = os.environ.get("PHASES", "E,C0,O,C1").split(",")

        def gather_chunks(snd, rcv):
            def post(P):
                for s_, r_ in zip(snd, rcv):
                    P.coll("AllGather", s_[:], r_[:], GROUPS, s_.k(), r_.k() + [("collchain", 0)], "dcc")
            return post

        io = {k[2:]: v for k, v in din.items() if k.startswith("e_")}
        io.update(yo=ysnd0, xsrc_k=[], xsrc=lambda t0, hf: din["xT"][hf * 1024:(hf + 1) * 1024, t0:t0 + 256],
                  post=gather_chunks(ysnd0.chunks, yrcv0.chunks))
        if "E" in PH:
            phase_E(nc, outer, io, T)
        nc.all_engine_barrier()
        io = {k[3:]: v for k, v in din.items() if k.startswith("c0_")}
        io.update(yrcv=yrcv0, pT=din["pT0"], msk=din["msk"], xsrc_k=[], xdst_k=x1snd_k,
                  xsrc=lambda blk: [(0, 16, din["xTc"][:, 0:2] if blk < 0 else din["xTc"][:, 2 + blk * TBC:2 + (blk + 1) * TBC])],
                  xdst=lambda blk: [(8 * h, 8 * h + 8, x1snd[h][blk][:, :]) for h in range(2)],
                  post=gather_chunks([c for h in range(2) for c in x1snd[h]], [c for h in range(2) for c in x1rcv[h]]))
        if "C0" in PH:
            phase_C(nc, outer, io, TH // TBC, 2048, TB=TBC, alpha=DN_ALPHA, THALF=TH)
        nc.all_engine_barrier()
        io = {k[2:]: v for k, v in din.items() if k.startswith("o_")}
        io.update(yo=ysnd1, xsrc_k=[],
                  xsrc=lambda t0, hf: x1rcv[hf][(t0 % TH) // 512][(t0 // TH) * 1024:(t0 // TH + 1) * 1024, :],
                  post=gather_chunks(ysnd1.chunks, yrcv1.chunks))
        if "O" in PH:
            phase_O(nc, outer, io, T)
        nc.all_engine_barrier()
        io = {k[3:]: v for k, v in din.items() if k.startswith("c1_")}
        io.update(yrcv=yrcv1, pT=din["pT1"], msk=din["msk"], xsrc_k=[], xdst_k=xo.k(),
                  xsrc=lambda blk: [(8 * h, 8 * h + 8, x1rcv[h][NQ - 1][0:1024, 510:512] if blk < 0 else x1snd[h][blk][:, :]) for h in range(2)],
                  xdst=lambda blk: [(0, 16, xo[:, blk * TBC:(blk + 1) * TBC])])
        if "C1" in PH:
            phase_C(nc, outer, io, TH // TBC, 4096, TB=TBC, alpha=DN_ALPHA, THALF=TH)
    return nc


def kernel(**inp):
    inp = {k: np.asarray(v, dtype=np.float32) for k, v in inp.items()}
    x = inp["x"]
    p = inp["p"]
    B, T, D = x.shape
    TH = T // 2
    cores = list(range(8))
    perm_e = np.concatenate([np.arange(0, 512), np.arange(1024, 1536), np.arange(512, 1024), np.arange(1536, 2048)])
    perm_o = np.concatenate([np.arange(0, 1024), np.arange(2048, 3072), np.arange(1024, 2048), np.arange(3072, 4096)])
    c0 = prep_C_weights(inp, 0, inp["even_w_out"][0][perm_e])
    c1 = prep_C_weights(inp, 1, inp["odd_w_out"][0][perm_o])
    ins = []
    for c in cores:
        b, h = c // 2, c % 2
        m = {}
        e = prep_even(x[b], inp, h)
        m["xT"] = e.pop("xT")
        m.update({"e_" + k: v for k, v in e.items()})
        o = prep_odd(x[b][:8], inp, h)
        o.pop("xT")
        m.update({"o_" + k: v for k, v in o.items()})
        m.update({"c0_" + k: v for k, v in c0.items()})
        m.update({"c1_" + k: v for k, v in c1.items()})
        xTc = np.zeros((D, 2 + TH), np.float32)
        xTc[:, 2:] = x[b, h * TH:(h + 1) * TH].T
        if h > 0:
            xTc[:, :2] = x[b, TH - 2:TH].T
        m["xTc"] = xTc
        m["pT0"] = np.ascontiguousarray(p[0, b, h * TH:(h + 1) * TH].T)
        m["pT1"] = np.ascontiguousarray(p[1, b, h * TH:(h + 1) * TH].T)
        msk = np.zeros((128, 3), np.float32)
        msk[:, 0] = float(h > 0); msk[:, 1] = float(h == 0); msk[:, 2] = float(h == 1)
        m["msk"] = msk
        ins.append(m)
    shapes = {k: (v.shape, v.dtype) for k, v in ins[0].items()}
    nc = build_all(shapes, T=T, TH=TH)
    res = run_bass_kernel_spmd(nc, ins, core_ids=cores)
    out = np.empty_like(x)
    for c in cores:
        out[c // 2, (c % 2) * TH:(c % 2 + 1) * TH] = res.results[c]["xoT"].T
    return out
```

```python
from contextlib import ExitStack

import numpy as np
import concourse.bass as bass
import concourse.mybir as mybir
from concourse.bass_utils import run_bass_kernel_spmd

F32 = mybir.dt.float32
BF16 = mybir.dt.bfloat16
ALU = mybir.AluOpType
AF = mybir.ActivationFunctionType
AX = mybir.AxisListType

ENGS = ("pe", "dve", "act", "pool", "sp")


class Tl:
    def __init__(self, h, name, nsub=1):
        self.h, self.name, self.nsub = h, name, nsub

    def __getitem__(self, idx):
        return self.h[idx]

    def k(self, i=None):
        if i is None:
            return [(self.name, j) for j in range(self.nsub)]
        if isinstance(i, (list, tuple, range)):
            return [(self.name, j) for j in i]
        return [(self.name, i)]


class Prog:
    def __init__(self, nc):
        self.nc = nc
        self.ops = []
        self.stack = None
        self.ntiles = 0
        self.dsems = {}
        self.psum_names = set()
        Prog.ninst = getattr(Prog, "ninst", 0) + 1
        self.pfx = f"g{Prog.ninst}_"

    def sb(self, shape, dt, name=None, nsub=1):
        self.ntiles += 1
        name = self.pfx + (name or f"t{self.ntiles}")
        h = self.stack.enter_context(self.nc.sbuf_tensor(name, list(shape), dt))
        return Tl(h, name, nsub)

    def ps(self, shape, dt, name=None, nsub=1):
        self.ntiles += 1
        name = self.pfx + (name or f"p{self.ntiles}")
        h = self.stack.enter_context(self.nc.psum_tensor(name, list(shape), dt))
        self.psum_names.add(name)
        return Tl(h, name, 1)

    def dram(self, name, shape, dt, kind, nsub=1):
        h = self.nc.dram_tensor(name, list(shape), dt, kind=kind)
        return Tl(h.ap(), name, nsub)

    def op(self, eng, fn, r=(), w=(), acc=False):
        w = list(w) + [k for k in r if k[0] in self.psum_names and k not in w]
        self.ops.append(dict(eng=eng, fn=fn, r=list(r), w=list(w), dma=None, acc=acc))

    def dma(self, eng, out, in_, r=(), w=(), sem="d0", **kw):
        def fn(e):
            return e.dma_start(out=out, in_=in_, **kw)
        self.ops.append(dict(eng=eng, fn=fn, r=list(r), w=list(w), dma=sem, acc=False))

    def coll(self, kind, ins_ap, out_ap, groups, r, w, sem, inc=1):
        def fn(e):
            return e.collective_compute(kind, ALU.bypass, replica_groups=groups, ins=[ins_ap], outs=[out_ap])
        self.ops.append(dict(eng="pool", fn=fn, r=list(r), w=list(w), dma=sem, acc=False, inc=inc))

    def fence(self, tiles):
        sc = self.sb([128, 1], F32, f"fence{self.ntiles}")
        keys = [k for t in tiles for k in t.k()]
        self.op("pool", lambda e: e.memset(sc[:], 0.0), r=keys, w=keys + sc.k())

    def emit(self):
        nc = self.nc
        st = self.stack
        ss = getattr(self, "semstack", None) or st
        Prog.nprog = getattr(Prog, "nprog", 0) + 1
        pfx = f"s{Prog.nprog}_"
        sem = {e: ss.enter_context(nc.semaphore(pfx + e)) for e in ENGS}
        dnames = sorted({o["dma"] for o in self.ops if o["dma"]})
        for d in dnames:
            sem[d] = ss.enter_context(nc.semaphore(pfx + d))
        cnt = {s: 0 for s in sem}
        clock = {e: {} for e in ENGS}
        lastw = {}
        readers = {}
        per_eng = {e: [] for e in ENGS}

        def merge(a, b):
            for s, c in b.items():
                if a.get(s, 0) < c:
                    a[s] = c

        for o in self.ops:
            e = o["eng"]
            ck = clock[e]
            need = []
            for key in o["r"]:
                ev = lastw.get(key)
                if ev is not None:
                    need.append(ev)
            for key in o["w"]:
                ev = lastw.get(key)
                if ev is not None and not ((o["acc"] or e == "pe") and ev[3] == e):
                    need.append(ev)
                for ev in readers.get(key, ()):
                    if e == "pe" and ev[3] == "pe":
                        continue
                    need.append(ev)
            waits = {}
            for (s, c, evck, _) in need:
                if ck.get(s, 0) >= c:
                    continue
                if waits.get(s, 0) < c:
                    waits[s] = c
            for (s, c, evck, _) in need:
                if s in waits and waits[s] >= c and ck.get(s, 0) < c:
                    merge(ck, evck)
            for s, c in waits.items():
                if ck.get(s, 0) < c:
                    ck[s] = c
            if o["dma"]:
                s = o["dma"]
                cnt[s] += o.get("inc", 16)
                evs = s
            else:
                cnt[e] += 1
                evs = e
            evck = dict(ck)
            evck[evs] = cnt[evs]
            ev = (evs, cnt[evs], evck, e)
            for key in o["w"]:
                lastw[key] = ev
                readers[key] = []
            for key in o["r"]:
                readers.setdefault(key, []).append(ev)
            per_eng[e].append((o, sorted(waits.items()), evs))
        self.final = {s: c for s, c in cnt.items() if c > 0}
        self.sem = sem
        self.per_eng = per_eng
        nE = {e: len(v) for e, v in per_eng.items()}
        self.stats = nE

        block = st.enter_context(nc.Block())

        def run(engname, eng):
            for (o, waits, evs) in per_eng[engname]:
                for s, c in waits:
                    eng.wait_ge(sem[s], c)
                ins = o["fn"](eng)
                ins.then_inc(sem[evs], o.get("inc", 16) if o["dma"] else 1)
            if engname == "sp":
                for s, c in self.final.items():
                    eng.wait_ge(sem[s], c)

        @block.tensor
        def _(eng):
            run("pe", eng)

        @block.vector
        def _(eng):
            run("dve", eng)

        @block.scalar
        def _(eng):
            run("act", eng)

        @block.gpsimd
        def _(eng):
            run("pool", eng)

        @block.sync
        def _(eng):
            run("sp", eng)


def _tt(P, eng, out, in0, in1, op, r, w):
    P.op(eng, lambda e: e.tensor_tensor(out=out, in0=in0, in1=in1, op=op), r, w)


def _ts(P, eng, out, in0, s1, s2, op0, op1, r, w):
    if op1 is None:
        P.op(eng, lambda e: e.tensor_scalar(out=out, in0=in0, scalar1=s1, scalar2=None, op0=op0), r, w)
    else:
        P.op(eng, lambda e: e.tensor_scalar(out=out, in0=in0, scalar1=s1, scalar2=s2, op0=op0, op1=op1), r, w)


def _stt(P, eng, out, in0, sc, in1, op0, op1, r, w):
    P.op(eng, lambda e: e.scalar_tensor_tensor(out=out, in0=in0, scalar=sc, in1=in1, op0=op0, op1=op1), r, w)


def _act(P, out, in_, func, r, w, bias=None, scale=None):
    kw = {}
    if bias is not None:
        kw["bias"] = bias
    if scale is not None:
        kw["scale"] = scale
    P.op("act", lambda e: e.activation(out=out, in_=in_, func=func, **kw), r, w)


def _cp(P, eng, out, in_, r, w):
    if eng == "act":
        P.op("act", lambda e: e.copy(out=out, in_=in_), r, w)
    else:
        P.op(eng, lambda e: e.tensor_copy(out=out, in_=in_), r, w)


def _mm(P, out, lhsT, rhs, start, stop, r, w):
    P.op("pe", lambda e: e.matmul(out, lhsT=lhsT, rhs=rhs, start=start, stop=stop), r, w, acc=not start)


def _tr(P, out, in_, ident, r, w):
    P.op("pe", lambda e: e.transpose(out, in_, ident), r, w)


Prog.tt, Prog.ts, Prog.stt, Prog.act, Prog.cp, Prog.mm, Prog.tr = _tt, _ts, _stt, _act, _cp, _mm, _tr


class TlV(Tl):
    def __init__(self, ap, parent):
        self.h, self.name, self.nsub = ap, parent.name, parent.nsub

    def k(self, i=None):
        return [(self.name, j) for j in range(self.nsub)]


class ChunkT:
    def __init__(self, chunks, W):
        self.chunks, self.W = chunks, W

    def ap(self, rows, c0, n):
        q = c0 // self.W
        assert (c0 + n - 1) // self.W == q, (c0, n, self.W)
        return self.chunks[q][rows, (c0 % self.W):(c0 % self.W) + n]

    def k(self, i=None):
        return [k for c in self.chunks for k in c.k()]

    def kq(self, c0):
        return self.chunks[c0 // self.W].k()


LN_EPS = 1e-5
RMS_EPS = 1e-6


KS = 8


def relayout_w(w):
    K, N = w.shape
    return np.ascontiguousarray(w.reshape(K // 128, 128, N // 128, 128).transpose(2, 1, 0, 3))


def relayout_v(v):
    return np.ascontiguousarray(v.reshape(-1, 128).T)


class WStream:
    def __init__(self, P, kcmax=KS, nst=6, nbf=4, cast=("pool",), dmaq=("sp",)):
        self.P = P
        self.cast = cast
        self.dmaq = dmaq
        self.st = [P.sb([128, KS, 128], F32, f"wst{i}") for i in range(nst)]
        self.bf = [P.sb([128, KS, 128], BF16, f"wbf{i}") for i in range(nbf)]
        self.i = 0

    def get(self, wd, ct, k0, k1, cw=128):
        P = self.P
        i = self.i
        self.i += 1
        st, bf = self.st[i % len(self.st)], self.bf[i % len(self.bf)]
        n = k1 - k0
        P.dma(self.dmaq[i % len(self.dmaq)], st[:, :n, :cw], wd[ct][:, k0:k1, :], w=st.k(), sem=f"dw{i % len(self.st)}")
        P.cp(self.cast[i % len(self.cast)], bf[:, :n, :cw], st[:, :n, :cw], r=st.k(), w=bf.k())
        return bf


def proj(P, ws, wd, order, KC, act, N, pss, consume, cw=128):
    for j, ct in enumerate(order):
        ps = pss[j % len(pss)]
        for k0 in range(0, KC, KS):
            k1 = min(KC, k0 + KS)
            bf = ws.get(wd, ct, k0, k1, cw)
            for kk in range(k0, k1):
                P.mm(ps[:cw, :N], bf[:, kk - k0, :cw], act[:, kk, :N], kk == 0, kk == KC - 1, r=bf.k() + act.k(), w=ps.k())
        consume(j, ct, ps)


def phase_C(nc, outer, io, NB, CO, TB=512, D=2048, FF=5632, PLE=256, alpha=1.0, THALF=2048):
    P = Prog(nc)
    P.semstack = outer
    KY, KD, KF, KP = CO // 128, D // 128, FF // 128, PLE // 128
    NFT = 2 * KF
    NTOK = NB * TB
    with ExitStack() as st:
        P.stack = st
        yrcv, pT, msk = io["yrcv"], io["pT"], io["msk"]
        w_out, w_up, w_down, w_gate, w_ple, vecs, cvw = (io[k] for k in ("w_out", "w_up", "w_down", "w_gate", "w_ple", "vecs", "cvw"))
        ybB = P.sb([128, 4, TB], BF16, "ybB")

        vs = P.sb([128, 6, KD], F32, "vs")
        cw = P.sb([128, 4, NFT], F32, "cw")
        hms = P.sb([128, 3], F32, "hms")
        ones = P.sb([128, 128], F32, "ones")
        eT = P.sb([128, KD, TB], F32, "eT")
        assert KY <= 2 * KD
        yb = Tl(eT.h[:].rearrange("p k t -> p (k t)").bitcast(BF16)[:, :KY * TB].rearrange("p (k t) -> p k t", k=KY), eT.name)
        sx = P.sb([128, KD, TB], F32, "sx")
        hb = P.sb([128, KD, TB], BF16, "hb")
        actT = P.sb([128, KF, TB], BF16, "actT")
        pst = P.sb([128, KP, TB], F32, "pst")
        pb = P.sb([128, KP, TB], BF16, "pb")
        halo = P.sb([128, NFT, 2], F32, "halo", nsub=NFT)
        ub = [P.sb([128, TB + 2], F32, f"ub{i}") for i in range(3)]
        cv = [P.sb([128, TB], F32, f"cv{i}") for i in range(3)]
        sg = [P.sb([128, TB], F32, f"sg{i}") for i in range(2)]
        sq = [P.sb([128, TB], F32, f"sq{i}") for i in range(2)]
        mean = P.sb([128, TB], F32, "mean")
        rstd = P.sb([128, TB], F32, "rstd")
        tmp = [P.sb([128, TB], F32, f"tmp{i}") for i in range(2)]
        pss = [P.ps([128, 512], F32, f"ps{i}") for i in range(4)]
        pm = P.ps([128, 512], F32, "pm")
        pv = P.ps([128, 512], F32, "pv")
        ws = WStream(P, cast=("act", "act", "dve"))

        P.dma("act", vs[:], vecs[:], w=vs.k(), sem="dc")
        P.dma("act", cw[:], cvw[:], w=cw.k(), sem="dc")
        P.dma("act", hms[:], msk[:], w=hms.k(), sem="dc")
        P.fence([vs, cw, hms])
        P.op("pool", lambda e: e.memset(ones[:], 1.0 / D), w=ones.k())

        def layer_norm(N, gi, bi):
            for k in range(KD):
                s = sq[k % 2]
                P.tt("pool", s[:, :N], sx[:, k, :N], sx[:, k, :N], ALU.mult, r=sx.k(), w=s.k())
                P.mm(pm[:, :N], ones[:], sx[:, k, :N], k == 0, k == KD - 1, r=ones.k() + sx.k(), w=pm.k())
                P.mm(pv[:, :N], ones[:], s[:, :N], k == 0, k == KD - 1, r=ones.k() + s.k(), w=pv.k())
            P.cp("act", mean[:, :N], pm[:, :N], r=pm.k(), w=mean.k())
            t = tmp[0]
            P.tt("dve", t[:, :N], mean[:, :N], mean[:, :N], ALU.mult, r=mean.k(), w=t.k())
            P.tt("dve", t[:, :N], pv[:, :N], t[:, :N], ALU.subtract, r=pv.k() + t.k(), w=t.k())
            P.act(rstd[:, :N], t[:, :N], AF.Sqrt, r=t.k(), w=rstd.k(), bias=LN_EPS)
            P.op("dve", lambda e: e.reciprocal(out=rstd[:, :N], in_=rstd[:, :N]), r=rstd.k(), w=rstd.k())
            for k in range(KD):
                t = tmp[k % 2]
                P.tt("dve", t[:, :N], sx[:, k, :N], mean[:, :N], ALU.subtract, r=sx.k() + mean.k(), w=t.k())
                P.tt("dve", t[:, :N], t[:, :N], rstd[:, :N], ALU.mult, r=t.k() + rstd.k(), w=t.k())
                P.ts("dve", sx[:, k, :N], t[:, :N], vs[:, gi, k:k + 1], vs[:, bi, k:k + 1], ALU.mult, ALU.add,
                     r=t.k() + vs.k(), w=sx.k())
                P.cp("act", hb[:, k, :N], sx[:, k, :N], r=sx.k(), w=hb.k())

        for blk in range(-1, NB):
            N = 2 if blk < 0 else TB
            t0 = 0 if blk < 0 else 2 + blk * TB
            cA = 0 if blk < 0 else blk * TB
            cB = THALF - 2 if blk < 0 else THALF + blk * TB
            P.dma("act", yb[:, :, :N], yrcv.ap(slice(None), cA, N).rearrange("(k p) t -> p k t", p=128), r=yrcv.k(), w=yb.k(), sem="dy")
            for k0 in range(0, KY, 4):
                P.dma("act", ybB[:, :, :N], yrcv.ap(slice(k0 * 128, (k0 + 4) * 128), cB, N).rearrange("(k p) t -> p k t", p=128),
                      r=yrcv.k(), w=ybB.k(), sem="dyb")
                P.ts("dve", yb[:, k0:k0 + 4, :N], yb[:, k0:k0 + 4, :N], hms[:, 1:2], None, ALU.mult, None, r=yb.k() + hms.k(), w=yb.k())
                P.stt("dve", yb[:, k0:k0 + 4, :N], ybB[:, :, :N], hms[:, 2:3], yb[:, k0:k0 + 4, :N], ALU.mult, ALU.add,
                      r=ybB.k() + hms.k() + yb.k(), w=yb.k())
            for (k0_, k1_, ap_) in io["xsrc"](blk):
                P.dma("act", sx[:, k0_:k1_, :N], ap_.rearrange("(k p) t -> p k t", p=128), r=io["xsrc_k"], w=sx.k(), sem="dx")

            def c_out(j, ct, ps, N=N):
                P.stt("dve", sx[:, ct, :N], sx[:, ct, :N], float(alpha), ps[:, :N], ALU.mult, ALU.add,
                      r=sx.k() + ps.k(), w=sx.k())
            proj(P, ws, w_out, range(KD), KY, yb, N, pss, c_out)
            layer_norm(N, 0, 1)

            order = []
            for c in range(KF):
                order += [c, KF + c]

            def c_up(j, ct, ps, N=N, blk=blk):
                if blk < 0:
                    P.ts("dve", halo[:, ct, :], ps[:, :2], hms[:, 0:1], None, ALU.mult, None,
                         r=ps.k() + hms.k(), w=halo.k(ct))
                    return
                u = ub[j % 3]
                c_ = cv[j % 3]
                P.cp("act", u[:, 2:2 + N], ps[:, :N], r=ps.k(), w=u.k())
                P.cp("pool", u[:, 0:2], halo[:, ct, :], r=halo.k(ct), w=u.k())
                P.cp("act", halo[:, ct, :], u[:, N:N + 2], r=u.k(), w=halo.k(ct))
                P.ts("dve", c_[:, :N], u[:, 2:2 + N], cw[:, 2, ct:ct + 1], cw[:, 3, ct:ct + 1], ALU.mult, ALU.add,
                     r=u.k() + cw.k(), w=c_.k())
                P.stt("dve", c_[:, :N], u[:, 1:1 + N], cw[:, 1, ct:ct + 1], c_[:, :N], ALU.mult, ALU.add,
                      r=u.k() + cw.k() + c_.k(), w=c_.k())
                P.stt("dve", c_[:, :N], u[:, 0:N], cw[:, 0, ct:ct + 1], c_[:, :N], ALU.mult, ALU.add,
                      r=u.k() + cw.k() + c_.k(), w=c_.k())
                if ct < KF:
                    s = sg[ct % 2]
                    P.act(s[:, :N], c_[:, :N], AF.Silu, r=c_.k(), w=s.k())
                else:
                    c = ct - KF
                    s = sg[c % 2]
                    P.tt("dve", actT[:, c, :N], s[:, :N], c_[:, :N], ALU.mult, r=s.k() + c_.k(), w=actT.k())
            proj(P, ws, w_up, order, KD, hb, N, pss, c_up)
            if blk < 0:
                continue

            def c_down(j, ct, ps, N=N):
                P.stt("dve", sx[:, ct, :N], sx[:, ct, :N], float(alpha), ps[:, :N], ALU.mult, ALU.add,
                      r=sx.k() + ps.k(), w=sx.k())
            proj(P, ws, w_down, range(KD), KF, actT, N, pss, c_down)
            layer_norm(N, 2, 3)

            P.dma("act", pst[:], pT[:, blk * TB:(blk + 1) * TB].rearrange("(k p) t -> p k t", p=128), r=[], w=pst.k(), sem="dp")
            P.cp("pool", pb[:], pst[:], r=pst.k(), w=pb.k())

            def c_ple(j, ct, ps, N=N):
                P.cp("act", eT[:, ct, :N], ps[:, :N], r=ps.k(), w=eT.k())
                s = sq[ct % 2]
                P.tt("pool", s[:, :N], eT[:, ct, :N], eT[:, ct, :N], ALU.mult, r=eT.k(), w=s.k())
                P.mm(pv[:, :N], ones[:], s[:, :N], ct == 0, ct == KD - 1, r=ones.k() + s.k(), w=pv.k())
            proj(P, ws, w_ple, range(KD), KP, pb, N, pss, c_ple)
            P.act(rstd[:, :N], pv[:, :N], AF.Sqrt, r=pv.k(), w=rstd.k(), bias=RMS_EPS)
            P.op("dve", lambda e, N=N: e.reciprocal(out=rstd[:, :N], in_=rstd[:, :N]), r=rstd.k(), w=rstd.k())

            def c_gate(j, ct, ps, N=N, blk=blk):
                g = sg[ct % 2]
                P.act(g[:, :N], ps[:, :N], AF.Sigmoid, r=ps.k() + vs.k(), w=g.k(), bias=vs[:, 5, ct:ct + 1])
                t = tmp[ct % 2]
                P.stt("dve", t[:, :N], eT[:, ct, :N], vs[:, 4, ct:ct + 1], rstd[:, :N], ALU.mult, ALU.mult,
                      r=eT.k() + vs.k() + rstd.k(), w=t.k())
                P.tt("dve", t[:, :N], t[:, :N], g[:, :N], ALU.mult, r=t.k() + g.k(), w=t.k())
                P.tt("dve", eT[:, ct, :N], t[:, :N], sx[:, ct, :N], ALU.add, r=t.k() + sx.k(), w=eT.k())
            proj(P, ws, w_gate, range(KD), KD, hb, N, pss, c_gate)
            for (k0_, k1_, ap_) in io["xdst"](blk):
                P.dma("act", ap_.rearrange("(k p) t -> p k t", p=128), eT[:, k0_:k1_, :], r=eT.k(),
                      w=(io["xdst_kf"](blk) if "xdst_kf" in io else io["xdst_k"]), sem="do")
            if io.get("post_blk"):
                io["post_blk"](P, blk)
        if io.get("post"):
            io["post"](P)
        P.emit()
    return P


NEG = -30000.0


def consts_np():
    i = np.arange(128)
    ident = np.eye(128, dtype=np.float32)
    tri = (i[:, None] <= i[None, :]).astype(np.float32)
    maskneg = np.where(i[None, :] >= i[:, None], 0.0, NEG).astype(np.float32)
    return np.ascontiguousarray(np.stack([ident, tri, maskneg], 1))


def phase_O(nc, outer, io, T, TB=512, D=2048, L=128):
    P = Prog(nc)
    P.semstack = outer
    KD = D // 128
    NB = T // TB
    NCH = TB // L
    NH = 16
    with ExitStack() as st:
        P.stack = st
        w_in, cst, mcw, mv, mdn, lcw, lv, lwa, lwx, yo = (io[k] for k in ("w_in", "cst", "mcw", "mv", "mdn", "lcw", "lv", "lwa", "lwx", "yo"))

        cs = P.sb([128, 3, 128], F32, "cs")
        ident, tri, maskneg = cs[:, 0, :], cs[:, 1, :], cs[:, 2, :]
        ones = P.sb([128, 128], F32, "ones")
        onesg = P.sb([128, 128], F32, "onesg")
        mcs = P.sb([128, 5, 12], F32, "mcs")
        mvs = P.sb([128, 4], F32, "mvs")
        negA = P.sb([128, 1], F32, "negA")
        mds = P.sb([128, 2, 8], F32, "mds")
        lcs = P.sb([128, 5, 8], F32, "lcs")
        lvs = P.sb([128, 3, 8], F32, "lvs")
        c8 = P.sb([128, 8], F32, "c8")
        was = P.sb([128, 8, 128], F32, "was")
        wxs = P.sb([128, 8, 128], F32, "wxs")
        wab = P.sb([128, 8, 128], BF16, "wab")
        wxb = P.sb([128, 8, 128], BF16, "wxb")
        for (t_, d_) in ((cs, cst), (mcs, mcw), (mvs, mv), (mds, mdn), (lcs, lcw), (lvs, lv)):
            P.dma("act", t_[:], d_[:], w=t_.k(), sem="dc")
        P.dma("act", was[:], lwa[:].rearrange("n d e -> d n e"), w=was.k(), sem="dc")
        P.dma("act", wxs[:], lwx[:].rearrange("n d e -> d n e"), w=wxs.k(), sem="dc")
        P.fence([cs, mcs, mvs, mds, lcs, lvs, was, wxs])
        P.cp("pool", wab[:], was[:], r=was.k(), w=wab.k())
        P.cp("pool", wxb[:], wxs[:], r=wxs.k(), w=wxb.k())
        P.op("pool", lambda e: e.memset(ones[:], 1.0), w=ones.k())
        P.op("pool", lambda e: e.memset(onesg[:], 1.0 / 512.0), w=onesg.k())
        P.act(negA[:], mvs[:, 1:2], AF.Exp, r=mvs.k(), w=negA.k())
        P.ts("dve", negA[:], negA[:], -1.0, None, ALU.mult, None, r=negA.k(), w=negA.k())
        P.act(c8[:], lvs[:, 2, :], AF.Exp, r=lvs.k(), w=c8.k(), scale=-1.0)
        P.act(c8[:], c8[:], AF.Ln, r=c8.k(), w=c8.k(), bias=1.0)
        P.ts("dve", c8[:], c8[:], -8.0, None, ALU.mult, None, r=c8.k(), w=c8.k())

        xst = P.sb([128, KD // 2, TB], F32, "xst")
        xb = P.sb([128, KD, TB], BF16, "xb")
        ws = WStream(P)
        pss = [P.ps([128, 512], F32, f"ps{i}") for i in range(2)]
        pY = P.ps([128, 1024], F32, "pY")
        pO = P.ps([128, 1024], F32, "pO")
        pT = P.ps([128, 512], F32, "pT")
        pB = P.ps([128, 512], F32, "pB")
        pbs = [pB, pss[0], pss[1]]

        craw = P.sb([128, 12, TB + 3], F32, "craw", nsub=12)
        cfm = P.sb([128, 12, TB], F32, "cfm", nsub=12)
        bcb = P.sb([128, 4, TB], BF16, "bcb", nsub=4)
        P.op("pool", lambda e: e.memset(craw[:], 0.0), w=craw.k())
        dtf = P.sb([128, TB], F32, "dtf")
        Af = P.sb([128, TB], F32, "Af")
        S = P.sb([128, NH, 64], F32, "S")
        Sb = P.sb([128, NH, 64], BF16, "Sb")
        P.op("pool", lambda e: e.memset(S[:], 0.0), w=S.k())
        P.op("pool", lambda e: e.memset(Sb[:], 0.0), w=Sb.k())
        yfm = P.sb([128, 8, TB], F32, "yfm", nsub=8)
        xt = P.sb([128, NH, 64], F32, "xt")
        Xd = P.sb([128, NH, 64], BF16, "Xd")
        Xdec = P.sb([128, NH, 64], BF16, "Xdec")
        Bt = P.sb([128, 2, 128], BF16, "Bt")
        tm = P.sb([128, 3, NH], F32, "tm")
        cumT = P.sb([128, NH], F32, "cumT")
        ncum = P.sb([128, NH], F32, "ncum")
        ecum = P.sb([128, NH], F32, "ecum")
        decs = P.sb([128, NH], F32, "decs")
        etot = P.sb([128, NH], F32, "etot")
        Abc = P.sb([128, NH, 128], F32, "Abc")
        Gt = P.sb([128, 2, 128], F32, "Gt")
        wt = [P.sb([128, 128], F32, f"wt{i}") for i in range(2)]
        we = [P.sb([128, 128], F32, f"we{i}") for i in range(2)]
        Stt = [P.sb([128, 128], BF16, f"Stt{i}") for i in range(3)]
        Ytm = P.sb([128, NH, 64], F32, "Ytm")
        Yof = P.sb([128, NH, 64], F32, "Yof")
        ft = [P.sb([128, TB + 3], F32, f"ft{i}") for i in range(6)]
        fb = [P.sb([128, TB], BF16, f"fb{i}") for i in range(2)]
        ob = [P.sb([128, TB], BF16, f"ob{i}") for i in range(2)]
        lhalo = P.sb([128, 8, 3], F32, "lhalo", nsub=8)
        hst = P.sb([128, 8], F32, "hst", nsub=8)
        msq = P.sb([128, TB], F32, "msq")
        P.op("pool", lambda e: e.memset(lhalo[:], 0.0), w=lhalo.k())
        P.op("pool", lambda e: e.memset(hst[:], 0.0), w=hst.k())
        oi = [0]

        def conv4(out, src, cwt, ci, N, eng="dve"):
            P.ts(eng, out, src[:, 3:3 + N], cwt[:, 3, ci:ci + 1], cwt[:, 4, ci:ci + 1], ALU.mult, ALU.add,
                 r=src_k[0] + cwt_k[0], w=out_k[0])
            for j in range(3):
                P.stt(eng, out, src[:, j:j + N], cwt[:, j, ci:ci + 1], out, ALU.mult, ALU.add,
                      r=src_k[0] + cwt_k[0] + out_k[0], w=out_k[0])
        src_k, cwt_k, out_k = [None], [None], [None]

        for blk in range(NB):
            N = TB
            t0 = blk * TB
            for hf in range(2):
                P.dma("act", xst[:], io["xsrc"](t0, hf).rearrange("(k p) t -> p k t", p=128), r=io["xsrc_k"], w=xst.k(), sem="dx")
                P.cp("dve" if hf else "pool", xb[:, hf * (KD // 2):(hf + 1) * (KD // 2), :], xst[:], r=xst.k(), w=xb.k())

            def c_ssd(j, ct, ps):
                if ct < 12:
                    P.cp("act", craw[:, ct, 3:3 + N], ps[:, :N], r=ps.k(), w=craw.k(ct))
                    src_k[0], cwt_k[0], out_k[0] = craw.k(ct), mcs.k(), cfm.k(ct)
                    conv4(cfm[:, ct, :], craw[:, ct, :], mcs, ct, N)
                    P.cp("pool", craw[:, ct, 0:3], craw[:, ct, N:N + 3], r=craw.k(ct), w=craw.k(ct))
                    P.act(cfm[:, ct, :], cfm[:, ct, :], AF.Silu, r=cfm.k(ct), w=cfm.k(ct))
                    if ct >= 8:
                        P.cp("pool", bcb[:, ct - 8, :], cfm[:, ct, :], r=cfm.k(ct), w=bcb.k(ct - 8))
                else:
                    P.act(dtf[:], ps[:, :N], AF.Exp, r=ps.k() + mvs.k(), w=dtf.k(), bias=mvs[:, 0:1])
                    P.act(dtf[:], dtf[:], AF.Ln, r=dtf.k(), w=dtf.k(), bias=1.0)
                    P.ts("dve", Af[:], dtf[:], negA[:, 0:1], None, ALU.mult, None, r=dtf.k() + negA.k(), w=Af.k())
            proj(P, ws, w_in, range(13), KD, xb, N, pss, c_ssd)

            for ch in range(NCH):
                c0 = ch * L
                for half in range(2):
                    for q in range(4):
                        i = half * 4 + q
                        P.tr(pT[:, q * 128:(q + 1) * 128], cfm[:, i, c0:c0 + L], ident, r=cfm.k(i) + cs.k(), w=pT.k())
                    P.cp("act", xt[:, half * 8:(half + 1) * 8, :].rearrange("p h d -> p (h d)"), pT[:, :], r=pT.k(), w=xt.k())
                for g in range(2):
                    P.tr(pT[:, g * 128:(g + 1) * 128], cfm[:, 8 + g, c0:c0 + L], ident, r=cfm.k(8 + g) + cs.k(), w=pT.k())
                P.tr(pT[:, 256:384], dtf[:, c0:c0 + L], ident, r=dtf.k() + cs.k(), w=pT.k())
                P.tr(pT[:, 384:512], Af[:, c0:c0 + L], ident, r=Af.k() + cs.k(), w=pT.k())
                P.cp("act", Bt[:].rearrange("p g n -> p (g n)"), pT[:, 0:256], r=pT.k(), w=Bt.k())
                P.cp("dve", tm[:, 0, :], pT[:, 256:256 + NH], r=pT.k(), w=tm.k())
                P.cp("dve", tm[:, 1, :], pT[:, 384:384 + NH], r=pT.k(), w=tm.k())
                P.mm(pT[:, 0:NH], tri, tm[:, 1, :], True, True, r=cs.k() + tm.k(), w=pT.k())
                P.mm(pT[:, 32:32 + NH], ones[:], tm[:, 1, :], True, True, r=ones.k() + tm.k(), w=pT.k())
                P.cp("dve", cumT[:], pT[:, 0:NH], r=pT.k(), w=cumT.k())
                P.ts("dve", ncum[:], pT[:, 0:NH], -1.0, None, ALU.mult, None, r=pT.k(), w=ncum.k())
                P.act(ecum[:], pT[:, 0:NH], AF.Exp, r=pT.k(), w=ecum.k())
                P.act(etot[:], pT[:, 32:32 + NH], AF.Exp, r=pT.k(), w=etot.k())
                P.tt("dve", decs[:], pT[:, 32:32 + NH], cumT[:], ALU.subtract, r=pT.k() + cumT.k(), w=decs.k())
                P.act(decs[:], decs[:], AF.Exp, r=decs.k(), w=decs.k())
                P.tt("dve", Xd[:], xt[:], tm[:, 0, :].unsqueeze(2).to_broadcast([128, NH, 64]), ALU.mult,
                     r=xt.k() + tm.k(), w=Xd.k())
                P.tt("pool", Xdec[:], Xd[:], decs[:].unsqueeze(2).to_broadcast([128, NH, 64]), ALU.mult,
                     r=Xd.k() + decs.k(), w=Xdec.k())
                P.cp("pool", Abc[:], tm[:, 1, :].unsqueeze(2).to_broadcast([128, NH, 128]), r=tm.k(), w=Abc.k())
                for g in range(2):
                    P.mm(pT[:, 128 + g * 128:256 + g * 128], bcb[:, g, c0:c0 + L], bcb[:, 2 + g, c0:c0 + L], True, True,
                         r=bcb.k(g) + bcb.k(2 + g), w=pT.k())
                P.cp("act", Gt[:].rearrange("p g n -> p (g n)"), pT[:, 128:384], r=pT.k(), w=Gt.k())
                for h in range(NH):
                    g = h // 8
                    P.mm(pO[:, h * 64:(h + 1) * 64], bcb[:, 2 + g, c0:c0 + L], Sb[:, h, :], True, True,
                         r=bcb.k(2 + g) + Sb.k(), w=pO.k())
                P.tt("dve", Yof[:], pO[:].rearrange("p (h d) -> p h d", h=NH), ecum[:].unsqueeze(2).to_broadcast([128, NH, 64]),
                     ALU.mult, r=pO.k() + ecum.k(), w=Yof.k())
                for h in range(NH):
                    g = h // 8
                    pb_ = pbs[h % 3]
                    P.mm(pb_[:, 0:128], Abc[:, h, :], tri, True, True, r=Abc.k() + cs.k(), w=pb_.k())
                    w_ = wt[h % 2]
                    e_ = we[h % 2]
                    s_ = Stt[h % 3]
                    P.tt("dve", w_[:], pb_[:, 0:128], maskneg, ALU.add, r=pb_.k() + cs.k(), w=w_.k())
                    P.act(e_[:], w_[:], AF.Exp, r=w_.k() + ncum.k(), w=e_.k(), bias=ncum[:, h:h + 1])
                    P.tt("pool", s_[:], e_[:], Gt[:, g, :], ALU.mult, r=e_.k() + Gt.k(), w=s_.k())
                    P.mm(pY[:, h * 64:(h + 1) * 64], s_[:], Xd[:, h, :], True, True, r=s_.k() + Xd.k(), w=pY.k())
                P.tt("dve", Ytm[:].rearrange("p h d -> p (h d)"), pY[:], Yof[:].rearrange("p h d -> p (h d)"), ALU.add,
                     r=pY.k() + Yof.k(), w=Ytm.k())
                for h in range(NH):
                    g = h // 8
                    P.mm(pO[:, h * 64:(h + 1) * 64], Bt[:, g, :], Xdec[:, h, :], True, True, r=Bt.k() + Xdec.k(), w=pO.k())
                P.tt("dve", S[:], S[:], etot[:].unsqueeze(2).to_broadcast([128, NH, 64]), ALU.mult, r=S.k() + etot.k(), w=S.k())
                P.tt("dve", S[:].rearrange("p h d -> p (h d)"), S[:].rearrange("p h d -> p (h d)"), pO[:], ALU.add,
                     r=S.k() + pO.k(), w=S.k())
                P.cp("act", Sb[:], S[:], r=S.k(), w=Sb.k())
                for half in range(2):
                    for q in range(4):
                        i = half * 4 + q
                        P.tr(pT[:, q * 128:(q + 1) * 128], Ytm[:, 2 * i:2 * i + 2, :].rearrange("p h d -> p (h d)"), ident,
                             r=Ytm.k() + cs.k(), w=pT.k())
                    for q in range(4):
                        i = half * 4 + q
                        P.cp("act" if q % 2 else "dve", yfm[:, i, c0:c0 + L], pT[:, q * 128:(q + 1) * 128], r=pT.k(), w=yfm.k(i))

            def c_z(j, ct, ps):
                i = ct - 13
                zs = ft[0]
                P.act(zs[:, :N], ps[:, :N], AF.Silu, r=ps.k(), w=zs.k())
                P.stt("dve", yfm[:, i, :], cfm[:, i, :], mds[:, 0, i:i + 1], yfm[:, i, :], ALU.mult, ALU.add,
                      r=cfm.k(i) + mds.k() + yfm.k(i), w=yfm.k(i))
                P.tt("dve", yfm[:, i, :], yfm[:, i, :], zs[:, :N], ALU.mult, r=yfm.k(i) + zs.k(), w=yfm.k(i))
                sq_ = ft[1 + (i % 2)]
                P.tt("pool", sq_[:, :N], yfm[:, i, :], yfm[:, i, :], ALU.mult, r=yfm.k(i), w=sq_.k())
                P.mm(pT[:, :N], onesg[:], sq_[:, :N], i % 4 == 0, i % 4 == 3, r=onesg.k() + sq_.k(), w=pT.k())
                if i % 4 == 3:
                    P.act(msq[:], pT[:, :N], AF.Sqrt, r=pT.k(), w=msq.k(), bias=RMS_EPS)
                    P.op("dve", lambda e: e.reciprocal(out=msq[:], in_=msq[:]), r=msq.k(), w=msq.k())
                    for i2 in range(i - 3, i + 1):
                        o_ = ob[oi[0] % 2]
                        oi[0] += 1
                        P.stt("dve", o_[:], yfm[:, i2, :], mds[:, 1, i2:i2 + 1], msq[:], ALU.mult, ALU.mult,
                              r=yfm.k(i2) + mds.k() + msq.k(), w=o_.k())
                        P.dma("act", yo.ap(slice(i2 * 128, (i2 + 1) * 128), t0, N), o_[:], r=o_.k(), w=yo.kq(t0), sem=f"do{oi[0] % 2}")
            proj(P, ws, w_in, range(13, 21), KD, xb, N, pss, c_z)

            order = []
            for n in range(8):
                order += [29 + n, 21 + n]
            xc, xcb, gl = ft[3], fb[0], ft[5]

            def c_lru(j, ct, ps):
                if ct >= 29:
                    n = ct - 29
                    u = ft[2]
                    P.cp("act", u[:, 3:3 + N], ps[:, :N], r=ps.k(), w=u.k())
                    P.cp("pool", u[:, 0:3], lhalo[:, n, :], r=lhalo.k(n), w=u.k())
                    P.cp("pool", lhalo[:, n, :], u[:, N:N + 3], r=u.k(), w=lhalo.k(n))
                    src_k[0], cwt_k[0], out_k[0] = u.k(), lcs.k(), xc.k()
                    conv4(xc[:, :N], u, lcs, n, N)
                    P.cp("act", xcb[:], xc[:, :N], r=xc.k(), w=xcb.k())
                    P.mm(pY[:, :N], wab[:, n, :], xcb[:], True, True, r=wab.k() + xcb.k(), w=pY.k())
                    P.mm(pO[:, :N], wxb[:, n, :], xcb[:], True, True, r=wxb.k() + xcb.k(), w=pO.k())
                    r_, i_ = ft[0], ft[1]
                    P.act(r_[:, :N], pY[:, :N], AF.Sigmoid, r=pY.k() + lvs.k(), w=r_.k(), bias=lvs[:, 0, n:n + 1])
                    P.act(i_[:, :N], pO[:, :N], AF.Sigmoid, r=pO.k() + lvs.k(), w=i_.k(), bias=lvs[:, 1, n:n + 1])
                    a_ = ft[4]
                    P.act(a_[:, :N], r_[:, :N], AF.Exp, r=r_.k() + c8.k(), w=a_.k(), scale=c8[:, n:n + 1])
                    P.tt("pool", r_[:, :N], a_[:, :N], a_[:, :N], ALU.mult, r=a_.k(), w=r_.k())
                    P.ts("dve", r_[:, :N], r_[:, :N], -1.0, 1.0, ALU.mult, ALU.add, r=r_.k(), w=r_.k())
                    P.act(r_[:, :N], r_[:, :N], AF.Sqrt, r=r_.k(), w=r_.k())
                    P.tt("dve", i_[:, :N], i_[:, :N], xc[:, :N], ALU.mult, r=i_.k() + xc.k(), w=i_.k())
                    P.tt("dve", i_[:, :N], i_[:, :N], r_[:, :N], ALU.mult, r=i_.k() + r_.k(), w=i_.k())
                    P.op("dve", lambda e, n=n: e.tensor_tensor_scan(out=xc[:, :N], data0=a_[:, :N], data1=i_[:, :N],
                                                                   initial=hst[:, n:n + 1], op0=ALU.mult, op1=ALU.add),
                         r=a_.k() + i_.k() + hst.k(n), w=xc.k())
                    P.cp("pool", hst[:, n:n + 1], xc[:, N - 1:N], r=xc.k(), w=hst.k(n))
                else:
                    n = ct - 21
                    y_ = ft[0]
                    P.cp("act", y_[:, :N], ps[:, :N], r=ps.k(), w=y_.k())
                    y2 = ft[1]
                    P.tt("pool", y2[:, :N], y_[:, :N], y_[:, :N], ALU.mult, r=y_.k(), w=y2.k())
                    P.ts("dve", y2[:, :N], y2[:, :N], 0.044715, 1.0, ALU.mult, ALU.add, r=y2.k(), w=y2.k())
                    P.tt("dve", y2[:, :N], y2[:, :N], y_[:, :N], ALU.mult, r=y2.k() + y_.k(), w=y2.k())
                    P.act(y2[:, :N], y2[:, :N], AF.Sigmoid, r=y2.k(), w=y2.k(), scale=1.5957691216057308)
                    P.tt("dve", y2[:, :N], y2[:, :N], y_[:, :N], ALU.mult, r=y2.k() + y_.k(), w=y2.k())
                    o_ = ob[oi[0] % 2]
                    oi[0] += 1
                    P.tt("dve", o_[:], y2[:, :N], xc[:, :N], ALU.mult, r=y2.k() + xc.k(), w=o_.k())
                    P.dma("act", yo.ap(slice(1024 + n * 128, 1024 + (n + 1) * 128), t0, N), o_[:], r=o_.k(), w=yo.kq(t0), sem=f"do{oi[0] % 2}")
            proj(P, ws, w_in, order, KD, xb, N, pss, c_lru)
            if io.get("post_blk"):
                io["post_blk"](P, blk)
        if io.get("post"):
            io["post"](P)
        P.emit()
    return P


NEG = -30000.0
C = 64
GN_EPS = 64e-5


def consts_e():
    i = np.arange(64)
    su = (i[None, :] > i[:, None]).astype(np.float32)
    iu = (i[None, :] >= i[:, None]).astype(np.float32)
    out = np.zeros((128, 5, 128), np.float32)
    out[:, 0, :] = np.eye(128)
    out[:64, 1, :] = np.concatenate([su, iu], 1)
    out[:64, 2, :64] = np.where(iu > 0, 0.0, NEG)
    out[:64, 2, 64:] = np.where(su > 0, 0.0, NEG)
    cm = np.ones(128, np.float32); cm[0] = 0; cm[64] = 0
    out[:, 3, :] = cm[None, :]
    return out


def sel_np():
    s = np.zeros((128, 8, 128), np.float32)
    for i in range(8):
        s[i, i, :] = 1.0
    return s


def relayout_w64(w):
    K, N = w.shape
    return np.ascontiguousarray(w.reshape(K // 128, 128, N // 64, 64).transpose(2, 1, 0, 3))


def neumann(P, Nall, Lall, X, pN, pL, pX, NHD, ident64b):
    P.tt("dve", X[:], Nall[:], ident64b, ALU.add, r=Nall.k(), w=X.k())
    for step in range(5):
        last = step == 4
        for h in range(NHD):
            P.mm(pL[0:64, h * 64:(h + 1) * 64], Nall[:, h, :], Lall[:, h, :], True, True, r=Nall.k() + Lall.k(), w=pL.k())
        if not last:
            for h in range(NHD):
                P.mm(pN[0:64, h * 64:(h + 1) * 64], Lall[:, h, :], Nall[:, h, :], True, True, r=Nall.k() + Lall.k(), w=pN.k())
        P.cp("act", Lall[:].rearrange("p h d -> p (h d)"), pL[0:64, :NHD * 64], r=pL.k(), w=Lall.k())
        if not last:
            P.cp("dve", Nall[:].rearrange("p h d -> p (h d)"), pN[0:64, :NHD * 64], r=pN.k(), w=Nall.k())
        for h in range(NHD):
            P.mm(pX[0:64, h * 64:(h + 1) * 64], Lall[:, h, :], X[:, h, :], True, True, r=Lall.k() + X.k(), w=pX.k())
        P.tt("dve", X[:].rearrange("p h d -> p (h d)"), X[:].rearrange("p h d -> p (h d)"), pX[0:64, :NHD * 64], ALU.add,
             r=X.k() + pX.k(), w=X.k())


def phase_E(nc, outer, io, T, TB=256, D=2048, do_gdn=True):
    P = Prog(nc)
    P.semstack = outer
    KD = D // 128
    NB = T // TB
    NCH = TB // C
    N = TB
    with ExitStack() as st:
        P.stack = st
        w_a, w_b, cst, rv, rmu, rw2, gcw, gv, yo = (io[k] for k in ("w_a", "w_b", "cst", "rv", "rmu", "rw2", "gcw", "gv", "yo"))

        cs = P.sb([128, 5, 128], F32, "cs")
        ident = cs[:, 0, :]
        mask2 = cs[0:64, 1, :]
        cmask = cs[:, 3, :]
        rvs = P.sb([64, 10, 8], F32, "rvs")
        omka = P.sb([64, 8], F32, "omka")
        rmus = P.sb([128, 4], F32, "rmus")
        xst = P.sb([128, KD // 2, TB], F32, "xst")
        assert (KD // 2) * TB == 2048
        w2s = TlV(xst.h[:].rearrange("p k t -> p (k t)").rearrange("p (a b) -> p a b", a=4), xst)
        w2b = P.sb([128, 4, 512], BF16, "w2b")
        gcs = P.sb([128, 4, 12], F32, "gcs")
        gvs = P.sb([128, 4], F32, "gvs")
        ones64 = P.sb([64, 64], F32, "ones64")
        ones64m = P.sb([64, 64], F32, "ones64m")
        for (t_, d_) in ((cs, cst), (rvs, rv), (rmus, rmu), (w2s, rw2), (gcs, gcw), (gvs, gv)):
            P.dma("act", t_[:], d_[:], w=t_.k(), sem="dc")
        selc = P.sb([128, 8, 128], F32, "selc")
        P.dma("act", selc[:], io["sel"][:], w=selc.k(), sem="dc")
        P.fence([cs, rvs, rmus, w2s, gcs, gvs, selc])
        P.cp("pool", w2b[:], w2s[:], r=w2s.k(), w=w2b.k())
        P.op("pool", lambda e: e.memset(ones64[:], 1.0), w=ones64.k())
        P.op("pool", lambda e: e.memset(ones64m[:], 1.0 / 64.0), w=ones64m.k())
        P.ts("dve", omka[:], rvs[:, 6, :], -1.0, 1.0, ALU.mult, ALU.add, r=rvs.k(), w=omka.k())

        xb = P.sb([128, KD, TB], BF16, "xb")
        ws = WStream(P)
        pq = [P.ps([128, 512], F32, f"pq{i}") for i in range(8)]
        pss = pq[0:2]

        halo1 = P.sb([128, 28, 1], F32, "halo1", nsub=28)
        P.op("pool", lambda e: e.memset(halo1[:], 0.0), w=halo1.k())
        raw = [P.sb([128, N + 1], F32, f"raw{i}") for i in range(3)]
        lo_b = P.sb([128, 4, N], BF16, "lo_b", nsub=4)
        ar128 = P.sb([128, 8, NCH, 128], F32, "ar", nsub=8)
        ar = Tl(ar128.h[0:64], ar128.name, 8)
        bt128 = P.sb([128, 8, N], F32, "bt", nsub=8)
        bt = Tl(bt128.h[0:64], bt128.name, 8)
        kt128 = P.sb([128, 8, N], F32, "kt", nsub=8)
        kt = Tl(kt128.h[0:64], kt128.name, 8)
        vv128 = P.sb([128, 8, N], F32, "vv", nsub=8)
        vv = Tl(vv128.h[0:64], vv128.name, 8)
        bon128 = P.sb([128, 8, N], F32, "bon", nsub=8)
        bon = Tl(bon128.h[0:64], bon128.name, 8)
        gg128 = P.sb([128, 8, N], F32, "gg", nsub=8)
        gg = Tl(gg128.h[0:64], gg128.name, 8)
        PC = P.sb([64, 8, NCH], F32, "PC", nsub=8)
        yall = P.sb([64, 8, N], F32, "yall")
        H = P.sb([64, 8, 64], F32, "H")
        P.op("pool", lambda e: e.memset(H[:], 0.0), w=H.k())
        rr = [P.sb([64, N], F32, f"rr{i}") for i in range(2)]
        kk_ = [P.sb([64, N], F32, f"kk{i}") for i in range(2)]
        tA = [P.sb([64, N], F32, f"tA{i}") for i in range(8)]
        vtm = P.sb([64, 8, 64], F32, "vtm")
        btm = P.sb([64, 8, 64], F32, "btm")
        ktm = P.sb([64, 8, 64], F32, "ktm")
        sc1 = P.sb([64, 8, 128], F32, "sc1")
        sc2 = P.sb([64, 8, 128], F32, "sc2")
        Nall = P.sb([64, 8, 64], F32, "Nall")
        Lall = P.sb([64, 8, 64], F32, "Lall")
        X = P.sb([64, 8, 64], F32, "X")
        rhs_sb = P.sb([64, 8, 64], F32, "rhs_sb")
        U = P.sb([64, 8, 64], F32, "U")
        ob = [P.sb([128, N], BF16, f"ob{i}") for i in range(2)]
        oi = [0]
        ident64b = cs[0:64, 0, 0:64].unsqueeze(1).to_broadcast([64, 8, 64])


        ones128m = P.sb([128, 128], F32, "ones128m")
        ones128 = P.sb([128, 128], F32, "ones128")
        P.op("pool", lambda e: e.memset(ones128m[:], 1.0 / 128.0), w=ones128m.k())
        P.op("pool", lambda e: e.memset(ones128[:], 1.0), w=ones128.k())
        gnegA = P.sb([128, 1], F32, "gnegA")
        P.act(gnegA[:], gvs[:, 0:1], AF.Exp, r=gvs.k(), w=gnegA.k())
        P.ts("dve", gnegA[:], gnegA[:], -1.0, None, ALU.mult, None, r=gnegA.k(), w=gnegA.k())
        ghalo = P.sb([128, 12, 3], F32, "ghalo", nsub=12)
        P.op("pool", lambda e: e.memset(ghalo[:], 0.0), w=ghalo.k())
        graw = [P.sb([128, N + 3], F32, f"graw{i}") for i in range(2)]
        gq = TlV(bt128.h[:, 0:4, :], bt128)
        gk = TlV(bt128.h[:, 4:8, :], bt128)
        gvv = TlV(kt128.h[:, 0:4, :], kt128)
        kq = TlV(ar128.h[:, 0:4], ar128)
        vb = TlV(kt128.h[:, 4:8, :], kt128)
        nkbg = TlV(vv128.h[:, 0:4, :], vv128)
        qd = TlV(vv128.h[:, 4:8, :], vv128)
        kdec = TlV(bon128.h[:, 0:4, :], bon128)
        gcbc = TlV(bon128.h[:, 4:8, :], bon128)
        egc = TlV(gg128.h[:, 0:4, :], gg128)
        oall = TlV(gg128.h[:, 4:8, :], gg128)
        sgm = P.sb([128, N], F32, "sgm")
        gcf = P.sb([128, N], F32, "gcf")
        gS = P.sb([128, 4, 128], F32, "gS")
        P.op("pool", lambda e: e.memset(gS[:], 0.0), w=gS.k())
        gT = [P.sb([128, N], F32, f"gT{i}") for i in range(3)]
        ngc = P.sb([64, 128], F32, "ngc")
        wd_ = P.sb([64, 4, 64], F32, "wd_")
        Di = P.sb([64, 4, 64], F32, "Di")
        Ds = P.sb([64, 4, 64], F32, "Ds")
        attT = P.sb([64, 4, 64], F32, "attT")
        kdtm = P.sb([64, 4, 128], F32, "kdtm")
        ident64b4 = cs[0:64, 0, 0:64].unsqueeze(1).to_broadcast([64, 4, 64])
        mneg_i = cs[0:64, 2, 0:64]
        su_b4 = cs[0:64, 1, 0:64].unsqueeze(1).to_broadcast([64, 4, 64])

        cmk128 = P.sb([128, N], F32, "cmk128")
        for i_ in range(N // 128):
            P.cp("pool", cmk128[:, i_ * 128:(i_ + 1) * 128], cs[:, 3, :], r=cs.k(), w=cmk128.k())
        cmk = P.sb([64, N], F32, "cmk")
        for i_ in range(N // 128):
            P.cp("pool", cmk[:, i_ * 128:(i_ + 1) * 128], cs[0:64, 3, :], r=cs.k(), w=cmk.k())

        for blk in range(NB):
            t0 = blk * TB
            for hf in range(2):
                P.dma("act", xst[:], io["xsrc"](t0, hf).rearrange("(k p) t -> p k t", p=128), r=io["xsrc_k"], w=xst.k(), sem="dx")
                P.cp("dve" if hf else "pool", xb[:, hf * (KD // 2):(hf + 1) * (KD // 2), :], xst[:], r=xst.k(), w=xb.k())

            def c_lo(j, ct, ps):
                rawt = raw[j % 3]
                P.cp("act", rawt[:, 1:N + 1], ps[:, :N], r=ps.k(), w=rawt.k())
                P.cp("pool", rawt[:, 0:1], halo1[:, 24 + ct, :], r=halo1.k(24 + ct), w=rawt.k())
                P.cp("pool", halo1[:, 24 + ct, :], rawt[:, N:N + 1], r=rawt.k(), w=halo1.k(24 + ct))
                d32 = raw[(j + 1) % 3]
                P.tt("dve", d32[:, 0:N], rawt[:, 0:N], rawt[:, 1:N + 1], ALU.subtract, r=rawt.k(), w=d32.k())
                P.stt("dve", d32[:, 0:N], d32[:, 0:N], rmus[:, ct:ct + 1], rawt[:, 1:N + 1], ALU.mult, ALU.add,
                      r=d32.k() + rmus.k() + rawt.k(), w=d32.k())
                if ct == 0:
                    P.act(lo_b[:, 0, :], d32[:, 0:N], AF.Tanh, r=d32.k(), w=lo_b.k(0))
                elif ct == 1:
                    P.cp("act", lo_b[:, 1, :], d32[:, 0:N], r=d32.k(), w=lo_b.k(1))
                else:
                    P.act(lo_b[:, ct, :], d32[:, 0:N], AF.Sigmoid, r=d32.k(), w=lo_b.k(ct))
            proj(P, ws, w_b, range(4), KD, xb, N, pss, c_lo)

            def c_rkv(j, ct, ps):
                h, which = ct // 3, ct % 3
                rawt = raw[j % 3]
                P.cp("act", rawt[0:64, 1:N + 1], ps[0:64, :N], r=ps.k(), w=rawt.k())
                P.cp("pool", rawt[0:64, 0:1], halo1[0:64, ct, :], r=halo1.k(ct), w=rawt.k())
                P.cp("pool", halo1[0:64, ct, :], rawt[0:64, N:N + 1], r=rawt.k(), w=halo1.k(ct))
                dst = (rr[h % 2], kk_[h % 2], None)[which]
                dst_ap = vv[:, h, :] if which == 2 else dst[:, :]
                dst_k = vv.k(h) if which == 2 else dst.k()
                d = tA[7]
                P.tt("dve", d[:, :], rawt[0:64, 0:N], rawt[0:64, 1:N + 1], ALU.subtract, r=rawt.k(), w=d.k())
                P.stt("dve", dst_ap, d[:, :], rvs[:, which, h:h + 1], rawt[0:64, 1:N + 1], ALU.mult, ALU.add,
                      r=d.k() + rvs.k() + rawt.k(), w=dst_k)
                if which != 2:
                    return
                r_, k_ = rr[h % 2], kk_[h % 2]
                hs = slice(h * 64, (h + 1) * 64)
                pw, pa, pg, pn = pq[2], pq[3], pq[4], pq[5]
                P.mm(pw[0:64, :N], w2b[:, 0, hs], lo_b[:, 0, :], True, True, r=w2b.k() + lo_b.k(0), w=pw.k())
                P.mm(pa[0:64, :N], w2b[:, 1, hs], lo_b[:, 1, :], True, True, r=w2b.k() + lo_b.k(1), w=pa.k())
                P.mm(pg[0:64, :N], w2b[:, 2, hs], lo_b[:, 2, :], True, False, r=w2b.k() + lo_b.k(2), w=pg.k())
                P.mm(pg[0:64, :N], w2b[:, 3, hs], lo_b[:, 3, :], False, True, r=w2b.k() + lo_b.k(3), w=pg.k())
                lw, cl, asig, e1, e2, kkn, tmp = tA[0], tA[1], tA[2], tA[3], tA[4], tA[5], tA[6]
                P.act(lw[:], pw[0:64, :N], AF.Sigmoid, r=pw.k() + rvs.k(), w=lw.k(), bias=rvs[:, 3, h:h + 1])
                P.ts("dve", lw[:], lw[:], -0.6065306597126334, None, ALU.mult, None, r=lw.k(), w=lw.k())
                P.act(asig[:], pa[0:64, :N], AF.Sigmoid, r=pa.k() + rvs.k(), w=asig.k(), bias=rvs[:, 4, h:h + 1])
                P.cp("act", gg[:, h, :], pg[0:64, :N], r=pg.k(), w=gg.k(h))
                P.op("dve", lambda e: e.tensor_tensor_scan(out=cl[:], data0=cmk[:], data1=lw[:], initial=0.0,
                                                           op0=ALU.mult, op1=ALU.add),
                     r=lw.k() + cmk.k(), w=cl.k())
                P.act(e1[:], cl[:], AF.Exp, r=cl.k(), w=e1.k())
                P.tt("dve", ar[:, h, :, 64:128], r_[:].rearrange("p (c t) -> p c t", t=64), e1[:].rearrange("p (c t) -> p c t", t=64),
                     ALU.mult, r=r_.k() + e1.k(), w=ar.k(h))
                P.cp("pool", PC[:, h, :], e1[:].rearrange("p (c t) -> p c t", t=64)[:, :, 63], r=e1.k(), w=PC.k(h))
                P.act(e2[:], cl[:], AF.Exp, r=cl.k(), w=e2.k(), scale=-1.0)
                P.ts("dve", kkn[:], k_[:], rvs[:, 5, h:h + 1], None, ALU.mult, None, r=k_.k() + rvs.k(), w=kkn.k())
                P.tt("pool", tmp[:], kkn[:], kkn[:], ALU.mult, r=kkn.k(), w=tmp.k())
                P.mm(pn[0:64, :N], ones64[:], tmp[:], True, True, r=ones64.k() + tmp.k(), w=pn.k())
                P.act(tmp[:], pn[0:64, :N], AF.Sqrt, r=pn.k(), w=tmp.k(), bias=1e-6)
                P.op("dve", lambda e: e.reciprocal(out=tmp[:], in_=tmp[:]), r=tmp.k(), w=tmp.k())
                P.tt("dve", kkn[:], kkn[:], tmp[:], ALU.mult, r=kkn.k() + tmp.k(), w=kkn.k())
                P.tt("dve", tmp[:], kkn[:], asig[:], ALU.mult, r=kkn.k() + asig.k(), w=tmp.k())
                P.tt("dve", bt[:, h, :], tmp[:], e2[:], ALU.mult, r=tmp.k() + e2.k(), w=bt.k(h))
                P.tt("dve", tmp[:], cl[:], lw[:], ALU.subtract, r=cl.k() + lw.k(), w=tmp.k())
                P.act(tmp[:], tmp[:], AF.Exp, r=tmp.k(), w=tmp.k())
                P.stt("dve", ar[:, h, :, 0:64], kkn[:].rearrange("p (c t) -> p c t", t=64), -1.0,
                      tmp[:].rearrange("p (c t) -> p c t", t=64), ALU.mult, ALU.mult, r=kkn.k() + tmp.k(), w=ar.k(h))
                P.ts("dve", tmp[:], asig[:], rvs[:, 6, h:h + 1], omka[:, h:h + 1], ALU.mult, ALU.add, r=asig.k() + rvs.k() + omka.k(), w=tmp.k())
                P.tt("dve", tmp[:], tmp[:], k_[:], ALU.mult, r=tmp.k() + k_.k(), w=tmp.k())
                P.tt("dve", kt[:, h, :], tmp[:], e2[:], ALU.mult, r=tmp.k() + e2.k(), w=kt.k(h))
                P.stt("dve", tmp[:], tmp[:], rvs[:, 7, h:h + 1], r_[:], ALU.mult, ALU.mult, r=tmp.k() + rvs.k() + r_.k(), w=tmp.k())
                P.mm(pn[0:64, :N], ones64[:], tmp[:], True, True, r=ones64.k() + tmp.k(), w=pn.k())
                P.tt("dve", bon[:, h, :], pn[0:64, :N], vv[:, h, :], ALU.mult, r=pn.k() + vv.k(h), w=bon.k(h))
            proj(P, ws, w_a, range(24), KD, xb, N, pss, c_rkv, cw=64)

            for c in range(NCH):
                cc = slice(c * C, (c + 1) * C)
                pT1, pT2, pT3 = pq[7], pq[6], pq[5]
                for h in range(8):
                    hs = slice(h * 64, (h + 1) * 64)
                    P.tr(pT1[0:64, hs], vv[:, h, cc], ident[0:64, 0:64], r=vv.k(h) + cs.k(), w=pT1.k())
                    P.tr(pT2[0:64, hs], bt[:, h, cc], ident[0:64, 0:64], r=bt.k(h) + cs.k(), w=pT2.k())
                    P.tr(pT3[0:64, hs], kt[:, h, cc], ident[0:64, 0:64], r=kt.k(h) + cs.k(), w=pT3.k())
                P.cp("act", vtm[:].rearrange("p h d -> p (h d)"), pT1[0:64, :], r=pT1.k(), w=vtm.k())
                P.cp("dve", btm[:].rearrange("p h d -> p (h d)"), pT2[0:64, :], r=pT2.k(), w=btm.k())
                P.cp("act", ktm[:].rearrange("p h d -> p (h d)"), pT3[0:64, :], r=pT3.k(), w=ktm.k())
                for h in range(8):
                    pa_ = pq[0] if h < 4 else pq[1]
                    pb_ = pq[2] if h < 4 else pq[3]
                    o = (h % 4) * 128
                    P.mm(pa_[0:64, o:o + 128], bt[:, h, cc], ar[:, h, c, :], True, True, r=bt.k(h) + ar.k(h), w=pa_.k())
                    P.mm(pb_[0:64, o:o + 128], kt[:, h, cc], ar[:, h, c, :], True, True, r=kt.k(h) + ar.k(h), w=pb_.k())
                m2b = mask2.unsqueeze(1).to_broadcast([64, 4, 128])
                for half in range(2):
                    P.tt("dve", sc1[:, half * 4:(half + 1) * 4, :], pq[half][0:64, :].rearrange("p (h d) -> p h d", h=4), m2b, ALU.mult,
                         r=pq[half].k() + cs.k(), w=sc1.k())
                    P.tt("dve", sc2[:, half * 4:(half + 1) * 4, :], pq[2 + half][0:64, :].rearrange("p (h d) -> p h d", h=4), m2b, ALU.mult,
                         r=pq[2 + half].k() + cs.k(), w=sc2.k())
                P.cp("pool", Nall[:], sc1[:, :, 0:64], r=sc1.k(), w=Nall.k())
                pL, pN, pX = pq[4], pq[5], pq[6]
                for h in range(8):
                    P.tr(pL[0:64, h * 64:(h + 1) * 64], sc1[:, h, 0:64], ident[0:64, 0:64], r=sc1.k() + cs.k(), w=pL.k())
                P.cp("act", Lall[:].rearrange("p h d -> p (h d)"), pL[0:64, :], r=pL.k(), w=Lall.k())
                neumann(P, Nall, Lall, X, pN, pL, pX, 8, ident64b)
                pR, pU, pY, pH = pq[0], pq[1], pq[2], pq[3]
                for h in range(8):
                    hs = slice(h * 64, (h + 1) * 64)
                    P.mm(pR[0:64, hs], ar[:, h, c, 0:64], H[:, h, :], True, False, r=ar.k(h) + H.k(), w=pR.k())
                    P.mm(pR[0:64, hs], sc2[:, h, 0:64], vtm[:, h, :], False, True, r=sc2.k() + vtm.k(), w=pR.k())
                P.cp("act", rhs_sb[:].rearrange("p h d -> p (h d)"), pR[0:64, :], r=pR.k(), w=rhs_sb.k())
                for h in range(8):
                    hs = slice(h * 64, (h + 1) * 64)
                    P.mm(pU[0:64, hs], X[:, h, :], rhs_sb[:, h, :], True, True, r=X.k() + rhs_sb.k(), w=pU.k())
                P.cp("act", U[:].rearrange("p h d -> p (h d)"), pU[0:64, :], r=pU.k(), w=U.k())
                for h in range(8):
                    hs = slice(h * 64, (h + 1) * 64)
                    P.mm(pY[0:64, hs], H[:, h, :], ar[:, h, c, 64:128], True, False, r=ar.k(h) + H.k(), w=pY.k())
                    P.mm(pY[0:64, hs], U[:, h, :], sc1[:, h, 64:128], False, False, r=U.k() + sc1.k(), w=pY.k())
                    P.mm(pY[0:64, hs], vtm[:, h, :], sc2[:, h, 64:128], False, True, r=vtm.k() + sc2.k(), w=pY.k())
                    P.mm(pH[0:64, hs], btm[:, h, :], U[:, h, :], True, False, r=btm.k() + U.k(), w=pH.k())
                    P.mm(pH[0:64, hs], ktm[:, h, :], vtm[:, h, :], False, True, r=ktm.k() + vtm.k(), w=pH.k())
                P.cp("act", yall[:, :, cc], pY[0:64, :].rearrange("p (h d) -> p h d", h=8), r=pY.k(), w=yall.k())
                P.tt("dve", H[:].rearrange("p h d -> p (h d)"), H[:].rearrange("p h d -> p (h d)"), pH[0:64, :], ALU.add, r=H.k() + pH.k(), w=H.k())
                P.tt("dve", H[:], H[:], PC[:, :, c].unsqueeze(2).to_broadcast([64, 8, 64]), ALU.mult, r=H.k() + PC.k(), w=H.k())

            for h in range(8):
                pm, pv = pq[4], pq[5]
                y_ = yall[:, h, :]
                sq_, mean, t_ = tA[0], tA[1], tA[2]
                P.tt("pool", sq_[:], y_, y_, ALU.mult, r=yall.k(), w=sq_.k())
                P.mm(pm[0:64, :N], ones64m[:], y_, True, True, r=ones64m.k() + yall.k(), w=pm.k())
                P.mm(pv[0:64, :N], ones64m[:], sq_[:], True, True, r=ones64m.k() + sq_.k(), w=pv.k())
                P.cp("act", mean[:], pm[0:64, :N], r=pm.k(), w=mean.k())
                P.tt("dve", t_[:], mean[:], mean[:], ALU.mult, r=mean.k(), w=t_.k())
                P.tt("dve", t_[:], pv[0:64, :N], t_[:], ALU.subtract, r=pv.k() + t_.k(), w=t_.k())
                P.act(t_[:], t_[:], AF.Sqrt, r=t_.k(), w=t_.k(), bias=GN_EPS)
                P.op("dve", lambda e, t_=t_: e.reciprocal(out=t_[:], in_=t_[:]), r=t_.k(), w=t_.k())
                P.tt("dve", mean[:], y_, mean[:], ALU.subtract, r=yall.k() + mean.k(), w=mean.k())
                P.tt("dve", mean[:], mean[:], t_[:], ALU.mult, r=mean.k() + t_.k(), w=mean.k())
                P.ts("dve", mean[:], mean[:], rvs[:, 8, h:h + 1], rvs[:, 9, h:h + 1], ALU.mult, ALU.add, r=mean.k() + rvs.k(), w=mean.k())
                P.tt("dve", mean[:], mean[:], bon[:, h, :], ALU.add, r=mean.k() + bon.k(h), w=mean.k())
                o_ = ob[oi[0] % 2]
                oi[0] += 1
                P.tt("dve", o_[0:64, :], mean[:], gg[:, h, :], ALU.mult, r=mean.k() + gg.k(h), w=o_.k())
                P.dma("act", yo.ap(slice(h * 64, (h + 1) * 64), t0, N), o_[0:64, :], r=o_.k(), w=yo.kq(t0), sem=f"do{oi[0] % 2}")

            if not do_gdn:
                continue
            def c_ba(j, ct, ps):
                P.act(sgm[:], ps[:, :N], AF.Sigmoid, r=ps.k(), w=sgm.k())
                t_ = gT[0]
                P.act(t_[:], ps[:, :N], AF.Exp, r=ps.k() + gvs.k(), w=t_.k(), bias=gvs[:, 1:2])
                P.act(t_[:], t_[:], AF.Ln, r=t_.k(), w=t_.k(), bias=1.0)
                P.ts("dve", t_[:], t_[:], gnegA[:, 0:1], None, ALU.mult, None, r=t_.k() + gnegA.k(), w=t_.k())
                P.op("dve", lambda e: e.tensor_tensor_scan(out=gcf[:], data0=cmk128[:], data1=t_[:], initial=0.0,
                                                           op0=ALU.mult, op1=ALU.add), r=t_.k() + cmk128.k(), w=gcf.k())
                for h in range(4):
                    pb_ = pq[2 + (h % 2)]
                    P.mm(pb_[:, :N], selc[:, 4 + h, :], gcf[:], True, True, r=selc.k() + gcf.k(), w=pb_.k())
                    P.cp("act", gcbc[:, h, :], pb_[:, :N], r=pb_.k(), w=gcbc.k(h))
                    P.act(egc[:, h, :], pb_[:, :N], AF.Exp, r=pb_.k(), w=egc.k(h))
            proj(P, ws, w_b, [20], KD, xb, N, pss, c_ba)

            def c_qkv(j, ct, ps):
                ti = ct - 4
                which, h = ti // 4, ti % 4
                u = graw[j % 2]
                P.cp("act", u[:, 3:3 + N], ps[:, :N], r=ps.k(), w=u.k())
                P.cp("pool", u[:, 0:3], ghalo[:, ti, :], r=ghalo.k(ti), w=u.k())
                P.cp("pool", ghalo[:, ti, :], u[:, N:N + 3], r=u.k(), w=ghalo.k(ti))
                dst = (gq, gk, gvv)[which]
                o_ = dst[:, h, :]
                P.ts("dve", o_, u[:, 3:3 + N], gcs[:, 3, ti:ti + 1], None, ALU.mult, None, r=u.k() + gcs.k(), w=dst.k(h))
                for jj in range(3):
                    P.stt("dve", o_, u[:, jj:jj + N], gcs[:, jj, ti:ti + 1], o_, ALU.mult, ALU.add, r=u.k() + gcs.k() + dst.k(h), w=dst.k(h))
                P.act(o_, o_, AF.Silu, r=dst.k(h), w=dst.k(h))
                if which < 2:
                    sq_ = gT[1]
                    pn = pq[4]
                    P.tt("pool", sq_[:], o_, o_, ALU.mult, r=dst.k(h), w=sq_.k())
                    P.mm(pn[:, :N], ones128[:], sq_[:], True, True, r=ones128.k() + sq_.k(), w=pn.k())
                    P.act(sq_[:], pn[:, :N], AF.Sqrt, r=pn.k(), w=sq_.k(), bias=1e-6)
                    P.op("dve", lambda e: e.reciprocal(out=sq_[:], in_=sq_[:]), r=sq_.k(), w=sq_.k())
                    if which == 0:
                        P.stt("dve", o_, o_, float(128 ** -0.5), sq_[:], ALU.mult, ALU.mult, r=dst.k(h) + sq_.k(), w=dst.k(h))
                    else:
                        P.tt("dve", o_, o_, sq_[:], ALU.mult, r=dst.k(h) + sq_.k(), w=dst.k(h))
                if which != 2:
                    return
                pbb = pq[5]
                P.mm(pbb[:, :N], selc[:, h, :], sgm[:], True, True, r=selc.k() + sgm.k(), w=pbb.k())
                c3 = lambda ap: ap.rearrange("p (c t) -> p c t", t=64)
                P.tt("dve", kq[:, h, :, 0:64], c3(gk[:, h, :]), c3(pbb[:, :N]), ALU.mult, r=gk.k(h) + pbb.k(), w=kq.k(h))
                P.cp("pool", kq[:, h, :, 64:128], c3(gq[:, h, :]), r=gq.k(h), w=kq.k(h))
                P.tt("dve", vb[:, h, :], gvv[:, h, :], pbb[:, :N], ALU.mult, r=gvv.k(h) + pbb.k(), w=vb.k(h))
                t1 = gT[2]
                P.tt("dve", t1[:], gk[:, h, :], pbb[:, :N], ALU.mult, r=gk.k(h) + pbb.k(), w=t1.k())
                P.stt("dve", nkbg[:, h, :], t1[:], -1.0, egc[:, h, :], ALU.mult, ALU.mult, r=t1.k() + egc.k(h), w=nkbg.k(h))
                P.tt("dve", qd[:, h, :], gq[:, h, :], egc[:, h, :], ALU.mult, r=gq.k(h) + egc.k(h), w=qd.k(h))
                P.tt("dve", c3(t1[:]), c3(gcbc[:, h, :])[:, :, 63:64].to_broadcast([128, NCH, 64]), c3(gcbc[:, h, :]), ALU.subtract,
                     r=gcbc.k(h), w=t1.k())
                P.act(t1[:], t1[:], AF.Exp, r=t1.k(), w=t1.k())
                P.tt("dve", kdec[:, h, :], gk[:, h, :], t1[:], ALU.mult, r=gk.k(h) + t1.k(), w=kdec.k(h))
            proj(P, ws, w_b, range(4, 16), KD, xb, N, pss, c_qkv)

            for c in range(NCH):
                cc = slice(c * C, (c + 1) * C)
                pSc, pTr, pL, pN, pX, pR, pO, pSt = pq[0], pq[1], pq[2], pq[3], pq[4], pq[5], pq[6], pq[7]
                P.tr(pTr[0:64, 0:128], gcf[:, cc], ident, r=gcf.k() + cs.k(), w=pTr.k())
                P.ts("dve", ngc[:], pTr[0:64, 0:128], -1.0, None, ALU.mult, None, r=pTr.k(), w=ngc.k())
                for h in range(4):
                    P.mm(pSc[0:64, h * 128:(h + 1) * 128], gk[:, h, cc], kq[:, h, c, :], True, True, r=gk.k(h) + kq.k(h), w=pSc.k())
                P.tt("dve", wd_[:], gcbc[0:64, :, cc], mneg_i.unsqueeze(1).to_broadcast([64, 4, 64]), ALU.add, r=gcbc.k() + cs.k(), w=wd_.k())
                for h in range(4):
                    P.act(Di[:, h, :], wd_[:, h, :], AF.Exp, r=wd_.k() + ngc.k(), w=Di.k(), bias=ngc[:, 4 + h:5 + h])
                P.tt("dve", Ds[:], Di[:], su_b4, ALU.mult, r=Di.k() + cs.k(), w=Ds.k())
                ps3 = pSc[0:64, :].rearrange("p (h d) -> p h d", h=4)
                P.stt("dve", Nall[:, 0:4, :], ps3[:, :, 0:64], -1.0, Ds[:], ALU.mult, ALU.mult, r=pSc.k() + Ds.k(), w=Nall.k())
                P.tt("dve", attT[:], ps3[:, :, 64:128], Di[:], ALU.mult, r=pSc.k() + Di.k(), w=attT.k())
                for h in range(4):
                    P.tr(pL[0:64, h * 64:(h + 1) * 64], Nall[:, h, :], ident[0:64, 0:64], r=Nall.k() + cs.k(), w=pL.k())
                P.cp("act", Lall[:, 0:4, :].rearrange("p h d -> p (h d)"), pL[0:64, 0:256], r=pL.k(), w=Lall.k())
                neumann(P, Tl(Nall.h[:, 0:4, :], Nall.name), Tl(Lall.h[:, 0:4, :], Lall.name), Tl(X.h[:, 0:4, :], X.name), pN, pL, pX, 4, ident64b4)
                for h in range(4):
                    P.tr(pTr[0:64, h * 128:(h + 1) * 128], kdec[:, h, cc], ident, r=kdec.k(h) + cs.k(), w=pTr.k())
                P.cp("act", kdtm[:].rearrange("p h d -> p (h d)"), pTr[0:64, :], r=pTr.k(), w=kdtm.k())
                rhs4 = rhs_sb[:].rearrange("p h d -> p (h d)")
                U4 = U[:].rearrange("p h d -> p (h d)")
                for h in range(4):
                    hs = slice(h * 128, (h + 1) * 128)
                    P.mm(pR[0:64, hs], vb[:, h, cc], ident, True, False, r=vb.k(h) + cs.k(), w=pR.k())
                    P.mm(pR[0:64, hs], nkbg[:, h, cc], gS[:, h, :], False, True, r=nkbg.k(h) + gS.k(), w=pR.k())
                P.cp("act", rhs4, pR[0:64, :], r=pR.k(), w=rhs_sb.k())
                for h in range(4):
                    hs = slice(h * 128, (h + 1) * 128)
                    P.mm(pO[0:64, hs], X[:, h, :], rhs4[:, hs], True, True, r=X.k() + rhs_sb.k(), w=pO.k())
                P.cp("act", U4, pO[0:64, :], r=pO.k(), w=U.k())
                for h in range(4):
                    hs = slice(h * 128, (h + 1) * 128)
                    P.mm(pR[:, h * 64:(h + 1) * 64], gS[:, h, :], qd[:, h, cc], True, True, r=gS.k() + qd.k(h), w=pR.k())
                    P.mm(pX[:, h * 64:(h + 1) * 64], U4[:, hs], attT[:, h, :], True, True, r=U.k() + attT.k(), w=pX.k())
                    P.mm(pSt[:, hs], kdtm[:, h, :], U4[:, hs], True, True, r=kdtm.k() + U.k(), w=pSt.k())
                t_ = gT[0]
                P.cp("act", t_[:, 0:256], pR[:, 0:256], r=pR.k(), w=t_.k())
                P.tt("dve", oall[:, :, cc], pX[:, 0:256].rearrange("p (h d) -> p h d", h=4), t_[:, 0:256].rearrange("p (h d) -> p h d", h=4), ALU.add,
                     r=pX.k() + t_.k(), w=oall.k())
                for h in range(4):
                    hs = slice(h * 128, (h + 1) * 128)
                    P.stt("dve", gS[:, h, :], gS[:, h, :], egc[:, h, c * C + C - 1:c * C + C], pSt[:, hs], ALU.mult, ALU.add,
                          r=gS.k() + egc.k(h) + pSt.k(), w=gS.k())

            def c_z(j, ct, ps):
                h = ct - 16
                zs, sq_ = gT[0], gT[1]
                pn = pq[4]
                P.act(zs[:], ps[:, :N], AF.Silu, r=ps.k(), w=zs.k())
                P.tt("pool", sq_[:], oall[:, h, :], oall[:, h, :], ALU.mult, r=oall.k(), w=sq_.k())
                P.mm(pn[:, :N], ones128m[:], sq_[:], True, True, r=ones128m.k() + sq_.k(), w=pn.k())
                P.act(sq_[:], pn[:, :N], AF.Sqrt, r=pn.k(), w=sq_.k(), bias=RMS_EPS)
                P.op("dve", lambda e: e.reciprocal(out=sq_[:], in_=sq_[:]), r=sq_.k(), w=sq_.k())
                P.stt("dve", sq_[:], oall[:, h, :], gvs[:, 2:3], sq_[:], ALU.mult, ALU.mult, r=oall.k() + gvs.k() + sq_.k(), w=sq_.k())
                o_ = ob[oi[0] % 2]
                oi[0] += 1
                P.tt("dve", o_[:], sq_[:], zs[:], ALU.mult, r=sq_.k() + zs.k(), w=o_.k())
                P.dma("act", yo.ap(slice(512 + h * 128, 512 + (h + 1) * 128), t0, N), o_[:], r=o_.k(), w=yo.kq(t0), sem=f"do{oi[0] % 2}")
            proj(P, ws, w_b, range(16, 20), KD, xb, N, pss, c_z)
            if io.get("post_blk"):
                io["post_blk"](P, blk)
        if io.get("post"):
            io["post"](P)
        P.emit()
    return P

DN_ALPHA = (2.0 * 2) ** 0.25
A_COLS = 3520


def pad128(a):
    o = np.zeros((128,) + a.shape[1:], np.float32)
    o[:a.shape[0]] = a
    return o


def prep_even(xb_, inp, hh):
    w = inp["even_w_in"][0]
    o = 512 * hh
    hv = lambda v: np.ascontiguousarray(v.reshape(-1, 64).T)
    cols = []
    for h in range(8):
        for which in range(3):
            cols.append(w[:, which * 1024 + o + h * 64: which * 1024 + o + (h + 1) * 64])
    w_a = relayout_w64(np.concatenate(cols, 1))
    wlo = np.zeros((2048, 128), np.float32); wlo[:, :96] = w[:, 3072:3168]
    alo = np.zeros((2048, 128), np.float32); alo[:, :96] = w[:, 3168:3264]
    glo = w[:, 3264:3520]
    gb = A_COLS
    q = w[:, gb + o: gb + o + 512]; k = w[:, gb + 1024 + o: gb + 1024 + o + 512]; v = w[:, gb + 2048 + o: gb + 2048 + o + 512]
    z = w[:, gb + 3072 + o: gb + 3072 + o + 512]
    ba = np.zeros((2048, 128), np.float32)
    ba[:, 0:4] = w[:, gb + 4096 + 4 * hh: gb + 4096 + 4 * hh + 4]; ba[:, 4:8] = w[:, gb + 4104 + 4 * hh: gb + 4104 + 4 * hh + 4]
    w_b = relayout_w(np.concatenate([wlo, alo, glo, q, k, v, z, ba], 1))
    mu = inp["rwkv_mu"][0]
    rv = np.stack([hv(mu[o:o + 512]), hv(mu[1024 + o:1024 + o + 512]), hv(mu[2048 + o:2048 + o + 512]),
                   hv(inp["rwkv_w0"][0][o:o + 512]), hv(inp["rwkv_a0"][0][o:o + 512]), hv(inp["rwkv_k_k"][0][o:o + 512]),
                   hv(inp["rwkv_k_a"][0][o:o + 512]), hv(inp["rwkv_r_k"][0].reshape(-1)[o:o + 512]),
                   hv(inp["rwkv_gn_g"][0].reshape(-1)[o:o + 512]), hv(inp["rwkv_gn_b"][0].reshape(-1)[o:o + 512])], 1)
    rmu = np.zeros((128, 4), np.float32)
    rmu[:96, 0] = mu[3072:3168]; rmu[:96, 1] = mu[3168:3264]; rmu[:, 2] = mu[3264:3392]; rmu[:, 3] = mu[3392:3520]
    rw2 = np.stack([pad128(inp["rwkv_w2"][0][:, o:o + 512]), pad128(inp["rwkv_a2"][0][:, o:o + 512]),
                    inp["rwkv_g2"][0][0:128, o:o + 512], inp["rwkv_g2"][0][128:256, o:o + 512]], 1)
    gc = inp["gdn_conv_w"][0]
    idx = np.concatenate([o + np.arange(512), 1024 + o + np.arange(512), 2048 + o + np.arange(512)])
    gcw = np.stack([relayout_v(gc[j, idx]) for j in range(4)], 1)
    gv = np.zeros((128, 4), np.float32)
    gv[4:8, 0] = inp["gdn_A_log"][0][4 * hh:4 * hh + 4]; gv[4:8, 1] = inp["gdn_dt_bias"][0][4 * hh:4 * hh + 4]
    gv[:, 2] = inp["gdn_norm_g"][0]
    return dict(xT=np.ascontiguousarray(xb_.T), w_a=w_a, w_b=w_b, cst=consts_e(), rv=np.ascontiguousarray(rv), rmu=rmu,
                rw2=np.ascontiguousarray(rw2), gcw=np.ascontiguousarray(gcw), gv=gv, sel=sel_np())


def prep_odd(xb_, inp, hh):
    w = inp["odd_w_in"][0]
    o = 1024 * hh
    zc = w[:, o:o + 1024]
    xs = w[:, 2048 + o:2048 + o + 1024]
    Bc = w[:, 4096 + 256 * hh:4096 + 256 * hh + 256]
    Cc = w[:, 4608 + 256 * hh:4608 + 256 * hh + 256]
    dtc = np.zeros((2048, 128), np.float32); dtc[:, :16] = w[:, 5120 + 16 * hh:5120 + 16 * hh + 16]
    yb = w[:, 5152 + o:5152 + o + 1024]
    xbr = w[:, 7200 + o:7200 + o + 1024]
    wc = np.concatenate([xs, Bc, Cc, dtc, zc, yb, xbr], 1)
    mc = inp["mamba_conv_w"][0]; mb = inp["mamba_conv_b"][0]
    idx = np.concatenate([np.arange(o, o + 1024), 2048 + 256 * hh + np.arange(256), 2560 + 256 * hh + np.arange(256)])
    mcw = np.stack([relayout_v(mc[j, idx]) for j in range(4)] + [relayout_v(mb[idx])], 1)
    mv = np.zeros((128, 4), np.float32)
    mv[:16, 0] = inp["mamba_dt_bias"][0][16 * hh:16 * hh + 16]; mv[:16, 1] = inp["mamba_A_log"][0][16 * hh:16 * hh + 16]
    Dexp = np.repeat(inp["mamba_D"][0][16 * hh:16 * hh + 16], 64)
    mdn = np.stack([relayout_v(Dexp), relayout_v(inp["mamba_norm_g"][0][o:o + 1024])], 1)
    lc = inp["lru_conv_w"][0][:, o:o + 1024]
    lcw = np.stack([relayout_v(lc[j]) for j in range(4)] + [relayout_v(inp["lru_conv_b"][0][o:o + 1024])], 1)
    lv = np.stack([relayout_v(inp[k][0][o:o + 1024]) for k in ("lru_ba", "lru_bx", "lru_lambda")], 1)
    return dict(xT=np.ascontiguousarray(xb_.T), w_in=relayout_w(wc), cst=consts_np(),
                mcw=np.ascontiguousarray(mcw), mv=mv, mdn=np.ascontiguousarray(mdn), lcw=np.ascontiguousarray(lcw),
                lv=np.ascontiguousarray(lv), lwa=np.ascontiguousarray(inp["lru_wa"][0][8 * hh:8 * hh + 8]),
                lwx=np.ascontiguousarray(inp["lru_wx"][0][8 * hh:8 * hh + 8]))


def prep_C_weights(inp, i, w_out):
    cw = inp["ffn_conv_w"][i]
    return dict(w_out=relayout_w(w_out), w_up=relayout_w(inp["ffn_up"][i]), w_down=relayout_w(inp["ffn_down"][i]),
                w_gate=relayout_w(inp["ple_gate_w"][i]), w_ple=relayout_w(inp["ple_proj"][i]),
                vecs=np.ascontiguousarray(np.stack([relayout_v(inp[k][i]) for k in ("ln1_g", "ln1_b", "ln2_g", "ln2_b", "ple_norm_g", "ple_gate_b")], 1)),
                cvw=np.ascontiguousarray(np.stack([relayout_v(v) for v in (cw[0], cw[1], cw[2], inp["ffn_conv_b"][i])], 1)))


GROUPS = [[0, 1], [2, 3], [4, 5], [6, 7]]


def build_all(shapes, T=4096, TH=2048):
    import ml_dtypes
    nc = bass.Bass("TRN2", target_bir_lowering=False)
    D = 2048
    with ExitStack() as outer:
        din = {}
        for name, (shp, dt) in shapes.items():
            bdt = BF16 if dt == ml_dtypes.bfloat16 else F32
            din[name] = Tl(nc.dram_tensor(name, list(shp), bdt, kind="ExternalInput").ap(), name)
        def internal(name, shp, dt):
            return Tl(nc.dram_tensor(name, list(shp), dt, kind="Internal").ap(), name)
        def chunked(name, rows, cols, dt, W):
            return ChunkT([internal(f"{name}_{q}", [rows, W], dt) for q in range(cols // W)], W)
        ysnd0 = chunked("ysnd0", 1024, T, BF16, 1024)
        yrcv0 = chunked("yrcv0", 2048, T, BF16, 1024)
        ysnd1 = chunked("ysnd1", 2048, T, BF16, 512)
        yrcv1 = chunked("yrcv1", 4096, T, BF16, 512)
        NQ = TH // 512
        x1snd = [[internal(f"x1snd_{h}_{q}", [1024, 512], F32) for q in range(NQ)] for h in range(2)]
        x1rcv = [[internal(f"x1rcv_{h}_{q}", [2048, 512], F32) for q in range(NQ)] for h in range(2)]
        x1snd_k = [k for h in range(2) for c in x1snd[h] for k in c.k()]
        x1rcv_k = [k for h in range(2) for c in x1rcv[h] for k in c.k()]
        xo = Tl(nc.dram_tensor("xoT", [D, TH], F32, kind="ExternalOutput").ap(), "xoT")
        TBC = 512
        import os
        PH = os.environ.get("PHASES", "E,C0,O,C1").split(",")

        def gather1(P, s_, r_):
            P.coll("AllGather", s_[:], r_[:], GROUPS, s_.k(), r_.k() + [("collchain", 0)], "dcc")

        io = {k[2:]: v for k, v in din.items() if k.startswith("e_")}
        io.update(yo=ysnd0, xsrc_k=[], xsrc=lambda t0, hf: din["xT"][hf * 1024:(hf + 1) * 1024, t0:t0 + 256],
                  post_blk=lambda P, blk: gather1(P, ysnd0.chunks[blk // 4], yrcv0.chunks[blk // 4]) if blk % 4 == 3 else None)
        if "E" in PH:
            phase_E(nc, outer, io, T)
        nc.all_engine_barrier()
        io = {k[3:]: v for k, v in din.items() if k.startswith("c0_")}
        io.update(yrcv=yrcv0, pT=din["pT0"], msk=din["msk"], xsrc_k=[], xdst_k=x1snd_k,
                  xsrc=lambda blk: [(0, 16, din["xTc"][:, 0:2] if blk < 0 else din["xTc"][:, 2 + blk * TBC:2 + (blk + 1) * TBC])],
                  xdst=lambda blk: [(8 * h, 8 * h + 8, x1snd[h][blk][:, :]) for h in range(2)],
                  xdst_kf=lambda blk: x1snd[0][blk].k() + x1snd[1][blk].k(),
                  post_blk=lambda P, blk: [gather1(P, x1snd[h][blk], x1rcv[h][blk]) for h in range(2)])
        if "C0" in PH:
            phase_C(nc, outer, io, TH // TBC, 2048, TB=TBC, alpha=DN_ALPHA, THALF=TH)
        nc.all_engine_barrier()
        io = {k[2:]: v for k, v in din.items() if k.startswith("o_")}
        io.update(yo=ysnd1, xsrc_k=[],
                  xsrc=lambda t0, hf: x1rcv[hf][(t0 % TH) // 512][(t0 // TH) * 1024:(t0 // TH + 1) * 1024, :],
                  post_blk=lambda P, blk: gather1(P, ysnd1.chunks[blk], yrcv1.chunks[blk]))
        if "O" in PH:
            phase_O(nc, outer, io, T)
        nc.all_engine_barrier()
        io = {k[3:]: v for k, v in din.items() if k.startswith("c1_")}
        io.update(yrcv=yrcv1, pT=din["pT1"], msk=din["msk"], xsrc_k=[], xdst_k=xo.k(),
                  xsrc=lambda blk: [(8 * h, 8 * h + 8, x1rcv[h][NQ - 1][0:1024, 510:512] if blk < 0 else x1snd[h][blk][:, :]) for h in range(2)],
                  xdst=lambda blk: [(0, 16, xo[:, blk * TBC:(blk + 1) * TBC])])
        if "C1" in PH:
            phase_C(nc, outer, io, TH // TBC, 4096, TB=TBC, alpha=DN_ALPHA, THALF=TH)
    return nc


def kernel(**inp):
    inp = {k: np.asarray(v, dtype=np.float32) for k, v in inp.items()}
    x = inp["x"]
    p = inp["p"]
    B, T, D = x.shape
    TH = T // 2
    cores = list(range(8))
    perm_e = np.concatenate([np.arange(0, 512), np.arange(1024, 1536), np.arange(512, 1024), np.arange(1536, 2048)])
    perm_o = np.concatenate([np.arange(0, 1024), np.arange(2048, 3072), np.arange(1024, 2048), np.arange(3072, 4096)])
    c0 = prep_C_weights(inp, 0, inp["even_w_out"][0][perm_e])
    c1 = prep_C_weights(inp, 1, inp["odd_w_out"][0][perm_o])
    ins = []
    for c in cores:
        b, h = c // 2, c % 2
        m = {}
        e = prep_even(x[b], inp, h)
        m["xT"] = e.pop("xT")
        m.update({"e_" + k: v for k, v in e.items()})
        o = prep_odd(x[b][:8], inp, h)
        o.pop("xT")
        m.update({"o_" + k: v for k, v in o.items()})
        m.update({"c0_" + k: v for k, v in c0.items()})
        m.update({"c1_" + k: v for k, v in c1.items()})
        xTc = np.zeros((D, 2 + TH), np.float32)
        xTc[:, 2:] = x[b, h * TH:(h + 1) * TH].T
        if h > 0:
            xTc[:, :2] = x[b, TH - 2:TH].T
        m["xTc"] = xTc
        m["pT0"] = np.ascontiguousarray(p[0, b, h * TH:(h + 1) * TH].T)
        m["pT1"] = np.ascontiguousarray(p[1, b, h * TH:(h + 1) * TH].T)
        msk = np.zeros((128, 3), np.float32)
        msk[:, 0] = float(h > 0); msk[:, 1] = float(h == 0); msk[:, 2] = float(h == 1)
        m["msk"] = msk
        ins.append(m)
    shapes = {k: (v.shape, v.dtype) for k, v in ins[0].items()}
    nc = build_all(shapes, T=T, TH=TH)
    res = run_bass_kernel_spmd(nc, ins, core_ids=cores)
    out = np.empty_like(x)
    for c in cores:
        out[c // 2, (c % 2) * TH:(c % 2 + 1) * TH] = res.results[c]["xoT"].T
    return out
```

```python
from contextlib import ExitStack

import numpy as np
import concourse.bass as bass
import concourse.mybir as mybir
from concourse.bass_utils import run_bass_kernel_spmd

F32 = mybir.dt.float32
BF16 = mybir.dt.bfloat16
ALU = mybir.AluOpType
AF = mybir.ActivationFunctionType
AX = mybir.AxisListType

ENGS = ("pe", "dve", "act", "pool", "sp")


class Tl:
    def __init__(self, h, name, nsub=1):
        self.h, self.name, self.nsub = h, name, nsub

    def __getitem__(self, idx):
        return self.h[idx]

    def k(self, i=None):
        if i is None:
            return [(self.name, j) for j in range(self.nsub)]
        if isinstance(i, (list, tuple, range)):
            return [(self.name, j) for j in i]
        return [(self.name, i)]


class Prog:
    def __init__(self, nc):
        self.nc = nc
        self.ops = []
        self.stack = None
        self.ntiles = 0
        self.dsems = {}
        self.psum_names = set()
        Prog.ninst = getattr(Prog, "ninst", 0) + 1
        self.pfx = f"g{Prog.ninst}_"

    def sb(self, shape, dt, name=None, nsub=1):
        self.ntiles += 1
        name = self.pfx + (name or f"t{self.ntiles}")
        h = self.stack.enter_context(self.nc.sbuf_tensor(name, list(shape), dt))
        return Tl(h, name, nsub)

    def ps(self, shape, dt, name=None, nsub=1):
        self.ntiles += 1
        name = self.pfx + (name or f"p{self.ntiles}")
        h = self.stack.enter_context(self.nc.psum_tensor(name, list(shape), dt))
        self.psum_names.add(name)
        return Tl(h, name, 1)

    def dram(self, name, shape, dt, kind, nsub=1):
        h = self.nc.dram_tensor(name, list(shape), dt, kind=kind)
        return Tl(h.ap(), name, nsub)

    def op(self, eng, fn, r=(), w=(), acc=False):
        w = list(w) + [k for k in r if k[0] in self.psum_names and k not in w]
        self.ops.append(dict(eng=eng, fn=fn, r=list(r), w=list(w), dma=None, acc=acc))

    def dma(self, eng, out, in_, r=(), w=(), sem="d0", **kw):
        def fn(e):
            return e.dma_start(out=out, in_=in_, **kw)
        self.ops.append(dict(eng=eng, fn=fn, r=list(r), w=list(w), dma=sem, acc=False))

    def coll(self, kind, ins_ap, out_ap, groups, r, w, sem, inc=1):
        def fn(e):
            return e.collective_compute(kind, ALU.bypass, replica_groups=groups, ins=[ins_ap], outs=[out_ap])
        self.ops.append(dict(eng="pool", fn=fn, r=list(r), w=list(w), dma=sem, acc=False, inc=inc))

    def fence(self, tiles):
        sc = self.sb([128, 1], F32, f"fence{self.ntiles}")
        keys = [k for t in tiles for k in t.k()]
        self.op("pool", lambda e: e.memset(sc[:], 0.0), r=keys, w=keys + sc.k())

    def emit(self):
        nc = self.nc
        st = self.stack
        ss = getattr(self, "semstack", None) or st
        Prog.nprog = getattr(Prog, "nprog", 0) + 1
        pfx = f"s{Prog.nprog}_"
        sem = {e: ss.enter_context(nc.semaphore(pfx + e)) for e in ENGS}
        dnames = sorted({o["dma"] for o in self.ops if o["dma"]})
        for d in dnames:
            sem[d] = ss.enter_context(nc.semaphore(pfx + d))
        cnt = {s: 0 for s in sem}
        clock = {e: {} for e in ENGS}
        lastw = {}
        readers = {}
        per_eng = {e: [] for e in ENGS}

        def merge(a, b):
            for s, c in b.items():
                if a.get(s, 0) < c:
                    a[s] = c

        for o in self.ops:
            e = o["eng"]
            ck = clock[e]
            need = []
            for key in o["r"]:
                ev = lastw.get(key)
                if ev is not None:
                    need.append(ev)
            for key in o["w"]:
                ev = lastw.get(key)
                if ev is not None and not ((o["acc"] or e == "pe") and ev[3] == e):
                    need.append(ev)
                for ev in readers.get(key, ()):
                    if e == "pe" and ev[3] == "pe":
                        continue
                    need.append(ev)
            waits = {}
            for (s, c, evck, _) in need:
                if ck.get(s, 0) >= c:
                    continue
                if waits.get(s, 0) < c:
                    waits[s] = c
            for (s, c, evck, _) in need:
                if s in waits and waits[s] >= c and ck.get(s, 0) < c:
                    merge(ck, evck)
            for s, c in waits.items():
                if ck.get(s, 0) < c:
                    ck[s] = c
            if o["dma"]:
                s = o["dma"]
                cnt[s] += o.get("inc", 16)
                evs = s
            else:
                cnt[e] += 1
                evs = e
            evck = dict(ck)
            evck[evs] = cnt[evs]
            ev = (evs, cnt[evs], evck, e)
            for key in o["w"]:
                lastw[key] = ev
                readers[key] = []
            for key in o["r"]:
                readers.setdefault(key, []).append(ev)
            per_eng[e].append((o, sorted(waits.items()), evs))
        self.final = {s: c for s, c in cnt.items() if c > 0}
        self.sem = sem
        self.per_eng = per_eng
        nE = {e: len(v) for e, v in per_eng.items()}
        self.stats = nE

        block = st.enter_context(nc.Block())

        def run(engname, eng):
            for (o, waits, evs) in per_eng[engname]:
                for s, c in waits:
                    eng.wait_ge(sem[s], c)
                ins = o["fn"](eng)
                ins.then_inc(sem[evs], o.get("inc", 16) if o["dma"] else 1)
            if engname == "sp":
                for s, c in self.final.items():
                    eng.wait_ge(sem[s], c)

        @block.tensor
        def _(eng):
            run("pe", eng)

        @block.vector
        def _(eng):
            run("dve", eng)

        @block.scalar
        def _(eng):
            run("act", eng)

        @block.gpsimd
        def _(eng):
            run("pool", eng)

        @block.sync
        def _(eng):
            run("sp", eng)


def _tt(P, eng, out, in0, in1, op, r, w):
    P.op(eng, lambda e: e.tensor_tensor(out=out, in0=in0, in1=in1, op=op), r, w)


def _ts(P, eng, out, in0, s1, s2, op0, op1, r, w):
    if op1 is None:
        P.op(eng, lambda e: e.tensor_scalar(out=out, in0=in0, scalar1=s1, scalar2=None, op0=op0), r, w)
    else:
        P.op(eng, lambda e: e.tensor_scalar(out=out, in0=in0, scalar1=s1, scalar2=s2, op0=op0, op1=op1), r, w)


def _stt(P, eng, out, in0, sc, in1, op0, op1, r, w):
    P.op(eng, lambda e: e.scalar_tensor_tensor(out=out, in0=in0, scalar=sc, in1=in1, op0=op0, op1=op1), r, w)


def _act(P, out, in_, func, r, w, bias=None, scale=None):
    kw = {}
    if bias is not None:
        kw["bias"] = bias
    if scale is not None:
        kw["scale"] = scale
    P.op("act", lambda e: e.activation(out=out, in_=in_, func=func, **kw), r, w)


def _cp(P, eng, out, in_, r, w):
    if eng == "act":
        P.op("act", lambda e: e.copy(out=out, in_=in_), r, w)
    else:
        P.op(eng, lambda e: e.tensor_copy(out=out, in_=in_), r, w)


def _mm(P, out, lhsT, rhs, start, stop, r, w):
    P.op("pe", lambda e: e.matmul(out, lhsT=lhsT, rhs=rhs, start=start, stop=stop), r, w, acc=not start)


def _tr(P, out, in_, ident, r, w):
    P.op("pe", lambda e: e.transpose(out, in_, ident), r, w)


Prog.tt, Prog.ts, Prog.stt, Prog.act, Prog.cp, Prog.mm, Prog.tr = _tt, _ts, _stt, _act, _cp, _mm, _tr


class TlV(Tl):
    def __init__(self, ap, parent):
        self.h, self.name, self.nsub = ap, parent.name, parent.nsub

    def k(self, i=None):
        return [(self.name, j) for j in range(self.nsub)]


class ChunkT:
    def __init__(self, chunks, W):
        self.chunks, self.W = chunks, W

    def ap(self, rows, c0, n):
        q = c0 // self.W
        assert (c0 + n - 1) // self.W == q, (c0, n, self.W)
        return self.chunks[q][rows, (c0 % self.W):(c0 % self.W) + n]

    def k(self, i=None):
        return [k for c in self.chunks for k in c.k()]

    def kq(self, c0):
        return self.chunks[c0 // self.W].k()


LN_EPS = 1e-5
RMS_EPS = 1e-6


KS = 8


def relayout_w(w):
    K, N = w.shape
    return np.ascontiguousarray(w.reshape(K // 128, 128, N // 128, 128).transpose(2, 1, 0, 3))


def relayout_v(v):
    return np.ascontiguousarray(v.reshape(-1, 128).T)


class WStream:
    def __init__(self, P, kcmax=KS, nst=6, nbf=4, cast=("pool",), dmaq=("sp",)):
        self.P = P
        self.cast = cast
        self.dmaq = dmaq
        self.st = [P.sb([128, KS, 128], F32, f"wst{i}") for i in range(nst)]
        self.bf = [P.sb([128, KS, 128], BF16, f"wbf{i}") for i in range(nbf)]
        self.i = 0

    def get(self, wd, ct, k0, k1, cw=128):
        P = self.P
        i = self.i
        self.i += 1
        st, bf = self.st[i % len(self.st)], self.bf[i % len(self.bf)]
        n = k1 - k0
        P.dma(self.dmaq[i % len(self.dmaq)], st[:, :n, :cw], wd[ct][:, k0:k1, :], w=st.k(), sem=f"dw{i % len(self.st)}")
        P.cp(self.cast[i % len(self.cast)], bf[:, :n, :cw], st[:, :n, :cw], r=st.k(), w=bf.k())
        return bf


def proj(P, ws, wd, order, KC, act, N, pss, consume, cw=128):
    for j, ct in enumerate(order):
        ps = pss[j % len(pss)]
        for k0 in range(0, KC, KS):
            k1 = min(KC, k0 + KS)
            bf = ws.get(wd, ct, k0, k1, cw)
            for kk in range(k0, k1):
                P.mm(ps[:cw, :N], bf[:, kk - k0, :cw], act[:, kk, :N], kk == 0, kk == KC - 1, r=bf.k() + act.k(), w=ps.k())
        consume(j, ct, ps)


def phase_C(nc, outer, io, NB, CO, TB=512, D=2048, FF=5632, PLE=256, alpha=1.0, THALF=2048):
    P = Prog(nc)
    P.semstack = outer
    KY, KD, KF, KP = CO // 128, D // 128, FF // 128, PLE // 128
    NFT = 2 * KF
    NTOK = NB * TB
    with ExitStack() as st:
        P.stack = st
        yrcv, pT, msk = io["yrcv"], io["pT"], io["msk"]
        w_out, w_up, w_down, w_gate, w_ple, vecs, cvw = (io[k] for k in ("w_out", "w_up", "w_down", "w_gate", "w_ple", "vecs", "cvw"))
        ybB = P.sb([128, 4, TB], BF16, "ybB")

        vs = P.sb([128, 6, KD], F32, "vs")
        cw = P.sb([128, 4, NFT], F32, "cw")
        hms = P.sb([128, 3], F32, "hms")
        ones = P.sb([128, 128], F32, "ones")
        eT = P.sb([128, KD, TB], F32, "eT")
        assert KY <= 2 * KD
        yb = Tl(eT.h[:].rearrange("p k t -> p (k t)").bitcast(BF16)[:, :KY * TB].rearrange("p (k t) -> p k t", k=KY), eT.name)
        sx = P.sb([128, KD, TB], F32, "sx")
        hb = P.sb([128, KD, TB], BF16, "hb")
        actT = P.sb([128, KF, TB], BF16, "actT")
        pst = P.sb([128, KP, TB], F32, "pst")
        pb = P.sb([128, KP, TB], BF16, "pb")
        halo = P.sb([128, NFT, 2], F32, "halo", nsub=NFT)
        ub = [P.sb([128, TB + 2], F32, f"ub{i}") for i in range(3)]
        cv = [P.sb([128, TB], F32, f"cv{i}") for i in range(3)]
        sg = [P.sb([128, TB], F32, f"sg{i}") for i in range(2)]
        sq = [P.sb([128, TB], F32, f"sq{i}") for i in range(2)]
        mean = P.sb([128, TB], F32, "mean")
        rstd = P.sb([128, TB], F32, "rstd")
        tmp = [P.sb([128, TB], F32, f"tmp{i}") for i in range(2)]
        pss = [P.ps([128, 512], F32, f"ps{i}") for i in range(4)]
        pm = P.ps([128, 512], F32, "pm")
        pv = P.ps([128, 512], F32, "pv")
        ws = WStream(P, cast=("act", "act", "dve"))

        P.dma("act", vs[:], vecs[:], w=vs.k(), sem="dc")
        P.dma("act", cw[:], cvw[:], w=cw.k(), sem="dc")
        P.dma("act", hms[:], msk[:], w=hms.k(), sem="dc")
        P.fence([vs, cw, hms])
        P.op("pool", lambda e: e.memset(ones[:], 1.0 / D), w=ones.k())

        def layer_norm(N, gi, bi):
            for k in range(KD):
                s = sq[k % 2]
                P.tt("pool", s[:, :N], sx[:, k, :N], sx[:, k, :N], ALU.mult, r=sx.k(), w=s.k())
                P.mm(pm[:, :N], ones[:], sx[:, k, :N], k == 0, k == KD - 1, r=ones.k() + sx.k(), w=pm.k())
                P.mm(pv[:, :N], ones[:], s[:, :N], k == 0, k == KD - 1, r=ones.k() + s.k(), w=pv.k())
            P.cp("act", mean[:, :N], pm[:, :N], r=pm.k(), w=mean.k())
            t = tmp[0]
            P.tt("dve", t[:, :N], mean[:, :N], mean[:, :N], ALU.mult, r=mean.k(), w=t.k())
            P.tt("dve", t[:, :N], pv[:, :N], t[:, :N], ALU.subtract, r=pv.k() + t.k(), w=t.k())
            P.act(rstd[:, :N], t[:, :N], AF.Sqrt, r=t.k(), w=rstd.k(), bias=LN_EPS)
            P.op("dve", lambda e: e.reciprocal(out=rstd[:, :N], in_=rstd[:, :N]), r=rstd.k(), w=rstd.k())
            for k in range(KD):
                t = tmp[k % 2]
                P.tt("dve", t[:, :N], sx[:, k, :N], mean[:, :N], ALU.subtract, r=sx.k() + mean.k(), w=t.k())
                P.tt("dve", t[:, :N], t[:, :N], rstd[:, :N], ALU.mult, r=t.k() + rstd.k(), w=t.k())
                P.ts("dve", sx[:, k, :N], t[:, :N], vs[:, gi, k:k + 1], vs[:, bi, k:k + 1], ALU.mult, ALU.add,
                     r=t.k() + vs.k(), w=sx.k())
                P.cp("act", hb[:, k, :N], sx[:, k, :N], r=sx.k(), w=hb.k())

        for blk in range(-1, NB):
            N = 2 if blk < 0 else TB
            t0 = 0 if blk < 0 else 2 + blk * TB
            cA = 0 if blk < 0 else blk * TB
            cB = THALF - 2 if blk < 0 else THALF + blk * TB
            P.dma("act", yb[:, :, :N], yrcv.ap(slice(None), cA, N).rearrange("(k p) t -> p k t", p=128), r=yrcv.k(), w=yb.k(), sem="dy")
            for k0 in range(0, KY, 4):
                P.dma("act", ybB[:, :, :N], yrcv.ap(slice(k0 * 128, (k0 + 4) * 128), cB, N).rearrange("(k p) t -> p k t", p=128),
                      r=yrcv.k(), w=ybB.k(), sem="dyb")
                P.ts("dve", yb[:, k0:k0 + 4, :N], yb[:, k0:k0 + 4, :N], hms[:, 1:2], None, ALU.mult, None, r=yb.k() + hms.k(), w=yb.k())
                P.stt("dve", yb[:, k0:k0 + 4, :N], ybB[:, :, :N], hms[:, 2:3], yb[:, k0:k0 + 4, :N], ALU.mult, ALU.add,
                      r=ybB.k() + hms.k() + yb.k(), w=yb.k())
            for (k0_, k1_, ap_) in io["xsrc"](blk):
                P.dma("act", sx[:, k0_:k1_, :N], ap_.rearrange("(k p) t -> p k t", p=128), r=io["xsrc_k"], w=sx.k(), sem="dx")

            def c_out(j, ct, ps, N=N):
                P.stt("dve", sx[:, ct, :N], sx[:, ct, :N], float(alpha), ps[:, :N], ALU.mult, ALU.add,
                      r=sx.k() + ps.k(), w=sx.k())
            proj(P, ws, w_out, range(KD), KY, yb, N, pss, c_out)
            layer_norm(N, 0, 1)

            order = []
            for c in range(KF):
                order += [c, KF + c]

            def c_up(j, ct, ps, N=N, blk=blk):
                if blk < 0:
                    P.ts("dve", halo[:, ct, :], ps[:, :2], hms[:, 0:1], None, ALU.mult, None,
                         r=ps.k() + hms.k(), w=halo.k(ct))
                    return
                u = ub[j % 3]
                c_ = cv[j % 3]
                P.cp("act", u[:, 2:2 + N], ps[:, :N], r=ps.k(), w=u.k())
                P.cp("pool", u[:, 0:2], halo[:, ct, :], r=halo.k(ct), w=u.k())
                P.cp("act", halo[:, ct, :], u[:, N:N + 2], r=u.k(), w=halo.k(ct))
                P.ts("dve", c_[:, :N], u[:, 2:2 + N], cw[:, 2, ct:ct + 1], cw[:, 3, ct:ct + 1], ALU.mult, ALU.add,
                     r=u.k() + cw.k(), w=c_.k())
                P.stt("dve", c_[:, :N], u[:, 1:1 + N], cw[:, 1, ct:ct + 1], c_[:, :N], ALU.mult, ALU.add,
                      r=u.k() + cw.k() + c_.k(), w=c_.k())
                P.stt("dve", c_[:, :N], u[:, 0:N], cw[:, 0, ct:ct + 1], c_[:, :N], ALU.mult, ALU.add,
                      r=u.k() + cw.k() + c_.k(), w=c_.k())
                if ct < KF:
                    s = sg[ct % 2]
                    P.act(s[:, :N], c_[:, :N], AF.Silu, r=c_.k(), w=s.k())
                else:
                    c = ct - KF
                    s = sg[c % 2]
                    P.tt("dve", actT[:, c, :N], s[:, :N], c_[:, :N], ALU.mult, r=s.k() + c_.k(), w=actT.k())
            proj(P, ws, w_up, order, KD, hb, N, pss, c_up)
            if blk < 0:
                continue

            def c_down(j, ct, ps, N=N):
                P.stt("dve", sx[:, ct, :N], sx[:, ct, :N], float(alpha), ps[:, :N], ALU.mult, ALU.add,
                      r=sx.k() + ps.k(), w=sx.k())
            proj(P, ws, w_down, range(KD), KF, actT, N, pss, c_down)
            layer_norm(N, 2, 3)

            P.dma("act", pst[:], pT[:, blk * TB:(blk + 1) * TB].rearrange("(k p) t -> p k t", p=128), r=[], w=pst.k(), sem="dp")
            P.cp("pool", pb[:], pst[:], r=pst.k(), w=pb.k())

            def c_ple(j, ct, ps, N=N):
                P.cp("act", eT[:, ct, :N], ps[:, :N], r=ps.k(), w=eT.k())
                s = sq[ct % 2]
                P.tt("pool", s[:, :N], eT[:, ct, :N], eT[:, ct, :N], ALU.mult, r=eT.k(), w=s.k())
                P.mm(pv[:, :N], ones[:], s[:, :N], ct == 0, ct == KD - 1, r=ones.k() + s.k(), w=pv.k())
            proj(P, ws, w_ple, range(KD), KP, pb, N, pss, c_ple)
            P.act(rstd[:, :N], pv[:, :N], AF.Sqrt, r=pv.k(), w=rstd.k(), bias=RMS_EPS)
            P.op("dve", lambda e, N=N: e.reciprocal(out=rstd[:, :N], in_=rstd[:, :N]), r=rstd.k(), w=rstd.k())

            def c_gate(j, ct, ps, N=N, blk=blk):
                g = sg[ct % 2]
                P.act(g[:, :N], ps[:, :N], AF.Sigmoid, r=ps.k() + vs.k(), w=g.k(), bias=vs[:, 5, ct:ct + 1])
                t = tmp[ct % 2]
                P.stt("dve", t[:, :N], eT[:, ct, :N], vs[:, 4, ct:ct + 1], rstd[:, :N], ALU.mult, ALU.mult,
                      r=eT.k() + vs.k() + rstd.k(), w=t.k())
                P.tt("dve", t[:, :N], t[:, :N], g[:, :N], ALU.mult, r=t.k() + g.k(), w=t.k())
                P.tt("dve", eT[:, ct, :N], t[:, :N], sx[:, ct, :N], ALU.add, r=t.k() + sx.k(), w=eT.k())
            proj(P, ws, w_gate, range(KD), KD, hb, N, pss, c_gate)
            for (k0_, k1_, ap_) in io["xdst"](blk):
                P.dma("act", ap_.rearrange("(k p) t -> p k t", p=128), eT[:, k0_:k1_, :], r=eT.k(),
                      w=(io["xdst_kf"](blk) if "xdst_kf" in io else io["xdst_k"]), sem="do")
            if io.get("post_blk"):
                io["post_blk"](P, blk)
        if io.get("post"):
            io["post"](P)
        P.emit()
    return P


NEG = -30000.0


def consts_np():
    i = np.arange(128)
    ident = np.eye(128, dtype=np.float32)
    tri = (i[:, None] <= i[None, :]).astype(np.float32)
    maskneg = np.where(i[None, :] >= i[:, None], 0.0, NEG).astype(np.float32)
    return np.ascontiguousarray(np.stack([ident, tri, maskneg], 1))


def phase_O(nc, outer, io, T, TB=512, D=2048, L=128):
    P = Prog(nc)
    P.semstack = outer
    KD = D // 128
    NB = T // TB
    NCH = TB // L
    NH = 16
    with ExitStack() as st:
        P.stack = st
        w_in, cst, mcw, mv, mdn, lcw, lv, lwa, lwx, yo = (io[k] for k in ("w_in", "cst", "mcw", "mv", "mdn", "lcw", "lv", "lwa", "lwx", "yo"))

        cs = P.sb([128, 3, 128], F32, "cs")
        ident, tri, maskneg = cs[:, 0, :], cs[:, 1, :], cs[:, 2, :]
        ones = P.sb([128, 128], F32, "ones")
        onesg = P.sb([128, 128], F32, "onesg")
        mcs = P.sb([128, 5, 12], F32, "mcs")
        mvs = P.sb([128, 4], F32, "mvs")
        negA = P.sb([128, 1], F32, "negA")
        mds = P.sb([128, 2, 8], F32, "mds")
        lcs = P.sb([128, 5, 8], F32, "lcs")
        lvs = P.sb([128, 3, 8], F32, "lvs")
        c8 = P.sb([128, 8], F32, "c8")
        was = P.sb([128, 8, 128], F32, "was")
        wxs = P.sb([128, 8, 128], F32, "wxs")
        wab = P.sb([128, 8, 128], BF16, "wab")
        wxb = P.sb([128, 8, 128], BF16, "wxb")
        for (t_, d_) in ((cs, cst), (mcs, mcw), (mvs, mv), (mds, mdn), (lcs, lcw), (lvs, lv)):
            P.dma("act", t_[:], d_[:], w=t_.k(), sem="dc")
        P.dma("act", was[:], lwa[:].rearrange("n d e -> d n e"), w=was.k(), sem="dc")
        P.dma("act", wxs[:], lwx[:].rearrange("n d e -> d n e"), w=wxs.k(), sem="dc")
        P.fence([cs, mcs, mvs, mds, lcs, lvs, was, wxs])
        P.cp("pool", wab[:], was[:], r=was.k(), w=wab.k())
        P.cp("pool", wxb[:], wxs[:], r=wxs.k(), w=wxb.k())
        P.op("pool", lambda e: e.memset(ones[:], 1.0), w=ones.k())
        P.op("pool", lambda e: e.memset(onesg[:], 1.0 / 512.0), w=onesg.k())
        P.act(negA[:], mvs[:, 1:2], AF.Exp, r=mvs.k(), w=negA.k())
        P.ts("dve", negA[:], negA[:], -1.0, None, ALU.mult, None, r=negA.k(), w=negA.k())
        P.act(c8[:], lvs[:, 2, :], AF.Exp, r=lvs.k(), w=c8.k(), scale=-1.0)
        P.act(c8[:], c8[:], AF.Ln, r=c8.k(), w=c8.k(), bias=1.0)
        P.ts("dve", c8[:], c8[:], -8.0, None, ALU.mult, None, r=c8.k(), w=c8.k())

        xst = P.sb([128, KD // 2, TB], F32, "xst")
        xb = P.sb([128, KD, TB], BF16, "xb")
        ws = WStream(P)
        pss = [P.ps([128, 512], F32, f"ps{i}") for i in range(2)]
        pY = P.ps([128, 1024], F32, "pY")
        pO = P.ps([128, 1024], F32, "pO")
        pT = P.ps([128, 512], F32, "pT")
        pB = P.ps([128, 512], F32, "pB")
        pbs = [pB, pss[0], pss[1]]

        craw = P.sb([128, 12, TB + 3], F32, "craw", nsub=12)
        cfm = P.sb([128, 12, TB], F32, "cfm", nsub=12)
        bcb = P.sb([128, 4, TB], BF16, "bcb", nsub=4)
        P.op("pool", lambda e: e.memset(craw[:], 0.0), w=craw.k())
        dtf = P.sb([128, TB], F32, "dtf")
        Af = P.sb([128, TB], F32, "Af")
        S = P.sb([128, NH, 64], F32, "S")
        Sb = P.sb([128, NH, 64], BF16, "Sb")
        P.op("pool", lambda e: e.memset(S[:], 0.0), w=S.k())
        P.op("pool", lambda e: e.memset(Sb[:], 0.0), w=Sb.k())
        yfm = P.sb([128, 8, TB], F32, "yfm", nsub=8)
        xt = P.sb([128, NH, 64], F32, "xt")
        Xd = P.sb([128, NH, 64], BF16, "Xd")
        Xdec = P.sb([128, NH, 64], BF16, "Xdec")
        Bt = P.sb([128, 2, 128], BF16, "Bt")
        tm = P.sb([128, 3, NH], F32, "tm")
        cumT = P.sb([128, NH], F32, "cumT")
        ncum = P.sb([128, NH], F32, "ncum")
        ecum = P.sb([128, NH], F32, "ecum")
        decs = P.sb([128, NH], F32, "decs")
        etot = P.sb([128, NH], F32, "etot")
        Abc = P.sb([128, NH, 128], F32, "Abc")
        Gt = P.sb([128, 2, 128], F32, "Gt")
        wt = [P.sb([128, 128], F32, f"wt{i}") for i in range(2)]
        we = [P.sb([128, 128], F32, f"we{i}") for i in range(2)]
        Stt = [P.sb([128, 128], BF16, f"Stt{i}") for i in range(3)]
        Ytm = P.sb([128, NH, 64], F32, "Ytm")
        Yof = P.sb([128, NH, 64], F32, "Yof")
        ft = [P.sb([128, TB + 3], F32, f"ft{i}") for i in range(6)]
        fb = [P.sb([128, TB], BF16, f"fb{i}") for i in range(2)]
        ob = [P.sb([128, TB], BF16, f"ob{i}") for i in range(2)]
        lhalo = P.sb([128, 8, 3], F32, "lhalo", nsub=8)
        hst = P.sb([128, 8], F32, "hst", nsub=8)
        msq = P.sb([128, TB], F32, "msq")
        P.op("pool", lambda e: e.memset(lhalo[:], 0.0), w=lhalo.k())
        P.op("pool", lambda e: e.memset(hst[:], 0.0), w=hst.k())
        oi = [0]

        def conv4(out, src, cwt, ci, N, eng="dve"):
            P.ts(eng, out, src[:, 3:3 + N], cwt[:, 3, ci:ci + 1], cwt[:, 4, ci:ci + 1], ALU.mult, ALU.add,
                 r=src_k[0] + cwt_k[0], w=out_k[0])
            for j in range(3):
                P.stt(eng, out, src[:, j:j + N], cwt[:, j, ci:ci + 1], out, ALU.mult, ALU.add,
                      r=src_k[0] + cwt_k[0] + out_k[0], w=out_k[0])
        src_k, cwt_k, out_k = [None], [None], [None]

        for blk in range(NB):
            N = TB
            t0 = blk * TB
            for hf in range(2):
                P.dma("act", xst[:], io["xsrc"](t0, hf).rearrange("(k p) t -> p k t", p=128), r=io["xsrc_k"], w=xst.k(), sem="dx")
                P.cp("dve" if hf else "pool", xb[:, hf * (KD // 2):(hf + 1) * (KD // 2), :], xst[:], r=xst.k(), w=xb.k())

            def c_ssd(j, ct, ps):
                if ct < 12:
                    P.cp("act", craw[:, ct, 3:3 + N], ps[:, :N], r=ps.k(), w=craw.k(ct))
                    src_k[0], cwt_k[0], out_k[0] = craw.k(ct), mcs.k(), cfm.k(ct)
                    conv4(cfm[:, ct, :], craw[:, ct, :], mcs, ct, N)
                    P.cp("pool", craw[:, ct, 0:3], craw[:, ct, N:N + 3], r=craw.k(ct), w=craw.k(ct))
                    P.act(cfm[:, ct, :], cfm[:, ct, :], AF.Silu, r=cfm.k(ct), w=cfm.k(ct))
                    if ct >= 8:
                        P.cp("pool", bcb[:, ct - 8, :], cfm[:, ct, :], r=cfm.k(ct), w=bcb.k(ct - 8))
                else:
                    P.act(dtf[:], ps[:, :N], AF.Exp, r=ps.k() + mvs.k(), w=dtf.k(), bias=mvs[:, 0:1])
                    P.act(dtf[:], dtf[:], AF.Ln, r=dtf.k(), w=dtf.k(), bias=1.0)
                    P.ts("dve", Af[:], dtf[:], negA[:, 0:1], None, ALU.mult, None, r=dtf.k() + negA.k(), w=Af.k())
            proj(P, ws, w_in, range(13), KD, xb, N, pss, c_ssd)

            for ch in range(NCH):
                c0 = ch * L
                for half in range(2):
                    for q in range(4):
                        i = half * 4 + q
                        P.tr(pT[:, q * 128:(q + 1) * 128], cfm[:, i, c0:c0 + L], ident, r=cfm.k(i) + cs.k(), w=pT.k())
                    P.cp("act", xt[:, half * 8:(half + 1) * 8, :].rearrange("p h d -> p (h d)"), pT[:, :], r=pT.k(), w=xt.k())
                for g in range(2):
                    P.tr(pT[:, g * 128:(g + 1) * 128], cfm[:, 8 + g, c0:c0 + L], ident, r=cfm.k(8 + g) + cs.k(), w=pT.k())
                P.tr(pT[:, 256:384], dtf[:, c0:c0 + L], ident, r=dtf.k() + cs.k(), w=pT.k())
                P.tr(pT[:, 384:512], Af[:, c0:c0 + L], ident, r=Af.k() + cs.k(), w=pT.k())
                P.cp("act", Bt[:].rearrange("p g n -> p (g n)"), pT[:, 0:256], r=pT.k(), w=Bt.k())
                P.cp("dve", tm[:, 0, :], pT[:, 256:256 + NH], r=pT.k(), w=tm.k())
                P.cp("dve", tm[:, 1, :], pT[:, 384:384 + NH], r=pT.k(), w=tm.k())
                P.mm(pT[:, 0:NH], tri, tm[:, 1, :], True, True, r=cs.k() + tm.k(), w=pT.k())
                P.mm(pT[:, 32:32 + NH], ones[:], tm[:, 1, :], True, True, r=ones.k() + tm.k(), w=pT.k())
                P.cp("dve", cumT[:], pT[:, 0:NH], r=pT.k(), w=cumT.k())
                P.ts("dve", ncum[:], pT[:, 0:NH], -1.0, None, ALU.mult, None, r=pT.k(), w=ncum.k())
                P.act(ecum[:], pT[:, 0:NH], AF.Exp, r=pT.k(), w=ecum.k())
                P.act(etot[:], pT[:, 32:32 + NH], AF.Exp, r=pT.k(), w=etot.k())
                P.tt("dve", decs[:], pT[:, 32:32 + NH], cumT[:], ALU.subtract, r=pT.k() + cumT.k(), w=decs.k())
                P.act(decs[:], decs[:], AF.Exp, r=decs.k(), w=decs.k())
                P.tt("dve", Xd[:], xt[:], tm[:, 0, :].unsqueeze(2).to_broadcast([128, NH, 64]), ALU.mult,
                     r=xt.k() + tm.k(), w=Xd.k())
                P.tt("dve", Xdec[:], Xd[:], decs[:].unsqueeze(2).to_broadcast([128, NH, 64]), ALU.mult,
                     r=Xd.k() + decs.k(), w=Xdec.k())
                P.cp("act", Abc[:], tm[:, 1, :].unsqueeze(2).to_broadcast([128, NH, 128]), r=tm.k(), w=Abc.k())
                for g in range(2):
                    P.mm(pT[:, 128 + g * 128:256 + g * 128], bcb[:, g, c0:c0 + L], bcb[:, 2 + g, c0:c0 + L], True, True,
                         r=bcb.k(g) + bcb.k(2 + g), w=pT.k())
                P.cp("act", Gt[:].rearrange("p g n -> p (g n)"), pT[:, 128:384], r=pT.k(), w=Gt.k())
                for h in range(NH):
                    g = h // 8
                    P.mm(pO[:, h * 64:(h + 1) * 64], bcb[:, 2 + g, c0:c0 + L], Sb[:, h, :], True, True,
                         r=bcb.k(2 + g) + Sb.k(), w=pO.k())
                P.tt("dve", Yof[:], pO[:].rearrange("p (h d) -> p h d", h=NH), ecum[:].unsqueeze(2).to_broadcast([128, NH, 64]),
                     ALU.mult, r=pO.k() + ecum.k(), w=Yof.k())
                for h in range(NH):
                    g = h // 8
                    pb_ = pbs[h % 3]
                    P.mm(pb_[:, 0:128], Abc[:, h, :], tri, True, True, r=Abc.k() + cs.k(), w=pb_.k())
                    w_ = wt[h % 2]
                    e_ = we[h % 2]
                    s_ = Stt[h % 3]
                    P.tt("dve", w_[:], pb_[:, 0:128], maskneg, ALU.add, r=pb_.k() + cs.k(), w=w_.k())
                    P.act(e_[:], w_[:], AF.Exp, r=w_.k() + ncum.k(), w=e_.k(), bias=ncum[:, h:h + 1])
                    P.tt("dve", s_[:], e_[:], Gt[:, g, :], ALU.mult, r=e_.k() + Gt.k(), w=s_.k())
                    P.mm(pY[:, h * 64:(h + 1) * 64], s_[:], Xd[:, h, :], True, True, r=s_.k() + Xd.k(), w=pY.k())
                P.tt("dve", Ytm[:].rearrange("p h d -> p (h d)"), pY[:], Yof[:].rearrange("p h d -> p (h d)"), ALU.add,
                     r=pY.k() + Yof.k(), w=Ytm.k())
                for h in range(NH):
                    g = h // 8
                    P.mm(pO[:, h * 64:(h + 1) * 64], Bt[:, g, :], Xdec[:, h, :], True, True, r=Bt.k() + Xdec.k(), w=pO.k())
                P.tt("dve", S[:], S[:], etot[:].unsqueeze(2).to_broadcast([128, NH, 64]), ALU.mult, r=S.k() + etot.k(), w=S.k())
                P.tt("dve", S[:].rearrange("p h d -> p (h d)"), S[:].rearrange("p h d -> p (h d)"), pO[:], ALU.add,
                     r=S.k() + pO.k(), w=S.k())
                P.cp("act", Sb[:], S[:], r=S.k(), w=Sb.k())
                for half in range(2):
                    for q in range(4):
                        i = half * 4 + q
                        P.tr(pT[:, q * 128:(q + 1) * 128], Ytm[:, 2 * i:2 * i + 2, :].rearrange("p h d -> p (h d)"), ident,
                             r=Ytm.k() + cs.k(), w=pT.k())
                    for q in range(4):
                        i = half * 4 + q
                        P.cp("act" if q % 2 else "dve", yfm[:, i, c0:c0 + L], pT[:, q * 128:(q + 1) * 128], r=pT.k(), w=yfm.k(i))

            def c_z(j, ct, ps):
                i = ct - 13
                zs = ft[0]
                P.act(zs[:, :N], ps[:, :N], AF.Silu, r=ps.k(), w=zs.k())
                P.stt("dve", yfm[:, i, :], cfm[:, i, :], mds[:, 0, i:i + 1], yfm[:, i, :], ALU.mult, ALU.add,
                      r=cfm.k(i) + mds.k() + yfm.k(i), w=yfm.k(i))
                P.tt("dve", yfm[:, i, :], yfm[:, i, :], zs[:, :N], ALU.mult, r=yfm.k(i) + zs.k(), w=yfm.k(i))
                sq_ = ft[1 + (i % 2)]
                P.tt("pool", sq_[:, :N], yfm[:, i, :], yfm[:, i, :], ALU.mult, r=yfm.k(i), w=sq_.k())
                P.mm(pT[:, :N], onesg[:], sq_[:, :N], i % 4 == 0, i % 4 == 3, r=onesg.k() + sq_.k(), w=pT.k())
                if i % 4 == 3:
                    P.act(msq[:], pT[:, :N], AF.Sqrt, r=pT.k(), w=msq.k(), bias=RMS_EPS)
                    P.op("dve", lambda e: e.reciprocal(out=msq[:], in_=msq[:]), r=msq.k(), w=msq.k())
                    for i2 in range(i - 3, i + 1):
                        o_ = ob[oi[0] % 2]
                        oi[0] += 1
                        P.stt("dve", o_[:], yfm[:, i2, :], mds[:, 1, i2:i2 + 1], msq[:], ALU.mult, ALU.mult,
                              r=yfm.k(i2) + mds.k() + msq.k(), w=o_.k())
                        P.dma("act", yo.ap(slice(i2 * 128, (i2 + 1) * 128), t0, N), o_[:], r=o_.k(), w=yo.kq(t0), sem=f"do{oi[0] % 2}")
            proj(P, ws, w_in, range(13, 21), KD, xb, N, pss, c_z)

            order = []
            for n in range(8):
                order += [29 + n, 21 + n]
            xc, xcb, gl = ft[3], fb[0], ft[5]

            def c_lru(j, ct, ps):
                if ct >= 29:
                    n = ct - 29
                    u = ft[2]
                    P.cp("act", u[:, 3:3 + N], ps[:, :N], r=ps.k(), w=u.k())
                    P.cp("pool", u[:, 0:3], lhalo[:, n, :], r=lhalo.k(n), w=u.k())
                    P.cp("pool", lhalo[:, n, :], u[:, N:N + 3], r=u.k(), w=lhalo.k(n))
                    src_k[0], cwt_k[0], out_k[0] = u.k(), lcs.k(), xc.k()
                    conv4(xc[:, :N], u, lcs, n, N)
                    P.cp("act", xcb[:], xc[:, :N], r=xc.k(), w=xcb.k())
                    P.mm(pY[:, :N], wab[:, n, :], xcb[:], True, True, r=wab.k() + xcb.k(), w=pY.k())
                    P.mm(pO[:, :N], wxb[:, n, :], xcb[:], True, True, r=wxb.k() + xcb.k(), w=pO.k())
                    r_, i_ = ft[0], ft[1]
                    P.act(r_[:, :N], pY[:, :N], AF.Sigmoid, r=pY.k() + lvs.k(), w=r_.k(), bias=lvs[:, 0, n:n + 1])
                    P.act(i_[:, :N], pO[:, :N], AF.Sigmoid, r=pO.k() + lvs.k(), w=i_.k(), bias=lvs[:, 1, n:n + 1])
                    a_ = ft[4]
                    P.act(a_[:, :N], r_[:, :N], AF.Exp, r=r_.k() + c8.k(), w=a_.k(), scale=c8[:, n:n + 1])
                    P.tt("pool", r_[:, :N], a_[:, :N], a_[:, :N], ALU.mult, r=a_.k(), w=r_.k())
                    P.ts("dve", r_[:, :N], r_[:, :N], -1.0, 1.0, ALU.mult, ALU.add, r=r_.k(), w=r_.k())
                    P.act(r_[:, :N], r_[:, :N], AF.Sqrt, r=r_.k(), w=r_.k())
                    P.tt("dve", i_[:, :N], i_[:, :N], xc[:, :N], ALU.mult, r=i_.k() + xc.k(), w=i_.k())
                    P.tt("dve", i_[:, :N], i_[:, :N], r_[:, :N], ALU.mult, r=i_.k() + r_.k(), w=i_.k())
                    P.op("dve", lambda e, n=n: e.tensor_tensor_scan(out=xc[:, :N], data0=a_[:, :N], data1=i_[:, :N],
                                                                   initial=hst[:, n:n + 1], op0=ALU.mult, op1=ALU.add),
                         r=a_.k() + i_.k() + hst.k(n), w=xc.k())
                    P.cp("pool", hst[:, n:n + 1], xc[:, N - 1:N], r=xc.k(), w=hst.k(n))
                else:
                    n = ct - 21
                    y_ = ft[0]
                    P.cp("act", y_[:, :N], ps[:, :N], r=ps.k(), w=y_.k())
                    y2 = ft[1]
                    P.tt("pool", y2[:, :N], y_[:, :N], y_[:, :N], ALU.mult, r=y_.k(), w=y2.k())
                    P.ts("dve", y2[:, :N], y2[:, :N], 0.044715, 1.0, ALU.mult, ALU.add, r=y2.k(), w=y2.k())
                    P.tt("dve", y2[:, :N], y2[:, :N], y_[:, :N], ALU.mult, r=y2.k() + y_.k(), w=y2.k())
                    P.act(y2[:, :N], y2[:, :N], AF.Sigmoid, r=y2.k(), w=y2.k(), scale=1.5957691216057308)
                    P.tt("dve", y2[:, :N], y2[:, :N], y_[:, :N], ALU.mult, r=y2.k() + y_.k(), w=y2.k())
                    o_ = ob[oi[0] % 2]
                    oi[0] += 1
                    P.tt("dve", o_[:], y2[:, :N], xc[:, :N], ALU.mult, r=y2.k() + xc.k(), w=o_.k())
                    P.dma("act", yo.ap(slice(1024 + n * 128, 1024 + (n + 1) * 128), t0, N), o_[:], r=o_.k(), w=yo.kq(t0), sem=f"do{oi[0] % 2}")
            proj(P, ws, w_in, order, KD, xb, N, pss, c_lru)
            if io.get("post_blk"):
                io["post_blk"](P, blk)
        if io.get("post"):
            io["post"](P)
        P.emit()
    return P


NEG = -30000.0
C = 64
GN_EPS = 64e-5


def consts_e():
    i = np.arange(64)
    su = (i[None, :] > i[:, None]).astype(np.float32)
    iu = (i[None, :] >= i[:, None]).astype(np.float32)
    out = np.zeros((128, 5, 128), np.float32)
    out[:, 0, :] = np.eye(128)
    out[:64, 1, :] = np.concatenate([su, iu], 1)
    out[:64, 2, :64] = np.where(iu > 0, 0.0, NEG)
    out[:64, 2, 64:] = np.where(su > 0, 0.0, NEG)
    cm = np.ones(128, np.float32); cm[0] = 0; cm[64] = 0
    out[:, 3, :] = cm[None, :]
    return out


def sel_np():
    s = np.zeros((128, 8, 128), np.float32)
    for i in range(8):
        s[i, i, :] = 1.0
    return s


def relayout_w64(w):
    K, N = w.shape
    return np.ascontiguousarray(w.reshape(K // 128, 128, N // 64, 64).transpose(2, 1, 0, 3))


def neumann(P, Nall, Lall, X, pN, pL, pX, NHD, ident64b):
    P.tt("dve", X[:], Nall[:], ident64b, ALU.add, r=Nall.k(), w=X.k())
    for step in range(5):
        last = step == 4
        for h in range(NHD):
            P.mm(pL[0:64, h * 64:(h + 1) * 64], Nall[:, h, :], Lall[:, h, :], True, True, r=Nall.k() + Lall.k(), w=pL.k())
        if not last:
            for h in range(NHD):
                P.mm(pN[0:64, h * 64:(h + 1) * 64], Lall[:, h, :], Nall[:, h, :], True, True, r=Nall.k() + Lall.k(), w=pN.k())
        P.cp("act", Lall[:].rearrange("p h d -> p (h d)"), pL[0:64, :NHD * 64], r=pL.k(), w=Lall.k())
        if not last:
            P.cp("dve", Nall[:].rearrange("p h d -> p (h d)"), pN[0:64, :NHD * 64], r=pN.k(), w=Nall.k())
        for h in range(NHD):
            P.mm(pX[0:64, h * 64:(h + 1) * 64], Lall[:, h, :], X[:, h, :], True, True, r=Lall.k() + X.k(), w=pX.k())
        P.tt("dve", X[:].rearrange("p h d -> p (h d)"), X[:].rearrange("p h d -> p (h d)"), pX[0:64, :NHD * 64], ALU.add,
             r=X.k() + pX.k(), w=X.k())


def phase_E(nc, outer, io, T, TB=256, D=2048, do_gdn=True):
    P = Prog(nc)
    P.semstack = outer
    KD = D // 128
    NB = T // TB
    NCH = TB // C
    N = TB
    with ExitStack() as st:
        P.stack = st
        w_a, w_b, cst, rv, rmu, rw2, gcw, gv, yo = (io[k] for k in ("w_a", "w_b", "cst", "rv", "rmu", "rw2", "gcw", "gv", "yo"))

        cs = P.sb([128, 5, 128], F32, "cs")
        ident = cs[:, 0, :]
        mask2 = cs[0:64, 1, :]
        cmask = cs[:, 3, :]
        rvs = P.sb([64, 10, 8], F32, "rvs")
        omka = P.sb([64, 8], F32, "omka")
        rmus = P.sb([128, 4], F32, "rmus")
        xst = P.sb([128, KD // 2, TB], F32, "xst")
        assert (KD // 2) * TB == 2048
        w2s = TlV(xst.h[:].rearrange("p k t -> p (k t)").rearrange("p (a b) -> p a b", a=4), xst)
        w2b = P.sb([128, 4, 512], BF16, "w2b")
        gcs = P.sb([128, 4, 12], F32, "gcs")
        gvs = P.sb([128, 4], F32, "gvs")
        ones64 = P.sb([64, 64], F32, "ones64")
        ones64m = P.sb([64, 64], F32, "ones64m")
        for (t_, d_) in ((cs, cst), (rvs, rv), (rmus, rmu), (w2s, rw2), (gcs, gcw), (gvs, gv)):
            P.dma("act", t_[:], d_[:], w=t_.k(), sem="dc")
        selc = P.sb([128, 8, 128], F32, "selc")
        P.dma("act", selc[:], io["sel"][:], w=selc.k(), sem="dc")
        P.fence([cs, rvs, rmus, w2s, gcs, gvs, selc])
        P.cp("pool", w2b[:], w2s[:], r=w2s.k(), w=w2b.k())
        P.op("pool", lambda e: e.memset(ones64[:], 1.0), w=ones64.k())
        P.op("pool", lambda e: e.memset(ones64m[:], 1.0 / 64.0), w=ones64m.k())
        P.ts("dve", omka[:], rvs[:, 6, :], -1.0, 1.0, ALU.mult, ALU.add, r=rvs.k(), w=omka.k())

        xb = P.sb([128, KD, TB], BF16, "xb")
        ws = WStream(P)
        pq = [P.ps([128, 512], F32, f"pq{i}") for i in range(8)]
        pss = pq[0:2]

        halo1 = P.sb([128, 28, 1], F32, "halo1", nsub=28)
        P.op("pool", lambda e: e.memset(halo1[:], 0.0), w=halo1.k())
        raw = [P.sb([128, N + 1], F32, f"raw{i}") for i in range(3)]
        lo_b = P.sb([128, 4, N], BF16, "lo_b", nsub=4)
        ar128 = P.sb([128, 8, NCH, 128], F32, "ar", nsub=8)
        ar = Tl(ar128.h[0:64], ar128.name, 8)
        bt128 = P.sb([128, 8, N], F32, "bt", nsub=8)
        bt = Tl(bt128.h[0:64], bt128.name, 8)
        kt128 = P.sb([128, 8, N], F32, "kt", nsub=8)
        kt = Tl(kt128.h[0:64], kt128.name, 8)
        vv128 = P.sb([128, 8, N], F32, "vv", nsub=8)
        vv = Tl(vv128.h[0:64], vv128.name, 8)
        bon128 = P.sb([128, 8, N], F32, "bon", nsub=8)
        bon = Tl(bon128.h[0:64], bon128.name, 8)
        gg128 = P.sb([128, 8, N], F32, "gg", nsub=8)
        gg = Tl(gg128.h[0:64], gg128.name, 8)
        PC = P.sb([64, 8, NCH], F32, "PC", nsub=8)
        yall = P.sb([64, 8, N], F32, "yall")
        H = P.sb([64, 8, 64], F32, "H")
        P.op("pool", lambda e: e.memset(H[:], 0.0), w=H.k())
        rr = [P.sb([64, N], F32, f"rr{i}") for i in range(2)]
        kk_ = [P.sb([64, N], F32, f"kk{i}") for i in range(2)]
        tA = [P.sb([64, N], F32, f"tA{i}") for i in range(8)]
        vtm = P.sb([64, 8, 64], F32, "vtm")
        btm = P.sb([64, 8, 64], F32, "btm")
        ktm = P.sb([64, 8, 64], F32, "ktm")
        sc1 = P.sb([64, 8, 128], F32, "sc1")
        sc2 = P.sb([64, 8, 128], F32, "sc2")
        Nall = P.sb([64, 8, 64], F32, "Nall")
        Lall = P.sb([64, 8, 64], F32, "Lall")
        X = P.sb([64, 8, 64], F32, "X")
        rhs_sb = P.sb([64, 8, 64], F32, "rhs_sb")
        U = P.sb([64, 8, 64], F32, "U")
        ob = [P.sb([128, N], BF16, f"ob{i}") for i in range(2)]
        oi = [0]
        ident64b = cs[0:64, 0, 0:64].unsqueeze(1).to_broadcast([64, 8, 64])


        ones128m = P.sb([128, 128], F32, "ones128m")
        ones128 = P.sb([128, 128], F32, "ones128")
        P.op("pool", lambda e: e.memset(ones128m[:], 1.0 / 128.0), w=ones128m.k())
        P.op("pool", lambda e: e.memset(ones128[:], 1.0), w=ones128.k())
        gnegA = P.sb([128, 1], F32, "gnegA")
        P.act(gnegA[:], gvs[:, 0:1], AF.Exp, r=gvs.k(), w=gnegA.k())
        P.ts("dve", gnegA[:], gnegA[:], -1.0, None, ALU.mult, None, r=gnegA.k(), w=gnegA.k())
        ghalo = P.sb([128, 12, 3], F32, "ghalo", nsub=12)
        P.op("pool", lambda e: e.memset(ghalo[:], 0.0), w=ghalo.k())
        graw = [P.sb([128, N + 3], F32, f"graw{i}") for i in range(2)]
        gq = TlV(bt128.h[:, 0:4, :], bt128)
        gk = TlV(bt128.h[:, 4:8, :], bt128)
        gvv = TlV(kt128.h[:, 0:4, :], kt128)
        kq = TlV(ar128.h[:, 0:4], ar128)
        vb = TlV(kt128.h[:, 4:8, :], kt128)
        nkbg = TlV(vv128.h[:, 0:4, :], vv128)
        qd = TlV(vv128.h[:, 4:8, :], vv128)
        kdec = TlV(bon128.h[:, 0:4, :], bon128)
        gcbc = TlV(bon128.h[:, 4:8, :], bon128)
        egc = TlV(gg128.h[:, 0:4, :], gg128)
        oall = TlV(gg128.h[:, 4:8, :], gg128)
        sgm = P.sb([128, N], F32, "sgm")
        gcf = P.sb([128, N], F32, "gcf")
        gS = P.sb([128, 4, 128], F32, "gS")
        P.op("pool", lambda e: e.memset(gS[:], 0.0), w=gS.k())
        gT = [P.sb([128, N], F32, f"gT{i}") for i in range(3)]
        ngc = P.sb([64, 128], F32, "ngc")
        wd_ = P.sb([64, 4, 64], F32, "wd_")
        Di = P.sb([64, 4, 64], F32, "Di")
        Ds = P.sb([64, 4, 64], F32, "Ds")
        attT = P.sb([64, 4, 64], F32, "attT")
        kdtm = P.sb([64, 4, 128], F32, "kdtm")
        ident64b4 = cs[0:64, 0, 0:64].unsqueeze(1).to_broadcast([64, 4, 64])
        mneg_i = cs[0:64, 2, 0:64]
        su_b4 = cs[0:64, 1, 0:64].unsqueeze(1).to_broadcast([64, 4, 64])

        cmk128 = P.sb([128, N], F32, "cmk128")
        for i_ in range(N // 128):
            P.cp("pool", cmk128[:, i_ * 128:(i_ + 1) * 128], cs[:, 3, :], r=cs.k(), w=cmk128.k())
        cmk = P.sb([64, N], F32, "cmk")
        for i_ in range(N // 128):
            P.cp("pool", cmk[:, i_ * 128:(i_ + 1) * 128], cs[0:64, 3, :], r=cs.k(), w=cmk.k())

        for blk in range(NB):
            t0 = blk * TB
            for hf in range(2):
                P.dma("act", xst[:], io["xsrc"](t0, hf).rearrange("(k p) t -> p k t", p=128), r=io["xsrc_k"], w=xst.k(), sem="dx")
                P.cp("dve" if hf else "pool", xb[:, hf * (KD // 2):(hf + 1) * (KD // 2), :], xst[:], r=xst.k(), w=xb.k())

            def c_lo(j, ct, ps):
                rawt = raw[j % 3]
                P.cp("act", rawt[:, 1:N + 1], ps[:, :N], r=ps.k(), w=rawt.k())
                P.cp("pool", rawt[:, 0:1], halo1[:, 24 + ct, :], r=halo1.k(24 + ct), w=rawt.k())
                P.cp("pool", halo1[:, 24 + ct, :], rawt[:, N:N + 1], r=rawt.k(), w=halo1.k(24 + ct))
                d32 = raw[(j + 1) % 3]
                P.tt("dve", d32[:, 0:N], rawt[:, 0:N], rawt[:, 1:N + 1], ALU.subtract, r=rawt.k(), w=d32.k())
                P.stt("dve", d32[:, 0:N], d32[:, 0:N], rmus[:, ct:ct + 1], rawt[:, 1:N + 1], ALU.mult, ALU.add,
                      r=d32.k() + rmus.k() + rawt.k(), w=d32.k())
                if ct == 0:
                    P.act(lo_b[:, 0, :], d32[:, 0:N], AF.Tanh, r=d32.k(), w=lo_b.k(0))
                elif ct == 1:
                    P.cp("act", lo_b[:, 1, :], d32[:, 0:N], r=d32.k(), w=lo_b.k(1))
                else:
                    P.act(lo_b[:, ct, :], d32[:, 0:N], AF.Sigmoid, r=d32.k(), w=lo_b.k(ct))
            proj(P, ws, w_b, range(4), KD, xb, N, pss, c_lo)

            def c_rkv(j, ct, ps):
                h, which = ct // 3, ct % 3
                rawt = raw[j % 3]
                P.cp("act", rawt[0:64, 1:N + 1], ps[0:64, :N], r=ps.k(), w=rawt.k())
                P.cp("pool", rawt[0:64, 0:1], halo1[0:64, ct, :], r=halo1.k(ct), w=rawt.k())
                P.cp("pool", halo1[0:64, ct, :], rawt[0:64, N:N + 1], r=rawt.k(), w=halo1.k(ct))
                dst = (rr[h % 2], kk_[h % 2], None)[which]
                dst_ap = vv[:, h, :] if which == 2 else dst[:, :]
                dst_k = vv.k(h) if which == 2 else dst.k()
                d = tA[7]
                P.tt("dve", d[:, :], rawt[0:64, 0:N], rawt[0:64, 1:N + 1], ALU.subtract, r=rawt.k(), w=d.k())
                P.stt("dve", dst_ap, d[:, :], rvs[:, which, h:h + 1], rawt[0:64, 1:N + 1], ALU.mult, ALU.add,
                      r=d.k() + rvs.k() + rawt.k(), w=dst_k)
                if which != 2:
                    return
                r_, k_ = rr[h % 2], kk_[h % 2]
                hs = slice(h * 64, (h + 1) * 64)
                pw, pa, pg, pn = pq[2], pq[3], pq[4], pq[5]
                P.mm(pw[0:64, :N], w2b[:, 0, hs], lo_b[:, 0, :], True, True, r=w2b.k() + lo_b.k(0), w=pw.k())
                P.mm(pa[0:64, :N], w2b[:, 1, hs], lo_b[:, 1, :], True, True, r=w2b.k() + lo_b.k(1), w=pa.k())
                P.mm(pg[0:64, :N], w2b[:, 2, hs], lo_b[:, 2, :], True, False, r=w2b.k() + lo_b.k(2), w=pg.k())
                P.mm(pg[0:64, :N], w2b[:, 3, hs], lo_b[:, 3, :], False, True, r=w2b.k() + lo_b.k(3), w=pg.k())
                lw, cl, asig, e1, e2, kkn, tmp = tA[0], tA[1], tA[2], tA[3], tA[4], tA[5], tA[6]
                P.act(lw[:], pw[0:64, :N], AF.Sigmoid, r=pw.k() + rvs.k(), w=lw.k(), bias=rvs[:, 3, h:h + 1])
                P.ts("dve", lw[:], lw[:], -0.6065306597126334, None, ALU.mult, None, r=lw.k(), w=lw.k())
                P.act(asig[:], pa[0:64, :N], AF.Sigmoid, r=pa.k() + rvs.k(), w=asig.k(), bias=rvs[:, 4, h:h + 1])
                P.cp("act", gg[:, h, :], pg[0:64, :N], r=pg.k(), w=gg.k(h))
                P.op("dve", lambda e: e.tensor_tensor_scan(out=cl[:], data0=cmk[:], data1=lw[:], initial=0.0,
                                                           op0=ALU.mult, op1=ALU.add),
                     r=lw.k() + cmk.k(), w=cl.k())
                P.act(e1[:], cl[:], AF.Exp, r=cl.k(), w=e1.k())
                P.tt("dve", ar[:, h, :, 64:128], r_[:].rearrange("p (c t) -> p c t", t=64), e1[:].rearrange("p (c t) -> p c t", t=64),
                     ALU.mult, r=r_.k() + e1.k(), w=ar.k(h))
                P.cp("pool", PC[:, h, :], e1[:].rearrange("p (c t) -> p c t", t=64)[:, :, 63], r=e1.k(), w=PC.k(h))
                P.act(e2[:], cl[:], AF.Exp, r=cl.k(), w=e2.k(), scale=-1.0)
                P.ts("dve", kkn[:], k_[:], rvs[:, 5, h:h + 1], None, ALU.mult, None, r=k_.k() + rvs.k(), w=kkn.k())
                P.tt("pool", tmp[:], kkn[:], kkn[:], ALU.mult, r=kkn.k(), w=tmp.k())
                P.mm(pn[0:64, :N], ones64[:], tmp[:], True, True, r=ones64.k() + tmp.k(), w=pn.k())
                P.act(tmp[:], pn[0:64, :N], AF.Sqrt, r=pn.k(), w=tmp.k(), bias=1e-6)
                P.op("dve", lambda e: e.reciprocal(out=tmp[:], in_=tmp[:]), r=tmp.k(), w=tmp.k())
                P.tt("dve", kkn[:], kkn[:], tmp[:], ALU.mult, r=kkn.k() + tmp.k(), w=kkn.k())
                P.tt("dve", tmp[:], kkn[:], asig[:], ALU.mult, r=kkn.k() + asig.k(), w=tmp.k())
                P.tt("dve", bt[:, h, :], tmp[:], e2[:], ALU.mult, r=tmp.k() + e2.k(), w=bt.k(h))
                P.tt("dve", tmp[:], cl[:], lw[:], ALU.subtract, r=cl.k() + lw.k(), w=tmp.k())
                P.act(tmp[:], tmp[:], AF.Exp, r=tmp.k(), w=tmp.k())
                P.stt("dve", ar[:, h, :, 0:64], kkn[:].rearrange("p (c t) -> p c t", t=64), -1.0,
                      tmp[:].rearrange("p (c t) -> p c t", t=64), ALU.mult, ALU.mult, r=kkn.k() + tmp.k(), w=ar.k(h))
                P.ts("dve", tmp[:], asig[:], rvs[:, 6, h:h + 1], omka[:, h:h + 1], ALU.mult, ALU.add, r=asig.k() + rvs.k() + omka.k(), w=tmp.k())
                P.tt("dve", tmp[:], tmp[:], k_[:], ALU.mult, r=tmp.k() + k_.k(), w=tmp.k())
                P.tt("dve", kt[:, h, :], tmp[:], e2[:], ALU.mult, r=tmp.k() + e2.k(), w=kt.k(h))
                P.stt("dve", tmp[:], tmp[:], rvs[:, 7, h:h + 1], r_[:], ALU.mult, ALU.mult, r=tmp.k() + rvs.k() + r_.k(), w=tmp.k())
                P.mm(pn[0:64, :N], ones64[:], tmp[:], True, True, r=ones64.k() + tmp.k(), w=pn.k())
                P.tt("dve", bon[:, h, :], pn[0:64, :N], vv[:, h, :], ALU.mult, r=pn.k() + vv.k(h), w=bon.k(h))
            proj(P, ws, w_a, range(24), KD, xb, N, pss, c_rkv, cw=64)

            for c in range(NCH):
                cc = slice(c * C, (c + 1) * C)
                pT1, pT2, pT3 = pq[7], pq[6], pq[5]
                for h in range(8):
                    hs = slice(h * 64, (h + 1) * 64)
                    P.tr(pT1[0:64, hs], vv[:, h, cc], ident[0:64, 0:64], r=vv.k(h) + cs.k(), w=pT1.k())
                    P.tr(pT2[0:64, hs], bt[:, h, cc], ident[0:64, 0:64], r=bt.k(h) + cs.k(), w=pT2.k())
                    P.tr(pT3[0:64, hs], kt[:, h, cc], ident[0:64, 0:64], r=kt.k(h) + cs.k(), w=pT3.k())
                P.cp("act", vtm[:].rearrange("p h d -> p (h d)"), pT1[0:64, :], r=pT1.k(), w=vtm.k())
                P.cp("dve", btm[:].rearrange("p h d -> p (h d)"), pT2[0:64, :], r=pT2.k(), w=btm.k())
                P.cp("act", ktm[:].rearrange("p h d -> p (h d)"), pT3[0:64, :], r=pT3.k(), w=ktm.k())
                for h in range(8):
                    pa_ = pq[0] if h < 4 else pq[1]
                    pb_ = pq[2] if h < 4 else pq[3]
                    o = (h % 4) * 128
                    P.mm(pa_[0:64, o:o + 128], bt[:, h, cc], ar[:, h, c, :], True, True, r=bt.k(h) + ar.k(h), w=pa_.k())
                    P.mm(pb_[0:64, o:o + 128], kt[:, h, cc], ar[:, h, c, :], True, True, r=kt.k(h) + ar.k(h), w=pb_.k())
                m2b = mask2.unsqueeze(1).to_broadcast([64, 4, 128])
                for half in range(2):
                    P.tt("dve", sc1[:, half * 4:(half + 1) * 4, :], pq[half][0:64, :].rearrange("p (h d) -> p h d", h=4), m2b, ALU.mult,
                         r=pq[half].k() + cs.k(), w=sc1.k())
                    P.tt("dve", sc2[:, half * 4:(half + 1) * 4, :], pq[2 + half][0:64, :].rearrange("p (h d) -> p h d", h=4), m2b, ALU.mult,
                         r=pq[2 + half].k() + cs.k(), w=sc2.k())
                P.cp("pool", Nall[:], sc1[:, :, 0:64], r=sc1.k(), w=Nall.k())
                pL, pN, pX = pq[4], pq[5], pq[6]
                for h in range(8):
                    P.tr(pL[0:64, h * 64:(h + 1) * 64], sc1[:, h, 0:64], ident[0:64, 0:64], r=sc1.k() + cs.k(), w=pL.k())
                P.cp("act", Lall[:].rearrange("p h d -> p (h d)"), pL[0:64, :], r=pL.k(), w=Lall.k())
                neumann(P, Nall, Lall, X, pN, pL, pX, 8, ident64b)
                pR, pU, pY, pH = pq[0], pq[1], pq[2], pq[3]
                for h in range(8):
                    hs = slice(h * 64, (h + 1) * 64)
                    P.mm(pR[0:64, hs], ar[:, h, c, 0:64], H[:, h, :], True, False, r=ar.k(h) + H.k(), w=pR.k())
                    P.mm(pR[0:64, hs], sc2[:, h, 0:64], vtm[:, h, :], False, True, r=sc2.k() + vtm.k(), w=pR.k())
                P.cp("act", rhs_sb[:].rearrange("p h d -> p (h d)"), pR[0:64, :], r=pR.k(), w=rhs_sb.k())
                for h in range(8):
                    hs = slice(h * 64, (h + 1) * 64)
                    P.mm(pU[0:64, hs], X[:, h, :], rhs_sb[:, h, :], True, True, r=X.k() + rhs_sb.k(), w=pU.k())
                P.cp("act", U[:].rearrange("p h d -> p (h d)"), pU[0:64, :], r=pU.k(), w=U.k())
                for h in range(8):
                    hs = slice(h * 64, (h + 1) * 64)
                    P.mm(pY[0:64, hs], H[:, h, :], ar[:, h, c, 64:128], True, False, r=ar.k(h) + H.k(), w=pY.k())
                    P.mm(pY[0:64, hs], U[:, h, :], sc1[:, h, 64:128], False, False, r=U.k() + sc1.k(), w=pY.k())
                    P.mm(pY[0:64, hs], vtm[:, h, :], sc2[:, h, 64:128], False, True, r=vtm.k() + sc2.k(), w=pY.k())
                    P.mm(pH[0:64, hs], btm[:, h, :], U[:, h, :], True, False, r=btm.k() + U.k(), w=pH.k())
                    P.mm(pH[0:64, hs], ktm[:, h, :], vtm[:, h, :], False, True, r=ktm.k() + vtm.k(), w=pH.k())
                P.cp("act", yall[:, :, cc], pY[0:64, :].rearrange("p (h d) -> p h d", h=8), r=pY.k(), w=yall.k())
                P.tt("dve", H[:].rearrange("p h d -> p (h d)"), H[:].rearrange("p h d -> p (h d)"), pH[0:64, :], ALU.add, r=H.k() + pH.k(), w=H.k())
                P.tt("dve", H[:], H[:], PC[:, :, c].unsqueeze(2).to_broadcast([64, 8, 64]), ALU.mult, r=H.k() + PC.k(), w=H.k())

            for h in range(8):
                pm, pv = pq[4], pq[5]
                y_ = yall[:, h, :]
                sq_, mean, t_ = tA[0], tA[1], tA[2]
                P.tt("pool", sq_[:], y_, y_, ALU.mult, r=yall.k(), w=sq_.k())
                P.mm(pm[0:64, :N], ones64m[:], y_, True, True, r=ones64m.k() + yall.k(), w=pm.k())
                P.mm(pv[0:64, :N], ones64m[:], sq_[:], True, True, r=ones64m.k() + sq_.k(), w=pv.k())
                P.cp("act", mean[:], pm[0:64, :N], r=pm.k(), w=mean.k())
                P.tt("dve", t_[:], mean[:], mean[:], ALU.mult, r=mean.k(), w=t_.k())
                P.tt("dve", t_[:], pv[0:64, :N], t_[:], ALU.subtract, r=pv.k() + t_.k(), w=t_.k())
                P.act(t_[:], t_[:], AF.Sqrt, r=t_.k(), w=t_.k(), bias=GN_EPS)
                P.op("dve", lambda e, t_=t_: e.reciprocal(out=t_[:], in_=t_[:]), r=t_.k(), w=t_.k())
                P.tt("dve", mean[:], y_, mean[:], ALU.subtract, r=yall.k() + mean.k(), w=mean.k())
                P.tt("dve", mean[:], mean[:], t_[:], ALU.mult, r=mean.k() + t_.k(), w=mean.k())
                P.ts("dve", mean[:], mean[:], rvs[:, 8, h:h + 1], rvs[:, 9, h:h + 1], ALU.mult, ALU.add, r=mean.k() + rvs.k(), w=mean.k())
                P.tt("dve", mean[:], mean[:], bon[:, h, :], ALU.add, r=mean.k() + bon.k(h), w=mean.k())
                o_ = ob[oi[0] % 2]
                oi[0] += 1
                P.tt("dve", o_[0:64, :], mean[:], gg[:, h, :], ALU.mult, r=mean.k() + gg.k(h), w=o_.k())
                P.dma("act", yo.ap(slice(h * 64, (h + 1) * 64), t0, N), o_[0:64, :], r=o_.k(), w=yo.kq(t0), sem=f"do{oi[0] % 2}")

            if not do_gdn:
                continue
            def c_ba(j, ct, ps):
                P.act(sgm[:], ps[:, :N], AF.Sigmoid, r=ps.k(), w=sgm.k())
                t_ = gT[0]
                P.act(t_[:], ps[:, :N], AF.Exp, r=ps.k() + gvs.k(), w=t_.k(), bias=gvs[:, 1:2])
                P.act(t_[:], t_[:], AF.Ln, r=t_.k(), w=t_.k(), bias=1.0)
                P.ts("dve", t_[:], t_[:], gnegA[:, 0:1], None, ALU.mult, None, r=t_.k() + gnegA.k(), w=t_.k())
                P.op("dve", lambda e: e.tensor_tensor_scan(out=gcf[:], data0=cmk128[:], data1=t_[:], initial=0.0,
                                                           op0=ALU.mult, op1=ALU.add), r=t_.k() + cmk128.k(), w=gcf.k())
                for h in range(4):
                    pb_ = pq[2 + (h % 2)]
                    P.mm(pb_[:, :N], selc[:, 4 + h, :], gcf[:], True, True, r=selc.k() + gcf.k(), w=pb_.k())
                    P.cp("act", gcbc[:, h, :], pb_[:, :N], r=pb_.k(), w=gcbc.k(h))
                    P.act(egc[:, h, :], pb_[:, :N], AF.Exp, r=pb_.k(), w=egc.k(h))
            proj(P, ws, w_b, [20], KD, xb, N, pss, c_ba)

            def c_qkv(j, ct, ps):
                ti = ct - 4
                which, h = ti // 4, ti % 4
                u = graw[j % 2]
                P.cp("act", u[:, 3:3 + N], ps[:, :N], r=ps.k(), w=u.k())
                P.cp("pool", u[:, 0:3], ghalo[:, ti, :], r=ghalo.k(ti), w=u.k())
                P.cp("pool", ghalo[:, ti, :], u[:, N:N + 3], r=u.k(), w=ghalo.k(ti))
                dst = (gq, gk, gvv)[which]
                o_ = dst[:, h, :]
                P.ts("dve", o_, u[:, 3:3 + N], gcs[:, 3, ti:ti + 1], None, ALU.mult, None, r=u.k() + gcs.k(), w=dst.k(h))
                for jj in range(3):
                    P.stt("dve", o_, u[:, jj:jj + N], gcs[:, jj, ti:ti + 1], o_, ALU.mult, ALU.add, r=u.k() + gcs.k() + dst.k(h), w=dst.k(h))
                P.act(o_, o_, AF.Silu, r=dst.k(h), w=dst.k(h))
                if which < 2:
                    sq_ = gT[1]
                    pn = pq[4]
                    P.tt("pool", sq_[:], o_, o_, ALU.mult, r=dst.k(h), w=sq_.k())
                    P.mm(pn[:, :N], ones128[:], sq_[:], True, True, r=ones128.k() + sq_.k(), w=pn.k())
                    P.act(sq_[:], pn[:, :N], AF.Sqrt, r=pn.k(), w=sq_.k(), bias=1e-6)
                    P.op("dve", lambda e: e.reciprocal(out=sq_[:], in_=sq_[:]), r=sq_.k(), w=sq_.k())
                    if which == 0:
                        P.stt("dve", o_, o_, float(128 ** -0.5), sq_[:], ALU.mult, ALU.mult, r=dst.k(h) + sq_.k(), w=dst.k(h))
                    else:
                        P.tt("dve", o_, o_, sq_[:], ALU.mult, r=dst.k(h) + sq_.k(), w=dst.k(h))
                if which != 2:
                    return
                pbb = pq[5]
                P.mm(pbb[:, :N], selc[:, h, :], sgm[:], True, True, r=selc.k() + sgm.k(), w=pbb.k())
                c3 = lambda ap: ap.rearrange("p (c t) -> p c t", t=64)
                P.tt("dve", kq[:, h, :, 0:64], c3(gk[:, h, :]), c3(pbb[:, :N]), ALU.mult, r=gk.k(h) + pbb.k(), w=kq.k(h))
                P.cp("pool", kq[:, h, :, 64:128], c3(gq[:, h, :]), r=gq.k(h), w=kq.k(h))
                P.tt("dve", vb[:, h, :], gvv[:, h, :], pbb[:, :N], ALU.mult, r=gvv.k(h) + pbb.k(), w=vb.k(h))
                t1 = gT[2]
                P.tt("dve", t1[:], gk[:, h, :], pbb[:, :N], ALU.mult, r=gk.k(h) + pbb.k(), w=t1.k())
                P.stt("dve", nkbg[:, h, :], t1[:], -1.0, egc[:, h, :], ALU.mult, ALU.mult, r=t1.k() + egc.k(h), w=nkbg.k(h))
                P.tt("dve", qd[:, h, :], gq[:, h, :], egc[:, h, :], ALU.mult, r=gq.k(h) + egc.k(h), w=qd.k(h))
                P.tt("dve", c3(t1[:]), c3(gcbc[:, h, :])[:, :, 63:64].to_broadcast([128, NCH, 64]), c3(gcbc[:, h, :]), ALU.subtract,
                     r=gcbc.k(h), w=t1.k())
                P.act(t1[:], t1[:], AF.Exp, r=t1.k(), w=t1.k())
                P.tt("dve", kdec[:, h, :], gk[:, h, :], t1[:], ALU.mult, r=gk.k(h) + t1.k(), w=kdec.k(h))
            proj(P, ws, w_b, range(4, 16), KD, xb, N, pss, c_qkv)

            for c in range(NCH):
                cc = slice(c * C, (c + 1) * C)
                pSc, pTr, pL, pN, pX, pR, pO, pSt = pq[0], pq[1], pq[2], pq[3], pq[4], pq[5], pq[6], pq[7]
                P.tr(pTr[0:64, 0:128], gcf[:, cc], ident, r=gcf.k() + cs.k(), w=pTr.k())
                P.ts("dve", ngc[:], pTr[0:64, 0:128], -1.0, None, ALU.mult, None, r=pTr.k(), w=ngc.k())
                for h in range(4):
                    P.mm(pSc[0:64, h * 128:(h + 1) * 128], gk[:, h, cc], kq[:, h, c, :], True, True, r=gk.k(h) + kq.k(h), w=pSc.k())
                P.tt("dve", wd_[:], gcbc[0:64, :, cc], mneg_i.unsqueeze(1).to_broadcast([64, 4, 64]), ALU.add, r=gcbc.k() + cs.k(), w=wd_.k())
                for h in range(4):
                    P.act(Di[:, h, :], wd_[:, h, :], AF.Exp, r=wd_.k() + ngc.k(), w=Di.k(), bias=ngc[:, 4 + h:5 + h])
                P.tt("dve", Ds[:], Di[:], su_b4, ALU.mult, r=Di.k() + cs.k(), w=Ds.k())
                ps3 = pSc[0:64, :].rearrange("p (h d) -> p h d", h=4)
                P.stt("dve", Nall[:, 0:4, :], ps3[:, :, 0:64], -1.0, Ds[:], ALU.mult, ALU.mult, r=pSc.k() + Ds.k(), w=Nall.k())
                P.tt("dve", attT[:], ps3[:, :, 64:128], Di[:], ALU.mult, r=pSc.k() + Di.k(), w=attT.k())
                for h in range(4):
                    P.tr(pL[0:64, h * 64:(h + 1) * 64], Nall[:, h, :], ident[0:64, 0:64], r=Nall.k() + cs.k(), w=pL.k())
                P.cp("act", Lall[:, 0:4, :].rearrange("p h d -> p (h d)"), pL[0:64, 0:256], r=pL.k(), w=Lall.k())
                neumann(P, Tl(Nall.h[:, 0:4, :], Nall.name), Tl(Lall.h[:, 0:4, :], Lall.name), Tl(X.h[:, 0:4, :], X.name), pN, pL, pX, 4, ident64b4)
                for h in range(4):
                    P.tr(pTr[0:64, h * 128:(h + 1) * 128], kdec[:, h, cc], ident, r=kdec.k(h) + cs.k(), w=pTr.k())
                P.cp("act", kdtm[:].rearrange("p h d -> p (h d)"), pTr[0:64, :], r=pTr.k(), w=kdtm.k())
                rhs4 = rhs_sb[:].rearrange("p h d -> p (h d)")
                U4 = U[:].rearrange("p h d -> p (h d)")
                for h in range(4):
                    hs = slice(h * 128, (h + 1) * 128)
                    P.mm(pR[0:64, hs], vb[:, h, cc], ident, True, False, r=vb.k(h) + cs.k(), w=pR.k())
                    P.mm(pR[0:64, hs], nkbg[:, h, cc], gS[:, h, :], False, True, r=nkbg.k(h) + gS.k(), w=pR.k())
                P.cp("act", rhs4, pR[0:64, :], r=pR.k(), w=rhs_sb.k())
                for h in range(4):
                    hs = slice(h * 128, (h + 1) * 128)
                    P.mm(pO[0:64, hs], X[:, h, :], rhs4[:, hs], True, True, r=X.k() + rhs_sb.k(), w=pO.k())
                P.cp("act", U4, pO[0:64, :], r=pO.k(), w=U.k())
                for h in range(4):
                    hs = slice(h * 128, (h + 1) * 128)
                    P.mm(pR[:, h * 64:(h + 1) * 64], gS[:, h, :], qd[:, h, cc], True, True, r=gS.k() + qd.k(h), w=pR.k())
                    P.mm(pX[:, h * 64:(h + 1) * 64], U4[:, hs], attT[:, h, :], True, True, r=U.k() + attT.k(), w=pX.k())
                    P.mm(pSt[:, hs], kdtm[:, h, :], U4[:, hs], True, True, r=kdtm.k() + U.k(), w=pSt.k())
                t_ = gT[0]
                P.cp("act", t_[:, 0:256], pR[:, 0:256], r=pR.k(), w=t_.k())
                P.tt("dve", oall[:, :, cc], pX[:, 0:256].rearrange("p (h d) -> p h d", h=4), t_[:, 0:256].rearrange("p (h d) -> p h d", h=4), ALU.add,
                     r=pX.k() + t_.k(), w=oall.k())
                for h in range(4):
                    hs = slice(h * 128, (h + 1) * 128)
                    P.stt("dve", gS[:, h, :], gS[:, h, :], egc[:, h, c * C + C - 1:c * C + C], pSt[:, hs], ALU.mult, ALU.add,
                          r=gS.k() + egc.k(h) + pSt.k(), w=gS.k())

            def c_z(j, ct, ps):
                h = ct - 16
                zs, sq_ = gT[0], gT[1]
                pn = pq[4]
                P.act(zs[:], ps[:, :N], AF.Silu, r=ps.k(), w=zs.k())
                P.tt("pool", sq_[:], oall[:, h, :], oall[:, h, :], ALU.mult, r=oall.k(), w=sq_.k())
                P.mm(pn[:, :N], ones128m[:], sq_[:], True, True, r=ones128m.k() + sq_.k(), w=pn.k())
                P.act(sq_[:], pn[:, :N], AF.Sqrt, r=pn.k(), w=sq_.k(), bias=RMS_EPS)
                P.op("dve", lambda e: e.reciprocal(out=sq_[:], in_=sq_[:]), r=sq_.k(), w=sq_.k())
                P.stt("dve", sq_[:], oall[:, h, :], gvs[:, 2:3], sq_[:], ALU.mult, ALU.mult, r=oall.k() + gvs.k() + sq_.k(), w=sq_.k())
                o_ = ob[oi[0] % 2]
                oi[0] += 1
                P.tt("dve", o_[:], sq_[:], zs[:], ALU.mult, r=sq_.k() + zs.k(), w=o_.k())
                P.dma("act", yo.ap(slice(512 + h * 128, 512 + (h + 1) * 128), t0, N), o_[:], r=o_.k(), w=yo.kq(t0), sem=f"do{oi[0] % 2}")
            proj(P, ws, w_b, range(16, 20), KD, xb, N, pss, c_z)
            if io.get("post_blk"):
                io["post_blk"](P, blk)
        if io.get("post"):
            io["post"](P)
        P.emit()
    return P

DN_ALPHA = (2.0 * 2) ** 0.25
A_COLS = 3520


def pad128(a):
    o = np.zeros((128,) + a.shape[1:], np.float32)
    o[:a.shape[0]] = a
    return o


def prep_even(xb_, inp, hh):
    w = inp["even_w_in"][0]
    o = 512 * hh
    hv = lambda v: np.ascontiguousarray(v.reshape(-1, 64).T)
    cols = []
    for h in range(8):
        for which in range(3):
            cols.append(w[:, which * 1024 + o + h * 64: which * 1024 + o + (h + 1) * 64])
    w_a = relayout_w64(np.concatenate(cols, 1))
    wlo = np.zeros((2048, 128), np.float32); wlo[:, :96] = w[:, 3072:3168]
    alo = np.zeros((2048, 128), np.float32); alo[:, :96] = w[:, 3168:3264]
    glo = w[:, 3264:3520]
    gb = A_COLS
    q = w[:, gb + o: gb + o + 512]; k = w[:, gb + 1024 + o: gb + 1024 + o + 512]; v = w[:, gb + 2048 + o: gb + 2048 + o + 512]
    z = w[:, gb + 3072 + o: gb + 3072 + o + 512]
    ba = np.zeros((2048, 128), np.float32)
    ba[:, 0:4] = w[:, gb + 4096 + 4 * hh: gb + 4096 + 4 * hh + 4]; ba[:, 4:8] = w[:, gb + 4104 + 4 * hh: gb + 4104 + 4 * hh + 4]
    w_b = relayout_w(np.concatenate([wlo, alo, glo, q, k, v, z, ba], 1))
    mu = inp["rwkv_mu"][0]
    rv = np.stack([hv(mu[o:o + 512]), hv(mu[1024 + o:1024 + o + 512]), hv(mu[2048 + o:2048 + o + 512]),
                   hv(inp["rwkv_w0"][0][o:o + 512]), hv(inp["rwkv_a0"][0][o:o + 512]), hv(inp["rwkv_k_k"][0][o:o + 512]),
                   hv(inp["rwkv_k_a"][0][o:o + 512]), hv(inp["rwkv_r_k"][0].reshape(-1)[o:o + 512]),
                   hv(inp["rwkv_gn_g"][0].reshape(-1)[o:o + 512]), hv(inp["rwkv_gn_b"][0].reshape(-1)[o:o + 512])], 1)
    rmu = np.zeros((128, 4), np.float32)
    rmu[:96, 0] = mu[3072:3168]; rmu[:96, 1] = mu[3168:3264]; rmu[:, 2] = mu[3264:3392]; rmu[:, 3] = mu[3392:3520]
    rw2 = np.stack([pad128(inp["rwkv_w2"][0][:, o:o + 512]), pad128(inp["rwkv_a2"][0][:, o:o + 512]),
                    inp["rwkv_g2"][0][0:128, o:o + 512], inp["rwkv_g2"][0][128:256, o:o + 512]], 1)
    gc = inp["gdn_conv_w"][0]
    idx = np.concatenate([o + np.arange(512), 1024 + o + np.arange(512), 2048 + o + np.arange(512)])
    gcw = np.stack([relayout_v(gc[j, idx]) for j in range(4)], 1)
    gv = np.zeros((128, 4), np.float32)
    gv[4:8, 0] = inp["gdn_A_log"][0][4 * hh:4 * hh + 4]; gv[4:8, 1] = inp["gdn_dt_bias"][0][4 * hh:4 * hh + 4]
    gv[:, 2] = inp["gdn_norm_g"][0]
    return dict(xT=np.ascontiguousarray(xb_.T), w_a=w_a, w_b=w_b, cst=consts_e(), rv=np.ascontiguousarray(rv), rmu=rmu,
                rw2=np.ascontiguousarray(rw2), gcw=np.ascontiguousarray(gcw), gv=gv, sel=sel_np())


def prep_odd(xb_, inp, hh):
    w = inp["odd_w_in"][0]
    o = 1024 * hh
    zc = w[:, o:o + 1024]
    xs = w[:, 2048 + o:2048 + o + 1024]
    Bc = w[:, 4096 + 256 * hh:4096 + 256 * hh + 256]
    Cc = w[:, 4608 + 256 * hh:4608 + 256 * hh + 256]
    dtc = np.zeros((2048, 128), np.float32); dtc[:, :16] = w[:, 5120 + 16 * hh:5120 + 16 * hh + 16]
    yb = w[:, 5152 + o:5152 + o + 1024]
    xbr = w[:, 7200 + o:7200 + o + 1024]
    wc = np.concatenate([xs, Bc, Cc, dtc, zc, yb, xbr], 1)
    mc = inp["mamba_conv_w"][0]; mb = inp["mamba_conv_b"][0]
    idx = np.concatenate([np.arange(o, o + 1024), 2048 + 256 * hh + np.arange(256), 2560 + 256 * hh + np.arange(256)])
    mcw = np.stack([relayout_v(mc[j, idx]) for j in range(4)] + [relayout_v(mb[idx])], 1)
    mv = np.zeros((128, 4), np.float32)
    mv[:16, 0] = inp["mamba_dt_bias"][0][16 * hh:16 * hh + 16]; mv[:16, 1] = inp["mamba_A_log"][0][16 * hh:16 * hh + 16]
    Dexp = np.repeat(inp["mamba_D"][0][16 * hh:16 * hh + 16], 64)
    mdn = np.stack([relayout_v(Dexp), relayout_v(inp["mamba_norm_g"][0][o:o + 1024])], 1)
    lc = inp["lru_conv_w"][0][:, o:o + 1024]
    lcw = np.stack([relayout_v(lc[j]) for j in range(4)] + [relayout_v(inp["lru_conv_b"][0][o:o + 1024])], 1)
    lv = np.stack([relayout_v(inp[k][0][o:o + 1024]) for k in ("lru_ba", "lru_bx", "lru_lambda")], 1)
    return dict(xT=np.ascontiguousarray(xb_.T), w_in=relayout_w(wc), cst=consts_np(),
                mcw=np.ascontiguousarray(mcw), mv=mv, mdn=np.ascontiguousarray(mdn), lcw=np.ascontiguousarray(lcw),
                lv=np.ascontiguousarray(lv), lwa=np.ascontiguousarray(inp["lru_wa"][0][8 * hh:8 * hh + 8]),
                lwx=np.ascontiguousarray(inp["lru_wx"][0][8 * hh:8 * hh + 8]))


def prep_C_weights(inp, i, w_out):
    cw = inp["ffn_conv_w"][i]
    return dict(w_out=relayout_w(w_out), w_up=relayout_w(inp["ffn_up"][i]), w_down=relayout_w(inp["ffn_down"][i]),
                w_gate=relayout_w(inp["ple_gate_w"][i]), w_ple=relayout_w(inp["ple_proj"][i]),
                vecs=np.ascontiguousarray(np.stack([relayout_v(inp[k][i]) for k in ("ln1_g", "ln1_b", "ln2_g", "ln2_b", "ple_norm_g", "ple_gate_b")], 1)),
                cvw=np.ascontiguousarray(np.stack([relayout_v(v) for v in (cw[0], cw[1], cw[2], inp["ffn_conv_b"][i])], 1)))


GROUPS = [[0, 1], [2, 3], [4, 5], [6, 7]]


def build_all(shapes, T=4096, TH=2048):
    import ml_dtypes
    nc = bass.Bass("TRN2", target_bir_lowering=False)
    D = 2048
    with ExitStack() as outer:
        din = {}
        for name, (shp, dt) in shapes.items():
            bdt = BF16 if dt == ml_dtypes.bfloat16 else F32
            din[name] = Tl(nc.dram_tensor(name, list(shp), bdt, kind="ExternalInput").ap(), name)
        def internal(name, shp, dt):
            return Tl(nc.dram_tensor(name, list(shp), dt, kind="Internal").ap(), name)
        def chunked(name, rows, cols, dt, W):
            return ChunkT([internal(f"{name}_{q}", [rows, W], dt) for q in range(cols // W)], W)
        ysnd0 = chunked("ysnd0", 1024, T, BF16, 1024)
        yrcv0 = chunked("yrcv0", 2048, T, BF16, 1024)
        ysnd1 = chunked("ysnd1", 2048, T, BF16, 512)
        yrcv1 = chunked("yrcv1", 4096, T, BF16, 512)
        NQ = TH // 512
        x1snd = [[internal(f"x1snd_{h}_{q}", [1024, 512], F32) for q in range(NQ)] for h in range(2)]
        x1rcv = [[internal(f"x1rcv_{h}_{q}", [2048, 512], F32) for q in range(NQ)] for h in range(2)]
        x1snd_k = [k for h in range(2) for c in x1snd[h] for k in c.k()]
        x1rcv_k = [k for h in range(2) for c in x1rcv[h] for k in c.k()]
        xo = Tl(nc.dram_tensor("xoT", [D, TH], F32, kind="ExternalOutput").ap(), "xoT")
        TBC = 512
        import os
        PH = os.environ.get("PHASES", "E,C0,O,C1").split(",")

        def gather1(P, s_, r_):
            P.coll("AllGather", s_[:], r_[:], GROUPS, s_.k(), r_.k() + [("collchain", 0)], "dcc")

        io = {k[2:]: v for k, v in din.items() if k.startswith("e_")}
        io.update(yo=ysnd0, xsrc_k=[], xsrc=lambda t0, hf: din["xT"][hf * 1024:(hf + 1) * 1024, t0:t0 + 256],
                  post=lambda P: [gather1(P, a_, b_) for a_, b_ in zip(ysnd0.chunks, yrcv0.chunks)])
        if "E" in PH:
            phase_E(nc, outer, io, T)
        nc.all_engine_barrier()
        io = {k[3:]: v for k, v in din.items() if k.startswith("c0_")}
        io.update(yrcv=yrcv0, pT=din["pT0"], msk=din["msk"], xsrc_k=[], xdst_k=x1snd_k,
                  xsrc=lambda blk: [(0, 16, din["xTc"][:, 0:2] if blk < 0 else din["xTc"][:, 2 + blk * TBC:2 + (blk + 1) * TBC])],
                  xdst=lambda blk: [(8 * h, 8 * h + 8, x1snd[h][blk][:, :]) for h in range(2)],
                  xdst_kf=lambda blk: x1snd[0][blk].k() + x1snd[1][blk].k(),
                  post=lambda P: [gather1(P, x1snd[h][q], x1rcv[h][q]) for h in range(2) for q in range(NQ)])
        if "C0" in PH:
            phase_C(nc, outer, io, TH // TBC, 2048, TB=TBC, alpha=DN_ALPHA, THALF=TH)
        nc.all_engine_barrier()
        io = {k[2:]: v for k, v in din.items() if k.startswith("o_")}
        io.update(yo=ysnd1, xsrc_k=[],
                  xsrc=lambda t0, hf: x1rcv[hf][(t0 % TH) // 512][(t0 // TH) * 1024:(t0 // TH + 1) * 1024, :],
                  post=lambda P: [gather1(P, a_, b_) for a_, b_ in zip(ysnd1.chunks, yrcv1.chunks)])
        if "O" in PH:
            phase_O(nc, outer, io, T)
        nc.all_engine_barrier()
        io = {k[3:]: v for k, v in din.items() if k.startswith("c1_")}
        io.update(yrcv=yrcv1, pT=din["pT1"], msk=din["msk"], xsrc_k=[], xdst_k=xo.k(),
                  xsrc=lambda blk: [(8 * h, 8 * h + 8, x1rcv[h][NQ - 1][0:1024, 510:512] if blk < 0 else x1snd[h][blk][:, :]) for h in range(2)],
                  xdst=lambda blk: [(0, 16, xo[:, blk * TBC:(blk + 1) * TBC])])
        if "C1" in PH:
            phase_C(nc, outer, io, TH // TBC, 4096, TB=TBC, alpha=DN_ALPHA, THALF=TH)
    return nc


def kernel(**inp):
    inp = {k: np.asarray(v, dtype=np.float32) for k, v in inp.items()}
    x = inp["x"]
    p = inp["p"]
    B, T, D = x.shape
    TH = T // 2
    cores = list(range(8))
    perm_e = np.concatenate([np.arange(0, 512), np.arange(1024, 1536), np.arange(512, 1024), np.arange(1536, 2048)])
    perm_o = np.concatenate([np.arange(0, 1024), np.arange(2048, 3072), np.arange(1024, 2048), np.arange(3072, 4096)])
    c0 = prep_C_weights(inp, 0, inp["even_w_out"][0][perm_e])
    c1 = prep_C_weights(inp, 1, inp["odd_w_out"][0][perm_o])
    ins = []
    for c in cores:
        b, h = c // 2, c % 2
        m = {}
        e = prep_even(x[b], inp, h)
        m["xT"] = e.pop("xT")
        m.update({"e_" + k: v for k, v in e.items()})
        o = prep_odd(x[b][:8], inp, h)
        o.pop("xT")
        m.update({"o_" + k: v for k, v in o.items()})
        m.update({"c0_" + k: v for k, v in c0.items()})
        m.update({"c1_" + k: v for k, v in c1.items()})
        xTc = np.zeros((D, 2 + TH), np.float32)
        xTc[:, 2:] = x[b, h * TH:(h + 1) * TH].T
        if h > 0:
            xTc[:, :2] = x[b, TH - 2:TH].T
        m["xTc"] = xTc
        m["pT0"] = np.ascontiguousarray(p[0, b, h * TH:(h + 1) * TH].T)
        m["pT1"] = np.ascontiguousarray(p[1, b, h * TH:(h + 1) * TH].T)
        msk = np.zeros((128, 3), np.float32)
        msk[:, 0] = float(h > 0); msk[:, 1] = float(h == 0); msk[:, 2] = float(h == 1)
        m["msk"] = msk
        ins.append(m)
    shapes = {k: (v.shape, v.dtype) for k, v in ins[0].items()}
    nc = build_all(shapes, T=T, TH=TH)
    res = run_bass_kernel_spmd(nc, ins, core_ids=cores)
    out = np.empty_like(x)
    for c in cores:
        out[c // 2, (c % 2) * TH:(c % 2 + 1) * TH] = res.results[c]["xoT"].T
    return out
```
